# Optimizing a Trainium2 kernel written in Bass

```python
import math
import jax, jax.numpy as jnp
from jax import lax
import numpy as np

D_MODEL = 1024
BATCH = 8
SEQ = 2048
DEPTH = 2
DEC_BATCH = 128
DEC_SEQ = 4
PAST_LEN = 16384
PAGE_SIZE = 128

MIX_WIDTH = D_MODEL
GROUP_WIDTH = MIX_WIDTH // 4
SSD_HEAD_DIM = 64
SSD_HEADS = GROUP_WIDTH // SSD_HEAD_DIM
SSD_GROUPS = 2
SSD_STATE = 128
SSD_CONV = 4
SSD_CHUNK = 128
SSD_CONV_DIM = GROUP_WIDTH + 2 * SSD_GROUPS * SSD_STATE
SSD_PROJ = GROUP_WIDTH + SSD_CONV_DIM + SSD_HEADS
RWKV_HEAD = 64
RWKV_HEADS = GROUP_WIDTH // RWKV_HEAD
RWKV_DECAY_LORA = 64
RWKV_ICLR_LORA = 64
RWKV_GATE_LORA = 128
RWKV_PROJ = 3 * GROUP_WIDTH + RWKV_DECAY_LORA + RWKV_ICLR_LORA + RWKV_GATE_LORA
RWKV_LN_EPS = 64e-5
S5_GROUP_CH = 16
S5_GROUPS = GROUP_WIDTH // S5_GROUP_CH
S5_STATE = 64
POOL_WINDOWS = (2, 4, 8, 16)
POOL_GROUPS = len(POOL_WINDOWS)
POOL_CH = GROUP_WIDTH // POOL_GROUPS
POOL_BUF = max(POOL_WINDOWS) - 1
IN_PROJ = SSD_PROJ + RWKV_PROJ + 2 * GROUP_WIDTH
D_FF = 2816
RMS_EPS = 1e-6

kernel_name = 'hymba_ssd_rwkv7_s5_pool_macaron_step'

F32 = jnp.float32


def _split_at(x, sizes):
    idx = [int(s) for s in np.cumsum(sizes)[:-1]]
    return jnp.split(x, idx, axis=-1)


def _rmsnorm(x, g):
    xf = x.astype(F32)
    y = xf * lax.rsqrt(jnp.mean(xf * xf, axis=-1, keepdims=True) + RMS_EPS)
    return (y * g.astype(F32)).astype(x.dtype)


def _swiglu(x, w_in, w_out):
    gate, up = jnp.split(x @ w_in, 2, axis=-1)
    return (jax.nn.silu(gate) * up) @ w_out


def _causal_dwconv(u, buf, w, b):
    k_w = w.shape[0]
    t = u.shape[1]
    full = jnp.concatenate([buf.astype(u.dtype), u], axis=1)
    out = b + full[:, 0:t] * w[0]
    for j in range(1, k_w):
        out = out + full[:, j:j + t] * w[j]
    return out, full[:, -(k_w - 1):]


def _ssd_scan(x, dt, a_neg, bm, cm, h0):
    bsz, t, nh, hp = x.shape
    n = bm.shape[-1]
    L = min(SSD_CHUNK, t)
    nc = -(-t // L)
    pad = nc * L - t
    if pad:
        padf = lambda z: jnp.pad(z, [(0, 0), (0, pad)] + [(0, 0)] * (z.ndim - 2))
        x, dt, bm, cm = padf(x), padf(dt), padf(bm), padf(cm)
    x = x.reshape(bsz, nc, L, nh, hp)
    dt = dt.reshape(bsz, nc, L, nh)
    bm = bm.reshape(bsz, nc, L, nh, n)
    cm = cm.reshape(bsz, nc, L, nh, n)
    acs = jnp.cumsum(dt * a_neg, axis=2)
    seg = acs[:, :, :, None, :] - acs[:, :, None, :, :]
    causal = jnp.tril(jnp.ones((L, L), dtype=bool))[None, None, :, :, None]
    decay = jnp.exp(jnp.where(causal, seg, -jnp.inf))
    xdt = x * dt[..., None]
    scores = jnp.einsum('bcihn,bcjhn->bcijh', cm, bm) * decay
    y_diag = jnp.einsum('bcijh,bcjhp->bcihp', scores, xdt)
    decay_end = jnp.exp(acs[:, :, -1:, :] - acs)
    chunk_states = jnp.einsum('bclhn,bclh,bclhp->bchpn', bm, decay_end, xdt)
    chunk_decay = jnp.exp(acs[:, :, -1, :])

    def step(h, inp):
        s_c, d_c = inp
        return h * d_c[:, :, None, None] + s_c, h

    h_last, h_prev = lax.scan(step, h0, (jnp.moveaxis(chunk_states, 1, 0), jnp.moveaxis(chunk_decay, 1, 0)))
    h_prev = jnp.moveaxis(h_prev, 0, 1)
    y_off = jnp.einsum('bclhn,bchpn->bclhp', cm, h_prev) * jnp.exp(acs)[..., None]
    y = (y_diag + y_off).reshape(bsz, nc * L, nh, hp)[:, :t]
    return y, h_last


def _ssd_mixer(u, conv_buf, h0, conv_w, conv_b, dt_bias, a_log, d_skip, norm_g):
    bsz, t, _ = u.shape
    z, xbc, dt_raw = _split_at(u, (GROUP_WIDTH, SSD_CONV_DIM, SSD_HEADS))
    xbc, new_buf = _causal_dwconv(xbc, conv_buf, conv_w, conv_b)
    xbc = jax.nn.silu(xbc.astype(F32))
    xs, bm, cm = _split_at(xbc, (GROUP_WIDTH, SSD_GROUPS * SSD_STATE, SSD_GROUPS * SSD_STATE))
    rep = SSD_HEADS // SSD_GROUPS
    xs = xs.reshape(bsz, t, SSD_HEADS, SSD_HEAD_DIM)
    bm = jnp.repeat(bm.reshape(bsz, t, SSD_GROUPS, SSD_STATE), rep, axis=2)
    cm = jnp.repeat(cm.reshape(bsz, t, SSD_GROUPS, SSD_STATE), rep, axis=2)
    dt = jax.nn.softplus(dt_raw.astype(F32) + dt_bias.astype(F32))
    a_neg = -jnp.exp(a_log.astype(F32))
    y, h_last = _ssd_scan(xs, dt, a_neg, bm, cm, h0.astype(F32))
    y = (y + xs * d_skip.astype(F32)[:, None]).reshape(bsz, t, GROUP_WIDTH)
    y = _rmsnorm(y * jax.nn.silu(z.astype(F32)), norm_g)
    return y.astype(u.dtype), new_buf.astype(conv_buf.dtype), h_last.astype(h0.dtype)


def _rwkv_mixer(u, shift_prev, s0, mu, w0, w2, a0, a2, g2, k_k, k_a, r_k, ln_g, ln_b):
    bsz, t, _ = u.shape
    uf = u.astype(F32)
    prev = jnp.concatenate([shift_prev[:, None, :].astype(F32), uf[:, :-1]], axis=1)
    xs = uf + (prev - uf) * mu.astype(F32)
    r, k, v, wd, ad, gd = _split_at(xs, (GROUP_WIDTH, GROUP_WIDTH, GROUP_WIDTH, RWKV_DECAY_LORA, RWKV_ICLR_LORA, RWKV_GATE_LORA))
    w_raw = -jax.nn.softplus(-(w0.astype(F32) + jnp.tanh(wd) @ w2.astype(F32))) - 0.5
    decay = jnp.exp(-jnp.exp(w_raw))
    a = jax.nn.sigmoid(a0.astype(F32) + ad @ a2.astype(F32))
    g = jax.nn.sigmoid(gd) @ g2.astype(F32)
    hs = lambda z: z.reshape(bsz, t, RWKV_HEADS, RWKV_HEAD)
    r, k, v, decay, a = hs(r), hs(k), hs(v), hs(decay), hs(a)
    kk = k * k_k.astype(F32).reshape(RWKV_HEADS, RWKV_HEAD)
    kk = kk / jnp.maximum(jnp.sqrt(jnp.sum(kk * kk, axis=-1, keepdims=True)), 1e-12)
    k = k * (1.0 + (a - 1.0) * k_a.astype(F32).reshape(RWKV_HEADS, RWKV_HEAD))

    def step(s, inp):
        r_t, w_t, k_t, v_t, kk_t, b_t = inp
        sa = jnp.einsum('bhij,bhj->bhi', s, -kk_t)
        s = s * w_t[:, :, None, :] + v_t[..., None] * k_t[:, :, None, :] + sa[..., None] * b_t[:, :, None, :]
        return s, jnp.einsum('bhij,bhj->bhi', s, r_t)

    tm = lambda z: jnp.moveaxis(z, 1, 0)
    s_last, y = lax.scan(step, s0.astype(F32), (tm(r), tm(decay), tm(k), tm(v), tm(kk), tm(kk * a)))
    y = jnp.moveaxis(y, 0, 1)
    mean = jnp.mean(y, axis=-1, keepdims=True)
    var = jnp.mean(jnp.square(y - mean), axis=-1, keepdims=True)
    y = ((y - mean) * lax.rsqrt(var + RWKV_LN_EPS)).reshape(bsz, t, GROUP_WIDTH)
    y = y * ln_g.astype(F32) + ln_b.astype(F32)
    bonus = jnp.sum(r * k * r_k.astype(F32), axis=-1, keepdims=True) * v
    y = (y + bonus.reshape(bsz, t, GROUP_WIDTH)) * g
    return y.astype(u.dtype), u[:, -1].astype(shift_prev.dtype), s_last.astype(s0.dtype)


def _s5_mixer(u, h0_re, h0_im, lam_re, lam_im, log_step, b_re, b_im, c_re, c_im, d_skip, glu_w, glu_b):
    bsz, t, _ = u.shape
    uf = u.astype(F32).reshape(bsz, t, S5_GROUPS, S5_GROUP_CH)
    lam = lax.complex(lam_re.astype(F32), lam_im.astype(F32))
    delta = jnp.exp(log_step.astype(F32))[:, None]
    a_bar = jnp.exp(lam * delta)
    b_bar = ((a_bar - 1.0) / lam)[..., None] * lax.complex(b_re.astype(F32), b_im.astype(F32))
    bu = jnp.einsum('gnc,btgc->btgn', b_bar, uf.astype(jnp.complex64))
    h0 = lax.complex(h0_re.astype(F32), h0_im.astype(F32))
    bu = bu.at[:, 0].add(a_bar * h0)
    a_seq = jnp.broadcast_to(a_bar, bu.shape)

    def combine(e1, e2):
        a1, b1 = e1
        a2, b2 = e2
        return a1 * a2, a2 * b1 + b2

    _, h = lax.associative_scan(combine, (a_seq, bu), axis=1)
    c_mat = lax.complex(c_re.astype(F32), c_im.astype(F32))
    y = jnp.real(jnp.einsum('gcn,btgn->btgc', c_mat, h)) + uf * d_skip.astype(F32).reshape(S5_GROUPS, S5_GROUP_CH)
    y = jax.nn.gelu(y.reshape(bsz, t, GROUP_WIDTH)).astype(u.dtype)
    ya, yb = jnp.split(y @ glu_w + glu_b, 2, axis=-1)
    out = ya * jax.nn.sigmoid(yb)
    h_last = h[:, -1]
    return out.astype(u.dtype), jnp.real(h_last).astype(h0_re.dtype), jnp.imag(h_last).astype(h0_im.dtype)


def _pool_mixer(u, buf, pos0, pool_w, pool_scale):
    bsz, t, _ = u.shape
    full = jnp.concatenate([buf.astype(u.dtype), u], axis=1)
    cs = jnp.pad(jnp.cumsum(full.astype(F32), axis=1), ((0, 0), (1, 0), (0, 0)))
    pos = (pos0 + jnp.arange(t)).astype(F32)
    outs = []
    for gi, w in enumerate(POOL_WINDOWS):
        sl = slice(gi * POOL_CH, (gi + 1) * POOL_CH)
        end = cs[:, POOL_BUF + 1:POOL_BUF + 1 + t, sl]
        start = cs[:, POOL_BUF + 1 - w:POOL_BUF + 1 - w + t, sl]
        cnt = jnp.minimum(pos + 1.0, float(w))[None, :, None]
        outs.append((end - start) / cnt)
    pooled = jnp.concatenate(outs, axis=-1) - u.astype(F32)
    pooled = pooled.reshape(bsz, t, POOL_GROUPS, POOL_CH)
    y = jnp.einsum('btgc,gcd->btgd', pooled, pool_w.astype(F32)).reshape(bsz, t, GROUP_WIDTH)
    y = y * pool_scale.astype(F32)
    return y.astype(u.dtype), full[:, -POOL_BUF:].astype(buf.dtype)


def _layer(x, pos0, st, p):
    conv_buf, ssd_h, rwkv_shift, rwkv_s, s5_re, s5_im, pool_buf = st
    x = x + 0.5 * _swiglu(_rmsnorm(x, p['norm_ffn1']), p['ffn1_in'], p['ffn1_out'])
    u = _rmsnorm(x, p['norm_mix']) @ p['w_in']
    u_ssd, u_rwkv, u_s5, u_pool = _split_at(u, (SSD_PROJ, RWKV_PROJ, GROUP_WIDTH, GROUP_WIDTH))
    y_ssd, conv_buf, ssd_h = _ssd_mixer(u_ssd, conv_buf, ssd_h, p['ssd_conv_w'], p['ssd_conv_b'], p['ssd_dt_bias'], p['ssd_a_log'], p['ssd_d'], p['ssd_norm'])
    y_rwkv, rwkv_shift, rwkv_s = _rwkv_mixer(u_rwkv, rwkv_shift, rwkv_s, p['rwkv_mu'], p['rwkv_w0'], p['rwkv_w2'], p['rwkv_a0'], p['rwkv_a2'], p['rwkv_g2'], p['rwkv_k_k'], p['rwkv_k_a'], p['rwkv_r_k'], p['rwkv_ln_g'], p['rwkv_ln_b'])
    y_s5, s5_re, s5_im = _s5_mixer(u_s5, s5_re, s5_im, p['s5_lam_re'], p['s5_lam_im'], p['s5_log_step'], p['s5_b_re'], p['s5_b_im'], p['s5_c_re'], p['s5_c_im'], p['s5_d'], p['s5_glu_w'], p['s5_glu_b'])
    y_pool, pool_buf = _pool_mixer(u_pool, pool_buf, pos0, p['pool_w'], p['pool_scale'])
    y = jnp.concatenate([y_ssd, y_rwkv, y_s5, y_pool], axis=-1).astype(x.dtype)
    x = x + y @ p['w_out']
    x = x + 0.5 * _swiglu(_rmsnorm(x, p['norm_ffn2']), p['ffn2_in'], p['ffn2_out'])
    return x, (conv_buf, ssd_h, rwkv_shift, rwkv_s, s5_re, s5_im, pool_buf)


def _trunk(x, pos0, states, params, norm_final):
    per_layer = []
    for l in range(DEPTH):
        pl = {name: arr[l] for name, arr in params.items()}
        x, st = _layer(x, pos0, [s[l] for s in states], pl)
        per_layer.append(st)
    new_states = [jnp.stack([st[i] for st in per_layer]) for i in range(len(states))]
    return _rmsnorm(x, norm_final), new_states


def _normal(k, shape, scale):
    return scale * jax.random.normal(k, shape, F32)


def setup_inputs(seed: int = 0) -> dict:
    key = jax.random.key(seed)
    keys = list(jax.random.split(key, 80))
    nk = keys.pop
    D, GW, L = D_MODEL, GROUP_WIDTH, DEPTH
    ones = lambda shape, s=0.02: 1.0 + _normal(nk(), shape, s)
    dt0 = jnp.exp(jax.random.uniform(nk(), (L, SSD_HEADS), F32, math.log(1e-3), math.log(1e-1)))
    w0_base = -5.5 + 5.0 * jnp.linspace(0.0, 1.0, GW, dtype=F32)
    inp = {
        'x_prompt': _normal(nk(), (BATCH, SEQ, D), 1.0),
        'x_sample': _normal(nk(), (DEC_BATCH, DEC_SEQ, D), 1.0),
        'state_ssd_conv': _normal(nk(), (L, DEC_BATCH, SSD_CONV - 1, SSD_CONV_DIM), 1.0),
        'state_ssd': _normal(nk(), (L, DEC_BATCH, SSD_HEADS, SSD_HEAD_DIM, SSD_STATE), 0.1),
        'state_rwkv_shift': _normal(nk(), (L, DEC_BATCH, RWKV_PROJ), 1.0),
        'state_rwkv': _normal(nk(), (L, DEC_BATCH, RWKV_HEADS, RWKV_HEAD, RWKV_HEAD), 0.3),
        'state_s5_re': _normal(nk(), (L, DEC_BATCH, S5_GROUPS, S5_STATE), 0.3),
        'state_s5_im': _normal(nk(), (L, DEC_BATCH, S5_GROUPS, S5_STATE), 0.3),
        'state_pool': _normal(nk(), (L, DEC_BATCH, POOL_BUF, GW), 1.0),
        'norm_ffn1': ones((L, D)),
        'ffn1_in': _normal(nk(), (L, D, 2 * D_FF), D ** -0.5),
        'ffn1_out': _normal(nk(), (L, D_FF, D), D_FF ** -0.5),
        'norm_mix': ones((L, D)),
        'w_in': _normal(nk(), (L, D, IN_PROJ), D ** -0.5),
        'ssd_conv_w': _normal(nk(), (L, SSD_CONV, SSD_CONV_DIM), SSD_CONV ** -0.5),
        'ssd_conv_b': _normal(nk(), (L, SSD_CONV_DIM), 0.02),
        'ssd_dt_bias': dt0 + jnp.log(-jnp.expm1(-dt0)),
        'ssd_a_log': jnp.log(jax.random.uniform(nk(), (L, SSD_HEADS), F32, 1.0, 16.0)),
        'ssd_d': ones((L, SSD_HEADS)),
        'ssd_norm': ones((L, GW)),
        'rwkv_mu': jax.random.uniform(nk(), (L, RWKV_PROJ), F32),
        'rwkv_w0': w0_base + _normal(nk(), (L, GW), 0.1),
        'rwkv_w2': _normal(nk(), (L, RWKV_DECAY_LORA, GW), 0.1 * RWKV_DECAY_LORA ** -0.5),
        'rwkv_a0': _normal(nk(), (L, GW), 0.1),
        'rwkv_a2': _normal(nk(), (L, RWKV_ICLR_LORA, GW), 0.1 * RWKV_ICLR_LORA ** -0.5),
        'rwkv_g2': _normal(nk(), (L, RWKV_GATE_LORA, GW), RWKV_GATE_LORA ** -0.5),
        'rwkv_k_k': 0.85 + _normal(nk(), (L, GW), 0.02),
        'rwkv_k_a': ones((L, GW)),
        'rwkv_r_k': -0.04 + _normal(nk(), (L, RWKV_HEADS, RWKV_HEAD), 0.1),
        'rwkv_ln_g': ones((L, GW)),
        'rwkv_ln_b': _normal(nk(), (L, GW), 0.02),
        's5_lam_re': -0.5 + _normal(nk(), (L, S5_GROUPS, S5_STATE), 0.01),
        's5_lam_im': math.pi * jnp.arange(S5_STATE, dtype=F32) + _normal(nk(), (L, S5_GROUPS, S5_STATE), 0.01),
        's5_log_step': jax.random.uniform(nk(), (L, S5_GROUPS), F32, math.log(1e-3), math.log(1e-1)),
        's5_b_re': _normal(nk(), (L, S5_GROUPS, S5_STATE, S5_GROUP_CH), (2 * S5_GROUP_CH) ** -0.5),
        's5_b_im': _normal(nk(), (L, S5_GROUPS, S5_STATE, S5_GROUP_CH), (2 * S5_GROUP_CH) ** -0.5),
        's5_c_re': _normal(nk(), (L, S5_GROUPS, S5_GROUP_CH, S5_STATE), (2 * S5_STATE) ** -0.5),
        's5_c_im': _normal(nk(), (L, S5_GROUPS, S5_GROUP_CH, S5_STATE), (2 * S5_STATE) ** -0.5),
        's5_d': _normal(nk(), (L, GW), 1.0),
        's5_glu_w': _normal(nk(), (L, GW, 2 * GW), GW ** -0.5),
        's5_glu_b': _normal(nk(), (L, 2 * GW), 0.02),
        'pool_w': _normal(nk(), (L, POOL_GROUPS, POOL_CH, POOL_CH), POOL_CH ** -0.5),
        'pool_scale': ones((L, GW)),
        'w_out': _normal(nk(), (L, MIX_WIDTH, D), MIX_WIDTH ** -0.5),
        'norm_ffn2': ones((L, D)),
        'ffn2_in': _normal(nk(), (L, D, 2 * D_FF), D ** -0.5),
        'ffn2_out': _normal(nk(), (L, D_FF, D), D_FF ** -0.5),
        'norm_final': ones((D,)),
    }
    return inp


def reference(x_prompt, x_sample, state_ssd_conv, state_ssd, state_rwkv_shift, state_rwkv, state_s5_re, state_s5_im, state_pool,
              norm_ffn1, ffn1_in, ffn1_out, norm_mix, w_in, ssd_conv_w, ssd_conv_b, ssd_dt_bias, ssd_a_log, ssd_d, ssd_norm,
              rwkv_mu, rwkv_w0, rwkv_w2, rwkv_a0, rwkv_a2, rwkv_g2, rwkv_k_k, rwkv_k_a, rwkv_r_k, rwkv_ln_g, rwkv_ln_b,
              s5_lam_re, s5_lam_im, s5_log_step, s5_b_re, s5_b_im, s5_c_re, s5_c_im, s5_d, s5_glu_w, s5_glu_b,
              pool_w, pool_scale, w_out, norm_ffn2, ffn2_in, ffn2_out, norm_final):
    params = {
        'norm_ffn1': norm_ffn1, 'ffn1_in': ffn1_in, 'ffn1_out': ffn1_out, 'norm_mix': norm_mix, 'w_in': w_in,
        'ssd_conv_w': ssd_conv_w, 'ssd_conv_b': ssd_conv_b, 'ssd_dt_bias': ssd_dt_bias, 'ssd_a_log': ssd_a_log,
        'ssd_d': ssd_d, 'ssd_norm': ssd_norm,
        'rwkv_mu': rwkv_mu, 'rwkv_w0': rwkv_w0, 'rwkv_w2': rwkv_w2, 'rwkv_a0': rwkv_a0, 'rwkv_a2': rwkv_a2,
        'rwkv_g2': rwkv_g2, 'rwkv_k_k': rwkv_k_k, 'rwkv_k_a': rwkv_k_a, 'rwkv_r_k': rwkv_r_k,
        'rwkv_ln_g': rwkv_ln_g, 'rwkv_ln_b': rwkv_ln_b,
        's5_lam_re': s5_lam_re, 's5_lam_im': s5_lam_im, 's5_log_step': s5_log_step, 's5_b_re': s5_b_re,
        's5_b_im': s5_b_im, 's5_c_re': s5_c_re, 's5_c_im': s5_c_im, 's5_d': s5_d, 's5_glu_w': s5_glu_w,
        's5_glu_b': s5_glu_b, 'pool_w': pool_w, 'pool_scale': pool_scale, 'w_out': w_out,
        'norm_ffn2': norm_ffn2, 'ffn2_in': ffn2_in, 'ffn2_out': ffn2_out,
    }
    b = x_prompt.shape[0]
    sd = state_ssd.dtype
    prompt_states = [
        jnp.zeros((DEPTH, b, SSD_CONV - 1, SSD_CONV_DIM), sd),
        jnp.zeros((DEPTH, b, SSD_HEADS, SSD_HEAD_DIM, SSD_STATE), sd),
        jnp.zeros((DEPTH, b, RWKV_PROJ), sd),
        jnp.zeros((DEPTH, b, RWKV_HEADS, RWKV_HEAD, RWKV_HEAD), sd),
        jnp.zeros((DEPTH, b, S5_GROUPS, S5_STATE), sd),
        jnp.zeros((DEPTH, b, S5_GROUPS, S5_STATE), sd),
        jnp.zeros((DEPTH, b, POOL_BUF, GROUP_WIDTH), sd),
    ]
    sample_states = [state_ssd_conv, state_ssd, state_rwkv_shift, state_rwkv, state_s5_re, state_s5_im, state_pool]
    y_prompt, (conv_p, ssd_p, shift_p, rwkv_p, s5re_p, s5im_p, pool_p) = _trunk(x_prompt, 0, prompt_states, params, norm_final)
    y_sample, (conv_s, ssd_s, shift_s, rwkv_s, s5re_s, s5im_s, pool_s) = _trunk(x_sample, PAST_LEN, sample_states, params, norm_final)
    return (y_prompt, y_sample, conv_p, conv_s, ssd_p, ssd_s, shift_p, shift_s, rwkv_p, rwkv_s, s5re_p, s5re_s, s5im_p, s5im_s, pool_p, pool_s)
```

```python
import contextlib
import numpy as np
import concourse.bass as bass
import concourse.mybir as mybir
from concourse.bass_utils import run_bass_kernel_spmd

F32 = mybir.dt.float32
BF16 = mybir.dt.bfloat16
AF = mybir.ActivationFunctionType
ALU = mybir.AluOpType
AX = mybir.AxisListType

NCORES = 8
D = 1024
TP = 2048
NSEQ = 16
TS = 4
TSAMP = NSEQ * TS
T = TP + TSAMP
DFF = 2816
NFF = DFF // 128
INP = 2564
DEPTH = 2
TILES = [(0, 512), (512, 512), (1024, 512), (1536, 512), (2048, 64)]


class Op:
    __slots__ = ("idx", "eng", "fn", "deps", "is_dma", "group", "marked", "count", "pos")

    def __init__(self, idx, eng, fn, is_dma, group):
        self.idx = idx
        self.eng = eng
        self.fn = fn
        self.deps = set()
        self.is_dma = is_dma
        self.group = group
        self.marked = False
        self.count = 0
        self.pos = 0


class Prog:
    def __init__(self, nc, stack):
        self.nc = nc
        self.stack = stack
        self.ops = []
        self.last_writer = {}
        self.readers = {}
        self.eng_obj = {"pe": nc.tensor, "act": nc.scalar, "dve": nc.vector,
                        "pool": nc.gpsimd, "sp": nc.sync}
        self.cnt = {}
        self.sems = {}
        self.waited = {}
        self.pos = {}
        self.emitted = 0
        self.barrier_req = None
        self.nwait = 0

    def sb(self, name, shape, dtype=F32, stack=None):
        self.uid = getattr(self, "uid", 0) + 1
        return (stack or self.stack).enter_context(self.nc.sbuf_tensor("%s_%d" % (name, self.uid), list(shape), dtype))

    def ps(self, name, shape, dtype=F32):
        return self.stack.enter_context(self.nc.psum_tensor(name, list(shape), dtype))

    def add(self, eng, fn, reads=(), writes=(), group=None):
        is_dma = group is not None
        op = Op(len(self.ops), eng, fn, is_dma, group)
        for r in reads:
            w = self.last_writer.get(r)
            if w is not None:
                op.deps.add(w)
            self.readers.setdefault(r, []).append(op.idx)
        for wr in writes:
            w = self.last_writer.get(wr)
            if w is not None:
                op.deps.add(w)
            for rd in self.readers.get(wr, ()):
                if rd != op.idx:
                    op.deps.add(rd)
            self.last_writer[wr] = op.idx
            self.readers[wr] = []
        self.ops.append(op)
        return op

    def pe(self, fn, reads=(), writes=()):
        return self.add("pe", fn, reads, writes)

    def act(self, fn, reads=(), writes=()):
        return self.add("act", fn, reads, writes)

    def dve(self, fn, reads=(), writes=()):
        return self.add("dve", fn, reads, writes)

    def pool(self, fn, reads=(), writes=()):
        return self.add("pool", fn, reads, writes)

    def dma(self, fn, reads=(), writes=(), group=None, q="sp"):
        if group is None:
            group = writes[0] if writes else "st:" + reads[0]
        return self.add(q, fn, reads, writes, group=group)

    def _sem(self, key):
        s = self.sems.get(key)
        if s is None:
            s = self.stack.enter_context(self.nc.semaphore("s_%s_%s" % key))
            self.sems[key] = s
        return s

    def flush(self, barrier=True):
        ops = self.ops
        batch = ops[self.emitted:]
        first = self.emitted
        for op in batch:
            p = self.pos.get(op.eng, 0)
            op.pos = p
            self.pos[op.eng] = p + 1
        need = []
        for op in batch:
            lst = []
            for d in op.deps:
                if d < first:
                    continue
                p = ops[d]
                if p.is_dma:
                    lst.append(d)
                elif p.eng != op.eng or op.is_dma:
                    lst.append(d)
                    p.marked = True
                else:
                    if op.eng == "pe":
                        continue
                    if op.pos - p.pos <= 1:
                        lst.append(d)
                        p.marked = True
            need.append(lst)
        if barrier:
            last = {}
            for op in batch:
                if not op.is_dma:
                    last[op.eng] = op
            for op in last.values():
                op.marked = True
        for op in batch:
            if op.is_dma:
                key = ("g", op.group)
                self.cnt[key] = self.cnt.get(key, 0) + 16
                op.count = self.cnt[key]
            elif op.marked:
                key = ("e", op.eng)
                self.cnt[key] = self.cnt.get(key, 0) + 1
                op.count = self.cnt[key]
        seen_eng = set()
        for op, lst in zip(batch, need):
            eng = self.eng_obj[op.eng]
            reqs = {}
            if self.barrier_req is not None and op.eng not in seen_eng:
                reqs.update(self.barrier_req)
            seen_eng.add(op.eng)
            for d in lst:
                p = ops[d]
                key = ("g", p.group) if p.is_dma else ("e", p.eng)
                if p.count > reqs.get(key, 0):
                    reqs[key] = p.count
            for key, val in reqs.items():
                if key == ("e", op.eng) and not op.is_dma and self.barrier_req is not None \
                        and val <= self.barrier_req.get(key, 0):
                    continue
                wk = (op.eng, key)
                if self.waited.get(wk, 0) >= val:
                    continue
                self.waited[wk] = val
                eng.wait_ge(self._sem(key), val)
                self.nwait += 1
                self.ninstr = getattr(self, "ninstr", {})
                self.ninstr[op.eng] = self.ninstr.get(op.eng, 0) + 1
            ins = op.fn(eng)
            self.ninstr = getattr(self, "ninstr", {})
            self.ninstr[op.eng] = self.ninstr.get(op.eng, 0) + 1
            if op.is_dma:
                ins.then_inc(self._sem(("g", op.group)), 16)
            elif op.marked:
                ins.then_inc(self._sem(("e", op.eng)), 1)
            op.fn = None
        self.emitted = len(ops)
        if barrier:
            self.barrier_req = dict(self.cnt)

    def finish(self):
        self.flush(barrier=True)
        for key, val in self.cnt.items():
            if key[0] == "g":
                self.nc.sync.wait_ge(self._sem(key), val)


class Builder:
    def __init__(self, nc, stack, stage):
        self.nc = nc
        self.P = Prog(nc, stack)
        self.stage = stage
        self.stack = stack
        self.dram = {}
        self.wq = 0

    def din(self, name, shape):
        ap = self.nc.dram_tensor(name, list(shape), F32, kind="ExternalInput").ap()
        self.dram[name] = ap
        return ap

    def dout(self, name, shape):
        ap = self.nc.dram_tensor(name, list(shape), F32, kind="ExternalOutput").ap()
        self.dram[name] = ap
        return ap

    def declare(self):
        d = self.din
        d("xp", [TP, D]); d("xs", [TSAMP, D])
        d("st_conv", [DEPTH, NSEQ, 3, 768]); d("st_ssd", [DEPTH, NSEQ, 4, 64, 128])
        d("st_shift", [DEPTH, NSEQ, 1024]); d("st_rwkv", [DEPTH, NSEQ, 4, 64, 64])
        d("st_s5re", [DEPTH, NSEQ, 16, 64]); d("st_s5im", [DEPTH, NSEQ, 16, 64])
        d("st_pool", [DEPTH, NSEQ, 15, 256])
        d("norm_ffn1", [DEPTH, D]); d("ffn1_in", [DEPTH, D, 2 * DFF]); d("ffn1_out", [DEPTH, DFF, D])
        d("norm_mix", [DEPTH, D]); d("w_in", [DEPTH, D, INP])
        d("ssd_conv_w", [DEPTH, 4, 768]); d("ssd_conv_b", [DEPTH, 768]); d("ssd_dt_bias", [DEPTH, 4])
        d("ssd_a_log", [DEPTH, 4]); d("ssd_d", [DEPTH, 4]); d("ssd_norm", [DEPTH, 256])
        d("rwkv_mu", [DEPTH, 1024]); d("rwkv_w0", [DEPTH, 256]); d("rwkv_w2", [DEPTH, 64, 256])
        d("rwkv_a0", [DEPTH, 256]); d("rwkv_a2", [DEPTH, 64, 256]); d("rwkv_g2", [DEPTH, 128, 256])
        d("rwkv_k_k", [DEPTH, 256]); d("rwkv_k_a", [DEPTH, 256]); d("rwkv_r_k", [DEPTH, 4, 64])
        d("rwkv_ln_g", [DEPTH, 256]); d("rwkv_ln_b", [DEPTH, 256])
        d("s5_lam_re", [DEPTH, 16, 64]); d("s5_lam_im", [DEPTH, 16, 64]); d("s5_log_step", [DEPTH, 16])
        d("s5_b_re", [DEPTH, 16, 64, 16]); d("s5_b_im", [DEPTH, 16, 64, 16])
        d("s5_c_re", [DEPTH, 16, 16, 64]); d("s5_c_im", [DEPTH, 16, 16, 64]); d("s5_d", [DEPTH, 256])
        d("s5_glu_w", [DEPTH, 256, 512]); d("s5_glu_b", [DEPTH, 512])
        d("pool_w", [DEPTH, 4, 64, 64]); d("pool_scale", [DEPTH, 256])
        d("w_out", [DEPTH, D, D]); d("norm_ffn2", [DEPTH, D]); d("ffn2_in", [DEPTH, D, 2 * DFF])
        d("ffn2_out", [DEPTH, DFF, D]); d("norm_final", [D])
        d("c_pool_cinv", [128, 2, 15]); d("c_gm", [128, 8]); d("c_swap", [128, 128])
        d("c_cmask", [4, T]); d("c_mask128", [128, 128]); d("c_U128", [128, 128]); d("c_L128", [128, 128]); d("c_elast", [128, 128])
        d("c_bdones", [128, 128]); d("c_I2", [128, 64]); d("c_Esel", [64, 2, 128]); d("c_Ehp", [128, 2, 64])
        d("c_maskS", [64, 64]); d("c_US", [64, 64]); d("c_LS", [64, 64]); d("c_seqm", [64, NSEQ]); d("c_sel", [4, 2, 128])
        o = self.dout
        o("y_p", [TP, D]); o("y_s", [TSAMP, D])
        o("conv_p", [DEPTH, 3, 768]); o("conv_s", [DEPTH, NSEQ, 3, 768])
        o("ssd_p", [DEPTH, 4, 64, 128]); o("ssd_s", [DEPTH, NSEQ, 4, 64, 128])
        o("shift_p", [DEPTH, 1024]); o("shift_s", [DEPTH, NSEQ, 1024])
        o("rwkv_p", [DEPTH, 4, 64, 64]); o("rwkv_s", [DEPTH, NSEQ, 4, 64, 64])
        o("s5re_p", [DEPTH, 16, 64]); o("s5re_s", [DEPTH, NSEQ, 16, 64])
        o("s5im_p", [DEPTH, 16, 64]); o("s5im_s", [DEPTH, NSEQ, 16, 64])
        o("pool_p", [DEPTH, 15, 256]); o("pool_s", [DEPTH, NSEQ, 15, 256])
        o("rw_scr", [2, 7, 128, T])

    def dump_x(self):
        xT = self.xT
        self.P.dma(lambda e: e.dma_start(out=self.dram["dbg"], in_=xT[:]),
                   reads=["xT%d_%d" % (k, ti) for k in range(8) for ti in range(5)], group="dbg")
        self.P.flush()

    def setup(self):
        P = self.P
        self.xT = P.sb("xT", [128, 8, T], F32)
        self.xn = P.sb("xn", [128, 8, T], BF16)
        self.ident = P.sb("ident", [128, 128], F32)
        self.identb = P.sb("identb", [128, 128], BF16)
        self.onesb = P.sb("onesb", [128, 128], BF16)
        self.gains = P.sb("gains", [128, 7, 8], F32)
        self.epsc = P.sb("epsc", [128, 1], F32)
        self.psum = [P.ps("ps%d" % i, [128, 512], F32) for i in range(8)]
        ident, identb, onesb = self.ident, self.identb, self.onesb
        P.dve(lambda e: e.memset(ident[:], 0.0), writes=["ident"])
        P.pool(lambda e: e.affine_select(out=ident[:], in_=ident[:], pattern=[[-1, 128]],
                                         compare_op=ALU.not_equal, fill=1.0, base=0,
                                         channel_multiplier=1), reads=["ident"], writes=["ident"])
        P.dve(lambda e: e.tensor_copy(out=identb[:], in_=ident[:]), reads=["ident"], writes=["identb"])
        P.dve(lambda e: e.memset(onesb[:], 1.0), writes=["onesb"])
        epsc = self.epsc
        P.dve(lambda e: e.memset(epsc[:], 1e-6), writes=["epsc"])
        gains = self.gains
        names = ["norm_ffn1", "norm_mix", "norm_ffn2"]
        for l in range(DEPTH):
            for i, nm in enumerate(names):
                src = self.dram[nm][l].rearrange("(k p) -> p k", p=128)
                P.dma(lambda e, s=src, w=l * 3 + i: e.dma_start(out=gains[:, w, :], in_=s),
                      writes=["gains"], group="const")
        src = self.dram["norm_final"].rearrange("(k p) -> p k", p=128)
        P.dma(lambda e, s=src: e.dma_start(out=gains[:, 6, :], in_=s), writes=["gains"], group="const")

    def load_x(self):
        P = self.P
        xT, ident = self.xT, self.ident
        with contextlib.ExitStack() as st:
            xtok = [P.sb("xtok%d" % i, [128, D], F32, stack=st) for i in range(3)]
            ntile = TP // 128 + 1
            for ti in range(ntile):
                buf = xtok[ti % 3]
                bn = "xtok%d" % (ti % 3)
                if ti < TP // 128:
                    src = self.dram["xp"][ti * 128:(ti + 1) * 128, :]
                    n = 128
                else:
                    src = self.dram["xs"][:, :]
                    n = TSAMP
                P.dma(lambda e, b=buf, s=src, n=n: e.dma_start(out=b[0:n, :], in_=s), writes=[bn])
                for half in range(2):
                    ps = self.psum[(ti * 2 + half) % 4]
                    pn = "ps%d" % ((ti * 2 + half) % 4)
                    for kk in range(4):
                        k = half * 4 + kk
                        P.pe(lambda e, ps=ps, b=buf, k=k, kk=kk, n=n: e.transpose(
                            ps[:, kk * 128:kk * 128 + n], b[0:n, k * 128:(k + 1) * 128], ident[0:n, 0:n]),
                            reads=[bn, "ident"], writes=[pn])
                    dst = xT[:, half * 4:half * 4 + 4, ti * 128:ti * 128 + n]
                    srcp = ps[:].rearrange("p (a b) -> p a b", a=4)[:, :, 0:n]
                    wr = ["xT%d_%d" % (k, ti // 4) for k in range(half * 4, half * 4 + 4)]
                    if half == 0:
                        P.dve(lambda e, d=dst, s=srcp: e.tensor_copy(out=d, in_=s), reads=[pn], writes=wr)
                    else:
                        P.act(lambda e, d=dst, s=srcp: e.activation(out=d, in_=s, func=AF.Copy), reads=[pn], writes=wr)
            P.flush()

    def rmsnorm(self, which, st):
        P = self.P
        xT, xn, onesb, gains, epsc = self.xT, self.xn, self.onesb, self.gains, self.epsc
        sq = [P.sb("nsq%d" % i, [128, 8, 512], BF16, stack=st) for i in range(2)]
        rstd = [P.sb("nrstd%d" % i, [128, 512], F32, stack=st) for i in range(2)]
        for ti, (s, n) in enumerate(TILES):
            q = sq[ti % 2]; qn = "nsq%d" % (ti % 2)
            r = rstd[ti % 2]; rn = "nrstd%d" % (ti % 2)
            ps = self.psum[6 + ti % 2]; pn = "ps%d" % (6 + ti % 2)
            for k in range(8):
                P.act(lambda e, q=q, k=k, s=s, n=n: e.activation(out=q[:, k, 0:n], in_=xT[:, k, s:s + n], func=AF.Square),
                      reads=["xT%d_%d" % (k, ti)], writes=[qn + "_%d" % k])
            for k in range(8):
                P.pe(lambda e, ps=ps, q=q, k=k, n=n: e.matmul(ps[:, 0:n], onesb[:], q[:, k, 0:n], start=(k == 0), stop=(k == 7)),
                     reads=[qn + "_%d" % k, "onesb"], writes=[pn])
            P.act(lambda e, r=r, ps=ps, n=n: e.activation(out=r[:, 0:n], in_=ps[:, 0:n], func=AF.Sqrt, bias=epsc[:], scale=1.0 / D),
                  reads=[pn, "epsc"], writes=[rn])
            P.dve(lambda e, r=r, n=n: e.reciprocal(out=r[:, 0:n], in_=r[:, 0:n]), reads=[rn], writes=[rn])
            for k in range(8):
                P.dve(lambda e, r=r, k=k, s=s, n=n: e.scalar_tensor_tensor(
                    out=xn[:, k, s:s + n], in0=xT[:, k, s:s + n], scalar=gains[:, which, k:k + 1], in1=r[:, 0:n],
                    op0=ALU.mult, op1=ALU.mult),
                    reads=["xT%d_%d" % (k, ti), rn, "gains"], writes=["xn%d_%d" % (k, ti)])

    def ffn(self, l, which, w_in, w_out):
        P = self.P
        xT, xn = self.xT, self.xn
        with contextlib.ExitStack() as st:
            self.rmsnorm(l * 3 + (0 if which == 1 else 2), st)
            P.flush()
        with contextlib.ExitStack() as st:
            NH = NFF // 2
            hid = P.sb("hid", [128, NH, T], BF16, stack=st)
            NS = 3
            stg = [P.sb("wstg%d" % i, [128, 8, 256], F32, stack=st) for i in range(NS)]
            wb = [P.sb("wbf%d" % i, [128, 8, 256], BF16, stack=st) for i in range(NS)]
            NSB = 2
            stgo = [P.sb("wostg%d" % i, [128, NH, 128], F32, stack=st) for i in range(NSB)]
            wob = [P.sb("wobf%d" % i, [128, NH, 128], BF16, stack=st) for i in range(NSB)]
            gt = [P.sb("gtmp%d" % i, [128, 512], F32, stack=st) for i in range(2)]
            w_in_v = w_in.rearrange("(k p) c -> p k c", p=128)
            w_out_v = w_out.rearrange("(j p) c -> p j c", p=128)
            jobs = []
            for half in range(2):
                for jl in range(NH):
                    jobs.append(("a", half, jl))
                for m in range(8):
                    jobs.append(("b", half, m))
            na = [0]
            nb = [0]
            slots = {}

            def issue_load(ji):
                kind, half, idx = jobs[ji]
                if kind == "a":
                    i = na[0] % NS; na[0] += 1
                    slots[ji] = i
                    j = half * NH + idx
                    s1 = w_in_v[:, :, j * 128:(j + 1) * 128]
                    s2 = w_in_v[:, :, DFF + j * 128:DFF + (j + 1) * 128]
                    b = stg[i]
                    P.dma(lambda e: e.dma_start(out=b[:, :, 0:128], in_=s1), writes=["wstg%d" % i])
                    P.dma(lambda e: e.dma_start(out=b[:, :, 128:256], in_=s2), writes=["wstg%d" % i])
                else:
                    i = nb[0] % NSB; nb[0] += 1
                    slots[ji] = i
                    s1 = w_out_v[:, half * NH:(half + 1) * NH, idx * 128:(idx + 1) * 128]
                    b = stgo[i]
                    P.dma(lambda e: e.dma_start(out=b[:], in_=s1), writes=["wostg%d" % i])

            def issue_cast(ji):
                kind, half, idx = jobs[ji]
                i = slots[ji]
                if kind == "a":
                    P.pool(lambda e: e.tensor_copy(out=wb[i][:], in_=stg[i][:]), reads=["wstg%d" % i], writes=["wbf%d" % i])
                else:
                    P.pool(lambda e: e.tensor_copy(out=wob[i][:], in_=stgo[i][:]), reads=["wostg%d" % i], writes=["wobf%d" % i])

            cnt = [0]

            def compute(ji):
                kind, half, idx = jobs[ji]
                i = slots[ji]
                if kind == "a":
                    w = wb[i]; wn = "wbf%d" % i
                    for ti, (s, n) in enumerate(TILES):
                        c = cnt[0]; cnt[0] += 1
                        pg = self.psum[(c % 2) * 2]; pgn = "ps%d" % ((c % 2) * 2)
                        pu = self.psum[(c % 2) * 2 + 1]; pun = "ps%d" % ((c % 2) * 2 + 1)
                        g = gt[c % 2]; gn = "gtmp%d" % (c % 2)
                        for k in range(8):
                            P.pe(lambda e, k=k, pg=pg, s=s, n=n: e.matmul(pg[:, 0:n], w[:, k, 0:128], xn[:, k, s:s + n], start=(k == 0), stop=(k == 7)),
                                 reads=[wn, "xn%d_%d" % (k, ti)], writes=[pgn])
                        for k in range(8):
                            P.pe(lambda e, k=k, pu=pu, s=s, n=n: e.matmul(pu[:, 0:n], w[:, k, 128:256], xn[:, k, s:s + n], start=(k == 0), stop=(k == 7)),
                                 reads=[wn, "xn%d_%d" % (k, ti)], writes=[pun])
                        P.act(lambda e, g=g, pg=pg, n=n: e.activation(out=g[:, 0:n], in_=pg[:, 0:n], func=AF.Silu), reads=[pgn], writes=[gn])
                        P.dve(lambda e, g=g, pu=pu, s=s, n=n: e.tensor_tensor(out=hid[:, idx, s:s + n], in0=g[:, 0:n], in1=pu[:, 0:n], op=ALU.mult),
                              reads=[gn, pun], writes=["hid%d_%d" % (idx, ti)])
                else:
                    w = wob[i]; wn = "wobf%d" % i
                    m = idx
                    for ti, (s, n) in enumerate(TILES):
                        c = cnt[0]; cnt[0] += 1
                        po = self.psum[4 + c % 2]; pon = "ps%d" % (4 + c % 2)
                        for jl in range(NH):
                            P.pe(lambda e, jl=jl, po=po, s=s, n=n: e.matmul(po[:, 0:n], w[:, jl, :], hid[:, jl, s:s + n], start=(jl == 0), stop=(jl == NH - 1)),
                                 reads=[wn, "hid%d_%d" % (jl, ti)], writes=[pon])
                        P.dve(lambda e, po=po, s=s, n=n: e.scalar_tensor_tensor(
                            out=xT[:, m, s:s + n], in0=po[:, 0:n], scalar=0.5, in1=xT[:, m, s:s + n], op0=ALU.mult, op1=ALU.add),
                            reads=[pon, "xT%d_%d" % (m, ti)], writes=["xT%d_%d" % (m, ti)])

            nj = len(jobs)
            issue_load(0); issue_load(1)
            issue_cast(0)
            for ji in range(nj):
                if ji + 2 < nj:
                    issue_load(ji + 2)
                if ji + 1 < nj:
                    issue_cast(ji + 1)
                compute(ji)
            P.flush()

    def inproj(self, l, chunks, st, dtype=BF16, pad=0):
        P = self.P
        xn = self.xn
        w_v = self.dram["w_in"][l].rearrange("(k p) c -> p k c", p=128)
        out = {}
        for ci, (name, c0, ncol) in enumerate(chunks):
            out[name] = P.sb("u_" + name, [128, pad + T], dtype, stack=st)
        with contextlib.ExitStack() as st2:
            self._inproj_body(l, chunks, out, st2, pad)
            P.flush()
        return out

    def _inproj_body(self, l, chunks, out, st, pad):
        P = self.P
        xn = self.xn
        w_v = self.dram["w_in"][l].rearrange("(k p) c -> p k c", p=128)
        stg = [P.sb("ipstg%d" % i, [128, 8, 128], F32, stack=st) for i in range(3)]
        wbs = [P.sb("ipwb%d" % i, [128, 8, 128], BF16, stack=st) for i in range(3)]
        for ci, (name, c0, ncol) in enumerate(chunks):
            if pad:
                P.dve(lambda e, t=out[name]: e.memset(t[:, 0:pad], 0.0), writes=["u_%s_pad" % name])
        for ci, (name, c0, ncol) in enumerate(chunks):
            i = ci % 3
            b = stg[i]; w = wbs[i]
            P.dma(lambda e, b=b, c0=c0, ncol=ncol: e.dma_start(out=b[:, :, 0:ncol], in_=w_v[:, :, c0:c0 + ncol]),
                  writes=["ipstg%d" % i])
            P.pool(lambda e, b=b, w=w, ncol=ncol: e.tensor_copy(out=w[:, :, 0:ncol], in_=b[:, :, 0:ncol]),
                   reads=["ipstg%d" % i], writes=["ipwb%d" % i])
            u = out[name]
            for ti, (s, n) in enumerate(TILES):
                c = ci * len(TILES) + ti
                ps = self.psum[c % 4]; pn = "ps%d" % (c % 4)
                for k in range(8):
                    P.pe(lambda e, k=k, ps=ps, w=w, s=s, n=n, ncol=ncol: e.matmul(
                        ps[0:ncol, 0:n], w[:, k, 0:ncol], xn[:, k, s:s + n], start=(k == 0), stop=(k == 7)),
                        reads=["ipwb%d" % i, "xn%d_%d" % (k, ti)], writes=[pn])
                if c % 2 == 0:
                    P.act(lambda e, u=u, ps=ps, s=s, n=n, ncol=ncol: e.activation(out=u[0:ncol, pad + s:pad + s + n], in_=ps[0:ncol, 0:n], func=AF.Copy),
                          reads=[pn], writes=["u_%s_%d" % (name, ti)])
                else:
                    P.dve(lambda e, u=u, ps=ps, s=s, n=n, ncol=ncol: e.tensor_copy(out=u[0:ncol, pad + s:pad + s + n], in_=ps[0:ncol, 0:n]),
                          reads=[pn], writes=["u_%s_%d" % (name, ti)])
        return out

    def ures(self, name, tis=None):
        if tis is None:
            tis = range(len(TILES))
        return ["u_%s_%d" % (name, ti) for ti in tis]

    def store_cols(self, u, nrows, col0, ncols, dst, reads, group="sto"):
        P = self.P
        P.dma(lambda e: e.dma_start(out=dst.rearrange("t f -> f t"), in_=u[0:nrows, col0:col0 + ncols],
                                    allow_slow_non_contiguous=True),
              reads=reads, group=group)

    def mix_rwkv_stub(self, l):
        P = self.P
        base = 1028
        with contextlib.ExitStack() as st:
            chunks = [("r0", base, 128), ("r1", base + 128, 128), ("k0", base + 256, 128), ("k1", base + 384, 128),
                      ("v0", base + 512, 128), ("v1", base + 640, 128), ("wa", base + 768, 128), ("gd", base + 896, 128)]
            u = self.inproj(l, chunks, st, dtype=BF16)
            shf = P.sb("shf", [128, 8, 1 + NSEQ], F32, stack=st)
            for ci, (name, c0, ncol) in enumerate(chunks):
                f0 = c0 - base
                P.dve(lambda e, ci=ci, t=u[name]: e.tensor_copy(out=shf[:, ci, 0:1], in_=t[:, TP - 1:TP]), reads=self.ures(name, [3]), writes=["shf%d" % ci])
                P.dve(lambda e, ci=ci, t=u[name]: e.tensor_copy(out=shf[:, ci, 1:1 + NSEQ], in_=t[:, TP:T].rearrange("p (s t) -> p s t", t=TS)[:, :, TS - 1]),
                      reads=self.ures(name, [4]), writes=["shf%d" % ci])
                dst = self.dram["shift_p"][l:l + 1, f0:f0 + ncol]
                P.dma(lambda e, dst=dst, ci=ci: e.dma_start(out=dst.rearrange("t f -> f t"), in_=shf[:, ci, 0:1]), reads=["shf%d" % ci], group="sto")
                dst = self.dram["shift_s"][l, :, f0:f0 + ncol]
                P.dma(lambda e, dst=dst, ci=ci: e.dma_start(out=dst.rearrange("s f -> f s"), in_=shf[:, ci, 1:1 + NSEQ]), reads=["shf%d" % ci], group="sto")
            P.flush()

    def mix_pool(self, l):
        P = self.P
        ymix = self.ymix
        PADP = 16
        with contextlib.ExitStack() as st:
            chunks = [("p0", 2308, 128), ("p1", 2436, 128)]
            u = self.inproj(l, chunks, st, dtype=F32, pad=PADP)
            N = PADP + TP
            lev = [P.sb("pl%d" % i, [128, N], F32, stack=st) for i in range(2)]
            slev = [P.sb("psl%d" % i, [128, NSEQ, 20], F32, stack=st) for i in range(2)]
            Es = P.sb("pEs", [128, NSEQ, 20], F32, stack=st)
            pooled = P.sb("ppooled", [128, T], BF16, stack=st)
            tmp15 = P.sb("ptmp15", [128, 15], F32, stack=st)
            Wf = P.sb("pWf", [128, 128], F32, stack=st)
            Wb = P.sb("pWb", [128, 128], BF16, stack=st)
            psc = P.sb("ppsc", [128, 2], F32, stack=st)
            cinv = P.sb("pcinv", [128, 2, 15], F32, stack=st)
            P.dma(lambda e: e.dma_start(out=psc[:], in_=self.dram["pool_scale"][l].rearrange("(c p) -> p c", p=128)), writes=["ppsc"])
            P.dma(lambda e: e.dma_start(out=cinv[:], in_=self.dram["c_pool_cinv"]), writes=["pcinv"])
            wins = [2, 4, 8, 16]
            for c, (name, c0, ncol) in enumerate(chunks):
                E = u[name]
                allr = self.ures(name) + ["u_%s_pad" % name]
                nlev = 2 if c == 0 else 4
                for i in range(nlev):
                    sh = 1 << i
                    a = E if i == 0 else lev[(i - 1) % 2]
                    o = lev[i % 2]
                    P.dve(lambda e, a=a, o=o, sh=sh: e.tensor_tensor(out=o[:, 2 * sh:N], in0=a[:, 2 * sh:N], in1=a[:, sh:N - sh], op=ALU.add),
                          reads=(allr if i == 0 else ["pl%d" % ((i - 1) % 2)]), writes=["pl%d" % (i % 2)])
                P.dve(lambda e: e.memset(Es[:, :, 0:1], 0.0), writes=["pEs"])
                for sq_ in range(NSEQ):
                    P.dma(lambda e, c=c, sq_=sq_: e.dma_start(out=Es[:, sq_, 1:16], in_=self.dram["st_pool"][l, sq_][:, c * 128:(c + 1) * 128].rearrange("r f -> f r")),
                          writes=["pEs"])
                P.dve(lambda e, E=E: e.tensor_copy(out=Es[:, :, 16:20], in_=E[:, PADP + TP:PADP + T].rearrange("p (s t) -> p s t", t=TS)),
                      reads=allr, writes=["pEs"])
                for i in range(nlev):
                    sh = 1 << i
                    a = Es if i == 0 else slev[(i - 1) % 2]
                    o = slev[i % 2]
                    P.dve(lambda e, a=a, o=o, sh=sh: e.tensor_tensor(out=o[:, :, 2 * sh:20], in0=a[:, :, 2 * sh:20], in1=a[:, :, sh:20 - sh], op=ALU.add),
                          reads=(["pEs"] if i == 0 else ["psl%d" % ((i - 1) % 2)]), writes=["psl%d" % (i % 2)])
                for hf in range(2):
                    rows = slice(hf * 64, hf * 64 + 64)
                    gi = 2 * c + hf
                    li = gi % 2
                    sw = lev[li]; ssw = slev[li]
                    winv = 1.0 / wins[gi]
                    P.dve(lambda e, sw=sw, rows=rows, winv=winv, E=E: e.scalar_tensor_tensor(
                        out=pooled[rows, 0:TP], in0=sw[rows, PADP:N], scalar=winv, in1=E[rows, PADP:N], op0=ALU.mult, op1=ALU.subtract),
                        reads=["pl%d" % li] + allr, writes=["ppooled"])
                    P.dve(lambda e, sw=sw, rows=rows, c=c: e.tensor_tensor(out=tmp15[rows, :], in0=sw[rows, PADP:PADP + 15], in1=cinv[rows, c, :], op=ALU.mult),
                          reads=["pl%d" % li, "pcinv"], writes=["ptmp15"])
                    P.dve(lambda e, rows=rows, E=E: e.tensor_tensor(out=pooled[rows, 0:15], in0=tmp15[rows, :], in1=E[rows, PADP:PADP + 15], op=ALU.subtract),
                          reads=["ptmp15"] + allr, writes=["ppooled"])
                    P.dve(lambda e, ssw=ssw, rows=rows, winv=winv: e.scalar_tensor_tensor(
                        out=pooled[rows, TP:T].rearrange("p (s t) -> p s t", t=TS), in0=ssw[rows, :, 16:20], scalar=winv, in1=Es[rows, :, 16:20],
                        op0=ALU.mult, op1=ALU.subtract), reads=["psl%d" % li, "pEs"], writes=["ppooled"])
                dst = self.dram["pool_p"][l][:, c * 128:(c + 1) * 128]
                self.store_cols(E, 128, PADP + TP - 15, 15, dst, allr)
                for sq_ in range(NSEQ):
                    P.dma(lambda e, c=c, sq_=sq_: e.dma_start(out=self.dram["pool_s"][l, sq_][:, c * 128:(c + 1) * 128].rearrange("r f -> f r"), in_=Es[:, sq_, 5:20]),
                          reads=["pEs"])
                P.dve(lambda e: e.memset(Wf[:], 0.0), writes=["pWf"])
                P.dma(lambda e, c=c: e.dma_start(out=Wf[0:64, 0:64], in_=self.dram["pool_w"][l, 2 * c]), writes=["pWf"])
                P.dma(lambda e, c=c: e.dma_start(out=Wf[64:128, 64:128], in_=self.dram["pool_w"][l, 2 * c + 1]), writes=["pWf"])
                P.dve(lambda e: e.tensor_copy(out=Wb[:], in_=Wf[:]), reads=["pWf"], writes=["pWb"])
                for ti, (s, n) in enumerate(TILES):
                    ps = self.psum[ti % 2]; pn = "ps%d" % (ti % 2)
                    P.pe(lambda e, ps=ps, s=s, n=n: e.matmul(ps[:, 0:n], Wb[:], pooled[:, s:s + n], start=True, stop=True),
                         reads=["pWb", "ppooled"], writes=[pn])
                    P.act(lambda e, ps=ps, s=s, n=n, c=c: e.activation(out=ymix[:, 6 + c, s:s + n], in_=ps[:, 0:n], func=AF.Copy, scale=psc[:, c:c + 1]),
                          reads=[pn, "ppsc"], writes=["ymix%d_%d" % (6 + c, ti)])
            P.flush()

    def mix_s5(self, l):
        P = self.P
        ymix, ident = self.ymix, self.ident
        HW = TP + NSEQ * 5
        PI = 3.14159265358979
        with contextlib.ExitStack() as st:
            u = self.inproj(l, [("s0", 2052, 128), ("s1", 2180, 128)], st, dtype=BF16)
            t_ = lambda nm, shp, dt=F32: P.sb("s5" + nm, shp, dt, stack=st)
            lre, lim, dl, ee, th, rr, kf, ff, s1, s2, c1, sn, cs = [t_(n, [128, 16]) for n in
                                                                     "lre lim dl ee th rr kf ff s1 s2 c1 sn cs".split()]
            ki = t_("ki", [128, 16], mybir.dt.int32)
            ar, ai, xx, den, cr, ci, tq, ais = [t_(n, [128, 16]) for n in "ar ai xx den cr ci tq ais".split()]
            X1 = t_("X1", [128, 16, 16]); X2 = t_("X2", [128, 16, 16]); bb = t_("bb", [128, 16, 16]); bb2 = t_("bb2", [128, 16, 16])
            BT = t_("BT", [128, 2, 128], BF16)
            CT = t_("CT", [128, 16, 16]); CTm = t_("CTm", [128, 2, 128], BF16)
            gm = t_("gm", [128, 8]); swp = t_("swp", [128, 128])
            dsk = t_("dsk", [128, 2]); glb = t_("glb", [128, 4])
            S = "s5c"
            dr = self.dram
            for hf in range(2):
                rs = slice(hf * 64, hf * 64 + 64)
                P.dma(lambda e, rs=rs: e.dma_start(out=lre[rs, :], in_=dr["s5_lam_re"][l].rearrange("g n -> n g")), writes=[S])
                P.dma(lambda e, rs=rs: e.dma_start(out=lim[rs, :], in_=dr["s5_lam_im"][l].rearrange("g n -> n g")), writes=[S])
            P.dma(lambda e: e.dma_start(out=dl[:], in_=dr["s5_log_step"][l].partition_broadcast(128)), writes=[S])
            P.dma(lambda e: e.dma_start(out=X1[0:64], in_=dr["s5_b_re"][l].rearrange("g n c -> n g c")), writes=[S])
            P.dma(lambda e: e.dma_start(out=X1[64:128], in_=dr["s5_b_im"][l].rearrange("g n c -> n g c")), writes=[S])
            P.dma(lambda e: e.dma_start(out=X2[0:64], in_=dr["s5_b_im"][l].rearrange("g n c -> n g c")), writes=[S])
            P.dma(lambda e: e.dma_start(out=X2[64:128], in_=dr["s5_b_re"][l].rearrange("g n c -> n g c")), writes=[S])
            P.dma(lambda e: e.dma_start(out=CT[0:64], in_=dr["s5_c_re"][l].rearrange("g c n -> n g c")), writes=[S])
            P.dma(lambda e: e.dma_start(out=CT[64:128], in_=dr["s5_c_im"][l].rearrange("g c n -> n g c")), writes=[S])
            P.dma(lambda e: e.dma_start(out=gm[:], in_=dr["c_gm"]), writes=[S])
            P.dma(lambda e: e.dma_start(out=swp[:], in_=dr["c_swap"]), writes=[S])
            P.dma(lambda e: e.dma_start(out=dsk[:], in_=dr["s5_d"][l].rearrange("(c p) -> p c", p=128)), writes=[S])
            P.dma(lambda e: e.dma_start(out=glb[:], in_=dr["s5_glu_b"][l].rearrange("(c p) -> p c", p=128)), writes=[S])
            V = lambda fn: P.dve(fn, reads=[S], writes=[S])
            A = lambda fn: P.act(fn, reads=[S], writes=[S])
            A(lambda e: e.activation(out=dl[:], in_=dl[:], func=AF.Exp))
            V(lambda e: e.tensor_tensor(out=th[:], in0=lre[:], in1=dl[:], op=ALU.mult))
            A(lambda e: e.activation(out=ee[:], in_=th[:], func=AF.Exp))
            V(lambda e: e.tensor_tensor(out=th[:], in0=lim[:], in1=dl[:], op=ALU.mult))
            V(lambda e: e.tensor_scalar(out=rr[:], in0=th[:], scalar1=1.0 / (2 * PI), scalar2=None, op0=ALU.mult))
            V(lambda e: e.tensor_copy(out=ki[:], in_=rr[:]))
            V(lambda e: e.tensor_copy(out=kf[:], in_=ki[:]))
            V(lambda e: e.tensor_tensor(out=ff[:], in0=rr[:], in1=kf[:], op=ALU.subtract))
            V(lambda e: e.tensor_scalar(out=s1[:], in0=ff[:], scalar1=2 * PI / 8, scalar2=None, op0=ALU.mult))
            V(lambda e: e.tensor_tensor(out=s2[:], in0=s1[:], in1=s1[:], op=ALU.mult))
            V(lambda e: e.tensor_scalar(out=sn[:], in0=s2[:], scalar1=1.0 / 362880, scalar2=None, op0=ALU.mult))
            for cf in (-1.0 / 5040, 1.0 / 120, -1.0 / 6):
                V(lambda e, cf=cf: e.scalar_tensor_tensor(out=sn[:], in0=sn[:], scalar=cf, in1=s2[:], op0=ALU.add, op1=ALU.mult))
            V(lambda e: e.scalar_tensor_tensor(out=sn[:], in0=sn[:], scalar=1.0, in1=s1[:], op0=ALU.add, op1=ALU.mult))
            V(lambda e: e.tensor_scalar(out=cs[:], in0=s2[:], scalar1=-1.0 / 3628800, scalar2=None, op0=ALU.mult))
            for cf in (1.0 / 40320, -1.0 / 720, 1.0 / 24, -0.5):
                V(lambda e, cf=cf: e.scalar_tensor_tensor(out=cs[:], in0=cs[:], scalar=cf, in1=s2[:], op0=ALU.add, op1=ALU.mult))
            V(lambda e: e.tensor_scalar(out=cs[:], in0=cs[:], scalar1=1.0, scalar2=None, op0=ALU.add))
            for _ in range(3):
                V(lambda e: e.tensor_tensor(out=c1[:], in0=cs[:], in1=cs[:], op=ALU.mult))
                V(lambda e: e.tensor_tensor(out=s1[:], in0=sn[:], in1=sn[:], op=ALU.mult))
                V(lambda e: e.scalar_tensor_tensor(out=sn[:], in0=sn[:], scalar=2.0, in1=cs[:], op0=ALU.mult, op1=ALU.mult))
                V(lambda e: e.tensor_tensor(out=cs[:], in0=c1[:], in1=s1[:], op=ALU.subtract))
            V(lambda e: e.tensor_tensor(out=ar[:], in0=ee[:], in1=cs[:], op=ALU.mult))
            V(lambda e: e.tensor_tensor(out=ai[:], in0=ee[:], in1=sn[:], op=ALU.mult))
            V(lambda e: e.tensor_scalar(out=xx[:], in0=ar[:], scalar1=-1.0, scalar2=None, op0=ALU.add))
            V(lambda e: e.tensor_tensor(out=den[:], in0=lre[:], in1=lre[:], op=ALU.mult))
            V(lambda e: e.tensor_tensor(out=tq[:], in0=lim[:], in1=lim[:], op=ALU.mult))
            V(lambda e: e.tensor_tensor(out=den[:], in0=den[:], in1=tq[:], op=ALU.add))
            V(lambda e: e.reciprocal(out=den[:], in_=den[:]))
            V(lambda e: e.tensor_tensor(out=cr[:], in0=xx[:], in1=lre[:], op=ALU.mult))
            V(lambda e: e.tensor_tensor(out=tq[:], in0=ai[:], in1=lim[:], op=ALU.mult))
            V(lambda e: e.tensor_tensor(out=cr[:], in0=cr[:], in1=tq[:], op=ALU.add))
            V(lambda e: e.tensor_tensor(out=cr[:], in0=cr[:], in1=den[:], op=ALU.mult))
            V(lambda e: e.tensor_tensor(out=ci[:], in0=ai[:], in1=lre[:], op=ALU.mult))
            V(lambda e: e.tensor_tensor(out=tq[:], in0=xx[:], in1=lim[:], op=ALU.mult))
            V(lambda e: e.tensor_tensor(out=ci[:], in0=ci[:], in1=tq[:], op=ALU.subtract))
            V(lambda e: e.tensor_tensor(out=ci[:], in0=ci[:], in1=den[:], op=ALU.mult))
            V(lambda e: e.tensor_scalar(out=ci[0:64, :], in0=ci[0:64, :], scalar1=-1.0, scalar2=None, op0=ALU.mult))
            V(lambda e: e.tensor_copy(out=ais[:], in_=ai[:]))
            V(lambda e: e.tensor_scalar(out=ais[64:128, :], in0=ais[64:128, :], scalar1=-1.0, scalar2=None, op0=ALU.mult))
            V(lambda e: e.tensor_scalar(out=CT[64:128], in0=CT[64:128], scalar1=-1.0, scalar2=None, op0=ALU.mult))
            V(lambda e: e.tensor_tensor(out=bb[:], in0=X1[:], in1=cr[:].unsqueeze(2).to_broadcast([128, 16, 16]), op=ALU.mult))
            V(lambda e: e.tensor_tensor(out=bb2[:], in0=X2[:], in1=ci[:].unsqueeze(2).to_broadcast([128, 16, 16]), op=ALU.mult))
            V(lambda e: e.tensor_tensor(out=bb[:], in0=bb[:], in1=bb2[:], op=ALU.add))
            for ch in range(2):
                ps = self.psum[ch]; pn = "ps%d" % ch
                P.pe(lambda e, ps=ps, ch=ch: e.transpose(ps[:, 0:128], bb[:, 8 * ch:8 * ch + 8, :].rearrange("p g c -> p (g c)"), ident[:]),
                     reads=[S, "ident"], writes=[pn])
                P.dve(lambda e, ps=ps, ch=ch: e.tensor_copy(out=BT[:, ch, :], in_=ps[:, 0:128]), reads=[pn], writes=[S])
            NB = 2
            Hb = [t_("Hb%d" % i, [128, HW], BF16) for i in range(NB)]
            ark = t_("ark", [128, 16]); aik = t_("aik", [128, 16]); ta = t_("ta", [128, 16]); tb = t_("tb", [128, 16])
            ysum = t_("ysum", [128, 2, T])
            Hlast = t_("Hlast", [128, 16]); HlastS = t_("HlastS", [128, 16, NSEQ])
            stA = contextlib.ExitStack()
            tA = lambda nm, shp, dt=F32: P.sb("s5" + nm, shp, dt, stack=stA)
            Hs = [tA("H%d" % i, [128, HW]) for i in range(NB)]
            Bl = [tA("Bl%d" % i, [128, 128], BF16) for i in range(NB)]
            Ml = [tA("Ml%d" % i, [128, 128], BF16) for i in range(2 * NB)]
            Mlo = [tA("Mlo%d" % i, [128, 128], BF16) for i in range(2 * NB)]
            Mf = [tA("Mf%d" % i, [128, 128]) for i in range(2)]
            mtmp = [tA("mt%d" % i, [128, 128]) for i in range(2)]
            shifts = [1 << k for k in range(11)]
            mcount = [0]
            for g0 in range(0, 16, NB):
                grp = list(range(g0, g0 + NB))
                ch = g0 // 8
                uc = u["s%d" % ch]
                ur = self.ures("s%d" % ch)
                for bi, g in enumerate(grp):
                    H = Hs[bi]; Hn = "s5H%d" % bi
                    P.dve(lambda e, bi=bi, g=g, ch=ch: e.tensor_scalar(out=Bl[bi][:], in0=BT[:, ch, :], scalar1=gm[:, g % 8:g % 8 + 1], scalar2=None, op0=ALU.mult),
                          reads=[S], writes=["s5Bl%d" % bi])
                    for ti, (s, n) in enumerate(TILES):
                        ps = self.psum[2 + (ti % 2)]; pn = "ps%d" % (2 + ti % 2)
                        P.pe(lambda e, ps=ps, bi=bi, s=s, n=n, uc=uc: e.matmul(ps[:, 0:n], Bl[bi][:], uc[:, s:s + n], start=True, stop=True),
                             reads=["s5Bl%d" % bi] + ur, writes=[pn])
                        if s < TP:
                            P.act(lambda e, ps=ps, H=H, s=s, n=n: e.activation(out=H[:, s:s + n], in_=ps[:, 0:n], func=AF.Copy), reads=[pn], writes=[Hn])
                        else:
                            P.act(lambda e, ps=ps, H=H: e.activation(out=H[:, TP:HW].rearrange("p (s t) -> p s t", t=5)[:, :, 1:5],
                                                                     in_=ps[:, 0:TSAMP].rearrange("p (s t) -> p s t", t=TS), func=AF.Copy), reads=[pn], writes=[Hn])
                    P.dma(lambda e, H=H, g=g: e.dma_start(out=H[0:64, TP:HW].rearrange("p (s t) -> p s t", t=5)[:, :, 0], in_=dr["st_s5re"][l][:, g, :].rearrange("s n -> n s")),
                          writes=[Hn])
                    P.dma(lambda e, H=H, g=g: e.dma_start(out=H[64:128, TP:HW].rearrange("p (s t) -> p s t", t=5)[:, :, 0], in_=dr["st_s5im"][l][:, g, :].rearrange("s n -> n s")),
                          writes=[Hn])
                P.dve(lambda e: e.tensor_copy(out=ark[:], in_=ar[:]), reads=[S], writes=["s5pw"])
                P.dve(lambda e: e.tensor_copy(out=aik[:], in_=ais[:]), reads=[S], writes=["s5pw"])
                for k, sh in enumerate(shifts):
                    for bi, g in enumerate(grp):
                        H = Hs[bi]; Hn = "s5H%d" % bi
                        hb = Hb[bi]; hbn = "s5Hb%d" % bi
                        mi = mcount[0] % (2 * NB); mcount[0] += 1
                        M = Ml[mi]; Mn = "s5M%d" % mi
                        mt = mtmp[mi % 2]; mtn = "s5mt%d" % (mi % 2)
                        P.dve(lambda e, mt=mt, g=g: e.tensor_scalar(out=mt[:], in0=swp[:], scalar1=aik[:, g:g + 1], scalar2=None, op0=ALU.mult),
                              reads=[S, "s5pw"], writes=[mtn])
                        mf = Mf[mi % 2]; mfn = "s5Mf%d" % (mi % 2)
                        M2 = Mlo[mi]
                        P.dve(lambda e, mt=mt, mf=mf, g=g: e.scalar_tensor_tensor(out=mf[:], in0=ident[:], scalar=ark[:, g:g + 1], in1=mt[:], op0=ALU.mult, op1=ALU.add),
                              reads=[mtn, "s5pw", "ident"], writes=[mfn])
                        P.act(lambda e, mf=mf, M=M: e.activation(out=M[:], in_=mf[:], func=AF.Copy), reads=[mfn], writes=[Mn])
                        P.dve(lambda e, mf=mf, M=M, M2=M2: e.tensor_tensor(out=M2[:], in0=mf[:], in1=M[:], op=ALU.subtract), reads=[mfn, Mn], writes=[Mn + "lo"])
                        P.act(lambda e, hb=hb, H=H: e.activation(out=hb[:], in_=H[:], func=AF.Copy), reads=[Hn], writes=[hbn])
                        if sh < TP:
                            L = TP - sh
                            pcs = [(c0, min(512, L - c0)) for c0 in range(0, L, 512)]
                            for pi, (c0, n) in enumerate(pcs):
                                ps = self.psum[4 + pi]; pn = "ps%d" % (4 + pi)
                                P.pe(lambda e, ps=ps, M=M, hb=hb, c0=c0, n=n: e.matmul(ps[:, 0:n], M[:], hb[:, c0:c0 + n], start=True, stop=False),
                                     reads=[Mn, hbn], writes=[pn])
                                P.pe(lambda e, ps=ps, M2=M2, hb=hb, c0=c0, n=n: e.matmul(ps[:, 0:n], M2[:], hb[:, c0:c0 + n], start=False, stop=True),
                                     reads=[Mn + "lo", hbn], writes=[pn])
                            for pi, (c0, n) in enumerate(pcs):
                                ps = self.psum[4 + pi]; pn = "ps%d" % (4 + pi)
                                P.dve(lambda e, ps=ps, H=H, c0=c0, n=n, sh=sh: e.tensor_tensor(out=H[:, sh + c0:sh + c0 + n], in0=H[:, sh + c0:sh + c0 + n], in1=ps[:, 0:n], op=ALU.add),
                                      reads=[pn, Hn], writes=[Hn])
                        if sh < 5:
                            ps = self.psum[2]; pn = "ps2"
                            w5 = 5 - sh
                            P.pe(lambda e, ps=ps, M=M, hb=hb, w5=w5: e.matmul(ps[:, 0:NSEQ * w5], M[:], hb[:, TP:HW].rearrange("p (s t) -> p s t", t=5)[:, :, 0:w5], start=True, stop=False),
                                 reads=[Mn, hbn], writes=[pn])
                            P.pe(lambda e, ps=ps, M2=M2, hb=hb, w5=w5: e.matmul(ps[:, 0:NSEQ * w5], M2[:], hb[:, TP:HW].rearrange("p (s t) -> p s t", t=5)[:, :, 0:w5], start=False, stop=True),
                                 reads=[Mn + "lo", hbn], writes=[pn])
                            P.dve(lambda e, ps=ps, H=H, w5=w5, sh=sh: e.tensor_tensor(
                                out=H[:, TP:HW].rearrange("p (s t) -> p s t", t=5)[:, :, sh:5], in0=H[:, TP:HW].rearrange("p (s t) -> p s t", t=5)[:, :, sh:5],
                                in1=ps[:, 0:NSEQ * w5].rearrange("p (s t) -> p s t", t=w5), op=ALU.add), reads=[pn, Hn], writes=[Hn])
                    W_ = lambda fn: P.dve(fn, reads=["s5pw"], writes=["s5pw"])
                    W_(lambda e: e.tensor_tensor(out=ta[:], in0=ark[:], in1=ark[:], op=ALU.mult))
                    W_(lambda e: e.tensor_tensor(out=tb[:], in0=aik[:], in1=aik[:], op=ALU.mult))
                    W_(lambda e: e.scalar_tensor_tensor(out=aik[:], in0=ark[:], scalar=2.0, in1=aik[:], op0=ALU.mult, op1=ALU.mult))
                    W_(lambda e: e.tensor_tensor(out=ark[:], in0=ta[:], in1=tb[:], op=ALU.subtract))
                for bi, g in enumerate(grp):
                    H = Hs[bi]; Hn = "s5H%d" % bi
                    hb = Hb[bi]; hbn = "s5Hb%d" % bi
                    P.act(lambda e, hb=hb, H=H: e.activation(out=hb[:], in_=H[:], func=AF.Copy), reads=[Hn], writes=[hbn])
                    P.dve(lambda e, H=H, g=g: e.tensor_copy(out=Hlast[:, g:g + 1], in_=H[:, TP - 1:TP]), reads=[Hn], writes=["s5last"])
                    P.dve(lambda e, H=H, g=g: e.tensor_copy(out=HlastS[:, g, :], in_=H[:, TP:HW].rearrange("p (s t) -> p s t", t=5)[:, :, 4]), reads=[Hn], writes=["s5last"])
                P.dve(lambda e: e.memset(CTm[:], 0.0), writes=["s5CTm"])
                for bi, g in enumerate(grp):
                    P.dve(lambda e, g=g, bi=bi: e.tensor_copy(out=CTm[:, bi, 16 * (g % 8):16 * (g % 8) + 16], in_=CT[:, g, :]), reads=[S], writes=["s5CTm"])
                for ti, (s, n) in enumerate(TILES):
                    ps = self.psum[ti % 2]; pn = "ps%d" % (ti % 2)
                    for bi, g in enumerate(grp):
                        hb = Hb[bi]; hbn = "s5Hb%d" % bi
                        if s < TP:
                            rhs = hb[:, s:s + n]
                        else:
                            rhs = hb[:, TP:HW].rearrange("p (s t) -> p s t", t=5)[:, :, 1:5]
                        P.pe(lambda e, ps=ps, g=g, rhs=rhs, n=n, bi=bi: e.matmul(ps[:, 0:n], CTm[:, bi, :], rhs, start=(bi == 0), stop=(bi == NB - 1)),
                             reads=["s5CTm", hbn], writes=[pn])
                    yn = "s5ys%d_%d" % (ch, ti)
                    if g0 % 8 == 0:
                        P.dve(lambda e, ps=ps, ch=ch, s=s, n=n, uc=uc: e.scalar_tensor_tensor(out=ysum[:, ch, s:s + n], in0=uc[:, s:s + n], scalar=dsk[:, ch:ch + 1], in1=ps[:, 0:n],
                                                                                       op0=ALU.mult, op1=ALU.add), reads=[pn, S] + ur, writes=[yn])
                    else:
                        P.dve(lambda e, ps=ps, ch=ch, s=s, n=n: e.tensor_tensor(out=ysum[:, ch, s:s + n], in0=ysum[:, ch, s:s + n], in1=ps[:, 0:n], op=ALU.add),
                              reads=[pn, yn], writes=[yn])
            P.dma(lambda e: e.dma_start(out=dr["s5re_p"][l].rearrange("g n -> n g"), in_=Hlast[0:64, :]), reads=["s5last"], group="sto")
            P.dma(lambda e: e.dma_start(out=dr["s5im_p"][l].rearrange("g n -> n g"), in_=Hlast[64:128, :]), reads=["s5last"], group="sto")
            for g in range(16):
                P.dma(lambda e, g=g: e.dma_start(out=dr["s5re_s"][l][:, g, :].rearrange("s n -> n s"), in_=HlastS[0:64, g, :]), reads=["s5last"], group="sto")
                P.dma(lambda e, g=g: e.dma_start(out=dr["s5im_s"][l][:, g, :].rearrange("s n -> n s"), in_=HlastS[64:128, g, :]), reads=["s5last"], group="sto")
            P.flush()
            stA.close()
            gwf = t_("gwf", [128, 2, 512]); gwb = t_("gwb", [128, 2, 512], BF16)
            P.dma(lambda e: e.dma_start(out=gwf[:], in_=dr["s5_glu_w"][l].rearrange("(k p) c -> p k c", p=128)), writes=["s5gw"])
            P.dve(lambda e: e.tensor_copy(out=gwb[:], in_=gwf[:]), reads=["s5gw"], writes=["s5gw"])
            g1 = [t_("g1_%d" % i, [128, 512]) for i in range(2)]
            g2 = [t_("g2_%d" % i, [128, 512]) for i in range(2)]
            cc = 0
            for ch in range(2):
                for ti, (s, n) in enumerate(TILES):
                    a = g1[cc % 2]; an = "s5g1_%d" % (cc % 2)
                    b = g2[cc % 2]; bn = "s5g2_%d" % (cc % 2)
                    cc += 1
                    yn = "s5ys%d_%d" % (ch, ti)
                    ys = ysum[:, ch, s:s + n]
                    P.act(lambda e, a=a, ys=ys, n=n: e.activation(out=a[:, 0:n], in_=ys, func=AF.Square), reads=[yn], writes=[an])
                    P.dve(lambda e, a=a, n=n: e.tensor_scalar(out=a[:, 0:n], in0=a[:, 0:n], scalar1=0.044715, scalar2=1.0, op0=ALU.mult, op1=ALU.add), reads=[an], writes=[an])
                    P.dve(lambda e, a=a, b=b, ys=ys, n=n: e.tensor_tensor(out=b[:, 0:n], in0=a[:, 0:n], in1=ys, op=ALU.mult), reads=[an, yn], writes=[bn])
                    P.act(lambda e, b=b, n=n: e.activation(out=b[:, 0:n], in_=b[:, 0:n], func=AF.Sigmoid, scale=1.5957691216), reads=[bn], writes=[bn])
                    P.dve(lambda e, b=b, ys=ys, ch=ch, s=s, n=n: e.tensor_tensor(out=Hb[ch][:, s:s + n], in0=b[:, 0:n], in1=ys, op=ALU.mult), reads=[bn, yn], writes=["s5yg%d_%d" % (ch, ti)])
            for m in range(2):
                for ti, (s, n) in enumerate(TILES):
                    pa = self.psum[(ti % 2) * 2]; pan = "ps%d" % ((ti % 2) * 2)
                    pb = self.psum[(ti % 2) * 2 + 1]; pbn = "ps%d" % ((ti % 2) * 2 + 1)
                    sg = g1[ti % 2]; sgn = "s5g1_%d" % (ti % 2)
                    for k in range(2):
                        P.pe(lambda e, pa=pa, k=k, m=m, s=s, n=n: e.matmul(pa[:, 0:n], gwb[:, k, m * 128:(m + 1) * 128], Hb[k][:, s:s + n], start=(k == 0), stop=(k == 1)),
                             reads=["s5gw", "s5yg%d_%d" % (k, ti)], writes=[pan])
                    for k in range(2):
                        P.pe(lambda e, pb=pb, k=k, m=m, s=s, n=n: e.matmul(pb[:, 0:n], gwb[:, k, 256 + m * 128:256 + (m + 1) * 128], Hb[k][:, s:s + n], start=(k == 0), stop=(k == 1)),
                             reads=["s5gw", "s5yg%d_%d" % (k, ti)], writes=[pbn])
                    P.act(lambda e, pb=pb, sg=sg, m=m, n=n: e.activation(out=sg[:, 0:n], in_=pb[:, 0:n], func=AF.Sigmoid, bias=glb[:, 2 + m:3 + m], scale=1.0), reads=[pbn, S], writes=[sgn])
                    P.dve(lambda e, pa=pa, sg=sg, m=m, s=s, n=n: e.scalar_tensor_tensor(out=ymix[:, 4 + m, s:s + n], in0=pa[:, 0:n], scalar=glb[:, m:m + 1], in1=sg[:, 0:n],
                                                                                  op0=ALU.add, op1=ALU.mult), reads=[pan, sgn, S], writes=["ymix%d_%d" % (4 + m, ti)])
            P.flush()

    def mix_ssd(self, l):
        P = self.P
        dr = self.dram
        ymix, ident, identb, onesb = self.ymix, self.ident, self.identb, self.onesb
        NCH = TP // 128
        with contextlib.ExitStack() as st:
            t_ = lambda nm, shp, dt=F32: P.sb("sd" + nm, shp, dt, stack=st)
            C_ = "sdc"
            tokq = t_("tokq", [128, 4, NCH + 1, 4])
            cdS = t_("cdS", [128, 2, NSEQ])
            eaS = t_("eaS", [128, 2, TSAMP])
            cw = t_("cw", [128, 6, 4]); cb = t_("cb", [128, 6]); dcol = t_("dcol", [128, 4]); ng = t_("ng", [128, 2])
            dtb = t_("dtb", [4, 1]); aneg = t_("aneg", [4, 1]); one4 = t_("one4", [4, 1])
            mk = t_("mk", [128, 128]); Ust = t_("Ust", [128, 128]); Lin = t_("Lin", [128, 128]); elast = t_("elast", [128, 128])
            mkS = t_("mkS", [64, 64]); UstS = t_("UstS", [64, 64]); LinS = t_("LinS", [64, 64]); seqm = t_("seqm", [64, NSEQ])
            sel = t_("sel", [4, 2, 128])
            for j in range(4):
                P.dma(lambda e, j=j: e.dma_start(out=cw[:, :, j], in_=dr["ssd_conv_w"][l, j].rearrange("(c p) -> p c", p=128)), writes=[C_])
            P.dma(lambda e: e.dma_start(out=cb[:], in_=dr["ssd_conv_b"][l].rearrange("(c p) -> p c", p=128)), writes=[C_])
            P.dma(lambda e: e.dma_start(out=dcol[:], in_=dr["ssd_d"][l].partition_broadcast(128)), writes=[C_])
            P.dma(lambda e: e.dma_start(out=ng[:], in_=dr["ssd_norm"][l].rearrange("(c p) -> p c", p=128)), writes=[C_])
            P.dma(lambda e: e.dma_start(out=dtb[:], in_=dr["ssd_dt_bias"][l].rearrange("(h o) -> h o", o=1)), writes=[C_])
            P.dma(lambda e: e.dma_start(out=aneg[:], in_=dr["ssd_a_log"][l].rearrange("(h o) -> h o", o=1)), writes=[C_])
            P.dma(lambda e: e.dma_start(out=mk[:], in_=dr["c_mask128"]), writes=[C_])
            P.dma(lambda e: e.dma_start(out=Ust[:], in_=dr["c_U128"]), writes=[C_])
            P.dma(lambda e: e.dma_start(out=Lin[:], in_=dr["c_L128"]), writes=[C_])
            P.dma(lambda e: e.dma_start(out=elast[:], in_=dr["c_elast"]), writes=[C_])
            P.dma(lambda e: e.dma_start(out=mkS[:], in_=dr["c_maskS"]), writes=[C_])
            P.dma(lambda e: e.dma_start(out=UstS[:], in_=dr["c_US"]), writes=[C_])
            P.dma(lambda e: e.dma_start(out=LinS[:], in_=dr["c_LS"]), writes=[C_])
            P.dma(lambda e: e.dma_start(out=seqm[:], in_=dr["c_seqm"]), writes=[C_])
            P.dma(lambda e: e.dma_start(out=sel[:], in_=dr["c_sel"]), writes=[C_])
            P.act(lambda e: e.activation(out=aneg[:], in_=aneg[:], func=AF.Exp), reads=[C_], writes=[C_])
            P.dve(lambda e: e.tensor_scalar(out=aneg[:], in0=aneg[:], scalar1=-1.0, scalar2=None, op0=ALU.mult), reads=[C_], writes=[C_])
            P.dve(lambda e: e.memset(one4[:], 1.0), writes=[C_])
            with contextlib.ExitStack() as s1:
                ud = self.inproj(l, [("dt", 1024, 4)], s1, dtype=F32)["dt"]
                dA = P.sb("sddA", [4, T], F32, stack=s1)
                dB = P.sb("sddB", [4, T], F32, stack=s1)
                cm = P.sb("sdcm", [4, T], F32, stack=s1)
                aend = P.sb("sdaend", [4, NCH + NSEQ], F32, stack=s1)
                P.dma(lambda e: e.dma_start(out=cm[:], in_=dr["c_cmask"]), writes=["sdcm"])
                Rd = self.ures("dt")

                def to_tok(src, q, rn):
                    ps = self.psum[q % 2]; pn = "ps%d" % (q % 2)
                    for c in range(NCH + 1):
                        L = 128 if c < NCH else TSAMP
                        P.pe(lambda e, ps=ps, c=c, L=L, src=src: e.transpose(ps[0:L, 4 * c:4 * c + 4], src[0:4, 128 * c:128 * c + L], ident[0:4, 0:4]),
                             reads=[rn, "ident"], writes=[pn])
                    P.dve(lambda e, ps=ps, q=q: e.tensor_copy(out=tokq[:, q, :, :], in_=ps[:, 0:4 * (NCH + 1)].rearrange("p (c h) -> p c h", h=4)),
                          reads=[pn], writes=["sdtokq"])
                P.act(lambda e: e.activation(out=dA[:], in_=ud[0:4, :], func=AF.Exp, bias=dtb[:], scale=1.0), reads=Rd + [C_], writes=["sddA"])
                P.act(lambda e: e.activation(out=dA[:], in_=dA[:], func=AF.Ln, bias=one4[:], scale=1.0), reads=["sddA", C_], writes=["sddA"])
                to_tok(dA, 0, "sddA")
                P.dve(lambda e: e.tensor_scalar(out=dB[:], in0=dA[:], scalar1=aneg[:, 0:1], scalar2=None, op0=ALU.mult), reads=["sddA", C_], writes=["sddB"])
                to_tok(dB, 1, "sddB")
                P.dve(lambda e: e.tensor_tensor_scan(out=dA[:], data0=cm[:], data1=dB[:], initial=0.0, op0=ALU.mult, op1=ALU.add),
                      reads=["sddB", "sdcm"], writes=["sddA"])
                P.dve(lambda e: e.tensor_copy(out=aend[:, 0:NCH], in_=dA[:, 0:TP].rearrange("p (c t) -> p c t", t=128)[:, :, 127]), reads=["sddA"], writes=["sdaend"])
                P.dve(lambda e: e.tensor_copy(out=aend[:, NCH:NCH + NSEQ], in_=dA[:, TP:T].rearrange("p (c t) -> p c t", t=TS)[:, :, TS - 1]), reads=["sddA"], writes=["sdaend"])
                P.act(lambda e: e.activation(out=dB[:], in_=dA[:], func=AF.Exp), reads=["sddA"], writes=["sddB"])
                to_tok(dB, 3, "sddB")
                for c in range(2):
                    ps = self.psum[2 + c]; pn = "ps%d" % (2 + c)
                    P.pe(lambda e, ps=ps, c=c: e.matmul(ps[:, 0:TSAMP], sel[:, c, :], dB[:, TP:T], start=True, stop=True), reads=["sddB", C_], writes=[pn])
                    P.act(lambda e, ps=ps, c=c: e.activation(out=eaS[:, c, :], in_=ps[:, 0:TSAMP], func=AF.Copy), reads=[pn], writes=["sdeaS"])
                P.dve(lambda e: e.tensor_tensor(out=dA[:, 0:TP].rearrange("p (c t) -> p c t", t=128), in0=aend[:, 0:NCH].unsqueeze(2).to_broadcast([4, NCH, 128]),
                                                in1=dA[:, 0:TP].rearrange("p (c t) -> p c t", t=128), op=ALU.subtract), reads=["sddA", "sdaend"], writes=["sddA"])
                P.dve(lambda e: e.tensor_tensor(out=dA[:, TP:T].rearrange("p (c t) -> p c t", t=TS), in0=aend[:, NCH:NCH + NSEQ].unsqueeze(2).to_broadcast([4, NSEQ, TS]),
                                                in1=dA[:, TP:T].rearrange("p (c t) -> p c t", t=TS), op=ALU.subtract), reads=["sddA", "sdaend"], writes=["sddA"])
                P.act(lambda e: e.activation(out=dA[:], in_=dA[:], func=AF.Exp), reads=["sddA"], writes=["sddA"])
                to_tok(dA, 2, "sddA")
                P.act(lambda e: e.activation(out=aend[:], in_=aend[:], func=AF.Exp), reads=["sdaend"], writes=["sdaend"])
                for c in range(2):
                    ps = self.psum[2 + c]; pn = "ps%d" % (2 + c)
                    P.pe(lambda e, ps=ps, c=c: e.matmul(ps[:, 0:NSEQ], sel[:, c, :], aend[:, NCH:NCH + NSEQ], start=True, stop=True), reads=["sdaend", C_], writes=[pn])
                    P.act(lambda e, ps=ps, c=c: e.activation(out=cdS[:, c, :], in_=ps[:, 0:NSEQ], func=AF.Copy), reads=[pn], writes=["sdcdS"])
                P.flush()
            import os
            SSDPH = int(os.environ.get("SSDPH", "9"))
            SUB = int(os.environ.get("SUB", "99"))
            if SSDPH < 2:
                return
            zz = self.inproj(l, [("z0", 0, 128), ("z1", 128, 128)], st, dtype=BF16)
            xc = [t_("xc%d" % c, [128, T], BF16) for c in range(6)]
            cst = t_("cst", [128, 6, 1 + NSEQ, 3])
            for c in range(6):
                with contextlib.ExitStack() as s2:
                    nm = "xb%d" % c
                    u = self.inproj(l, [(nm, 256 + 128 * c, 128)], s2, dtype=BF16, pad=3)[nm]
                    acc = P.sb("sdacc", [128, TP], F32, stack=s2)
                    Es = P.sb("sdEs", [128, NSEQ, 7], F32, stack=s2)
                    accs = P.sb("sdaccs", [128, NSEQ, TS], F32, stack=s2)
                    R = self.ures(nm) + ["u_%s_pad" % nm]
                    P.dve(lambda e, c=c, u=u: e.tensor_copy(out=cst[:, c, 0, :], in_=u[:, 3 + TP - 3:3 + TP]), reads=R, writes=["sdcst"])
                    P.dve(lambda e, c=c, u=u: e.tensor_copy(out=cst[:, c, 1:1 + NSEQ, :], in_=u[:, 3 + TP:3 + T].rearrange("p (s t) -> p s t", t=TS)[:, :, 1:4]), reads=R, writes=["sdcst"])
                    P.dma(lambda e, c=c: e.dma_start(out=dr["conv_p"][l][:, c * 128:(c + 1) * 128].rearrange("r f -> f r"), in_=cst[:, c, 0, :]), reads=["sdcst"], group="sto")
                    for r in range(3):
                        P.dma(lambda e, c=c, r=r: e.dma_start(out=dr["conv_s"][l][:, r, c * 128:(c + 1) * 128].rearrange("s f -> f s"), in_=cst[:, c, 1:1 + NSEQ, r]), reads=["sdcst"], group="sto")
                        P.dma(lambda e, c=c, r=r: e.dma_start(out=Es[:, :, r], in_=dr["st_conv"][l][:, r, c * 128:(c + 1) * 128].rearrange("s f -> f s")), writes=["sdEs"])
                    P.dve(lambda e, u=u: e.tensor_copy(out=Es[:, :, 3:7], in_=u[:, 3 + TP:3 + T].rearrange("p (s t) -> p s t", t=TS)), reads=R, writes=["sdEs"])
                    P.dve(lambda e, c=c, u=u: e.tensor_scalar(out=acc[:], in0=u[:, 3:3 + TP], scalar1=cw[:, c, 3:4], scalar2=cb[:, c:c + 1], op0=ALU.mult, op1=ALU.add),
                          reads=R + [C_], writes=["sdacc"])
                    for j in range(3):
                        P.dve(lambda e, c=c, u=u, j=j: e.scalar_tensor_tensor(out=acc[:], in0=u[:, j:j + TP], scalar=cw[:, c, j:j + 1], in1=acc[:], op0=ALU.mult, op1=ALU.add),
                              reads=R + [C_, "sdacc"], writes=["sdacc"])
                    P.act(lambda e, c=c: e.activation(out=xc[c][:, 0:TP], in_=acc[:], func=AF.Silu), reads=["sdacc"], writes=["sdxc%d" % c])
                    P.dve(lambda e, c=c: e.tensor_scalar(out=accs[:], in0=Es[:, :, 3:7], scalar1=cw[:, c, 3:4], scalar2=cb[:, c:c + 1], op0=ALU.mult, op1=ALU.add),
                          reads=["sdEs", C_], writes=["sdaccs"])
                    for j in range(3):
                        P.dve(lambda e, c=c, j=j: e.scalar_tensor_tensor(out=accs[:], in0=Es[:, :, j:j + TS], scalar=cw[:, c, j:j + 1], in1=accs[:], op0=ALU.mult, op1=ALU.add),
                              reads=["sdEs", C_, "sdaccs"], writes=["sdaccs"])
                    P.act(lambda e, c=c: e.activation(out=xc[c][:, TP:T].rearrange("p (s t) -> p s t", t=TS), in_=accs[:], func=AF.Silu), reads=["sdaccs"], writes=["sdxc%d" % c])
                    P.flush()
            if SSDPH < 3:
                return
            yss = t_("yss", [128, 2, T], BF16)
            hT = t_("hT", [128, 2, 128]); hTb = t_("hTb", [128, 2, 128], BF16)
            cdB = t_("cdB", [128, 4])
            xtok = t_("xtok", [128, 256]); xdt = t_("xdt", [128, 256], BF16); xdd = t_("xdd", [128, 256], BF16); Btok = t_("Btok", [128, 256], BF16)
            CBm = [t_("CBm%d" % g, [128, 128]) for g in range(2)]
            Ul = [t_("Ul%d" % i, [128, 128]) for i in range(2)]
            Ex = [t_("Ex%d" % i, [128, 128]) for i in range(2)]
            sc = [t_("sc%d" % h, [128, 128], BF16) for h in range(4)]
            ydg = t_("ydg", [128, 256]); ytok = t_("ytok", [128, 256])
            h0n = [t_("h0n%d" % i, [128, 2, 128]) for i in range(2)]
            h0T = [t_("h0T%d" % i, [128, 2, 128], BF16) for i in range(2)]
            hnew = [t_("hnew%d" % i, [128, 2, 128]) for i in range(2)]
            xdm = [t_("xdm%d" % i, [64, 256], BF16) for i in range(2)]
            yoS = t_("yoS", [128, 2, TSAMP])
            P.dve(lambda e: e.memset(hT[:], 0.0), writes=["sdhT"])
            P.dve(lambda e: e.memset(hTb[:], 0.0), writes=["sdhTb"])
            XS = ["sdxc0", "sdxc1"]; BS = ["sdxc2", "sdxc3"]; CS = ["sdxc4", "sdxc5"]
            for c in range(NCH + 1):
                if SSDPH == 3 and c >= 2:
                    break
                if SSDPH == 4 and c >= NCH:
                    break
                samp = (c == NCH)
                L = TSAMP if samp else 128
                t0 = 128 * c
                mK, uK, lK = (mkS, UstS, LinS) if samp else (mk, Ust, Lin)
                pxb = self.psum[0]
                for q in range(4):
                    P.pe(lambda e, q=q, L=L, t0=t0: e.matmul(pxb[0:L, 128 * q:128 * q + 128], xc[q][:, t0:t0 + L], identb[:], start=True, stop=True),
                         reads=["sdxc%d" % q, "identb"], writes=["ps0"])
                P.act(lambda e, L=L: e.activation(out=xtok[0:L, :], in_=pxb[0:L, 0:256], func=AF.Copy), reads=["ps0"], writes=["sdxtok"])
                P.act(lambda e, L=L: e.activation(out=Btok[0:L, :], in_=pxb[0:L, 256:512], func=AF.Copy), reads=["ps0"], writes=["sdBtok"])
                if SUB <= 1:
                    continue
                for h in range(4):
                    P.dve(lambda e, h=h, L=L, c=c: e.tensor_scalar(out=xdt[0:L, 64 * h:64 * h + 64], in0=xtok[0:L, 64 * h:64 * h + 64], scalar1=(1.0 if os.environ.get("IMM") else tokq[0:L, 0, c, h:h + 1]), scalar2=None, op0=ALU.mult),
                          reads=["sdxtok", "sdtokq"], writes=["sdxdt"])
                    P.dve(lambda e, h=h, L=L, c=c: e.tensor_scalar(out=xdd[0:L, 64 * h:64 * h + 64], in0=xtok[0:L, 64 * h:64 * h + 64], scalar1=(1.0 if os.environ.get("IMM") else tokq[0:L, 0, c, h:h + 1]), scalar2=(1.0 if os.environ.get("IMM") else tokq[0:L, 2, c, h:h + 1]), op0=ALU.mult, op1=ALU.mult),
                          reads=["sdxtok", "sdtokq"], writes=["sdxdd"])
                if SUB <= 2:
                    continue
                for g in range(2):
                    pc = self.psum[1 + g]; pcn = "ps%d" % (1 + g)
                    P.pe(lambda e, g=g, pc=pc, L=L, t0=t0: e.matmul(pc[0:L, 0:L], xc[2 + g][:, t0:t0 + L], xc[4 + g][:, t0:t0 + L], start=True, stop=True),
                         reads=[BS[g], CS[g]], writes=[pcn])
                    P.dve(lambda e, g=g, pc=pc, L=L, mK=mK: e.tensor_tensor(out=CBm[g][0:L, 0:L], in0=pc[0:L, 0:L], in1=mK[0:L, 0:L], op=ALU.mult),
                          reads=[pcn, C_], writes=["sdCBm%d" % g])
                if SUB <= 3:
                    continue
                for h in range(4):
                    ul = Ul[h % 2]; uln = "sdUl%d" % (h % 2)
                    ex = Ex[h % 2]; exn = "sdEx%d" % (h % 2)
                    pg = self.psum[3 + (h % 2)]; pgn = "ps%d" % (3 + h % 2)
                    P.dve(lambda e, h=h, ul=ul, L=L, c=c, uK=uK: e.tensor_scalar(out=ul[0:L, 0:L], in0=uK[0:L, 0:L], scalar1=tokq[0:L, 1, c, h:h + 1], scalar2=None, op0=ALU.mult),
                          reads=[C_, "sdtokq"], writes=[uln])
                    P.pe(lambda e, pg=pg, ul=ul, L=L, lK=lK: e.matmul(pg[0:L, 0:L], ul[0:L, 0:L], lK[0:L, 0:L], start=True, stop=True), reads=[uln, C_], writes=[pgn])
                    P.act(lambda e, pg=pg, ex=ex, L=L: e.activation(out=ex[0:L, 0:L], in_=pg[0:L, 0:L], func=AF.Exp), reads=[pgn], writes=[exn])
                    P.dve(lambda e, h=h, ex=ex, L=L: e.tensor_tensor(out=sc[h][0:L, 0:L], in0=ex[0:L, 0:L], in1=CBm[h // 2][0:L, 0:L], op=ALU.mult),
                          reads=[exn, "sdCBm%d" % (h // 2)], writes=["sdsc%d" % h])
                if SUB <= 4:
                    continue
                pyd = self.psum[5]
                for h in range(4):
                    P.pe(lambda e, h=h, L=L: e.matmul(pyd[0:L, 64 * h:64 * h + 64], sc[h][0:L, 0:L], xdt[0:L, 64 * h:64 * h + 64], start=True, stop=True),
                         reads=["sdsc%d" % h, "sdxdt"], writes=["ps5"])
                if SUB <= 5:
                    continue
                if not samp:
                    P.act(lambda e, L=L: e.activation(out=ydg[0:L, :], in_=pyd[0:L, 0:256], func=AF.Copy), reads=["ps5"], writes=["sdydg"])
                    pyo = self.psum[6]
                    for g in range(2):
                        P.pe(lambda e, g=g, t0=t0: e.matmul(pyo[:, 128 * g:128 * g + 128], xc[4 + g][:, t0:t0 + 128], hTb[:, g, :], start=True, stop=True),
                             reads=[CS[g], "sdhTb"], writes=["ps6"])
                    for h in range(4):
                        P.dve(lambda e, h=h, c=c: e.scalar_tensor_tensor(out=ytok[:, 64 * h:64 * h + 64], in0=pyo[:, 64 * h:64 * h + 64], scalar=tokq[:, 3, c, h:h + 1], in1=ydg[:, 64 * h:64 * h + 64],
                                                                      op0=ALU.mult, op1=ALU.add), reads=["ps6", "sdydg", "sdtokq"], writes=["sdytok"])
                else:
                    P.act(lambda e, L=L: e.activation(out=ytok[0:L, :], in_=pyd[0:L, 0:256], func=AF.Copy), reads=["ps5"], writes=["sdytok"])
                if SUB <= 6:
                    continue
                for h in range(4):
                    P.dve(lambda e, h=h, L=L: e.scalar_tensor_tensor(out=ytok[0:L, 64 * h:64 * h + 64], in0=xtok[0:L, 64 * h:64 * h + 64], scalar=dcol[0:L, h:h + 1], in1=ytok[0:L, 64 * h:64 * h + 64],
                                                                  op0=ALU.mult, op1=ALU.add), reads=["sdxtok", "sdytok", C_], writes=["sdytok"])
                if SUB <= 7:
                    continue
                pyt = self.psum[7]
                for g in range(2):
                    P.pe(lambda e, g=g, L=L: e.transpose(pyt[:, 128 * g:128 * g + L], ytok[0:L, 128 * g:128 * g + 128], ident[0:L, 0:L]), reads=["sdytok", "ident"], writes=["ps7"])
                if not samp:
                    P.act(lambda e, t0=t0: e.activation(out=yss[:, :, t0:t0 + 128], in_=pyt[:, 0:256].rearrange("p (g t) -> p g t", t=128), func=AF.Copy), reads=["ps7"], writes=["sdyss"])
                    if SUB <= 8:
                        continue
                    pcd = self.psum[6]
                    P.pe(lambda e, c=c: e.matmul(pcd[:, 256:260], elast[:], tokq[:, 3, c, :], start=True, stop=True), reads=[C_, "sdtokq"], writes=["ps6"])
                    P.act(lambda e: e.activation(out=cdB[:], in_=pcd[:, 256:260], func=AF.Copy), reads=["ps6"], writes=["sdcdB"])
                    pst = self.psum[0]
                    for g in range(2):
                        P.pe(lambda e, g=g: e.matmul(pst[:, 128 * g:128 * g + 128], Btok[:, 128 * g:128 * g + 128], xdd[:, 128 * g:128 * g + 128], start=True, stop=True),
                             reads=["sdBtok", "sdxdd"], writes=["ps0"])
                    for h in range(4):
                        g, hh = h // 2, h % 2
                        P.dve(lambda e, h=h, g=g, hh=hh: e.scalar_tensor_tensor(out=hT[:, g, 64 * hh:64 * hh + 64], in0=hT[:, g, 64 * hh:64 * hh + 64], scalar=cdB[:, h:h + 1],
                                                                             in1=pst[:, 64 * h:64 * h + 64], op0=ALU.mult, op1=ALU.add), reads=["sdhT", "sdcdB", "ps0"], writes=["sdhT"])
                    P.act(lambda e: e.activation(out=hTb[:], in_=hT[:], func=AF.Copy), reads=["sdhT"], writes=["sdhTb"])
                else:
                    pyo = self.psum[6]
                    for s in range(NSEQ):
                        hn = h0n[s % 2]; hnn = "sdh0n%d" % (s % 2)
                        ht = h0T[s % 2]; htn = "sdh0T%d" % (s % 2)
                        hw = hnew[s % 2]; hwn = "sdhnew%d" % (s % 2)
                        xm = xdm[s % 2]; xmn = "sdxdm%d" % (s % 2)
                        pt = self.psum[1 + (s % 2)]; ptn = "ps%d" % (1 + s % 2)
                        pn_ = self.psum[3 + (s % 2)]; pnn = "ps%d" % (3 + s % 2)
                        P.dma(lambda e, s=s, hn=hn: e.dma_start(out=hn[:], in_=dr["st_ssd"][l, s].rearrange("(g a) p n -> (a p) g n", g=2)), writes=[hnn])
                        for g in range(2):
                            P.pe(lambda e, g=g, hn=hn, pt=pt: e.transpose(pt[:, 128 * g:128 * g + 128], hn[:, g, :], ident[:]), reads=[hnn, "ident"], writes=[ptn])
                        P.act(lambda e, ht=ht, pt=pt: e.activation(out=ht[:], in_=pt[:, 0:256].rearrange("p (g t) -> p g t", t=128), func=AF.Copy), reads=[ptn], writes=[htn])
                        for g in range(2):
                            P.pe(lambda e, g=g, s=s, ht=ht: e.matmul(pyo[:, 64 * g + TS * s:64 * g + TS * s + TS], ht[:, g, :], xc[4 + g][:, TP + TS * s:TP + TS * s + TS], start=True, stop=True),
                                 reads=[htn, CS[g]], writes=["ps6"])
                        P.dve(lambda e, s=s, xm=xm: e.tensor_scalar(out=xm[:], in0=xdd[0:64, :], scalar1=seqm[:, s:s + 1], scalar2=None, op0=ALU.mult), reads=["sdxdd", C_], writes=[xmn])
                        for g in range(2):
                            P.pe(lambda e, g=g, xm=xm, pn_=pn_: e.matmul(pn_[:, 128 * g:128 * g + 128], xm[:, 128 * g:128 * g + 128], Btok[0:64, 128 * g:128 * g + 128], start=True, stop=True),
                                 reads=[xmn, "sdBtok"], writes=[pnn])
                        for g in range(2):
                            P.dve(lambda e, g=g, s=s, hn=hn, hw=hw, pn_=pn_: e.scalar_tensor_tensor(out=hw[:, g, :], in0=hn[:, g, :], scalar=cdS[:, g, s:s + 1], in1=pn_[:, 128 * g:128 * g + 128],
                                                                                              op0=ALU.mult, op1=ALU.add), reads=[hnn, "sdcdS", pnn], writes=[hwn])
                        P.dma(lambda e, s=s, hw=hw: e.dma_start(out=dr["ssd_s"][l, s].rearrange("(g a) p n -> (a p) g n", g=2), in_=hw[:]), reads=[hwn])
                    for g in range(2):
                        P.dve(lambda e, g=g: e.tensor_tensor(out=yoS[:, g, :], in0=pyo[:, 64 * g:64 * g + 64], in1=eaS[:, g, :], op=ALU.mult), reads=["ps6", "sdeaS"], writes=["sdyoS"])
                        P.dve(lambda e, g=g: e.tensor_tensor(out=yss[:, g, TP:T], in0=pyt[:, 128 * g:128 * g + TSAMP], in1=yoS[:, g, :], op=ALU.add), reads=["ps7", "sdyoS"], writes=["sdyss"])
            pf = self.psum[1]
            for g in range(2):
                P.pe(lambda e, g=g: e.transpose(pf[:, 128 * g:128 * g + 128], hT[:, g, :], ident[:]), reads=["sdhT", "ident"], writes=["ps1"])
            P.act(lambda e: e.activation(out=hnew[0][:], in_=pf[:, 0:256].rearrange("p (g t) -> p g t", t=128), func=AF.Copy), reads=["ps1"], writes=["sdhnew0"])
            P.dma(lambda e: e.dma_start(out=dr["ssd_p"][l].rearrange("(g a) p n -> (a p) g n", g=2), in_=hnew[0][:]), reads=["sdhnew0"])
            P.flush()
            gq = [t_("gq%d" % i, [128, 2, 512]) for i in range(1)] * 2
            gsq = [t_("gsq%d" % i, [128, 2, 512], BF16) for i in range(1)] * 2
            grs = [t_("grs%d" % i, [128, 512]) for i in range(1)] * 2
            epsc = self.epsc
            for ti, (s, n) in enumerate(TILES):
                q = gq[0]; qn = "sdgq0"
                sq = gsq[0]; sqn = "sdgsq0"
                rs = grs[0]; rsn = "sdgrs0"
                ps = self.psum[ti % 2]; pn = "ps%d" % (ti % 2)
                for k in range(2):
                    P.act(lambda e, k=k, q=q, s=s, n=n: e.activation(out=q[:, k, 0:n], in_=zz["z%d" % k][:, s:s + n], func=AF.Silu), reads=self.ures("z%d" % k, [ti]), writes=[qn])
                    P.dve(lambda e, k=k, q=q, s=s, n=n: e.tensor_tensor(out=q[:, k, 0:n], in0=q[:, k, 0:n], in1=yss[:, k, s:s + n], op=ALU.mult), reads=[qn, "sdyss"], writes=[qn])
                    P.act(lambda e, k=k, q=q, sq=sq, n=n: e.activation(out=sq[:, k, 0:n], in_=q[:, k, 0:n], func=AF.Square), reads=[qn], writes=[sqn])
                for k in range(2):
                    P.pe(lambda e, k=k, ps=ps, sq=sq, n=n: e.matmul(ps[:, 0:n], onesb[:], sq[:, k, 0:n], start=(k == 0), stop=(k == 1)), reads=[sqn, "onesb"], writes=[pn])
                P.act(lambda e, rs=rs, ps=ps, n=n: e.activation(out=rs[:, 0:n], in_=ps[:, 0:n], func=AF.Sqrt, bias=epsc[:], scale=1.0 / 256), reads=[pn, "epsc"], writes=[rsn])
                P.dve(lambda e, rs=rs, n=n: e.reciprocal(out=rs[:, 0:n], in_=rs[:, 0:n]), reads=[rsn], writes=[rsn])
                for k in range(2):
                    P.dve(lambda e, k=k, q=q, rs=rs, s=s, n=n: e.scalar_tensor_tensor(out=ymix[:, k, s:s + n], in0=q[:, k, 0:n], scalar=ng[:, k:k + 1], in1=rs[:, 0:n], op0=ALU.mult, op1=ALU.mult),
                          reads=[qn, rsn, C_], writes=["ymix%d_%d" % (k, ti)])
            P.flush()

    def mix_rwkv(self, l):
        P = self.P
        dr = self.dram
        ymix, ident = self.ymix, self.ident
        base = 1028
        scr = dr["rw_scr"]
        NKK, WW, KP, BB, RR, VV, GG = range(7)
        with contextlib.ExitStack() as st:
            t_ = lambda nm, shp, dt=F32: P.sb("rw" + nm, shp, dt, stack=st)
            C_ = "rwc"
            chunks = [("r0", base, 128), ("r1", base + 128, 128), ("k0", base + 256, 128), ("k1", base + 384, 128),
                      ("v0", base + 512, 128), ("v1", base + 640, 128), ("wa", base + 768, 128), ("gd", base + 896, 128)]
            U = self.inproj(l, chunks, st, dtype=BF16, pad=1)
            shf = t_("shf", [128, 8, 1 + NSEQ])
            for ci, (name, c0, ncol) in enumerate(chunks):
                f0 = c0 - base
                R = self.ures(name)
                P.dve(lambda e, ci=ci, t=U[name]: e.tensor_copy(out=shf[:, ci, 0:1], in_=t[:, TP:TP + 1]), reads=R, writes=["shf%d" % ci])
                P.dve(lambda e, ci=ci, t=U[name]: e.tensor_copy(out=shf[:, ci, 1:1 + NSEQ], in_=t[:, 1 + TP:1 + T].rearrange("p (s t) -> p s t", t=TS)[:, :, TS - 1]),
                      reads=R, writes=["shf%d" % ci])
                P.dma(lambda e, f0=f0, ci=ci: e.dma_start(out=dr["shift_p"][l:l + 1, f0:f0 + 128].rearrange("t f -> f t"), in_=shf[:, ci, 0:1]), reads=["shf%d" % ci], group="sto")
                P.dma(lambda e, f0=f0, ci=ci: e.dma_start(out=dr["shift_s"][l, :, f0:f0 + 128].rearrange("s f -> f s"), in_=shf[:, ci, 1:1 + NSEQ]), reads=["shf%d" % ci], group="sto")
            mu = t_("mu", [128, 8]); w0n = t_("w0n", [128, 2]); a0 = t_("a0", [128, 2]); kkc = t_("kkc", [128, 2]); kac = t_("kac", [128, 2]); ka1 = t_("ka1", [128, 2])
            rkc = t_("rkc", [128, 2]); lng = t_("lng", [128, 2]); lnb = t_("lnb", [128, 2]); mh = t_("mh", [128, 1]); e5 = t_("e5", [128, 1]); one1 = t_("one1", [128, 1])
            WAf = t_("WAf", [128, 256]); WAb = t_("WAb", [128, 256], BF16); G2f = t_("G2f", [128, 256]); G2b = t_("G2b", [128, 256], BF16)
            bdo = t_("bdo", [128, 128]); I2 = t_("I2", [128, 64]); Esel = t_("Esel", [64, 2, 128]); Ehp = t_("Ehp", [128, 2, 64]); onec = t_("onec", [128, 1])
            ld = lambda dst, src: P.dma(lambda e: e.dma_start(out=dst, in_=src), writes=[C_])
            ld(mu[:], dr["rwkv_mu"][l].rearrange("(c p) -> p c", p=128))
            for tile_, nm in [(w0n, "rwkv_w0"), (a0, "rwkv_a0"), (kkc, "rwkv_k_k"), (kac, "rwkv_k_a"), (lng, "rwkv_ln_g"), (lnb, "rwkv_ln_b")]:
                ld(tile_[:], dr[nm][l].rearrange("(c p) -> p c", p=128))
            ld(rkc[:], dr["rwkv_r_k"][l].rearrange("(c a) j -> (a j) c", c=2))
            ld(WAf[0:64, :], dr["rwkv_w2"][l]); ld(WAf[64:128, :], dr["rwkv_a2"][l]); ld(G2f[:], dr["rwkv_g2"][l])
            ld(bdo[:], dr["c_bdones"]); ld(I2[:], dr["c_I2"]); ld(Esel[:], dr["c_Esel"]); ld(Ehp[:], dr["c_Ehp"])
            V = lambda fn: P.dve(fn, reads=[C_], writes=[C_])
            V(lambda e: e.tensor_copy(out=WAb[:], in_=WAf[:]))
            V(lambda e: e.tensor_copy(out=G2b[:], in_=G2f[:]))
            V(lambda e: e.tensor_scalar(out=w0n[:], in0=w0n[:], scalar1=-1.0, scalar2=None, op0=ALU.mult))
            V(lambda e: e.tensor_scalar(out=ka1[:], in0=kac[:], scalar1=-1.0, scalar2=1.0, op0=ALU.mult, op1=ALU.add))
            V(lambda e: e.memset(mh[:], -0.5)); V(lambda e: e.memset(e5[:], 64e-5)); V(lambda e: e.memset(one1[:], 1.0)); V(lambda e: e.memset(onec[:], 1.0))
            with contextlib.ExitStack() as s2:
                w_ = lambda nm, dt=F32: P.sb("rw2" + nm, [128, 256], dt, stack=s2)
                dd = w_("dd"); xwa = w_("xwa"); xgd = w_("xgd"); wab = w_("wab", BF16); sgb = w_("sgb", BF16)
                xk = w_("xk"); kk = w_("kk"); sq = w_("sq"); nr = w_("nr"); ta = w_("ta"); tt = w_("tt")
                outs = [[w_("o%d_%d" % (i, j)) for j in range(7)] for i in range(2)]
                prevS = P.sb("rw2prevS", [128, 8, TSAMP], F32, stack=s2)
                for ci, (name, c0, ncol) in enumerate(chunks):
                    f0 = c0 - base
                    P.dma(lambda e, ci=ci, f0=f0: e.dma_start(out=prevS[:, ci, :].rearrange("p (s t) -> p s t", t=TS)[:, :, 0], in_=dr["st_shift"][l][:, f0:f0 + 128].rearrange("s f -> f s")),
                          writes=["rwprevS%d" % ci])
                    P.dve(lambda e, ci=ci, t=U[name]: e.tensor_copy(out=prevS[:, ci, :].rearrange("p (s t) -> p s t", t=TS)[:, :, 1:TS], in_=t[:, 1 + TP:1 + T].rearrange("p (s t) -> p s t", t=TS)[:, :, 0:TS - 1]),
                          reads=self.ures(name), writes=["rwprevS%d" % ci])
                cidx = {nm: i for i, (nm, _, _) in enumerate(chunks)}

                def xs_of(name, ti, s, n, dst, rn):
                    ci = cidx[name]
                    u = U[name]
                    R = self.ures(name) + ["u_%s_pad" % name]
                    if s < TP:
                        P.dve(lambda e, u=u, s=s, n=n: e.tensor_tensor(out=dd[:, 0:n], in0=u[:, s:s + n], in1=u[:, 1 + s:1 + s + n], op=ALU.subtract), reads=R, writes=["rwdd"])
                    else:
                        P.dve(lambda e, u=u, s=s, n=n, ci=ci: e.tensor_tensor(out=dd[:, 0:n], in0=prevS[:, ci, :], in1=u[:, 1 + s:1 + s + n], op=ALU.subtract),
                              reads=R + ["rwprevS%d" % ci], writes=["rwdd"])
                    P.dve(lambda e, u=u, s=s, n=n, ci=ci, dst=dst: e.scalar_tensor_tensor(out=dst[:, 0:n], in0=dd[:, 0:n], scalar=mu[:, ci:ci + 1], in1=u[:, 1 + s:1 + s + n], op0=ALU.mult, op1=ALU.add),
                          reads=R + ["rwdd", C_], writes=[rn])

                for ti, (s, n) in enumerate([(i * 256, 256) for i in range(8)] + [(TP, TSAMP)]):
                    xs_of("wa", ti, s, n, xwa, "rwxwa")
                    xs_of("gd", ti, s, n, xgd, "rwxgd")
                    P.act(lambda e, n=n: e.activation(out=wab[0:64, 0:n], in_=xwa[0:64, 0:n], func=AF.Tanh), reads=["rwxwa"], writes=["rwwab"])
                    P.act(lambda e, n=n: e.activation(out=wab[64:128, 0:n], in_=xwa[64:128, 0:n], func=AF.Copy), reads=["rwxwa"], writes=["rwwab"])
                    P.act(lambda e, n=n: e.activation(out=sgb[:, 0:n], in_=xgd[:, 0:n], func=AF.Sigmoid), reads=["rwxgd"], writes=["rwsgb"])
                    for c in range(2):
                        O = outs[c]
                        on = ["rwo%d_%d" % (c, j) for j in range(7)]
                        pw, pa, pg, pq = self.psum[0], self.psum[1], self.psum[2], self.psum[3]
                        cs = slice(128 * c, 128 * c + 128)
                        P.pe(lambda e, n=n, cs=cs: e.matmul(pw[:, 0:n], WAb[0:64, cs], wab[0:64, 0:n], start=True, stop=True), reads=["rwwab", C_], writes=["ps0"])
                        P.pe(lambda e, n=n, cs=cs: e.matmul(pa[:, 0:n], WAb[64:128, cs], wab[64:128, 0:n], start=True, stop=True), reads=["rwwab", C_], writes=["ps1"])
                        P.pe(lambda e, n=n, cs=cs: e.matmul(pg[:, 0:n], G2b[:, cs], sgb[:, 0:n], start=True, stop=True), reads=["rwsgb", C_], writes=["ps2"])
                        P.act(lambda e, n=n, c=c: e.activation(out=tt[:, 0:n], in_=pw[:, 0:n], func=AF.Exp, bias=w0n[:, c:c + 1], scale=-1.0), reads=["ps0", C_], writes=["rwtt"])
                        P.act(lambda e, n=n: e.activation(out=tt[:, 0:n], in_=tt[:, 0:n], func=AF.Ln, bias=one1[:], scale=1.0), reads=["rwtt", C_], writes=["rwtt"])
                        P.act(lambda e, n=n: e.activation(out=tt[:, 0:n], in_=tt[:, 0:n], func=AF.Exp, bias=mh[:], scale=-1.0), reads=["rwtt", C_], writes=["rwtt"])
                        P.act(lambda e, n=n, O=O: e.activation(out=O[WW][:, 0:n], in_=tt[:, 0:n], func=AF.Exp, scale=-1.0), reads=["rwtt"], writes=[on[WW]])
                        P.act(lambda e, n=n, c=c: e.activation(out=ta[:, 0:n], in_=pa[:, 0:n], func=AF.Sigmoid, bias=a0[:, c:c + 1], scale=1.0), reads=["ps1", C_], writes=["rwta"])
                        P.act(lambda e, n=n, O=O: e.activation(out=O[GG][:, 0:n], in_=pg[:, 0:n], func=AF.Copy), reads=["ps2"], writes=[on[GG]])
                        xs_of("k%d" % c, ti, s, n, xk, "rwxk")
                        xs_of("r%d" % c, ti, s, n, O[RR], on[RR])
                        xs_of("v%d" % c, ti, s, n, O[VV], on[VV])
                        P.dve(lambda e, n=n, c=c: e.tensor_scalar(out=kk[:, 0:n], in0=xk[:, 0:n], scalar1=kkc[:, c:c + 1], scalar2=None, op0=ALU.mult), reads=["rwxk", C_], writes=["rwkk"])
                        P.act(lambda e, n=n: e.activation(out=sq[:, 0:n], in_=kk[:, 0:n], func=AF.Square), reads=["rwkk"], writes=["rwsq"])
                        P.pe(lambda e, n=n: e.matmul(pq[:, 0:n], bdo[:], sq[:, 0:n], start=True, stop=True), reads=["rwsq", C_], writes=["ps3"])
                        P.act(lambda e, n=n: e.activation(out=nr[:, 0:n], in_=pq[:, 0:n], func=AF.Sqrt), reads=["ps3"], writes=["rwnr"])
                        P.dve(lambda e, n=n: e.tensor_scalar(out=nr[:, 0:n], in0=nr[:, 0:n], scalar1=1e-12, scalar2=None, op0=ALU.max), reads=["rwnr"], writes=["rwnr"])
                        P.dve(lambda e, n=n: e.reciprocal(out=nr[:, 0:n], in_=nr[:, 0:n]), reads=["rwnr"], writes=["rwnr"])
                        P.dve(lambda e, n=n, O=O: e.scalar_tensor_tensor(out=O[NKK][:, 0:n], in0=kk[:, 0:n], scalar=-1.0, in1=nr[:, 0:n], op0=ALU.mult, op1=ALU.mult), reads=["rwkk", "rwnr"], writes=[on[NKK]])
                        P.dve(lambda e, n=n, O=O: e.scalar_tensor_tensor(out=O[BB][:, 0:n], in0=O[NKK][:, 0:n], scalar=-1.0, in1=ta[:, 0:n], op0=ALU.mult, op1=ALU.mult), reads=[on[NKK], "rwta"], writes=[on[BB]])
                        P.dve(lambda e, n=n, c=c: e.tensor_scalar(out=ta[:, 0:n], in0=ta[:, 0:n], scalar1=kac[:, c:c + 1], scalar2=ka1[:, c:c + 1], op0=ALU.mult, op1=ALU.add), reads=["rwta", C_, on[BB]], writes=["rwta"])
                        P.dve(lambda e, n=n, O=O: e.tensor_tensor(out=O[KP][:, 0:n], in0=xk[:, 0:n], in1=ta[:, 0:n], op=ALU.mult), reads=["rwxk", "rwta"], writes=[on[KP]])
                        for j in range(7):
                            P.dma(lambda e, c=c, j=j, s=s, n=n, O=O: e.dma_start(out=scr[c, j, :, s:s + n], in_=O[j][:, 0:n]), reads=[on[j]], writes=["rwscr"], group="rwscr")
                P.flush()
            NBK = 256
            Bk = [t_("Bk%d" % c, [128, 7, NBK]) for c in range(2)]
            ST = [t_("ST%d" % c, [128, 64]) for c in range(2)]
            STk = [t_("STk%d" % c, [128, 64]) for c in range(2)]
            T1 = [t_("T1%d" % c, [128, 64]) for c in range(2)]
            T2 = [t_("T2%d" % c, [128, 64]) for c in range(2)]
            Vex = [t_("Vex%d" % c, [128, 8, 64]) for c in range(2)]
            Zl = [t_("Zl%d" % i, [64, 2, 128]) for i in range(2)]
            Sout = [t_("Sout%d" % i, [64, 2, 64]) for i in range(2)]
            Ysb = t_("Ysb", [64, 2 * NBK]); ysc = t_("ysc", [128, NBK]); ycen = t_("ycen", [128, NBK]); ysq = t_("ysq", [128, NBK]); yrs = t_("yrs", [128, NBK]); ypr = t_("ypr", [128, NBK])
            pSA = self.psum[0]
            pVb = [self.psum[1], self.psum[2]]
            pY = [self.psum[3], self.psum[4]]
            pM = self.psum[5]; pV2 = self.psum[6]; pX = self.psum[7]
            for c in range(2):
                P.dve(lambda e, c=c: e.memset(ST[c][:], 0.0), writes=["rwST%d" % c])
            for i in range(2):
                P.dve(lambda e, i=i: e.memset(Zl[i][:], 0.0), writes=["rwZl%d" % i])
            sa_slot = [0]

            def load_block(c, t0, n):
                P.dma(lambda e, c=c, t0=t0, n=n: e.dma_start(out=Bk[c][:, :, 0:n], in_=scr[c, :, :, t0:t0 + n].rearrange("a p t -> p a t")),
                      reads=["rwscr"], writes=["rwBk%d" % c])

            def vb_group(c, j0, ng):
                P.pool(lambda e, c=c, j0=j0, ng=ng: e.tensor_tensor(out=Vex[c][:, 0:ng, :], in0=I2[:].unsqueeze(1).to_broadcast([128, ng, 64]),
                                                                   in1=Bk[c][:, VV, j0:j0 + ng].unsqueeze(2).to_broadcast([128, ng, 64]), op=ALU.mult),
                       reads=["rwBk%d" % c, C_], writes=["rwVex%d" % c])
                P.pe(lambda e, c=c, ng=ng: e.matmul(pVb[c][:, 0:64 * ng], bdo[:], Vex[c][:, 0:ng, :].rearrange("p a b -> p (a b)"), start=True, stop=True),
                     reads=["rwVex%d" % c, C_], writes=["ps%d" % (1 + c)])

            def step(c, j, ycol):
                bn = "rwBk%d" % c; sn = "rwST%d" % c
                slot = sa_slot[0] % 8; sa_slot[0] += 1
                san = "rwsa%d" % slot
                vcol = (j % 8) * 64
                P.act(lambda e, c=c, j=j: e.activation(out=STk[c][:], in_=ST[c][:], func=AF.Copy, scale=Bk[c][:, NKK, j:j + 1]), reads=[sn, bn], writes=["rwSTk%d" % c])
                P.pool(lambda e, c=c, j=j: e.tensor_scalar(out=T1[c][:], in0=ST[c][:], scalar1=Bk[c][:, WW, j:j + 1], scalar2=None, op0=ALU.mult), reads=[sn, bn], writes=["rwT1%d" % c])
                P.pe(lambda e, c=c, slot=slot: e.matmul(pSA[:, 64 * slot:64 * slot + 64], bdo[:], STk[c][:], start=True, stop=True), reads=["rwSTk%d" % c, C_], writes=[san])
                P.dve(lambda e, c=c, j=j, vcol=vcol: e.scalar_tensor_tensor(out=T2[c][:], in0=pVb[c][:, vcol:vcol + 64], scalar=Bk[c][:, KP, j:j + 1], in1=T1[c][:], op0=ALU.mult, op1=ALU.add),
                      reads=["ps%d" % (1 + c), bn, "rwT1%d" % c], writes=["rwT2%d" % c])
                P.dve(lambda e, c=c, j=j, slot=slot: e.scalar_tensor_tensor(out=ST[c][:], in0=pSA[:, 64 * slot:64 * slot + 64], scalar=Bk[c][:, BB, j:j + 1], in1=T2[c][:], op0=ALU.mult, op1=ALU.add),
                      reads=[san, bn, "rwT2%d" % c], writes=[sn])
                for hp in range(2):
                    P.pe(lambda e, c=c, j=j, hp=hp, ycol=ycol: e.matmul(pY[c][0:64, NBK * hp + ycol:NBK * hp + ycol + 1], ST[c][64 * hp:64 * hp + 64, :], Bk[c][64 * hp:64 * hp + 64, RR, j:j + 1], start=True, stop=True),
                         reads=[sn, bn], writes=["ps%d" % (3 + c)])

            def post(c, t0, n):
                bn = "rwBk%d" % c
                P.act(lambda e, c=c: e.activation(out=Ysb[:], in_=pY[c][0:64, :], func=AF.Copy), reads=["ps%d" % (3 + c)], writes=["rwYsb"])
                for hp in range(2):
                    P.pe(lambda e, hp=hp, n=n: e.matmul(pX[:, 0:n], Esel[:, hp, :], Ysb[:, NBK * hp:NBK * hp + n], start=(hp == 0), stop=(hp == 1)), reads=["rwYsb", C_], writes=["ps7"])
                P.act(lambda e, n=n: e.activation(out=ysc[:, 0:n], in_=pX[:, 0:n], func=AF.Copy), reads=["ps7"], writes=["rwysc"])
                P.pe(lambda e, n=n: e.matmul(pM[:, 0:n], bdo[:], ysc[:, 0:n], start=True, stop=True), reads=["rwysc", C_], writes=["ps5"])
                P.dve(lambda e, n=n: e.scalar_tensor_tensor(out=ycen[:, 0:n], in0=pM[:, 0:n], scalar=-1.0 / 64, in1=ysc[:, 0:n], op0=ALU.mult, op1=ALU.add), reads=["ps5", "rwysc"], writes=["rwycen"])
                P.act(lambda e, n=n: e.activation(out=ysq[:, 0:n], in_=ycen[:, 0:n], func=AF.Square), reads=["rwycen"], writes=["rwysq"])
                P.pe(lambda e, n=n: e.matmul(pV2[:, 0:n], bdo[:], ysq[:, 0:n], start=True, stop=True), reads=["rwysq", C_], writes=["ps6"])
                P.act(lambda e, n=n: e.activation(out=yrs[:, 0:n], in_=pV2[:, 0:n], func=AF.Sqrt, bias=e5[:], scale=1.0 / 64), reads=["ps6", C_], writes=["rwyrs"])
                P.dve(lambda e, n=n: e.reciprocal(out=yrs[:, 0:n], in_=yrs[:, 0:n]), reads=["rwyrs"], writes=["rwyrs"])
                P.dve(lambda e, n=n: e.tensor_tensor(out=ycen[:, 0:n], in0=ycen[:, 0:n], in1=yrs[:, 0:n], op=ALU.mult), reads=["rwycen", "rwyrs"], writes=["rwycen"])
                P.dve(lambda e, n=n, c=c: e.tensor_scalar(out=ycen[:, 0:n], in0=ycen[:, 0:n], scalar1=lng[:, c:c + 1], scalar2=lnb[:, c:c + 1], op0=ALU.mult, op1=ALU.add), reads=["rwycen", C_], writes=["rwycen"])
                P.dve(lambda e, n=n, c=c: e.scalar_tensor_tensor(out=ypr[:, 0:n], in0=Bk[c][:, RR, 0:n], scalar=rkc[:, c:c + 1], in1=Bk[c][:, KP, 0:n], op0=ALU.mult, op1=ALU.mult), reads=[bn, C_], writes=["rwypr"])
                P.pe(lambda e, n=n: e.matmul(pM[:, 0:n], bdo[:], ypr[:, 0:n], start=True, stop=True), reads=["rwypr", C_], writes=["ps5"])
                P.dve(lambda e, n=n, c=c: e.tensor_tensor(out=ypr[:, 0:n], in0=pM[:, 0:n], in1=Bk[c][:, VV, 0:n], op=ALU.mult), reads=["ps5", bn], writes=["rwypr"])
                P.dve(lambda e, n=n: e.tensor_tensor(out=ycen[:, 0:n], in0=ycen[:, 0:n], in1=ypr[:, 0:n], op=ALU.add), reads=["rwycen", "rwypr"], writes=["rwycen"])
                ti0 = min(t0 // 512, 4)
                P.dve(lambda e, n=n, c=c, t0=t0: e.tensor_tensor(out=ymix[:, 2 + c, t0:t0 + n], in0=ycen[:, 0:n], in1=Bk[c][:, GG, 0:n], op=ALU.mult), reads=["rwycen", bn],
                      writes=["ymix%d_%d" % (2 + c, ti0)])

            def store_state(c, dst_fn, idx):
                so = Sout[idx % 2]; son = "rwSout%d" % (idx % 2)
                for hp in range(2):
                    P.pe(lambda e, c=c, hp=hp: e.matmul(pX[0:64, 256 + 64 * hp:256 + 64 * hp + 64], ST[c][:], Ehp[:, hp, :], start=True, stop=True), reads=["rwST%d" % c, C_], writes=["ps7"])
                P.act(lambda e, so=so: e.activation(out=so[:].rearrange("p a b -> p (a b)"), in_=pX[0:64, 256:384], func=AF.Copy), reads=["ps7"], writes=[son])
                for hp in range(2):
                    P.dma(lambda e, so=so, hp=hp, c=c: e.dma_start(out=dst_fn(2 * c + hp), in_=so[:, hp, :]), reads=[son])

            for b0 in range(0, TP, NBK):
                for c in range(2):
                    load_block(c, b0, NBK)
                for j in range(NBK):
                    if j % 8 == 0:
                        for c in range(2):
                            vb_group(c, j, 8)
                    for c in range(2):
                        step(c, j, j)
                for c in range(2):
                    post(c, b0, NBK)
            for c in range(2):
                store_state(c, lambda h: dr["rwkv_p"][l, h], c)
            for c in range(2):
                load_block(c, TP, TSAMP)
            for s in range(NSEQ):
                for c in range(2):
                    z = Zl[c]; zn = "rwZl%d" % c
                    for hp in range(2):
                        P.dma(lambda e, z=z, hp=hp, s=s, c=c: e.dma_start(out=z[:, hp, 64 * hp:64 * hp + 64], in_=dr["st_rwkv"][l, s, 2 * c + hp]), writes=[zn])
                    for hp in range(2):
                        P.pe(lambda e, z=z, hp=hp: e.matmul(pX[:, 384:448], z[:, hp, :], ident[0:64, 0:64], start=(hp == 0), stop=(hp == 1)), reads=[zn, "ident"], writes=["ps7"])
                    P.act(lambda e, c=c: e.activation(out=ST[c][:], in_=pX[:, 384:448], func=AF.Copy), reads=["ps7"], writes=["rwST%d" % c])
                for t in range(TS):
                    j = TS * s + t
                    if j % 8 == 0:
                        for c in range(2):
                            vb_group(c, j, 8)
                    for c in range(2):
                        step(c, j, j)
                for c in range(2):
                    store_state(c, (lambda s: (lambda h: dr["rwkv_s"][l, s, h]))(s), s * 2 + c)
            for c in range(2):
                post(c, TP, TSAMP)
            P.flush()

    def outproj(self, l):
        P = self.P
        xT, ymix = self.xT, self.ymix
        w_v = self.dram["w_out"][l].rearrange("(k p) c -> p k c", p=128)
        with contextlib.ExitStack() as st:
            stg = [P.sb("opstg%d" % i, [128, 8, 128], F32, stack=st) for i in range(2)]
            wbs = [P.sb("opwb%d" % i, [128, 8, 128], BF16, stack=st) for i in range(2)]
            for m in range(8):
                i = m % 2
                b = stg[i]; w = wbs[i]
                P.dma(lambda e, b=b, m=m: e.dma_start(out=b[:], in_=w_v[:, :, m * 128:(m + 1) * 128]), writes=["opstg%d" % i])
                P.pool(lambda e, b=b, w=w: e.tensor_copy(out=w[:], in_=b[:]), reads=["opstg%d" % i], writes=["opwb%d" % i])
                for ti, (s, n) in enumerate(TILES):
                    c = m * 5 + ti
                    ps = self.psum[c % 2]; pn = "ps%d" % (c % 2)
                    for k in range(8):
                        P.pe(lambda e, ps=ps, w=w, k=k, s=s, n=n: e.matmul(ps[:, 0:n], w[:, k, :], ymix[:, k, s:s + n], start=(k == 0), stop=(k == 7)),
                             reads=["opwb%d" % i, "ymix%d_%d" % (k, ti)], writes=[pn])
                    P.dve(lambda e, ps=ps, m=m, s=s, n=n: e.tensor_tensor(out=xT[:, m, s:s + n], in0=ps[:, 0:n], in1=xT[:, m, s:s + n], op=ALU.add),
                          reads=[pn, "xT%d_%d" % (m, ti)], writes=["xT%d_%d" % (m, ti)])
            P.flush()

    def final(self):
        P = self.P
        xT, onesb, gains, epsc, ident = self.xT, self.onesb, self.gains, self.epsc, self.ident
        with contextlib.ExitStack() as st:
            sq = P.sb("fsq", [128, 8, 512], BF16, stack=st)
            rstd = P.sb("frstd", [128, 512], F32, stack=st)
            xo = P.sb("fxo", [128, 8, 512], F32, stack=st)
            ytok = [P.sb("fytok%d" % i, [128, D], F32, stack=st) for i in range(2)]
            nblk = 0
            for ti, (s, n) in enumerate(TILES):
                ps = self.psum[6]; pn = "ps6"
                for k in range(8):
                    P.act(lambda e, k=k, s=s, n=n: e.activation(out=sq[:, k, 0:n], in_=xT[:, k, s:s + n], func=AF.Square),
                          reads=["xT%d_%d" % (k, ti)], writes=["fsq_%d" % k])
                for k in range(8):
                    P.pe(lambda e, k=k, n=n: e.matmul(ps[:, 0:n], onesb[:], sq[:, k, 0:n], start=(k == 0), stop=(k == 7)),
                         reads=["fsq_%d" % k, "onesb"], writes=[pn])
                P.act(lambda e, n=n: e.activation(out=rstd[:, 0:n], in_=ps[:, 0:n], func=AF.Sqrt, bias=epsc[:], scale=1.0 / D), reads=[pn, "epsc"], writes=["frstd"])
                P.dve(lambda e, n=n: e.reciprocal(out=rstd[:, 0:n], in_=rstd[:, 0:n]), reads=["frstd"], writes=["frstd"])
                for k in range(8):
                    P.dve(lambda e, k=k, s=s, n=n: e.scalar_tensor_tensor(out=xo[:, k, 0:n], in0=xT[:, k, s:s + n], scalar=gains[:, 6, k:k + 1], in1=rstd[:, 0:n],
                                                                        op0=ALU.mult, op1=ALU.mult),
                          reads=["xT%d_%d" % (k, ti), "frstd", "gains"], writes=["fxo_%d" % k])
                for b0 in range(0, n, 128):
                    nb = min(128, n - b0)
                    yt = ytok[nblk % 2]; ytn = "fytok%d" % (nblk % 2)
                    nblk += 1
                    for half in range(2):
                        pt = self.psum[half * 2 + (nblk % 2)]; ptn = "ps%d" % (half * 2 + (nblk % 2))
                        for kk in range(4):
                            k = half * 4 + kk
                            P.pe(lambda e, pt=pt, k=k, kk=kk, b0=b0, nb=nb: e.transpose(pt[0:nb, kk * 128:(kk + 1) * 128], xo[:, k, b0:b0 + nb], ident[:]),
                                 reads=["fxo_%d" % k, "ident"], writes=[ptn])
                        if half == 0:
                            P.act(lambda e, pt=pt, yt=yt, nb=nb: e.activation(out=yt[0:nb, 0:512], in_=pt[0:nb, :], func=AF.Copy), reads=[ptn], writes=[ytn])
                        else:
                            P.dve(lambda e, pt=pt, yt=yt, nb=nb: e.tensor_copy(out=yt[0:nb, 512:1024], in_=pt[0:nb, :]), reads=[ptn], writes=[ytn])
                    if s < TP:
                        dst = self.dram["y_p"][s + b0:s + b0 + nb, :]
                    else:
                        dst = self.dram["y_s"][b0:b0 + nb, :]
                    P.dma(lambda e, dst=dst, yt=yt, nb=nb: e.dma_start(out=dst, in_=yt[0:nb, :]), reads=[ytn])
            P.flush()

    def layer(self, l):
        P = self.P
        self.ffn(l, 1, self.dram["ffn1_in"][l], self.dram["ffn1_out"][l])
        with contextlib.ExitStack() as st:
            self.rmsnorm(l * 3 + 1, st)
            P.flush()
        with contextlib.ExitStack() as st:
            self.ymix = P.sb("ymix", [128, 8, T], BF16, stack=st)
            ymix = self.ymix
            done = set()
            if self.stage >= 3:
                self.mix_pool(l); done |= {6, 7}
            if self.stage >= 4:
                self.mix_s5(l); done |= {4, 5}
            if self.stage >= 5:
                self.mix_ssd(l); done |= {0, 1}
            if self.stage >= 6:
                self.mix_rwkv(l); done |= {2, 3}
            else:
                self.mix_rwkv_stub(l)
            for k in range(8):
                if k not in done:
                    P.pool(lambda e, k=k: e.memset(ymix[:, k, :], 0.0), writes=["ymix%d_%d" % (k, ti) for ti in range(5)])
            self.outproj(l)
        self.ffn(l, 2, self.dram["ffn2_in"][l], self.dram["ffn2_out"][l])

    def build(self):
        self.declare()
        self.setup()
        self.load_x()
        for l in range(DEPTH):
            self.layer(l)
        self.final()
        self.P.finish()


def build_nc(stage):
    nc = bass.Bass("TRN2", target_bir_lowering=False)
    with contextlib.ExitStack() as stack:
        stack.enter_context(nc.allow_non_contiguous_dma(reason="small strided parameter/state transfers"))
        b = Builder(nc, stack, stage)
        b.build()
    return nc


IN_NAMES = ["norm_ffn1", "ffn1_in", "ffn1_out", "norm_mix", "w_in", "ssd_conv_w", "ssd_conv_b", "ssd_dt_bias",
            "ssd_a_log", "ssd_d", "ssd_norm", "rwkv_mu", "rwkv_w0", "rwkv_w2", "rwkv_a0", "rwkv_a2", "rwkv_g2",
            "rwkv_k_k", "rwkv_k_a", "rwkv_r_k", "rwkv_ln_g", "rwkv_ln_b", "s5_lam_re", "s5_lam_im", "s5_log_step",
            "s5_b_re", "s5_b_im", "s5_c_re", "s5_c_im", "s5_d", "s5_glu_w", "s5_glu_b", "pool_w", "pool_scale",
            "w_out", "norm_ffn2", "ffn2_in", "ffn2_out", "norm_final"]
STATE_MAP = [("state_ssd_conv", "st_conv"), ("state_ssd", "st_ssd"), ("state_rwkv_shift", "st_shift"),
             ("state_rwkv", "st_rwkv"), ("state_s5_re", "st_s5re"), ("state_s5_im", "st_s5im"), ("state_pool", "st_pool")]
OUT_ORDER = [("y_p", "y_s"), ("conv_p", "conv_s"), ("ssd_p", "ssd_s"), ("shift_p", "shift_s"), ("rwkv_p", "rwkv_s"),
             ("s5re_p", "s5re_s"), ("s5im_p", "s5im_s"), ("pool_p", "pool_s")]


def host_consts():
    c = {}
    wins = [2, 4, 8, 16]
    cinv = np.zeros((128, 2, 15), np.float32)
    for ch in range(2):
        for p in range(128):
            w = wins[2 * ch + p // 64]
            for t in range(15):
                cinv[p, ch, t] = 1.0 / min(t + 1, w)
    c["c_pool_cinv"] = cinv
    gm = np.zeros((128, 8), np.float32)
    for p in range(128):
        gm[p, p // 16] = 1.0
    c["c_gm"] = gm
    sw = np.zeros((128, 128), np.float32)
    for k in range(128):
        sw[k, (k + 64) % 128] = 1.0
    c["c_swap"] = sw
    cm = np.ones((4, T), np.float32)
    cm[:, 0:TP:128] = 0.0
    cm[:, TP:T:TS] = 0.0
    c["c_cmask"] = cm
    ii = np.arange(128)
    c["c_mask128"] = (ii[None, :] >= ii[:, None]).astype(np.float32)
    c["c_U128"] = (ii[:, None] > ii[None, :]).astype(np.float32)
    c["c_L128"] = (ii[:, None] <= ii[None, :]).astype(np.float32)
    el = np.zeros((128, 128), np.float32); el[127, :] = 1.0
    c["c_elast"] = el
    i6 = np.arange(64)
    same = (i6[:, None] // TS) == (i6[None, :] // TS)
    c["c_maskS"] = (same & (i6[None, :] >= i6[:, None])).astype(np.float32)
    c["c_US"] = (same & (i6[:, None] > i6[None, :])).astype(np.float32)
    c["c_LS"] = (same & (i6[:, None] <= i6[None, :])).astype(np.float32)
    c["c_seqm"] = ((i6[:, None] // TS) == np.arange(NSEQ)[None, :]).astype(np.float32)
    sel = np.zeros((4, 2, 128), np.float32)
    for cc in range(2):
        for m in range(128):
            sel[2 * cc + m // 64, cc, m] = 1.0
    c["c_sel"] = sel
    c["c_bdones"] = ((ii[:, None] // 64) == (ii[None, :] // 64)).astype(np.float32)
    I2 = np.zeros((128, 64), np.float32)
    for p in range(128):
        I2[p, p % 64] = 1.0
    c["c_I2"] = I2
    Es = np.zeros((64, 2, 128), np.float32)
    Eh = np.zeros((128, 2, 64), np.float32)
    for i in range(64):
        for hp in range(2):
            Es[i, hp, 64 * hp + i] = 1.0
            Eh[64 * hp + i, hp, i] = 1.0
    c["c_Esel"] = Es
    c["c_Ehp"] = Eh
    return c


def kernel(stage=6, **inputs):
    f = lambda a: np.ascontiguousarray(np.asarray(a, dtype=np.float32))
    shared = {nm: f(inputs[nm]) for nm in IN_NAMES}
    shared.update(host_consts())
    in_maps = []
    for c in range(NCORES):
        m = dict(shared)
        m["xp"] = f(inputs["x_prompt"][c])
        m["xs"] = f(inputs["x_sample"][c * NSEQ:(c + 1) * NSEQ]).reshape(TSAMP, D)
        for src, dst in STATE_MAP:
            m[dst] = f(inputs[src][:, c * NSEQ:(c + 1) * NSEQ])
        in_maps.append(m)
    nc = build_nc(stage)
    import os
    if os.environ.get("KTRACE"):
        res = run_bass_kernel_spmd(nc, in_maps, core_ids=list(range(NCORES)), trace=True)
        print("EXEC_TIME_NS", res.exec_time_ns)
    else:
        res = run_bass_kernel_spmd(nc, in_maps, core_ids=list(range(NCORES)))
    R = res.results
    global DBG
    DBG = None
    outs = []
    yp = np.stack([R[c]["y_p"] for c in range(NCORES)], 0)
    ys = np.concatenate([R[c]["y_s"].reshape(NSEQ, TS, D) for c in range(NCORES)], 0)
    outs += [yp, ys]
    for pn, sn in OUT_ORDER[1:]:
        p = np.stack([R[c][pn] for c in range(NCORES)], 1)
        s = np.concatenate([R[c][sn] for c in range(NCORES)], 1)
        outs += [p, s]
    return tuple(np.ascontiguousarray(o, dtype=np.float32) for o in outs)
```

```python
import contextlib
import numpy as np
import concourse.bass as bass
import concourse.mybir as mybir
from concourse.bass_utils import run_bass_kernel_spmd

F32 = mybir.dt.float32
BF16 = mybir.dt.bfloat16
AF = mybir.ActivationFunctionType
ALU = mybir.AluOpType
AX = mybir.AxisListType

NCORES = 8
D = 1024
TP = 2048
NSEQ = 16
TS = 4
TSAMP = NSEQ * TS
T = TP + TSAMP
DFF = 2816
NFF = DFF // 128
INP = 2564
DEPTH = 2
TILES = [(0, 512), (512, 512), (1024, 512), (1536, 512), (2048, 64)]


class Op:
    __slots__ = ("idx", "eng", "fn", "deps", "is_dma", "group", "marked", "count", "pos")

    def __init__(self, idx, eng, fn, is_dma, group):
        self.idx = idx
        self.eng = eng
        self.fn = fn
        self.deps = set()
        self.is_dma = is_dma
        self.group = group
        self.marked = False
        self.count = 0
        self.pos = 0


class Prog:
    def __init__(self, nc, stack):
        self.nc = nc
        self.stack = stack
        self.ops = []
        self.last_writer = {}
        self.readers = {}
        self.eng_obj = {"pe": nc.tensor, "act": nc.scalar, "dve": nc.vector,
                        "pool": nc.gpsimd, "sp": nc.sync}
        self.cnt = {}
        self.sems = {}
        self.waited = {}
        self.pos = {}
        self.emitted = 0
        self.barrier_req = None
        self.nwait = 0

    def sb(self, name, shape, dtype=F32, stack=None):
        self.uid = getattr(self, "uid", 0) + 1
        return (stack or self.stack).enter_context(self.nc.sbuf_tensor("%s_%d" % (name, self.uid), list(shape), dtype))

    def ps(self, name, shape, dtype=F32):
        return self.stack.enter_context(self.nc.psum_tensor(name, list(shape), dtype))

    def add(self, eng, fn, reads=(), writes=(), group=None):
        is_dma = group is not None
        op = Op(len(self.ops), eng, fn, is_dma, group)
        for r in reads:
            w = self.last_writer.get(r)
            if w is not None:
                op.deps.add(w)
            self.readers.setdefault(r, []).append(op.idx)
        for wr in writes:
            w = self.last_writer.get(wr)
            if w is not None:
                op.deps.add(w)
            for rd in self.readers.get(wr, ()):
                if rd != op.idx:
                    op.deps.add(rd)
            self.last_writer[wr] = op.idx
            self.readers[wr] = []
        self.ops.append(op)
        return op

    def pe(self, fn, reads=(), writes=()):
        return self.add("pe", fn, reads, writes)

    def act(self, fn, reads=(), writes=()):
        return self.add("act", fn, reads, writes)

    def dve(self, fn, reads=(), writes=()):
        return self.add("dve", fn, reads, writes)

    def pool(self, fn, reads=(), writes=()):
        return self.add("pool", fn, reads, writes)

    def dma(self, fn, reads=(), writes=(), group=None, q="sp"):
        if group is None:
            group = writes[0] if writes else "st:" + reads[0]
        return self.add(q, fn, reads, writes, group=group)

    def _sem(self, key):
        s = self.sems.get(key)
        if s is None:
            s = self.stack.enter_context(self.nc.semaphore("s_%s_%s" % key))
            self.sems[key] = s
        return s

    def flush(self, barrier=True):
        ops = self.ops
        batch = ops[self.emitted:]
        first = self.emitted
        for op in batch:
            p = self.pos.get(op.eng, 0)
            op.pos = p
            self.pos[op.eng] = p + 1
        need = []
        for op in batch:
            lst = []
            for d in op.deps:
                if d < first:
                    continue
                p = ops[d]
                if p.is_dma:
                    lst.append(d)
                elif p.eng != op.eng or op.is_dma:
                    lst.append(d)
                    p.marked = True
                else:
                    if op.eng == "pe":
                        continue
                    if op.pos - p.pos <= 1:
                        lst.append(d)
                        p.marked = True
            need.append(lst)
        if barrier:
            last = {}
            for op in batch:
                if not op.is_dma:
                    last[op.eng] = op
            for op in last.values():
                op.marked = True
        for op in batch:
            if op.is_dma:
                key = ("g", op.group)
                self.cnt[key] = self.cnt.get(key, 0) + 16
                op.count = self.cnt[key]
            elif op.marked:
                key = ("e", op.eng)
                self.cnt[key] = self.cnt.get(key, 0) + 1
                op.count = self.cnt[key]
        seen_eng = set()
        for op, lst in zip(batch, need):
            eng = self.eng_obj[op.eng]
            reqs = {}
            if self.barrier_req is not None and op.eng not in seen_eng:
                reqs.update(self.barrier_req)
            seen_eng.add(op.eng)
            for d in lst:
                p = ops[d]
                key = ("g", p.group) if p.is_dma else ("e", p.eng)
                if p.count > reqs.get(key, 0):
                    reqs[key] = p.count
            for key, val in reqs.items():
                if key == ("e", op.eng) and not op.is_dma and self.barrier_req is not None \
                        and val <= self.barrier_req.get(key, 0):
                    continue
                wk = (op.eng, key)
                if self.waited.get(wk, 0) >= val:
                    continue
                self.waited[wk] = val
                eng.wait_ge(self._sem(key), val)
                self.nwait += 1
                self.ninstr = getattr(self, "ninstr", {})
                self.ninstr[op.eng] = self.ninstr.get(op.eng, 0) + 1
            ins = op.fn(eng)
            self.ninstr = getattr(self, "ninstr", {})
            self.ninstr[op.eng] = self.ninstr.get(op.eng, 0) + 1
            if op.is_dma:
                ins.then_inc(self._sem(("g", op.group)), 16)
            elif op.marked:
                ins.then_inc(self._sem(("e", op.eng)), 1)
            op.fn = None
        self.emitted = len(ops)
        if barrier:
            self.barrier_req = dict(self.cnt)

    def finish(self):
        self.flush(barrier=True)
        for key, val in self.cnt.items():
            if key[0] == "g":
                self.nc.sync.wait_ge(self._sem(key), val)


class Builder:
    def __init__(self, nc, stack, stage):
        self.nc = nc
        self.P = Prog(nc, stack)
        self.stage = stage
        self.stack = stack
        self.dram = {}
        self.wq = 0

    def din(self, name, shape):
        ap = self.nc.dram_tensor(name, list(shape), F32, kind="ExternalInput").ap()
        self.dram[name] = ap
        return ap

    def dout(self, name, shape):
        ap = self.nc.dram_tensor(name, list(shape), F32, kind="ExternalOutput").ap()
        self.dram[name] = ap
        return ap

    def declare(self):
        d = self.din
        d("xp", [TP, D]); d("xs", [TSAMP, D])
        d("st_conv", [DEPTH, NSEQ, 3, 768]); d("st_ssd", [DEPTH, NSEQ, 4, 64, 128])
        d("st_shift", [DEPTH, NSEQ, 1024]); d("st_rwkv", [DEPTH, NSEQ, 4, 64, 64])
        d("st_s5re", [DEPTH, NSEQ, 16, 64]); d("st_s5im", [DEPTH, NSEQ, 16, 64])
        d("st_pool", [DEPTH, NSEQ, 15, 256])
        d("norm_ffn1", [DEPTH, D]); d("ffn1_in", [DEPTH, D, 2 * DFF]); d("ffn1_out", [DEPTH, DFF, D])
        d("norm_mix", [DEPTH, D]); d("w_in", [DEPTH, D, INP])
        d("ssd_conv_w", [DEPTH, 4, 768]); d("ssd_conv_b", [DEPTH, 768]); d("ssd_dt_bias", [DEPTH, 4])
        d("ssd_a_log", [DEPTH, 4]); d("ssd_d", [DEPTH, 4]); d("ssd_norm", [DEPTH, 256])
        d("rwkv_mu", [DEPTH, 1024]); d("rwkv_w0", [DEPTH, 256]); d("rwkv_w2", [DEPTH, 64, 256])
        d("rwkv_a0", [DEPTH, 256]); d("rwkv_a2", [DEPTH, 64, 256]); d("rwkv_g2", [DEPTH, 128, 256])
        d("rwkv_k_k", [DEPTH, 256]); d("rwkv_k_a", [DEPTH, 256]); d("rwkv_r_k", [DEPTH, 4, 64])
        d("rwkv_ln_g", [DEPTH, 256]); d("rwkv_ln_b", [DEPTH, 256])
        d("s5_lam_re", [DEPTH, 16, 64]); d("s5_lam_im", [DEPTH, 16, 64]); d("s5_log_step", [DEPTH, 16])
        d("s5_b_re", [DEPTH, 16, 64, 16]); d("s5_b_im", [DEPTH, 16, 64, 16])
        d("s5_c_re", [DEPTH, 16, 16, 64]); d("s5_c_im", [DEPTH, 16, 16, 64]); d("s5_d", [DEPTH, 256])
        d("s5_glu_w", [DEPTH, 256, 512]); d("s5_glu_b", [DEPTH, 512])
        d("pool_w", [DEPTH, 4, 64, 64]); d("pool_scale", [DEPTH, 256])
        d("w_out", [DEPTH, D, D]); d("norm_ffn2", [DEPTH, D]); d("ffn2_in", [DEPTH, D, 2 * DFF])
        d("ffn2_out", [DEPTH, DFF, D]); d("norm_final", [D])
        d("c_pool_cinv", [128, 2, 15]); d("c_gm", [128, 8]); d("c_swap", [128, 128])
        d("c_cmask", [4, T]); d("c_mask128", [128, 128]); d("c_U128", [128, 128]); d("c_L128", [128, 128]); d("c_elast", [128, 128])
        d("c_hpsel", [128, 2]); d("c_bdones", [128, 128]); d("c_I2", [128, 64]); d("c_Esel", [64, 2, 128]); d("c_Ehp", [128, 2, 64])
        d("c_maskS", [64, 64]); d("c_US", [64, 64]); d("c_LS", [64, 64]); d("c_seqm", [64, NSEQ]); d("c_sel", [4, 2, 128])
        o = self.dout
        o("y_p", [TP, D]); o("y_s", [TSAMP, D])
        o("conv_p", [DEPTH, 3, 768]); o("conv_s", [DEPTH, NSEQ, 3, 768])
        o("ssd_p", [DEPTH, 4, 64, 128]); o("ssd_s", [DEPTH, NSEQ, 4, 64, 128])
        o("shift_p", [DEPTH, 1024]); o("shift_s", [DEPTH, NSEQ, 1024])
        o("rwkv_p", [DEPTH, 4, 64, 64]); o("rwkv_s", [DEPTH, NSEQ, 4, 64, 64])
        o("s5re_p", [DEPTH, 16, 64]); o("s5re_s", [DEPTH, NSEQ, 16, 64])
        o("s5im_p", [DEPTH, 16, 64]); o("s5im_s", [DEPTH, NSEQ, 16, 64])
        o("pool_p", [DEPTH, 15, 256]); o("pool_s", [DEPTH, NSEQ, 15, 256])
        o("rw_scr", [2, 7, 128, T])

    def dump_x(self):
        xT = self.xT
        self.P.dma(lambda e: e.dma_start(out=self.dram["dbg"], in_=xT[:]),
                   reads=["xT%d_%d" % (k, ti) for k in range(8) for ti in range(5)], group="dbg")
        self.P.flush()

    def setup(self):
        P = self.P
        self.xT = P.sb("xT", [128, 8, T], F32)
        self.xn = P.sb("xn", [128, 8, T], BF16)
        self.ident = P.sb("ident", [128, 128], F32)
        self.identb = P.sb("identb", [128, 128], BF16)
        self.onesb = P.sb("onesb", [128, 128], BF16)
        self.gains = P.sb("gains", [128, 7, 8], F32)
        self.epsc = P.sb("epsc", [128, 1], F32)
        self.psum = [P.ps("ps%d" % i, [128, 512], F32) for i in range(8)]
        ident, identb, onesb = self.ident, self.identb, self.onesb
        P.dve(lambda e: e.memset(ident[:], 0.0), writes=["ident"])
        P.pool(lambda e: e.affine_select(out=ident[:], in_=ident[:], pattern=[[-1, 128]],
                                         compare_op=ALU.not_equal, fill=1.0, base=0,
                                         channel_multiplier=1), reads=["ident"], writes=["ident"])
        P.dve(lambda e: e.tensor_copy(out=identb[:], in_=ident[:]), reads=["ident"], writes=["identb"])
        P.dve(lambda e: e.memset(onesb[:], 1.0), writes=["onesb"])
        epsc = self.epsc
        P.dve(lambda e: e.memset(epsc[:], 1e-6), writes=["epsc"])
        gains = self.gains
        names = ["norm_ffn1", "norm_mix", "norm_ffn2"]
        for l in range(DEPTH):
            for i, nm in enumerate(names):
                src = self.dram[nm][l].rearrange("(k p) -> p k", p=128)
                P.dma(lambda e, s=src, w=l * 3 + i: e.dma_start(out=gains[:, w, :], in_=s),
                      writes=["gains"], group="const")
        src = self.dram["norm_final"].rearrange("(k p) -> p k", p=128)
        P.dma(lambda e, s=src: e.dma_start(out=gains[:, 6, :], in_=s), writes=["gains"], group="const")

    def load_x(self):
        P = self.P
        xT, ident = self.xT, self.ident
        with contextlib.ExitStack() as st:
            xtok = [P.sb("xtok%d" % i, [128, D], F32, stack=st) for i in range(3)]
            ntile = TP // 128 + 1
            for ti in range(ntile):
                buf = xtok[ti % 3]
                bn = "xtok%d" % (ti % 3)
                if ti < TP // 128:
                    src = self.dram["xp"][ti * 128:(ti + 1) * 128, :]
                    n = 128
                else:
                    src = self.dram["xs"][:, :]
                    n = TSAMP
                P.dma(lambda e, b=buf, s=src, n=n: e.dma_start(out=b[0:n, :], in_=s), writes=[bn])
                for half in range(2):
                    ps = self.psum[(ti * 2 + half) % 4]
                    pn = "ps%d" % ((ti * 2 + half) % 4)
                    for kk in range(4):
                        k = half * 4 + kk
                        P.pe(lambda e, ps=ps, b=buf, k=k, kk=kk, n=n: e.transpose(
                            ps[:, kk * 128:kk * 128 + n], b[0:n, k * 128:(k + 1) * 128], ident[0:n, 0:n]),
                            reads=[bn, "ident"], writes=[pn])
                    dst = xT[:, half * 4:half * 4 + 4, ti * 128:ti * 128 + n]
                    srcp = ps[:].rearrange("p (a b) -> p a b", a=4)[:, :, 0:n]
                    wr = ["xT%d_%d" % (k, ti // 4) for k in range(half * 4, half * 4 + 4)]
                    if half == 0:
                        P.dve(lambda e, d=dst, s=srcp: e.tensor_copy(out=d, in_=s), reads=[pn], writes=wr)
                    else:
                        P.act(lambda e, d=dst, s=srcp: e.activation(out=d, in_=s, func=AF.Copy), reads=[pn], writes=wr)
            P.flush()

    def rmsnorm(self, which, st):
        P = self.P
        xT, xn, onesb, gains, epsc = self.xT, self.xn, self.onesb, self.gains, self.epsc
        sq = [P.sb("nsq%d" % i, [128, 8, 512], BF16, stack=st) for i in range(2)]
        rstd = [P.sb("nrstd%d" % i, [128, 512], F32, stack=st) for i in range(2)]
        for ti, (s, n) in enumerate(TILES):
            q = sq[ti % 2]; qn = "nsq%d" % (ti % 2)
            r = rstd[ti % 2]; rn = "nrstd%d" % (ti % 2)
            ps = self.psum[6 + ti % 2]; pn = "ps%d" % (6 + ti % 2)
            for k in range(8):
                P.act(lambda e, q=q, k=k, s=s, n=n: e.activation(out=q[:, k, 0:n], in_=xT[:, k, s:s + n], func=AF.Square),
                      reads=["xT%d_%d" % (k, ti)], writes=[qn + "_%d" % k])
            for k in range(8):
                P.pe(lambda e, ps=ps, q=q, k=k, n=n: e.matmul(ps[:, 0:n], onesb[:], q[:, k, 0:n], start=(k == 0), stop=(k == 7)),
                     reads=[qn + "_%d" % k, "onesb"], writes=[pn])
            P.act(lambda e, r=r, ps=ps, n=n: e.activation(out=r[:, 0:n], in_=ps[:, 0:n], func=AF.Sqrt, bias=epsc[:], scale=1.0 / D),
                  reads=[pn, "epsc"], writes=[rn])
            P.dve(lambda e, r=r, n=n: e.reciprocal(out=r[:, 0:n], in_=r[:, 0:n]), reads=[rn], writes=[rn])
            for k in range(8):
                P.dve(lambda e, r=r, k=k, s=s, n=n: e.scalar_tensor_tensor(
                    out=xn[:, k, s:s + n], in0=xT[:, k, s:s + n], scalar=gains[:, which, k:k + 1], in1=r[:, 0:n],
                    op0=ALU.mult, op1=ALU.mult),
                    reads=["xT%d_%d" % (k, ti), rn, "gains"], writes=["xn%d_%d" % (k, ti)])

    def ffn(self, l, which, w_in, w_out):
        P = self.P
        xT, xn = self.xT, self.xn
        with contextlib.ExitStack() as st:
            self.rmsnorm(l * 3 + (0 if which == 1 else 2), st)
            P.flush()
        with contextlib.ExitStack() as st:
            NH = NFF // 2
            hid = P.sb("hid", [128, NH, T], BF16, stack=st)
            NS = 3
            stg = [P.sb("wstg%d" % i, [128, 8, 256], F32, stack=st) for i in range(NS)]
            wb = [P.sb("wbf%d" % i, [128, 8, 256], BF16, stack=st) for i in range(NS)]
            NSB = 2
            stgo = [P.sb("wostg%d" % i, [128, NH, 128], F32, stack=st) for i in range(NSB)]
            wob = [P.sb("wobf%d" % i, [128, NH, 128], BF16, stack=st) for i in range(NSB)]
            gt = [P.sb("gtmp%d" % i, [128, 512], F32, stack=st) for i in range(2)]
            w_in_v = w_in.rearrange("(k p) c -> p k c", p=128)
            w_out_v = w_out.rearrange("(j p) c -> p j c", p=128)
            jobs = []
            for half in range(2):
                for jl in range(NH):
                    jobs.append(("a", half, jl))
                for m in range(8):
                    jobs.append(("b", half, m))
            na = [0]
            nb = [0]
            slots = {}

            def issue_load(ji):
                kind, half, idx = jobs[ji]
                if kind == "a":
                    i = na[0] % NS; na[0] += 1
                    slots[ji] = i
                    j = half * NH + idx
                    s1 = w_in_v[:, :, j * 128:(j + 1) * 128]
                    s2 = w_in_v[:, :, DFF + j * 128:DFF + (j + 1) * 128]
                    b = stg[i]
                    P.dma(lambda e: e.dma_start(out=b[:, :, 0:128], in_=s1), writes=["wstg%d" % i])
                    P.dma(lambda e: e.dma_start(out=b[:, :, 128:256], in_=s2), writes=["wstg%d" % i])
                else:
                    i = nb[0] % NSB; nb[0] += 1
                    slots[ji] = i
                    s1 = w_out_v[:, half * NH:(half + 1) * NH, idx * 128:(idx + 1) * 128]
                    b = stgo[i]
                    P.dma(lambda e: e.dma_start(out=b[:], in_=s1), writes=["wostg%d" % i])

            def issue_cast(ji):
                kind, half, idx = jobs[ji]
                i = slots[ji]
                if kind == "a":
                    P.pool(lambda e: e.tensor_copy(out=wb[i][:], in_=stg[i][:]), reads=["wstg%d" % i], writes=["wbf%d" % i])
                else:
                    P.pool(lambda e: e.tensor_copy(out=wob[i][:], in_=stgo[i][:]), reads=["wostg%d" % i], writes=["wobf%d" % i])

            cnt = [0]

            def compute(ji):
                kind, half, idx = jobs[ji]
                i = slots[ji]
                if kind == "a":
                    w = wb[i]; wn = "wbf%d" % i
                    for ti, (s, n) in enumerate(TILES):
                        c = cnt[0]; cnt[0] += 1
                        pg = self.psum[(c % 2) * 2]; pgn = "ps%d" % ((c % 2) * 2)
                        pu = self.psum[(c % 2) * 2 + 1]; pun = "ps%d" % ((c % 2) * 2 + 1)
                        g = gt[c % 2]; gn = "gtmp%d" % (c % 2)
                        for k in range(8):
                            P.pe(lambda e, k=k, pg=pg, s=s, n=n: e.matmul(pg[:, 0:n], w[:, k, 0:128], xn[:, k, s:s + n], start=(k == 0), stop=(k == 7)),
                                 reads=[wn, "xn%d_%d" % (k, ti)], writes=[pgn])
                        for k in range(8):
                            P.pe(lambda e, k=k, pu=pu, s=s, n=n: e.matmul(pu[:, 0:n], w[:, k, 128:256], xn[:, k, s:s + n], start=(k == 0), stop=(k == 7)),
                                 reads=[wn, "xn%d_%d" % (k, ti)], writes=[pun])
                        P.act(lambda e, g=g, pg=pg, n=n: e.activation(out=g[:, 0:n], in_=pg[:, 0:n], func=AF.Silu), reads=[pgn], writes=[gn])
                        P.dve(lambda e, g=g, pu=pu, s=s, n=n: e.tensor_tensor(out=hid[:, idx, s:s + n], in0=g[:, 0:n], in1=pu[:, 0:n], op=ALU.mult),
                              reads=[gn, pun], writes=["hid%d_%d" % (idx, ti)])
                else:
                    w = wob[i]; wn = "wobf%d" % i
                    m = idx
                    for ti, (s, n) in enumerate(TILES):
                        c = cnt[0]; cnt[0] += 1
                        po = self.psum[4 + c % 2]; pon = "ps%d" % (4 + c % 2)
                        for jl in range(NH):
                            P.pe(lambda e, jl=jl, po=po, s=s, n=n: e.matmul(po[:, 0:n], w[:, jl, :], hid[:, jl, s:s + n], start=(jl == 0), stop=(jl == NH - 1)),
                                 reads=[wn, "hid%d_%d" % (jl, ti)], writes=[pon])
                        P.dve(lambda e, po=po, s=s, n=n: e.scalar_tensor_tensor(
                            out=xT[:, m, s:s + n], in0=po[:, 0:n], scalar=0.5, in1=xT[:, m, s:s + n], op0=ALU.mult, op1=ALU.add),
                            reads=[pon, "xT%d_%d" % (m, ti)], writes=["xT%d_%d" % (m, ti)])

            nj = len(jobs)
            issue_load(0); issue_load(1)
            issue_cast(0)
            for ji in range(nj):
                if ji + 2 < nj:
                    issue_load(ji + 2)
                if ji + 1 < nj:
                    issue_cast(ji + 1)
                compute(ji)
            P.flush()

    def inproj(self, l, chunks, st, dtype=BF16, pad=0):
        P = self.P
        xn = self.xn
        w_v = self.dram["w_in"][l].rearrange("(k p) c -> p k c", p=128)
        out = {}
        for ci, (name, c0, ncol) in enumerate(chunks):
            out[name] = P.sb("u_" + name, [128, pad + T], dtype, stack=st)
        with contextlib.ExitStack() as st2:
            self._inproj_body(l, chunks, out, st2, pad)
            P.flush()
        return out

    def _inproj_body(self, l, chunks, out, st, pad):
        P = self.P
        xn = self.xn
        w_v = self.dram["w_in"][l].rearrange("(k p) c -> p k c", p=128)
        stg = [P.sb("ipstg%d" % i, [128, 8, 128], F32, stack=st) for i in range(3)]
        wbs = [P.sb("ipwb%d" % i, [128, 8, 128], BF16, stack=st) for i in range(3)]
        for ci, (name, c0, ncol) in enumerate(chunks):
            if pad:
                P.dve(lambda e, t=out[name]: e.memset(t[:, 0:pad], 0.0), writes=["u_%s_pad" % name])
        for ci, (name, c0, ncol) in enumerate(chunks):
            i = ci % 3
            b = stg[i]; w = wbs[i]
            P.dma(lambda e, b=b, c0=c0, ncol=ncol: e.dma_start(out=b[:, :, 0:ncol], in_=w_v[:, :, c0:c0 + ncol]),
                  writes=["ipstg%d" % i])
            P.pool(lambda e, b=b, w=w, ncol=ncol: e.tensor_copy(out=w[:, :, 0:ncol], in_=b[:, :, 0:ncol]),
                   reads=["ipstg%d" % i], writes=["ipwb%d" % i])
            u = out[name]
            for ti, (s, n) in enumerate(TILES):
                c = ci * len(TILES) + ti
                ps = self.psum[c % 4]; pn = "ps%d" % (c % 4)
                for k in range(8):
                    P.pe(lambda e, k=k, ps=ps, w=w, s=s, n=n, ncol=ncol: e.matmul(
                        ps[0:ncol, 0:n], w[:, k, 0:ncol], xn[:, k, s:s + n], start=(k == 0), stop=(k == 7)),
                        reads=["ipwb%d" % i, "xn%d_%d" % (k, ti)], writes=[pn])
                if c % 2 == 0:
                    P.act(lambda e, u=u, ps=ps, s=s, n=n, ncol=ncol: e.activation(out=u[0:ncol, pad + s:pad + s + n], in_=ps[0:ncol, 0:n], func=AF.Copy),
                          reads=[pn], writes=["u_%s_%d" % (name, ti)])
                else:
                    P.dve(lambda e, u=u, ps=ps, s=s, n=n, ncol=ncol: e.tensor_copy(out=u[0:ncol, pad + s:pad + s + n], in_=ps[0:ncol, 0:n]),
                          reads=[pn], writes=["u_%s_%d" % (name, ti)])
        return out

    def ures(self, name, tis=None):
        if tis is None:
            tis = range(len(TILES))
        return ["u_%s_%d" % (name, ti) for ti in tis]

    def store_cols(self, u, nrows, col0, ncols, dst, reads, group="sto"):
        P = self.P
        P.dma(lambda e: e.dma_start(out=dst.rearrange("t f -> f t"), in_=u[0:nrows, col0:col0 + ncols],
                                    allow_slow_non_contiguous=True),
              reads=reads, group=group)

    def mix_rwkv_stub(self, l):
        P = self.P
        base = 1028
        with contextlib.ExitStack() as st:
            chunks = [("r0", base, 128), ("r1", base + 128, 128), ("k0", base + 256, 128), ("k1", base + 384, 128),
                      ("v0", base + 512, 128), ("v1", base + 640, 128), ("wa", base + 768, 128), ("gd", base + 896, 128)]
            u = self.inproj(l, chunks, st, dtype=BF16)
            shf = P.sb("shf", [128, 8, 1 + NSEQ], F32, stack=st)
            for ci, (name, c0, ncol) in enumerate(chunks):
                f0 = c0 - base
                P.dve(lambda e, ci=ci, t=u[name]: e.tensor_copy(out=shf[:, ci, 0:1], in_=t[:, TP - 1:TP]), reads=self.ures(name, [3]), writes=["shf%d" % ci])
                P.dve(lambda e, ci=ci, t=u[name]: e.tensor_copy(out=shf[:, ci, 1:1 + NSEQ], in_=t[:, TP:T].rearrange("p (s t) -> p s t", t=TS)[:, :, TS - 1]),
                      reads=self.ures(name, [4]), writes=["shf%d" % ci])
                dst = self.dram["shift_p"][l:l + 1, f0:f0 + ncol]
                P.dma(lambda e, dst=dst, ci=ci: e.dma_start(out=dst.rearrange("t f -> f t"), in_=shf[:, ci, 0:1]), reads=["shf%d" % ci], group="sto")
                dst = self.dram["shift_s"][l, :, f0:f0 + ncol]
                P.dma(lambda e, dst=dst, ci=ci: e.dma_start(out=dst.rearrange("s f -> f s"), in_=shf[:, ci, 1:1 + NSEQ]), reads=["shf%d" % ci], group="sto")
            P.flush()

    def mix_pool(self, l):
        P = self.P
        ymix = self.ymix
        PADP = 16
        with contextlib.ExitStack() as st:
            chunks = [("p0", 2308, 128), ("p1", 2436, 128)]
            u = self.inproj(l, chunks, st, dtype=F32, pad=PADP)
            N = PADP + TP
            lev = [P.sb("pl%d" % i, [128, N], F32, stack=st) for i in range(2)]
            slev = [P.sb("psl%d" % i, [128, NSEQ, 20], F32, stack=st) for i in range(2)]
            Es = P.sb("pEs", [128, NSEQ, 20], F32, stack=st)
            pooled = P.sb("ppooled", [128, T], BF16, stack=st)
            tmp15 = P.sb("ptmp15", [128, 15], F32, stack=st)
            Wf = P.sb("pWf", [128, 128], F32, stack=st)
            Wb = P.sb("pWb", [128, 128], BF16, stack=st)
            psc = P.sb("ppsc", [128, 2], F32, stack=st)
            cinv = P.sb("pcinv", [128, 2, 15], F32, stack=st)
            P.dma(lambda e: e.dma_start(out=psc[:], in_=self.dram["pool_scale"][l].rearrange("(c p) -> p c", p=128)), writes=["ppsc"])
            P.dma(lambda e: e.dma_start(out=cinv[:], in_=self.dram["c_pool_cinv"]), writes=["pcinv"])
            wins = [2, 4, 8, 16]
            for c, (name, c0, ncol) in enumerate(chunks):
                E = u[name]
                allr = self.ures(name) + ["u_%s_pad" % name]
                nlev = 2 if c == 0 else 4
                for i in range(nlev):
                    sh = 1 << i
                    a = E if i == 0 else lev[(i - 1) % 2]
                    o = lev[i % 2]
                    P.dve(lambda e, a=a, o=o, sh=sh: e.tensor_tensor(out=o[:, 2 * sh:N], in0=a[:, 2 * sh:N], in1=a[:, sh:N - sh], op=ALU.add),
                          reads=(allr if i == 0 else ["pl%d" % ((i - 1) % 2)]), writes=["pl%d" % (i % 2)])
                P.dve(lambda e: e.memset(Es[:, :, 0:1], 0.0), writes=["pEs"])
                for sq_ in range(NSEQ):
                    P.dma(lambda e, c=c, sq_=sq_: e.dma_start(out=Es[:, sq_, 1:16], in_=self.dram["st_pool"][l, sq_][:, c * 128:(c + 1) * 128].rearrange("r f -> f r")),
                          writes=["pEs"])
                P.dve(lambda e, E=E: e.tensor_copy(out=Es[:, :, 16:20], in_=E[:, PADP + TP:PADP + T].rearrange("p (s t) -> p s t", t=TS)),
                      reads=allr, writes=["pEs"])
                for i in range(nlev):
                    sh = 1 << i
                    a = Es if i == 0 else slev[(i - 1) % 2]
                    o = slev[i % 2]
                    P.dve(lambda e, a=a, o=o, sh=sh: e.tensor_tensor(out=o[:, :, 2 * sh:20], in0=a[:, :, 2 * sh:20], in1=a[:, :, sh:20 - sh], op=ALU.add),
                          reads=(["pEs"] if i == 0 else ["psl%d" % ((i - 1) % 2)]), writes=["psl%d" % (i % 2)])
                for hf in range(2):
                    rows = slice(hf * 64, hf * 64 + 64)
                    gi = 2 * c + hf
                    li = gi % 2
                    sw = lev[li]; ssw = slev[li]
                    winv = 1.0 / wins[gi]
                    P.dve(lambda e, sw=sw, rows=rows, winv=winv, E=E: e.scalar_tensor_tensor(
                        out=pooled[rows, 0:TP], in0=sw[rows, PADP:N], scalar=winv, in1=E[rows, PADP:N], op0=ALU.mult, op1=ALU.subtract),
                        reads=["pl%d" % li] + allr, writes=["ppooled"])
                    P.dve(lambda e, sw=sw, rows=rows, c=c: e.tensor_tensor(out=tmp15[rows, :], in0=sw[rows, PADP:PADP + 15], in1=cinv[rows, c, :], op=ALU.mult),
                          reads=["pl%d" % li, "pcinv"], writes=["ptmp15"])
                    P.dve(lambda e, rows=rows, E=E: e.tensor_tensor(out=pooled[rows, 0:15], in0=tmp15[rows, :], in1=E[rows, PADP:PADP + 15], op=ALU.subtract),
                          reads=["ptmp15"] + allr, writes=["ppooled"])
                    P.dve(lambda e, ssw=ssw, rows=rows, winv=winv: e.scalar_tensor_tensor(
                        out=pooled[rows, TP:T].rearrange("p (s t) -> p s t", t=TS), in0=ssw[rows, :, 16:20], scalar=winv, in1=Es[rows, :, 16:20],
                        op0=ALU.mult, op1=ALU.subtract), reads=["psl%d" % li, "pEs"], writes=["ppooled"])
                dst = self.dram["pool_p"][l][:, c * 128:(c + 1) * 128]
                self.store_cols(E, 128, PADP + TP - 15, 15, dst, allr)
                for sq_ in range(NSEQ):
                    P.dma(lambda e, c=c, sq_=sq_: e.dma_start(out=self.dram["pool_s"][l, sq_][:, c * 128:(c + 1) * 128].rearrange("r f -> f r"), in_=Es[:, sq_, 5:20]),
                          reads=["pEs"])
                P.dve(lambda e: e.memset(Wf[:], 0.0), writes=["pWf"])
                P.dma(lambda e, c=c: e.dma_start(out=Wf[0:64, 0:64], in_=self.dram["pool_w"][l, 2 * c]), writes=["pWf"])
                P.dma(lambda e, c=c: e.dma_start(out=Wf[64:128, 64:128], in_=self.dram["pool_w"][l, 2 * c + 1]), writes=["pWf"])
                P.dve(lambda e: e.tensor_copy(out=Wb[:], in_=Wf[:]), reads=["pWf"], writes=["pWb"])
                for ti, (s, n) in enumerate(TILES):
                    ps = self.psum[ti % 2]; pn = "ps%d" % (ti % 2)
                    P.pe(lambda e, ps=ps, s=s, n=n: e.matmul(ps[:, 0:n], Wb[:], pooled[:, s:s + n], start=True, stop=True),
                         reads=["pWb", "ppooled"], writes=[pn])
                    P.act(lambda e, ps=ps, s=s, n=n, c=c: e.activation(out=ymix[:, 6 + c, s:s + n], in_=ps[:, 0:n], func=AF.Copy, scale=psc[:, c:c + 1]),
                          reads=[pn, "ppsc"], writes=["ymix%d_%d" % (6 + c, ti)])
            P.flush()

    def mix_s5(self, l):
        P = self.P
        ymix, ident = self.ymix, self.ident
        HW = TP + NSEQ * 5
        PI = 3.14159265358979
        with contextlib.ExitStack() as st:
            u = self.inproj(l, [("s0", 2052, 128), ("s1", 2180, 128)], st, dtype=BF16)
            t_ = lambda nm, shp, dt=F32: P.sb("s5" + nm, shp, dt, stack=st)
            lre, lim, dl, ee, th, rr, kf, ff, s1, s2, c1, sn, cs = [t_(n, [128, 16]) for n in
                                                                     "lre lim dl ee th rr kf ff s1 s2 c1 sn cs".split()]
            ki = t_("ki", [128, 16], mybir.dt.int32)
            ar, ai, xx, den, cr, ci, tq, ais = [t_(n, [128, 16]) for n in "ar ai xx den cr ci tq ais".split()]
            X1 = t_("X1", [128, 16, 16]); X2 = t_("X2", [128, 16, 16]); bb = t_("bb", [128, 16, 16]); bb2 = t_("bb2", [128, 16, 16])
            BT = t_("BT", [128, 2, 128], BF16)
            CT = t_("CT", [128, 16, 16]); CTm = t_("CTm", [128, 2, 128], BF16)
            gm = t_("gm", [128, 8]); swp = t_("swp", [128, 128])
            dsk = t_("dsk", [128, 2]); glb = t_("glb", [128, 4])
            S = "s5c"
            dr = self.dram
            for hf in range(2):
                rs = slice(hf * 64, hf * 64 + 64)
                P.dma(lambda e, rs=rs: e.dma_start(out=lre[rs, :], in_=dr["s5_lam_re"][l].rearrange("g n -> n g")), writes=[S])
                P.dma(lambda e, rs=rs: e.dma_start(out=lim[rs, :], in_=dr["s5_lam_im"][l].rearrange("g n -> n g")), writes=[S])
            P.dma(lambda e: e.dma_start(out=dl[:], in_=dr["s5_log_step"][l].partition_broadcast(128)), writes=[S])
            P.dma(lambda e: e.dma_start(out=X1[0:64], in_=dr["s5_b_re"][l].rearrange("g n c -> n g c")), writes=[S])
            P.dma(lambda e: e.dma_start(out=X1[64:128], in_=dr["s5_b_im"][l].rearrange("g n c -> n g c")), writes=[S])
            P.dma(lambda e: e.dma_start(out=X2[0:64], in_=dr["s5_b_im"][l].rearrange("g n c -> n g c")), writes=[S])
            P.dma(lambda e: e.dma_start(out=X2[64:128], in_=dr["s5_b_re"][l].rearrange("g n c -> n g c")), writes=[S])
            P.dma(lambda e: e.dma_start(out=CT[0:64], in_=dr["s5_c_re"][l].rearrange("g c n -> n g c")), writes=[S])
            P.dma(lambda e: e.dma_start(out=CT[64:128], in_=dr["s5_c_im"][l].rearrange("g c n -> n g c")), writes=[S])
            P.dma(lambda e: e.dma_start(out=gm[:], in_=dr["c_gm"]), writes=[S])
            P.dma(lambda e: e.dma_start(out=swp[:], in_=dr["c_swap"]), writes=[S])
            P.dma(lambda e: e.dma_start(out=dsk[:], in_=dr["s5_d"][l].rearrange("(c p) -> p c", p=128)), writes=[S])
            P.dma(lambda e: e.dma_start(out=glb[:], in_=dr["s5_glu_b"][l].rearrange("(c p) -> p c", p=128)), writes=[S])
            V = lambda fn: P.dve(fn, reads=[S], writes=[S])
            A = lambda fn: P.act(fn, reads=[S], writes=[S])
            A(lambda e: e.activation(out=dl[:], in_=dl[:], func=AF.Exp))
            V(lambda e: e.tensor_tensor(out=th[:], in0=lre[:], in1=dl[:], op=ALU.mult))
            A(lambda e: e.activation(out=ee[:], in_=th[:], func=AF.Exp))
            V(lambda e: e.tensor_tensor(out=th[:], in0=lim[:], in1=dl[:], op=ALU.mult))
            V(lambda e: e.tensor_scalar(out=rr[:], in0=th[:], scalar1=1.0 / (2 * PI), scalar2=None, op0=ALU.mult))
            V(lambda e: e.tensor_copy(out=ki[:], in_=rr[:]))
            V(lambda e: e.tensor_copy(out=kf[:], in_=ki[:]))
            V(lambda e: e.tensor_tensor(out=ff[:], in0=rr[:], in1=kf[:], op=ALU.subtract))
            V(lambda e: e.tensor_scalar(out=s1[:], in0=ff[:], scalar1=2 * PI / 8, scalar2=None, op0=ALU.mult))
            V(lambda e: e.tensor_tensor(out=s2[:], in0=s1[:], in1=s1[:], op=ALU.mult))
            V(lambda e: e.tensor_scalar(out=sn[:], in0=s2[:], scalar1=1.0 / 362880, scalar2=None, op0=ALU.mult))
            for cf in (-1.0 / 5040, 1.0 / 120, -1.0 / 6):
                V(lambda e, cf=cf: e.scalar_tensor_tensor(out=sn[:], in0=sn[:], scalar=cf, in1=s2[:], op0=ALU.add, op1=ALU.mult))
            V(lambda e: e.scalar_tensor_tensor(out=sn[:], in0=sn[:], scalar=1.0, in1=s1[:], op0=ALU.add, op1=ALU.mult))
            V(lambda e: e.tensor_scalar(out=cs[:], in0=s2[:], scalar1=-1.0 / 3628800, scalar2=None, op0=ALU.mult))
            for cf in (1.0 / 40320, -1.0 / 720, 1.0 / 24, -0.5):
                V(lambda e, cf=cf: e.scalar_tensor_tensor(out=cs[:], in0=cs[:], scalar=cf, in1=s2[:], op0=ALU.add, op1=ALU.mult))
            V(lambda e: e.tensor_scalar(out=cs[:], in0=cs[:], scalar1=1.0, scalar2=None, op0=ALU.add))
            for _ in range(3):
                V(lambda e: e.tensor_tensor(out=c1[:], in0=cs[:], in1=cs[:], op=ALU.mult))
                V(lambda e: e.tensor_tensor(out=s1[:], in0=sn[:], in1=sn[:], op=ALU.mult))
                V(lambda e: e.scalar_tensor_tensor(out=sn[:], in0=sn[:], scalar=2.0, in1=cs[:], op0=ALU.mult, op1=ALU.mult))
                V(lambda e: e.tensor_tensor(out=cs[:], in0=c1[:], in1=s1[:], op=ALU.subtract))
            V(lambda e: e.tensor_tensor(out=ar[:], in0=ee[:], in1=cs[:], op=ALU.mult))
            V(lambda e: e.tensor_tensor(out=ai[:], in0=ee[:], in1=sn[:], op=ALU.mult))
            V(lambda e: e.tensor_scalar(out=xx[:], in0=ar[:], scalar1=-1.0, scalar2=None, op0=ALU.add))
            V(lambda e: e.tensor_tensor(out=den[:], in0=lre[:], in1=lre[:], op=ALU.mult))
            V(lambda e: e.tensor_tensor(out=tq[:], in0=lim[:], in1=lim[:], op=ALU.mult))
            V(lambda e: e.tensor_tensor(out=den[:], in0=den[:], in1=tq[:], op=ALU.add))
            V(lambda e: e.reciprocal(out=den[:], in_=den[:]))
            V(lambda e: e.tensor_tensor(out=cr[:], in0=xx[:], in1=lre[:], op=ALU.mult))
            V(lambda e: e.tensor_tensor(out=tq[:], in0=ai[:], in1=lim[:], op=ALU.mult))
            V(lambda e: e.tensor_tensor(out=cr[:], in0=cr[:], in1=tq[:], op=ALU.add))
            V(lambda e: e.tensor_tensor(out=cr[:], in0=cr[:], in1=den[:], op=ALU.mult))
            V(lambda e: e.tensor_tensor(out=ci[:], in0=ai[:], in1=lre[:], op=ALU.mult))
            V(lambda e: e.tensor_tensor(out=tq[:], in0=xx[:], in1=lim[:], op=ALU.mult))
            V(lambda e: e.tensor_tensor(out=ci[:], in0=ci[:], in1=tq[:], op=ALU.subtract))
            V(lambda e: e.tensor_tensor(out=ci[:], in0=ci[:], in1=den[:], op=ALU.mult))
            V(lambda e: e.tensor_scalar(out=ci[0:64, :], in0=ci[0:64, :], scalar1=-1.0, scalar2=None, op0=ALU.mult))
            V(lambda e: e.tensor_copy(out=ais[:], in_=ai[:]))
            V(lambda e: e.tensor_scalar(out=ais[64:128, :], in0=ais[64:128, :], scalar1=-1.0, scalar2=None, op0=ALU.mult))
            V(lambda e: e.tensor_scalar(out=CT[64:128], in0=CT[64:128], scalar1=-1.0, scalar2=None, op0=ALU.mult))
            V(lambda e: e.tensor_tensor(out=bb[:], in0=X1[:], in1=cr[:].unsqueeze(2).to_broadcast([128, 16, 16]), op=ALU.mult))
            V(lambda e: e.tensor_tensor(out=bb2[:], in0=X2[:], in1=ci[:].unsqueeze(2).to_broadcast([128, 16, 16]), op=ALU.mult))
            V(lambda e: e.tensor_tensor(out=bb[:], in0=bb[:], in1=bb2[:], op=ALU.add))
            for ch in range(2):
                ps = self.psum[ch]; pn = "ps%d" % ch
                P.pe(lambda e, ps=ps, ch=ch: e.transpose(ps[:, 0:128], bb[:, 8 * ch:8 * ch + 8, :].rearrange("p g c -> p (g c)"), ident[:]),
                     reads=[S, "ident"], writes=[pn])
                P.dve(lambda e, ps=ps, ch=ch: e.tensor_copy(out=BT[:, ch, :], in_=ps[:, 0:128]), reads=[pn], writes=[S])
            NB = 2
            Hb = [t_("Hb%d" % i, [128, HW], BF16) for i in range(NB)]
            ark = t_("ark", [128, 16]); aik = t_("aik", [128, 16]); ta = t_("ta", [128, 16]); tb = t_("tb", [128, 16])
            ysum = t_("ysum", [128, 2, T])
            Hlast = t_("Hlast", [128, 16]); HlastS = t_("HlastS", [128, 16, NSEQ])
            stA = contextlib.ExitStack()
            tA = lambda nm, shp, dt=F32: P.sb("s5" + nm, shp, dt, stack=stA)
            Hs = [tA("H%d" % i, [128, HW]) for i in range(NB)]
            Bl = [tA("Bl%d" % i, [128, 128], BF16) for i in range(NB)]
            Ml = [tA("Ml%d" % i, [128, 128], BF16) for i in range(2 * NB)]
            Mlo = [tA("Mlo%d" % i, [128, 128], BF16) for i in range(2 * NB)]
            Mf = [tA("Mf%d" % i, [128, 128]) for i in range(2)]
            mtmp = [tA("mt%d" % i, [128, 128]) for i in range(2)]
            shifts = [1 << k for k in range(11)]
            mcount = [0]
            for g0 in range(0, 16, NB):
                grp = list(range(g0, g0 + NB))
                ch = g0 // 8
                uc = u["s%d" % ch]
                ur = self.ures("s%d" % ch)
                for bi, g in enumerate(grp):
                    H = Hs[bi]; Hn = "s5H%d" % bi
                    P.dve(lambda e, bi=bi, g=g, ch=ch: e.tensor_scalar(out=Bl[bi][:], in0=BT[:, ch, :], scalar1=gm[:, g % 8:g % 8 + 1], scalar2=None, op0=ALU.mult),
                          reads=[S], writes=["s5Bl%d" % bi])
                    for ti, (s, n) in enumerate(TILES):
                        ps = self.psum[2 + (ti % 2)]; pn = "ps%d" % (2 + ti % 2)
                        P.pe(lambda e, ps=ps, bi=bi, s=s, n=n, uc=uc: e.matmul(ps[:, 0:n], Bl[bi][:], uc[:, s:s + n], start=True, stop=True),
                             reads=["s5Bl%d" % bi] + ur, writes=[pn])
                        if s < TP:
                            P.act(lambda e, ps=ps, H=H, s=s, n=n: e.activation(out=H[:, s:s + n], in_=ps[:, 0:n], func=AF.Copy), reads=[pn], writes=[Hn])
                        else:
                            P.act(lambda e, ps=ps, H=H: e.activation(out=H[:, TP:HW].rearrange("p (s t) -> p s t", t=5)[:, :, 1:5],
                                                                     in_=ps[:, 0:TSAMP].rearrange("p (s t) -> p s t", t=TS), func=AF.Copy), reads=[pn], writes=[Hn])
                    P.dma(lambda e, H=H, g=g: e.dma_start(out=H[0:64, TP:HW].rearrange("p (s t) -> p s t", t=5)[:, :, 0], in_=dr["st_s5re"][l][:, g, :].rearrange("s n -> n s")),
                          writes=[Hn])
                    P.dma(lambda e, H=H, g=g: e.dma_start(out=H[64:128, TP:HW].rearrange("p (s t) -> p s t", t=5)[:, :, 0], in_=dr["st_s5im"][l][:, g, :].rearrange("s n -> n s")),
                          writes=[Hn])
                P.dve(lambda e: e.tensor_copy(out=ark[:], in_=ar[:]), reads=[S], writes=["s5pw"])
                P.dve(lambda e: e.tensor_copy(out=aik[:], in_=ais[:]), reads=[S], writes=["s5pw"])
                for k, sh in enumerate(shifts):
                    for bi, g in enumerate(grp):
                        H = Hs[bi]; Hn = "s5H%d" % bi
                        hb = Hb[bi]; hbn = "s5Hb%d" % bi
                        mi = mcount[0] % (2 * NB); mcount[0] += 1
                        M = Ml[mi]; Mn = "s5M%d" % mi
                        mt = mtmp[mi % 2]; mtn = "s5mt%d" % (mi % 2)
                        P.dve(lambda e, mt=mt, g=g: e.tensor_scalar(out=mt[:], in0=swp[:], scalar1=aik[:, g:g + 1], scalar2=None, op0=ALU.mult),
                              reads=[S, "s5pw"], writes=[mtn])
                        mf = Mf[mi % 2]; mfn = "s5Mf%d" % (mi % 2)
                        M2 = Mlo[mi]
                        P.dve(lambda e, mt=mt, mf=mf, g=g: e.scalar_tensor_tensor(out=mf[:], in0=ident[:], scalar=ark[:, g:g + 1], in1=mt[:], op0=ALU.mult, op1=ALU.add),
                              reads=[mtn, "s5pw", "ident"], writes=[mfn])
                        P.act(lambda e, mf=mf, M=M: e.activation(out=M[:], in_=mf[:], func=AF.Copy), reads=[mfn], writes=[Mn])
                        P.dve(lambda e, mf=mf, M=M, M2=M2: e.tensor_tensor(out=M2[:], in0=mf[:], in1=M[:], op=ALU.subtract), reads=[mfn, Mn], writes=[Mn + "lo"])
                        P.act(lambda e, hb=hb, H=H: e.activation(out=hb[:], in_=H[:], func=AF.Copy), reads=[Hn], writes=[hbn])
                        if sh < TP:
                            L = TP - sh
                            pcs = [(c0, min(512, L - c0)) for c0 in range(0, L, 512)]
                            for pi, (c0, n) in enumerate(pcs):
                                ps = self.psum[4 + pi]; pn = "ps%d" % (4 + pi)
                                P.pe(lambda e, ps=ps, M=M, hb=hb, c0=c0, n=n: e.matmul(ps[:, 0:n], M[:], hb[:, c0:c0 + n], start=True, stop=False),
                                     reads=[Mn, hbn], writes=[pn])
                                P.pe(lambda e, ps=ps, M2=M2, hb=hb, c0=c0, n=n: e.matmul(ps[:, 0:n], M2[:], hb[:, c0:c0 + n], start=False, stop=True),
                                     reads=[Mn + "lo", hbn], writes=[pn])
                            for pi, (c0, n) in enumerate(pcs):
                                ps = self.psum[4 + pi]; pn = "ps%d" % (4 + pi)
                                P.dve(lambda e, ps=ps, H=H, c0=c0, n=n, sh=sh: e.tensor_tensor(out=H[:, sh + c0:sh + c0 + n], in0=H[:, sh + c0:sh + c0 + n], in1=ps[:, 0:n], op=ALU.add),
                                      reads=[pn, Hn], writes=[Hn])
                        if sh < 5:
                            ps = self.psum[2]; pn = "ps2"
                            w5 = 5 - sh
                            P.pe(lambda e, ps=ps, M=M, hb=hb, w5=w5: e.matmul(ps[:, 0:NSEQ * w5], M[:], hb[:, TP:HW].rearrange("p (s t) -> p s t", t=5)[:, :, 0:w5], start=True, stop=False),
                                 reads=[Mn, hbn], writes=[pn])
                            P.pe(lambda e, ps=ps, M2=M2, hb=hb, w5=w5: e.matmul(ps[:, 0:NSEQ * w5], M2[:], hb[:, TP:HW].rearrange("p (s t) -> p s t", t=5)[:, :, 0:w5], start=False, stop=True),
                                 reads=[Mn + "lo", hbn], writes=[pn])
                            P.dve(lambda e, ps=ps, H=H, w5=w5, sh=sh: e.tensor_tensor(
                                out=H[:, TP:HW].rearrange("p (s t) -> p s t", t=5)[:, :, sh:5], in0=H[:, TP:HW].rearrange("p (s t) -> p s t", t=5)[:, :, sh:5],
                                in1=ps[:, 0:NSEQ * w5].rearrange("p (s t) -> p s t", t=w5), op=ALU.add), reads=[pn, Hn], writes=[Hn])
                    W_ = lambda fn: P.dve(fn, reads=["s5pw"], writes=["s5pw"])
                    W_(lambda e: e.tensor_tensor(out=ta[:], in0=ark[:], in1=ark[:], op=ALU.mult))
                    W_(lambda e: e.tensor_tensor(out=tb[:], in0=aik[:], in1=aik[:], op=ALU.mult))
                    W_(lambda e: e.scalar_tensor_tensor(out=aik[:], in0=ark[:], scalar=2.0, in1=aik[:], op0=ALU.mult, op1=ALU.mult))
                    W_(lambda e: e.tensor_tensor(out=ark[:], in0=ta[:], in1=tb[:], op=ALU.subtract))
                for bi, g in enumerate(grp):
                    H = Hs[bi]; Hn = "s5H%d" % bi
                    hb = Hb[bi]; hbn = "s5Hb%d" % bi
                    P.act(lambda e, hb=hb, H=H: e.activation(out=hb[:], in_=H[:], func=AF.Copy), reads=[Hn], writes=[hbn])
                    P.dve(lambda e, H=H, g=g: e.tensor_copy(out=Hlast[:, g:g + 1], in_=H[:, TP - 1:TP]), reads=[Hn], writes=["s5last"])
                    P.dve(lambda e, H=H, g=g: e.tensor_copy(out=HlastS[:, g, :], in_=H[:, TP:HW].rearrange("p (s t) -> p s t", t=5)[:, :, 4]), reads=[Hn], writes=["s5last"])
                P.dve(lambda e: e.memset(CTm[:], 0.0), writes=["s5CTm"])
                for bi, g in enumerate(grp):
                    P.dve(lambda e, g=g, bi=bi: e.tensor_copy(out=CTm[:, bi, 16 * (g % 8):16 * (g % 8) + 16], in_=CT[:, g, :]), reads=[S], writes=["s5CTm"])
                for ti, (s, n) in enumerate(TILES):
                    ps = self.psum[ti % 2]; pn = "ps%d" % (ti % 2)
                    for bi, g in enumerate(grp):
                        hb = Hb[bi]; hbn = "s5Hb%d" % bi
                        if s < TP:
                            rhs = hb[:, s:s + n]
                        else:
                            rhs = hb[:, TP:HW].rearrange("p (s t) -> p s t", t=5)[:, :, 1:5]
                        P.pe(lambda e, ps=ps, g=g, rhs=rhs, n=n, bi=bi: e.matmul(ps[:, 0:n], CTm[:, bi, :], rhs, start=(bi == 0), stop=(bi == NB - 1)),
                             reads=["s5CTm", hbn], writes=[pn])
                    yn = "s5ys%d_%d" % (ch, ti)
                    if g0 % 8 == 0:
                        P.dve(lambda e, ps=ps, ch=ch, s=s, n=n, uc=uc: e.scalar_tensor_tensor(out=ysum[:, ch, s:s + n], in0=uc[:, s:s + n], scalar=dsk[:, ch:ch + 1], in1=ps[:, 0:n],
                                                                                       op0=ALU.mult, op1=ALU.add), reads=[pn, S] + ur, writes=[yn])
                    else:
                        P.dve(lambda e, ps=ps, ch=ch, s=s, n=n: e.tensor_tensor(out=ysum[:, ch, s:s + n], in0=ysum[:, ch, s:s + n], in1=ps[:, 0:n], op=ALU.add),
                              reads=[pn, yn], writes=[yn])
            P.dma(lambda e: e.dma_start(out=dr["s5re_p"][l].rearrange("g n -> n g"), in_=Hlast[0:64, :]), reads=["s5last"], group="sto")
            P.dma(lambda e: e.dma_start(out=dr["s5im_p"][l].rearrange("g n -> n g"), in_=Hlast[64:128, :]), reads=["s5last"], group="sto")
            for g in range(16):
                P.dma(lambda e, g=g: e.dma_start(out=dr["s5re_s"][l][:, g, :].rearrange("s n -> n s"), in_=HlastS[0:64, g, :]), reads=["s5last"], group="sto")
                P.dma(lambda e, g=g: e.dma_start(out=dr["s5im_s"][l][:, g, :].rearrange("s n -> n s"), in_=HlastS[64:128, g, :]), reads=["s5last"], group="sto")
            P.flush()
            stA.close()
            gwf = t_("gwf", [128, 2, 512]); gwb = t_("gwb", [128, 2, 512], BF16)
            P.dma(lambda e: e.dma_start(out=gwf[:], in_=dr["s5_glu_w"][l].rearrange("(k p) c -> p k c", p=128)), writes=["s5gw"])
            P.dve(lambda e: e.tensor_copy(out=gwb[:], in_=gwf[:]), reads=["s5gw"], writes=["s5gw"])
            g1 = [t_("g1_%d" % i, [128, 512]) for i in range(2)]
            g2 = [t_("g2_%d" % i, [128, 512]) for i in range(2)]
            cc = 0
            for ch in range(2):
                for ti, (s, n) in enumerate(TILES):
                    a = g1[cc % 2]; an = "s5g1_%d" % (cc % 2)
                    b = g2[cc % 2]; bn = "s5g2_%d" % (cc % 2)
                    cc += 1
                    yn = "s5ys%d_%d" % (ch, ti)
                    ys = ysum[:, ch, s:s + n]
                    P.act(lambda e, a=a, ys=ys, n=n: e.activation(out=a[:, 0:n], in_=ys, func=AF.Square), reads=[yn], writes=[an])
                    P.dve(lambda e, a=a, n=n: e.tensor_scalar(out=a[:, 0:n], in0=a[:, 0:n], scalar1=0.044715, scalar2=1.0, op0=ALU.mult, op1=ALU.add), reads=[an], writes=[an])
                    P.dve(lambda e, a=a, b=b, ys=ys, n=n: e.tensor_tensor(out=b[:, 0:n], in0=a[:, 0:n], in1=ys, op=ALU.mult), reads=[an, yn], writes=[bn])
                    P.act(lambda e, b=b, n=n: e.activation(out=b[:, 0:n], in_=b[:, 0:n], func=AF.Sigmoid, scale=1.5957691216), reads=[bn], writes=[bn])
                    P.dve(lambda e, b=b, ys=ys, ch=ch, s=s, n=n: e.tensor_tensor(out=Hb[ch][:, s:s + n], in0=b[:, 0:n], in1=ys, op=ALU.mult), reads=[bn, yn], writes=["s5yg%d_%d" % (ch, ti)])
            for m in range(2):
                for ti, (s, n) in enumerate(TILES):
                    pa = self.psum[(ti % 2) * 2]; pan = "ps%d" % ((ti % 2) * 2)
                    pb = self.psum[(ti % 2) * 2 + 1]; pbn = "ps%d" % ((ti % 2) * 2 + 1)
                    sg = g1[ti % 2]; sgn = "s5g1_%d" % (ti % 2)
                    for k in range(2):
                        P.pe(lambda e, pa=pa, k=k, m=m, s=s, n=n: e.matmul(pa[:, 0:n], gwb[:, k, m * 128:(m + 1) * 128], Hb[k][:, s:s + n], start=(k == 0), stop=(k == 1)),
                             reads=["s5gw", "s5yg%d_%d" % (k, ti)], writes=[pan])
                    for k in range(2):
                        P.pe(lambda e, pb=pb, k=k, m=m, s=s, n=n: e.matmul(pb[:, 0:n], gwb[:, k, 256 + m * 128:256 + (m + 1) * 128], Hb[k][:, s:s + n], start=(k == 0), stop=(k == 1)),
                             reads=["s5gw", "s5yg%d_%d" % (k, ti)], writes=[pbn])
                    P.act(lambda e, pb=pb, sg=sg, m=m, n=n: e.activation(out=sg[:, 0:n], in_=pb[:, 0:n], func=AF.Sigmoid, bias=glb[:, 2 + m:3 + m], scale=1.0), reads=[pbn, S], writes=[sgn])
                    P.dve(lambda e, pa=pa, sg=sg, m=m, s=s, n=n: e.scalar_tensor_tensor(out=ymix[:, 4 + m, s:s + n], in0=pa[:, 0:n], scalar=glb[:, m:m + 1], in1=sg[:, 0:n],
                                                                                  op0=ALU.add, op1=ALU.mult), reads=[pan, sgn, S], writes=["ymix%d_%d" % (4 + m, ti)])
            P.flush()

    def mix_ssd(self, l):
        P = self.P
        dr = self.dram
        ymix, ident, identb, onesb = self.ymix, self.ident, self.identb, self.onesb
        NCH = TP // 128
        with contextlib.ExitStack() as st:
            t_ = lambda nm, shp, dt=F32: P.sb("sd" + nm, shp, dt, stack=st)
            C_ = "sdc"
            tokq = t_("tokq", [128, 4, NCH + 1, 4])
            cdS = t_("cdS", [128, 2, NSEQ])
            eaS = t_("eaS", [128, 2, TSAMP])
            cw = t_("cw", [128, 6, 4]); cb = t_("cb", [128, 6]); dcol = t_("dcol", [128, 4]); ng = t_("ng", [128, 2])
            dtb = t_("dtb", [4, 1]); aneg = t_("aneg", [4, 1]); one4 = t_("one4", [4, 1])
            mk = t_("mk", [128, 128]); Ust = t_("Ust", [128, 128]); Lin = t_("Lin", [128, 128]); elast = t_("elast", [128, 128])
            mkS = t_("mkS", [64, 64]); UstS = t_("UstS", [64, 64]); LinS = t_("LinS", [64, 64]); seqm = t_("seqm", [64, NSEQ])
            sel = t_("sel", [4, 2, 128])
            for j in range(4):
                P.dma(lambda e, j=j: e.dma_start(out=cw[:, :, j], in_=dr["ssd_conv_w"][l, j].rearrange("(c p) -> p c", p=128)), writes=[C_])
            P.dma(lambda e: e.dma_start(out=cb[:], in_=dr["ssd_conv_b"][l].rearrange("(c p) -> p c", p=128)), writes=[C_])
            P.dma(lambda e: e.dma_start(out=dcol[:], in_=dr["ssd_d"][l].partition_broadcast(128)), writes=[C_])
            P.dma(lambda e: e.dma_start(out=ng[:], in_=dr["ssd_norm"][l].rearrange("(c p) -> p c", p=128)), writes=[C_])
            P.dma(lambda e: e.dma_start(out=dtb[:], in_=dr["ssd_dt_bias"][l].rearrange("(h o) -> h o", o=1)), writes=[C_])
            P.dma(lambda e: e.dma_start(out=aneg[:], in_=dr["ssd_a_log"][l].rearrange("(h o) -> h o", o=1)), writes=[C_])
            P.dma(lambda e: e.dma_start(out=mk[:], in_=dr["c_mask128"]), writes=[C_])
            P.dma(lambda e: e.dma_start(out=Ust[:], in_=dr["c_U128"]), writes=[C_])
            P.dma(lambda e: e.dma_start(out=Lin[:], in_=dr["c_L128"]), writes=[C_])
            P.dma(lambda e: e.dma_start(out=elast[:], in_=dr["c_elast"]), writes=[C_])
            P.dma(lambda e: e.dma_start(out=mkS[:], in_=dr["c_maskS"]), writes=[C_])
            P.dma(lambda e: e.dma_start(out=UstS[:], in_=dr["c_US"]), writes=[C_])
            P.dma(lambda e: e.dma_start(out=LinS[:], in_=dr["c_LS"]), writes=[C_])
            P.dma(lambda e: e.dma_start(out=seqm[:], in_=dr["c_seqm"]), writes=[C_])
            P.dma(lambda e: e.dma_start(out=sel[:], in_=dr["c_sel"]), writes=[C_])
            P.act(lambda e: e.activation(out=aneg[:], in_=aneg[:], func=AF.Exp), reads=[C_], writes=[C_])
            P.dve(lambda e: e.tensor_scalar(out=aneg[:], in0=aneg[:], scalar1=-1.0, scalar2=None, op0=ALU.mult), reads=[C_], writes=[C_])
            P.dve(lambda e: e.memset(one4[:], 1.0), writes=[C_])
            with contextlib.ExitStack() as s1:
                ud = self.inproj(l, [("dt", 1024, 4)], s1, dtype=F32)["dt"]
                dA = P.sb("sddA", [4, T], F32, stack=s1)
                dB = P.sb("sddB", [4, T], F32, stack=s1)
                cm = P.sb("sdcm", [4, T], F32, stack=s1)
                aend = P.sb("sdaend", [4, NCH + NSEQ], F32, stack=s1)
                P.dma(lambda e: e.dma_start(out=cm[:], in_=dr["c_cmask"]), writes=["sdcm"])
                Rd = self.ures("dt")

                def to_tok(src, q, rn):
                    ps = self.psum[q % 2]; pn = "ps%d" % (q % 2)
                    for c in range(NCH + 1):
                        L = 128 if c < NCH else TSAMP
                        P.pe(lambda e, ps=ps, c=c, L=L, src=src: e.transpose(ps[0:L, 4 * c:4 * c + 4], src[0:4, 128 * c:128 * c + L], ident[0:4, 0:4]),
                             reads=[rn, "ident"], writes=[pn])
                    P.dve(lambda e, ps=ps, q=q: e.tensor_copy(out=tokq[:, q, :, :], in_=ps[:, 0:4 * (NCH + 1)].rearrange("p (c h) -> p c h", h=4)),
                          reads=[pn], writes=["sdtokq"])
                P.act(lambda e: e.activation(out=dA[:], in_=ud[0:4, :], func=AF.Exp, bias=dtb[:], scale=1.0), reads=Rd + [C_], writes=["sddA"])
                P.act(lambda e: e.activation(out=dA[:], in_=dA[:], func=AF.Ln, bias=one4[:], scale=1.0), reads=["sddA", C_], writes=["sddA"])
                to_tok(dA, 0, "sddA")
                P.dve(lambda e: e.tensor_scalar(out=dB[:], in0=dA[:], scalar1=aneg[:, 0:1], scalar2=None, op0=ALU.mult), reads=["sddA", C_], writes=["sddB"])
                to_tok(dB, 1, "sddB")
                P.dve(lambda e: e.tensor_tensor_scan(out=dA[:], data0=cm[:], data1=dB[:], initial=0.0, op0=ALU.mult, op1=ALU.add),
                      reads=["sddB", "sdcm"], writes=["sddA"])
                P.dve(lambda e: e.tensor_copy(out=aend[:, 0:NCH], in_=dA[:, 0:TP].rearrange("p (c t) -> p c t", t=128)[:, :, 127]), reads=["sddA"], writes=["sdaend"])
                P.dve(lambda e: e.tensor_copy(out=aend[:, NCH:NCH + NSEQ], in_=dA[:, TP:T].rearrange("p (c t) -> p c t", t=TS)[:, :, TS - 1]), reads=["sddA"], writes=["sdaend"])
                P.act(lambda e: e.activation(out=dB[:], in_=dA[:], func=AF.Exp), reads=["sddA"], writes=["sddB"])
                to_tok(dB, 3, "sddB")
                for c in range(2):
                    ps = self.psum[2 + c]; pn = "ps%d" % (2 + c)
                    P.pe(lambda e, ps=ps, c=c: e.matmul(ps[:, 0:TSAMP], sel[:, c, :], dB[:, TP:T], start=True, stop=True), reads=["sddB", C_], writes=[pn])
                    P.act(lambda e, ps=ps, c=c: e.activation(out=eaS[:, c, :], in_=ps[:, 0:TSAMP], func=AF.Copy), reads=[pn], writes=["sdeaS"])
                P.dve(lambda e: e.tensor_tensor(out=dA[:, 0:TP].rearrange("p (c t) -> p c t", t=128), in0=aend[:, 0:NCH].unsqueeze(2).to_broadcast([4, NCH, 128]),
                                                in1=dA[:, 0:TP].rearrange("p (c t) -> p c t", t=128), op=ALU.subtract), reads=["sddA", "sdaend"], writes=["sddA"])
                P.dve(lambda e: e.tensor_tensor(out=dA[:, TP:T].rearrange("p (c t) -> p c t", t=TS), in0=aend[:, NCH:NCH + NSEQ].unsqueeze(2).to_broadcast([4, NSEQ, TS]),
                                                in1=dA[:, TP:T].rearrange("p (c t) -> p c t", t=TS), op=ALU.subtract), reads=["sddA", "sdaend"], writes=["sddA"])
                P.act(lambda e: e.activation(out=dA[:], in_=dA[:], func=AF.Exp), reads=["sddA"], writes=["sddA"])
                to_tok(dA, 2, "sddA")
                P.act(lambda e: e.activation(out=aend[:], in_=aend[:], func=AF.Exp), reads=["sdaend"], writes=["sdaend"])
                for c in range(2):
                    ps = self.psum[2 + c]; pn = "ps%d" % (2 + c)
                    P.pe(lambda e, ps=ps, c=c: e.matmul(ps[:, 0:NSEQ], sel[:, c, :], aend[:, NCH:NCH + NSEQ], start=True, stop=True), reads=["sdaend", C_], writes=[pn])
                    P.act(lambda e, ps=ps, c=c: e.activation(out=cdS[:, c, :], in_=ps[:, 0:NSEQ], func=AF.Copy), reads=[pn], writes=["sdcdS"])
                P.flush()
            import os
            SSDPH = int(os.environ.get("SSDPH", "9"))
            SUB = int(os.environ.get("SUB", "99"))
            if SSDPH < 2:
                return
            zz = self.inproj(l, [("z0", 0, 128), ("z1", 128, 128)], st, dtype=BF16)
            xc = [t_("xc%d" % c, [128, T], BF16) for c in range(6)]
            cst = t_("cst", [128, 6, 1 + NSEQ, 3])
            for c in range(6):
                with contextlib.ExitStack() as s2:
                    nm = "xb%d" % c
                    u = self.inproj(l, [(nm, 256 + 128 * c, 128)], s2, dtype=BF16, pad=3)[nm]
                    acc = P.sb("sdacc", [128, TP], F32, stack=s2)
                    Es = P.sb("sdEs", [128, NSEQ, 7], F32, stack=s2)
                    accs = P.sb("sdaccs", [128, NSEQ, TS], F32, stack=s2)
                    R = self.ures(nm) + ["u_%s_pad" % nm]
                    P.dve(lambda e, c=c, u=u: e.tensor_copy(out=cst[:, c, 0, :], in_=u[:, 3 + TP - 3:3 + TP]), reads=R, writes=["sdcst"])
                    P.dve(lambda e, c=c, u=u: e.tensor_copy(out=cst[:, c, 1:1 + NSEQ, :], in_=u[:, 3 + TP:3 + T].rearrange("p (s t) -> p s t", t=TS)[:, :, 1:4]), reads=R, writes=["sdcst"])
                    P.dma(lambda e, c=c: e.dma_start(out=dr["conv_p"][l][:, c * 128:(c + 1) * 128].rearrange("r f -> f r"), in_=cst[:, c, 0, :]), reads=["sdcst"], group="sto")
                    for r in range(3):
                        P.dma(lambda e, c=c, r=r: e.dma_start(out=dr["conv_s"][l][:, r, c * 128:(c + 1) * 128].rearrange("s f -> f s"), in_=cst[:, c, 1:1 + NSEQ, r]), reads=["sdcst"], group="sto")
                        P.dma(lambda e, c=c, r=r: e.dma_start(out=Es[:, :, r], in_=dr["st_conv"][l][:, r, c * 128:(c + 1) * 128].rearrange("s f -> f s")), writes=["sdEs"])
                    P.dve(lambda e, u=u: e.tensor_copy(out=Es[:, :, 3:7], in_=u[:, 3 + TP:3 + T].rearrange("p (s t) -> p s t", t=TS)), reads=R, writes=["sdEs"])
                    P.dve(lambda e, c=c, u=u: e.tensor_scalar(out=acc[:], in0=u[:, 3:3 + TP], scalar1=cw[:, c, 3:4], scalar2=cb[:, c:c + 1], op0=ALU.mult, op1=ALU.add),
                          reads=R + [C_], writes=["sdacc"])
                    for j in range(3):
                        P.dve(lambda e, c=c, u=u, j=j: e.scalar_tensor_tensor(out=acc[:], in0=u[:, j:j + TP], scalar=cw[:, c, j:j + 1], in1=acc[:], op0=ALU.mult, op1=ALU.add),
                              reads=R + [C_, "sdacc"], writes=["sdacc"])
                    P.act(lambda e, c=c: e.activation(out=xc[c][:, 0:TP], in_=acc[:], func=AF.Silu), reads=["sdacc"], writes=["sdxc%d" % c])
                    P.dve(lambda e, c=c: e.tensor_scalar(out=accs[:], in0=Es[:, :, 3:7], scalar1=cw[:, c, 3:4], scalar2=cb[:, c:c + 1], op0=ALU.mult, op1=ALU.add),
                          reads=["sdEs", C_], writes=["sdaccs"])
                    for j in range(3):
                        P.dve(lambda e, c=c, j=j: e.scalar_tensor_tensor(out=accs[:], in0=Es[:, :, j:j + TS], scalar=cw[:, c, j:j + 1], in1=accs[:], op0=ALU.mult, op1=ALU.add),
                              reads=["sdEs", C_, "sdaccs"], writes=["sdaccs"])
                    P.act(lambda e, c=c: e.activation(out=xc[c][:, TP:T].rearrange("p (s t) -> p s t", t=TS), in_=accs[:], func=AF.Silu), reads=["sdaccs"], writes=["sdxc%d" % c])
                    P.flush()
            if SSDPH < 3:
                return
            yss = t_("yss", [128, 2, T], BF16)
            hT = t_("hT", [128, 2, 128]); hTb = t_("hTb", [128, 2, 128], BF16)
            cdB = t_("cdB", [128, 4])
            xtok = t_("xtok", [128, 256]); xdt = t_("xdt", [128, 256], BF16); xdd = t_("xdd", [128, 256], BF16); Btok = t_("Btok", [128, 256], BF16)
            CBm = [t_("CBm%d" % g, [128, 128]) for g in range(2)]
            Ul = [t_("Ul%d" % i, [128, 128]) for i in range(2)]
            Ex = [t_("Ex%d" % i, [128, 128]) for i in range(2)]
            sc = [t_("sc%d" % h, [128, 128], BF16) for h in range(4)]
            ydg = t_("ydg", [128, 256]); ytok = t_("ytok", [128, 256])
            h0n = [t_("h0n%d" % i, [128, 2, 128]) for i in range(2)]
            h0T = [t_("h0T%d" % i, [128, 2, 128], BF16) for i in range(2)]
            hnew = [t_("hnew%d" % i, [128, 2, 128]) for i in range(2)]
            xdm = [t_("xdm%d" % i, [64, 256], BF16) for i in range(2)]
            yoS = t_("yoS", [128, 2, TSAMP])
            P.dve(lambda e: e.memset(hT[:], 0.0), writes=["sdhT"])
            P.dve(lambda e: e.memset(hTb[:], 0.0), writes=["sdhTb"])
            XS = ["sdxc0", "sdxc1"]; BS = ["sdxc2", "sdxc3"]; CS = ["sdxc4", "sdxc5"]
            for c in range(NCH + 1):
                if SSDPH == 3 and c >= 2:
                    break
                if SSDPH == 4 and c >= NCH:
                    break
                samp = (c == NCH)
                L = TSAMP if samp else 128
                t0 = 128 * c
                mK, uK, lK = (mkS, UstS, LinS) if samp else (mk, Ust, Lin)
                pxb = self.psum[0]
                for q in range(4):
                    P.pe(lambda e, q=q, L=L, t0=t0: e.matmul(pxb[0:L, 128 * q:128 * q + 128], xc[q][:, t0:t0 + L], identb[:], start=True, stop=True),
                         reads=["sdxc%d" % q, "identb"], writes=["ps0"])
                P.act(lambda e, L=L: e.activation(out=xtok[0:L, :], in_=pxb[0:L, 0:256], func=AF.Copy), reads=["ps0"], writes=["sdxtok"])
                P.act(lambda e, L=L: e.activation(out=Btok[0:L, :], in_=pxb[0:L, 256:512], func=AF.Copy), reads=["ps0"], writes=["sdBtok"])
                if SUB <= 1:
                    continue
                for h in range(4):
                    P.dve(lambda e, h=h, L=L, c=c: e.tensor_scalar(out=xdt[0:L, 64 * h:64 * h + 64], in0=xtok[0:L, 64 * h:64 * h + 64], scalar1=(1.0 if os.environ.get("IMM") else tokq[0:L, 0, c, h:h + 1]), scalar2=None, op0=ALU.mult),
                          reads=["sdxtok", "sdtokq"], writes=["sdxdt"])
                    P.dve(lambda e, h=h, L=L, c=c: e.tensor_scalar(out=xdd[0:L, 64 * h:64 * h + 64], in0=xtok[0:L, 64 * h:64 * h + 64], scalar1=(1.0 if os.environ.get("IMM") else tokq[0:L, 0, c, h:h + 1]), scalar2=(1.0 if os.environ.get("IMM") else tokq[0:L, 2, c, h:h + 1]), op0=ALU.mult, op1=ALU.mult),
                          reads=["sdxtok", "sdtokq"], writes=["sdxdd"])
                if SUB <= 2:
                    continue
                for g in range(2):
                    pc = self.psum[1 + g]; pcn = "ps%d" % (1 + g)
                    P.pe(lambda e, g=g, pc=pc, L=L, t0=t0: e.matmul(pc[0:L, 0:L], xc[2 + g][:, t0:t0 + L], xc[4 + g][:, t0:t0 + L], start=True, stop=True),
                         reads=[BS[g], CS[g]], writes=[pcn])
                    P.dve(lambda e, g=g, pc=pc, L=L, mK=mK: e.tensor_tensor(out=CBm[g][0:L, 0:L], in0=pc[0:L, 0:L], in1=mK[0:L, 0:L], op=ALU.mult),
                          reads=[pcn, C_], writes=["sdCBm%d" % g])
                if SUB <= 3:
                    continue
                for h in range(4):
                    ul = Ul[h % 2]; uln = "sdUl%d" % (h % 2)
                    ex = Ex[h % 2]; exn = "sdEx%d" % (h % 2)
                    pg = self.psum[3 + (h % 2)]; pgn = "ps%d" % (3 + h % 2)
                    P.dve(lambda e, h=h, ul=ul, L=L, c=c, uK=uK: e.tensor_scalar(out=ul[0:L, 0:L], in0=uK[0:L, 0:L], scalar1=tokq[0:L, 1, c, h:h + 1], scalar2=None, op0=ALU.mult),
                          reads=[C_, "sdtokq"], writes=[uln])
                    P.pe(lambda e, pg=pg, ul=ul, L=L, lK=lK: e.matmul(pg[0:L, 0:L], ul[0:L, 0:L], lK[0:L, 0:L], start=True, stop=True), reads=[uln, C_], writes=[pgn])
                    P.act(lambda e, pg=pg, ex=ex, L=L: e.activation(out=ex[0:L, 0:L], in_=pg[0:L, 0:L], func=AF.Exp), reads=[pgn], writes=[exn])
                    P.dve(lambda e, h=h, ex=ex, L=L: e.tensor_tensor(out=sc[h][0:L, 0:L], in0=ex[0:L, 0:L], in1=CBm[h // 2][0:L, 0:L], op=ALU.mult),
                          reads=[exn, "sdCBm%d" % (h // 2)], writes=["sdsc%d" % h])
                if SUB <= 4:
                    continue
                pyd = self.psum[5]
                for h in range(4):
                    P.pe(lambda e, h=h, L=L: e.matmul(pyd[0:L, 64 * h:64 * h + 64], sc[h][0:L, 0:L], xdt[0:L, 64 * h:64 * h + 64], start=True, stop=True),
                         reads=["sdsc%d" % h, "sdxdt"], writes=["ps5"])
                if SUB <= 5:
                    continue
                if not samp:
                    P.act(lambda e, L=L: e.activation(out=ydg[0:L, :], in_=pyd[0:L, 0:256], func=AF.Copy), reads=["ps5"], writes=["sdydg"])
                    pyo = self.psum[6]
                    for g in range(2):
                        P.pe(lambda e, g=g, t0=t0: e.matmul(pyo[:, 128 * g:128 * g + 128], xc[4 + g][:, t0:t0 + 128], hTb[:, g, :], start=True, stop=True),
                             reads=[CS[g], "sdhTb"], writes=["ps6"])
                    for h in range(4):
                        P.dve(lambda e, h=h, c=c: e.scalar_tensor_tensor(out=ytok[:, 64 * h:64 * h + 64], in0=pyo[:, 64 * h:64 * h + 64], scalar=tokq[:, 3, c, h:h + 1], in1=ydg[:, 64 * h:64 * h + 64],
                                                                      op0=ALU.mult, op1=ALU.add), reads=["ps6", "sdydg", "sdtokq"], writes=["sdytok"])
                else:
                    P.act(lambda e, L=L: e.activation(out=ytok[0:L, :], in_=pyd[0:L, 0:256], func=AF.Copy), reads=["ps5"], writes=["sdytok"])
                if SUB <= 6:
                    continue
                for h in range(4):
                    P.dve(lambda e, h=h, L=L: e.scalar_tensor_tensor(out=ytok[0:L, 64 * h:64 * h + 64], in0=xtok[0:L, 64 * h:64 * h + 64], scalar=dcol[0:L, h:h + 1], in1=ytok[0:L, 64 * h:64 * h + 64],
                                                                  op0=ALU.mult, op1=ALU.add), reads=["sdxtok", "sdytok", C_], writes=["sdytok"])
                if SUB <= 7:
                    continue
                pyt = self.psum[7]
                for g in range(2):
                    P.pe(lambda e, g=g, L=L: e.transpose(pyt[:, 128 * g:128 * g + L], ytok[0:L, 128 * g:128 * g + 128], ident[0:L, 0:L]), reads=["sdytok", "ident"], writes=["ps7"])
                if not samp:
                    P.act(lambda e, t0=t0: e.activation(out=yss[:, :, t0:t0 + 128], in_=pyt[:, 0:256].rearrange("p (g t) -> p g t", t=128), func=AF.Copy), reads=["ps7"], writes=["sdyss"])
                    if SUB <= 8:
                        continue
                    pcd = self.psum[6]
                    P.pe(lambda e, c=c: e.matmul(pcd[:, 256:260], elast[:], tokq[:, 3, c, :], start=True, stop=True), reads=[C_, "sdtokq"], writes=["ps6"])
                    P.act(lambda e: e.activation(out=cdB[:], in_=pcd[:, 256:260], func=AF.Copy), reads=["ps6"], writes=["sdcdB"])
                    pst = self.psum[0]
                    for g in range(2):
                        P.pe(lambda e, g=g: e.matmul(pst[:, 128 * g:128 * g + 128], Btok[:, 128 * g:128 * g + 128], xdd[:, 128 * g:128 * g + 128], start=True, stop=True),
                             reads=["sdBtok", "sdxdd"], writes=["ps0"])
                    for h in range(4):
                        g, hh = h // 2, h % 2
                        P.dve(lambda e, h=h, g=g, hh=hh: e.scalar_tensor_tensor(out=hT[:, g, 64 * hh:64 * hh + 64], in0=hT[:, g, 64 * hh:64 * hh + 64], scalar=cdB[:, h:h + 1],
                                                                             in1=pst[:, 64 * h:64 * h + 64], op0=ALU.mult, op1=ALU.add), reads=["sdhT", "sdcdB", "ps0"], writes=["sdhT"])
                    P.act(lambda e: e.activation(out=hTb[:], in_=hT[:], func=AF.Copy), reads=["sdhT"], writes=["sdhTb"])
                else:
                    pyo = self.psum[6]
                    for s in range(NSEQ):
                        hn = h0n[s % 2]; hnn = "sdh0n%d" % (s % 2)
                        ht = h0T[s % 2]; htn = "sdh0T%d" % (s % 2)
                        hw = hnew[s % 2]; hwn = "sdhnew%d" % (s % 2)
                        xm = xdm[s % 2]; xmn = "sdxdm%d" % (s % 2)
                        pt = self.psum[1 + (s % 2)]; ptn = "ps%d" % (1 + s % 2)
                        pn_ = self.psum[3 + (s % 2)]; pnn = "ps%d" % (3 + s % 2)
                        P.dma(lambda e, s=s, hn=hn: e.dma_start(out=hn[:], in_=dr["st_ssd"][l, s].rearrange("(g a) p n -> (a p) g n", g=2)), writes=[hnn])
                        for g in range(2):
                            P.pe(lambda e, g=g, hn=hn, pt=pt: e.transpose(pt[:, 128 * g:128 * g + 128], hn[:, g, :], ident[:]), reads=[hnn, "ident"], writes=[ptn])
                        P.act(lambda e, ht=ht, pt=pt: e.activation(out=ht[:], in_=pt[:, 0:256].rearrange("p (g t) -> p g t", t=128), func=AF.Copy), reads=[ptn], writes=[htn])
                        for g in range(2):
                            P.pe(lambda e, g=g, s=s, ht=ht: e.matmul(pyo[:, 64 * g + TS * s:64 * g + TS * s + TS], ht[:, g, :], xc[4 + g][:, TP + TS * s:TP + TS * s + TS], start=True, stop=True),
                                 reads=[htn, CS[g]], writes=["ps6"])
                        P.dve(lambda e, s=s, xm=xm: e.tensor_scalar(out=xm[:], in0=xdd[0:64, :], scalar1=seqm[:, s:s + 1], scalar2=None, op0=ALU.mult), reads=["sdxdd", C_], writes=[xmn])
                        for g in range(2):
                            P.pe(lambda e, g=g, xm=xm, pn_=pn_: e.matmul(pn_[:, 128 * g:128 * g + 128], xm[:, 128 * g:128 * g + 128], Btok[0:64, 128 * g:128 * g + 128], start=True, stop=True),
                                 reads=[xmn, "sdBtok"], writes=[pnn])
                        for g in range(2):
                            P.dve(lambda e, g=g, s=s, hn=hn, hw=hw, pn_=pn_: e.scalar_tensor_tensor(out=hw[:, g, :], in0=hn[:, g, :], scalar=cdS[:, g, s:s + 1], in1=pn_[:, 128 * g:128 * g + 128],
                                                                                              op0=ALU.mult, op1=ALU.add), reads=[hnn, "sdcdS", pnn], writes=[hwn])
                        P.dma(lambda e, s=s, hw=hw: e.dma_start(out=dr["ssd_s"][l, s].rearrange("(g a) p n -> (a p) g n", g=2), in_=hw[:]), reads=[hwn])
                    for g in range(2):
                        P.dve(lambda e, g=g: e.tensor_tensor(out=yoS[:, g, :], in0=pyo[:, 64 * g:64 * g + 64], in1=eaS[:, g, :], op=ALU.mult), reads=["ps6", "sdeaS"], writes=["sdyoS"])
                        P.dve(lambda e, g=g: e.tensor_tensor(out=yss[:, g, TP:T], in0=pyt[:, 128 * g:128 * g + TSAMP], in1=yoS[:, g, :], op=ALU.add), reads=["ps7", "sdyoS"], writes=["sdyss"])
            pf = self.psum[1]
            for g in range(2):
                P.pe(lambda e, g=g: e.transpose(pf[:, 128 * g:128 * g + 128], hT[:, g, :], ident[:]), reads=["sdhT", "ident"], writes=["ps1"])
            P.act(lambda e: e.activation(out=hnew[0][:], in_=pf[:, 0:256].rearrange("p (g t) -> p g t", t=128), func=AF.Copy), reads=["ps1"], writes=["sdhnew0"])
            P.dma(lambda e: e.dma_start(out=dr["ssd_p"][l].rearrange("(g a) p n -> (a p) g n", g=2), in_=hnew[0][:]), reads=["sdhnew0"])
            P.flush()
            gq = [t_("gq%d" % i, [128, 2, 512]) for i in range(1)] * 2
            gsq = [t_("gsq%d" % i, [128, 2, 512], BF16) for i in range(1)] * 2
            grs = [t_("grs%d" % i, [128, 512]) for i in range(1)] * 2
            epsc = self.epsc
            for ti, (s, n) in enumerate(TILES):
                q = gq[0]; qn = "sdgq0"
                sq = gsq[0]; sqn = "sdgsq0"
                rs = grs[0]; rsn = "sdgrs0"
                ps = self.psum[ti % 2]; pn = "ps%d" % (ti % 2)
                for k in range(2):
                    P.act(lambda e, k=k, q=q, s=s, n=n: e.activation(out=q[:, k, 0:n], in_=zz["z%d" % k][:, s:s + n], func=AF.Silu), reads=self.ures("z%d" % k, [ti]), writes=[qn])
                    P.dve(lambda e, k=k, q=q, s=s, n=n: e.tensor_tensor(out=q[:, k, 0:n], in0=q[:, k, 0:n], in1=yss[:, k, s:s + n], op=ALU.mult), reads=[qn, "sdyss"], writes=[qn])
                    P.act(lambda e, k=k, q=q, sq=sq, n=n: e.activation(out=sq[:, k, 0:n], in_=q[:, k, 0:n], func=AF.Square), reads=[qn], writes=[sqn])
                for k in range(2):
                    P.pe(lambda e, k=k, ps=ps, sq=sq, n=n: e.matmul(ps[:, 0:n], onesb[:], sq[:, k, 0:n], start=(k == 0), stop=(k == 1)), reads=[sqn, "onesb"], writes=[pn])
                P.act(lambda e, rs=rs, ps=ps, n=n: e.activation(out=rs[:, 0:n], in_=ps[:, 0:n], func=AF.Sqrt, bias=epsc[:], scale=1.0 / 256), reads=[pn, "epsc"], writes=[rsn])
                P.dve(lambda e, rs=rs, n=n: e.reciprocal(out=rs[:, 0:n], in_=rs[:, 0:n]), reads=[rsn], writes=[rsn])
                for k in range(2):
                    P.dve(lambda e, k=k, q=q, rs=rs, s=s, n=n: e.scalar_tensor_tensor(out=ymix[:, k, s:s + n], in0=q[:, k, 0:n], scalar=ng[:, k:k + 1], in1=rs[:, 0:n], op0=ALU.mult, op1=ALU.mult),
                          reads=[qn, rsn, C_], writes=["ymix%d_%d" % (k, ti)])
            P.flush()

    def mix_rwkv(self, l):
        P = self.P
        dr = self.dram
        ymix, ident = self.ymix, self.ident
        base = 1028
        scr = dr["rw_scr"]
        NKK, WW, KP, BB, RR, VV, GG = range(7)
        with contextlib.ExitStack() as st:
            t_ = lambda nm, shp, dt=F32: P.sb("rw" + nm, shp, dt, stack=st)
            C_ = "rwc"
            chunks = [("r0", base, 128), ("r1", base + 128, 128), ("k0", base + 256, 128), ("k1", base + 384, 128),
                      ("v0", base + 512, 128), ("v1", base + 640, 128), ("wa", base + 768, 128), ("gd", base + 896, 128)]
            U = self.inproj(l, chunks, st, dtype=BF16, pad=1)
            shf = t_("shf", [128, 8, 1 + NSEQ])
            for ci, (name, c0, ncol) in enumerate(chunks):
                f0 = c0 - base
                R = self.ures(name)
                P.dve(lambda e, ci=ci, t=U[name]: e.tensor_copy(out=shf[:, ci, 0:1], in_=t[:, TP:TP + 1]), reads=R, writes=["shf%d" % ci])
                P.dve(lambda e, ci=ci, t=U[name]: e.tensor_copy(out=shf[:, ci, 1:1 + NSEQ], in_=t[:, 1 + TP:1 + T].rearrange("p (s t) -> p s t", t=TS)[:, :, TS - 1]),
                      reads=R, writes=["shf%d" % ci])
                P.dma(lambda e, f0=f0, ci=ci: e.dma_start(out=dr["shift_p"][l:l + 1, f0:f0 + 128].rearrange("t f -> f t"), in_=shf[:, ci, 0:1]), reads=["shf%d" % ci], group="sto")
                P.dma(lambda e, f0=f0, ci=ci: e.dma_start(out=dr["shift_s"][l, :, f0:f0 + 128].rearrange("s f -> f s"), in_=shf[:, ci, 1:1 + NSEQ]), reads=["shf%d" % ci], group="sto")
            mu = t_("mu", [128, 8]); w0n = t_("w0n", [128, 2]); a0 = t_("a0", [128, 2]); kkc = t_("kkc", [128, 2]); kac = t_("kac", [128, 2]); ka1 = t_("ka1", [128, 2])
            rkc = t_("rkc", [128, 2]); lng = t_("lng", [128, 2]); lnb = t_("lnb", [128, 2]); mh = t_("mh", [128, 1]); e5 = t_("e5", [128, 1]); one1 = t_("one1", [128, 1])
            WAf = t_("WAf", [128, 256]); WAb = t_("WAb", [128, 256], BF16); G2f = t_("G2f", [128, 256]); G2b = t_("G2b", [128, 256], BF16)
            bdo = t_("bdo", [128, 128]); I2 = t_("I2", [128, 64]); Esel = t_("Esel", [64, 2, 128]); Ehp = t_("Ehp", [128, 2, 64]); onec = t_("onec", [128, 1])
            ld = lambda dst, src: P.dma(lambda e: e.dma_start(out=dst, in_=src), writes=[C_])
            ld(mu[:], dr["rwkv_mu"][l].rearrange("(c p) -> p c", p=128))
            for tile_, nm in [(w0n, "rwkv_w0"), (a0, "rwkv_a0"), (kkc, "rwkv_k_k"), (kac, "rwkv_k_a"), (lng, "rwkv_ln_g"), (lnb, "rwkv_ln_b")]:
                ld(tile_[:], dr[nm][l].rearrange("(c p) -> p c", p=128))
            ld(rkc[:], dr["rwkv_r_k"][l].rearrange("(c a) j -> (a j) c", c=2))
            ld(WAf[0:64, :], dr["rwkv_w2"][l]); ld(WAf[64:128, :], dr["rwkv_a2"][l]); ld(G2f[:], dr["rwkv_g2"][l])
            ld(bdo[:], dr["c_bdones"]); ld(I2[:], dr["c_I2"]); ld(Esel[:], dr["c_Esel"]); ld(Ehp[:], dr["c_Ehp"])
            V = lambda fn: P.dve(fn, reads=[C_], writes=[C_])
            V(lambda e: e.tensor_copy(out=WAb[:], in_=WAf[:]))
            V(lambda e: e.tensor_copy(out=G2b[:], in_=G2f[:]))
            V(lambda e: e.tensor_scalar(out=w0n[:], in0=w0n[:], scalar1=-1.0, scalar2=None, op0=ALU.mult))
            V(lambda e: e.tensor_scalar(out=ka1[:], in0=kac[:], scalar1=-1.0, scalar2=1.0, op0=ALU.mult, op1=ALU.add))
            V(lambda e: e.memset(mh[:], -0.5)); V(lambda e: e.memset(e5[:], 64e-5)); V(lambda e: e.memset(one1[:], 1.0)); V(lambda e: e.memset(onec[:], 1.0))
            with contextlib.ExitStack() as s2:
                w_ = lambda nm, dt=F32: P.sb("rw2" + nm, [128, 256], dt, stack=s2)
                dd = w_("dd"); xwa = w_("xwa"); xgd = w_("xgd"); wab = w_("wab", BF16); sgb = w_("sgb", BF16)
                xk = w_("xk"); kk = w_("kk"); sq = w_("sq"); nr = w_("nr"); ta = w_("ta"); tt = w_("tt")
                outs = [[w_("o%d_%d" % (i, j)) for j in range(7)] for i in range(2)]
                prevS = P.sb("rw2prevS", [128, 8, TSAMP], F32, stack=s2)
                for ci, (name, c0, ncol) in enumerate(chunks):
                    f0 = c0 - base
                    P.dma(lambda e, ci=ci, f0=f0: e.dma_start(out=prevS[:, ci, :].rearrange("p (s t) -> p s t", t=TS)[:, :, 0], in_=dr["st_shift"][l][:, f0:f0 + 128].rearrange("s f -> f s")),
                          writes=["rwprevS%d" % ci])
                    P.dve(lambda e, ci=ci, t=U[name]: e.tensor_copy(out=prevS[:, ci, :].rearrange("p (s t) -> p s t", t=TS)[:, :, 1:TS], in_=t[:, 1 + TP:1 + T].rearrange("p (s t) -> p s t", t=TS)[:, :, 0:TS - 1]),
                          reads=self.ures(name), writes=["rwprevS%d" % ci])
                cidx = {nm: i for i, (nm, _, _) in enumerate(chunks)}

                def xs_of(name, ti, s, n, dst, rn):
                    ci = cidx[name]
                    u = U[name]
                    R = self.ures(name) + ["u_%s_pad" % name]
                    if s < TP:
                        P.dve(lambda e, u=u, s=s, n=n: e.tensor_tensor(out=dd[:, 0:n], in0=u[:, s:s + n], in1=u[:, 1 + s:1 + s + n], op=ALU.subtract), reads=R, writes=["rwdd"])
                    else:
                        P.dve(lambda e, u=u, s=s, n=n, ci=ci: e.tensor_tensor(out=dd[:, 0:n], in0=prevS[:, ci, :], in1=u[:, 1 + s:1 + s + n], op=ALU.subtract),
                              reads=R + ["rwprevS%d" % ci], writes=["rwdd"])
                    P.dve(lambda e, u=u, s=s, n=n, ci=ci, dst=dst: e.scalar_tensor_tensor(out=dst[:, 0:n], in0=dd[:, 0:n], scalar=mu[:, ci:ci + 1], in1=u[:, 1 + s:1 + s + n], op0=ALU.mult, op1=ALU.add),
                          reads=R + ["rwdd", C_], writes=[rn])

                for ti, (s, n) in enumerate([(i * 256, 256) for i in range(8)] + [(TP, TSAMP)]):
                    xs_of("wa", ti, s, n, xwa, "rwxwa")
                    xs_of("gd", ti, s, n, xgd, "rwxgd")
                    P.act(lambda e, n=n: e.activation(out=wab[0:64, 0:n], in_=xwa[0:64, 0:n], func=AF.Tanh), reads=["rwxwa"], writes=["rwwab"])
                    P.act(lambda e, n=n: e.activation(out=wab[64:128, 0:n], in_=xwa[64:128, 0:n], func=AF.Copy), reads=["rwxwa"], writes=["rwwab"])
                    P.act(lambda e, n=n: e.activation(out=sgb[:, 0:n], in_=xgd[:, 0:n], func=AF.Sigmoid), reads=["rwxgd"], writes=["rwsgb"])
                    for c in range(2):
                        O = outs[c]
                        on = ["rwo%d_%d" % (c, j) for j in range(7)]
                        pw, pa, pg, pq = self.psum[0], self.psum[1], self.psum[2], self.psum[3]
                        cs = slice(128 * c, 128 * c + 128)
                        P.pe(lambda e, n=n, cs=cs: e.matmul(pw[:, 0:n], WAb[0:64, cs], wab[0:64, 0:n], start=True, stop=True), reads=["rwwab", C_], writes=["ps0"])
                        P.pe(lambda e, n=n, cs=cs: e.matmul(pa[:, 0:n], WAb[64:128, cs], wab[64:128, 0:n], start=True, stop=True), reads=["rwwab", C_], writes=["ps1"])
                        P.pe(lambda e, n=n, cs=cs: e.matmul(pg[:, 0:n], G2b[:, cs], sgb[:, 0:n], start=True, stop=True), reads=["rwsgb", C_], writes=["ps2"])
                        P.act(lambda e, n=n, c=c: e.activation(out=tt[:, 0:n], in_=pw[:, 0:n], func=AF.Exp, bias=w0n[:, c:c + 1], scale=-1.0), reads=["ps0", C_], writes=["rwtt"])
                        P.act(lambda e, n=n: e.activation(out=tt[:, 0:n], in_=tt[:, 0:n], func=AF.Ln, bias=one1[:], scale=1.0), reads=["rwtt", C_], writes=["rwtt"])
                        P.act(lambda e, n=n: e.activation(out=tt[:, 0:n], in_=tt[:, 0:n], func=AF.Exp, bias=mh[:], scale=-1.0), reads=["rwtt", C_], writes=["rwtt"])
                        P.act(lambda e, n=n, O=O: e.activation(out=O[WW][:, 0:n], in_=tt[:, 0:n], func=AF.Exp, scale=-1.0), reads=["rwtt"], writes=[on[WW]])
                        P.act(lambda e, n=n, c=c: e.activation(out=ta[:, 0:n], in_=pa[:, 0:n], func=AF.Sigmoid, bias=a0[:, c:c + 1], scale=1.0), reads=["ps1", C_], writes=["rwta"])
                        P.act(lambda e, n=n, O=O: e.activation(out=O[GG][:, 0:n], in_=pg[:, 0:n], func=AF.Copy), reads=["ps2"], writes=[on[GG]])
                        xs_of("k%d" % c, ti, s, n, xk, "rwxk")
                        xs_of("r%d" % c, ti, s, n, O[RR], on[RR])
                        xs_of("v%d" % c, ti, s, n, O[VV], on[VV])
                        P.dve(lambda e, n=n, c=c: e.tensor_scalar(out=kk[:, 0:n], in0=xk[:, 0:n], scalar1=kkc[:, c:c + 1], scalar2=None, op0=ALU.mult), reads=["rwxk", C_], writes=["rwkk"])
                        P.act(lambda e, n=n: e.activation(out=sq[:, 0:n], in_=kk[:, 0:n], func=AF.Square), reads=["rwkk"], writes=["rwsq"])
                        P.pe(lambda e, n=n: e.matmul(pq[:, 0:n], bdo[:], sq[:, 0:n], start=True, stop=True), reads=["rwsq", C_], writes=["ps3"])
                        P.act(lambda e, n=n: e.activation(out=nr[:, 0:n], in_=pq[:, 0:n], func=AF.Sqrt), reads=["ps3"], writes=["rwnr"])
                        P.dve(lambda e, n=n: e.tensor_scalar(out=nr[:, 0:n], in0=nr[:, 0:n], scalar1=1e-12, scalar2=None, op0=ALU.max), reads=["rwnr"], writes=["rwnr"])
                        P.dve(lambda e, n=n: e.reciprocal(out=nr[:, 0:n], in_=nr[:, 0:n]), reads=["rwnr"], writes=["rwnr"])
                        P.dve(lambda e, n=n, O=O: e.scalar_tensor_tensor(out=O[NKK][:, 0:n], in0=kk[:, 0:n], scalar=-1.0, in1=nr[:, 0:n], op0=ALU.mult, op1=ALU.mult), reads=["rwkk", "rwnr"], writes=[on[NKK]])
                        P.dve(lambda e, n=n, O=O: e.scalar_tensor_tensor(out=O[BB][:, 0:n], in0=O[NKK][:, 0:n], scalar=-1.0, in1=ta[:, 0:n], op0=ALU.mult, op1=ALU.mult), reads=[on[NKK], "rwta"], writes=[on[BB]])
                        P.dve(lambda e, n=n, c=c: e.tensor_scalar(out=ta[:, 0:n], in0=ta[:, 0:n], scalar1=kac[:, c:c + 1], scalar2=ka1[:, c:c + 1], op0=ALU.mult, op1=ALU.add), reads=["rwta", C_, on[BB]], writes=["rwta"])
                        P.dve(lambda e, n=n, O=O: e.tensor_tensor(out=O[KP][:, 0:n], in0=xk[:, 0:n], in1=ta[:, 0:n], op=ALU.mult), reads=["rwxk", "rwta"], writes=[on[KP]])
                        for j in range(7):
                            P.dma(lambda e, c=c, j=j, s=s, n=n, O=O: e.dma_start(out=scr[c, j, :, s:s + n], in_=O[j][:, 0:n]), reads=[on[j]], writes=["rwscr"], group="rwscr")
                P.flush()
            NBK = 256
            Bk = [t_("Bk%d" % c, [128, 7, NBK]) for c in range(2)]
            STp = [[t_("ST%d_%d" % (c, q), [128, 64]) for q in range(2)] for c in range(2)]
            par = [0, 0]
            STk = [t_("STk%d" % c, [128, 64], BF16) for c in range(2)]
            STr = [t_("STr%d" % c, [128, 64], BF16) for c in range(2)]
            bdob = t_("bdob", [128, 128], BF16); hps = t_("hps", [128, 2]); hpsb = t_("hpsb", [128, 2], BF16)
            P.dma(lambda e: e.dma_start(out=hps[:], in_=dr["c_hpsel"]), writes=["rwhps"])
            P.dve(lambda e: e.tensor_copy(out=hpsb[:], in_=hps[:]), reads=["rwhps"], writes=["rwhps"])
            P.dve(lambda e: e.tensor_copy(out=bdob[:], in_=bdo[:]), reads=[C_], writes=["rwhps"])
            T1 = [t_("T1%d" % c, [128, 64]) for c in range(2)]
            T2 = [t_("T2%d" % c, [128, 64]) for c in range(2)]
            Vex = [t_("Vex%d" % c, [128, 8, 64], BF16) for c in range(2)]
            Zl = [t_("Zl%d" % i, [64, 2, 128]) for i in range(2)]
            Sout = [t_("Sout%d" % i, [64, 2, 64]) for i in range(2)]
            Ysb = t_("Ysb", [64, 2 * NBK]); ysc = t_("ysc", [128, NBK]); ycen = t_("ycen", [128, NBK]); ysq = t_("ysq", [128, NBK]); yrs = t_("yrs", [128, NBK]); ypr = t_("ypr", [128, NBK])
            pSA = self.psum[0]
            pVb = [self.psum[1], self.psum[2]]
            pY = [self.psum[3], self.psum[4]]
            pM = self.psum[5]; pV2 = self.psum[6]; pX = self.psum[7]
            for c in range(2):
                P.dve(lambda e, c=c: e.memset(STp[c][0][:], 0.0), writes=["rwST%d_0" % c])
            for i in range(2):
                P.dve(lambda e, i=i: e.memset(Zl[i][:], 0.0), writes=["rwZl%d" % i])
            sa_slot = [0]

            def load_block(c, t0, n):
                P.dma(lambda e, c=c, t0=t0, n=n: e.dma_start(out=Bk[c][:, :, 0:n], in_=scr[c, :, :, t0:t0 + n].rearrange("a p t -> p a t")),
                      reads=["rwscr"], writes=["rwBk%d" % c])

            def vb_group(c, j0, ng):
                P.pool(lambda e, c=c, j0=j0, ng=ng: e.tensor_tensor(out=Vex[c][:, 0:ng, :], in0=I2[:].unsqueeze(1).to_broadcast([128, ng, 64]),
                                                                   in1=Bk[c][:, VV, j0:j0 + ng].unsqueeze(2).to_broadcast([128, ng, 64]), op=ALU.mult),
                       reads=["rwBk%d" % c, C_], writes=["rwVex%d" % c])
                P.pe(lambda e, c=c, ng=ng: e.matmul(pVb[c][:, 0:64 * ng], bdob[:], Vex[c][:, 0:ng, :].rearrange("p a b -> p (a b)"), start=True, stop=True),
                     reads=["rwVex%d" % c, "rwhps"], writes=["ps%d" % (1 + c)])

            pend = []

            def flush_y():
                for (c, j, ycol, q) in pend:
                    P.act(lambda e, c=c, j=j, q=q: e.activation(out=STr[c][:], in_=STp[c][q][:], func=AF.Copy, scale=Bk[c][:, RR, j:j + 1]),
                          reads=["rwST%d_%d" % (c, q), "rwBk%d" % c], writes=["rwSTr%d" % c])
                for (c, j, ycol, q) in pend:
                    P.pe(lambda e, c=c, ycol=ycol: e.matmul(pY[c][0:64, 2 * ycol:2 * ycol + 2], STr[c][:], hpsb[:], start=True, stop=True),
                         reads=["rwSTr%d" % c, "rwhps"], writes=["ps%d" % (3 + c)])
                del pend[:]

            def step2(j, ycol):
                vcol = (j % 8) * 64
                slots = []
                for c in range(2):
                    slot = sa_slot[0] % 8; sa_slot[0] += 1
                    slots.append(slot)
                qo = [par[0], par[1]]
                qn = [1 - par[0], 1 - par[1]]
                for c in range(2):
                    P.dve(lambda e, c=c, j=j, q=qo[c]: e.tensor_scalar(out=STk[c][:], in0=STp[c][q][:], scalar1=Bk[c][:, NKK, j:j + 1], scalar2=None, op0=ALU.mult),
                          reads=["rwST%d_%d" % (c, qo[c]), "rwBk%d" % c], writes=["rwSTk%d" % c])
                for c in range(2):
                    P.pool(lambda e, c=c, j=j, q=qo[c]: e.tensor_scalar(out=T1[c][:], in0=STp[c][q][:], scalar1=Bk[c][:, WW, j:j + 1], scalar2=None, op0=ALU.mult),
                           reads=["rwST%d_%d" % (c, qo[c]), "rwBk%d" % c], writes=["rwT1%d" % c])
                for c in range(2):
                    P.pe(lambda e, c=c, slot=slots[c]: e.matmul(pSA[:, 64 * slot:64 * slot + 64], bdob[:], STk[c][:], start=True, stop=True),
                         reads=["rwSTk%d" % c, "rwhps"], writes=["rwsa%d" % slots[c]])
                flush_y()
                for c in range(2):
                    P.dve(lambda e, c=c, j=j, vcol=vcol: e.scalar_tensor_tensor(out=T2[c][:], in0=pVb[c][:, vcol:vcol + 64], scalar=Bk[c][:, KP, j:j + 1], in1=T1[c][:], op0=ALU.mult, op1=ALU.add),
                          reads=["ps%d" % (1 + c), "rwBk%d" % c, "rwT1%d" % c], writes=["rwT2%d" % c])
                for c in range(2):
                    P.dve(lambda e, c=c, j=j, slot=slots[c], q=qn[c]: e.scalar_tensor_tensor(out=STp[c][q][:], in0=pSA[:, 64 * slot:64 * slot + 64], scalar=Bk[c][:, BB, j:j + 1], in1=T2[c][:],
                                                                                       op0=ALU.mult, op1=ALU.add),
                          reads=["rwsa%d" % slots[c], "rwBk%d" % c, "rwT2%d" % c], writes=["rwST%d_%d" % (c, qn[c])])
                for c in range(2):
                    par[c] = qn[c]
                    pend.append((c, j, ycol, qn[c]))

            def post(c, t0, n):
                bn = "rwBk%d" % c
                P.act(lambda e, c=c: e.activation(out=Ysb[:], in_=pY[c][0:64, :], func=AF.Copy), reads=["ps%d" % (3 + c)], writes=["rwYsb"])
                for hp in range(2):
                    P.pe(lambda e, hp=hp, n=n: e.matmul(pX[:, 0:n], Esel[:, hp, :], Ysb[:, 0:2 * n].rearrange("p (t h) -> p t h", h=2)[:, :, hp], start=(hp == 0), stop=(hp == 1)), reads=["rwYsb", C_], writes=["ps7"])
                P.act(lambda e, n=n: e.activation(out=ysc[:, 0:n], in_=pX[:, 0:n], func=AF.Copy), reads=["ps7"], writes=["rwysc"])
                P.pe(lambda e, n=n: e.matmul(pM[:, 0:n], bdo[:], ysc[:, 0:n], start=True, stop=True), reads=["rwysc", C_], writes=["ps5"])
                P.dve(lambda e, n=n: e.scalar_tensor_tensor(out=ycen[:, 0:n], in0=pM[:, 0:n], scalar=-1.0 / 64, in1=ysc[:, 0:n], op0=ALU.mult, op1=ALU.add), reads=["ps5", "rwysc"], writes=["rwycen"])
                P.act(lambda e, n=n: e.activation(out=ysq[:, 0:n], in_=ycen[:, 0:n], func=AF.Square), reads=["rwycen"], writes=["rwysq"])
                P.pe(lambda e, n=n: e.matmul(pV2[:, 0:n], bdo[:], ysq[:, 0:n], start=True, stop=True), reads=["rwysq", C_], writes=["ps6"])
                P.act(lambda e, n=n: e.activation(out=yrs[:, 0:n], in_=pV2[:, 0:n], func=AF.Sqrt, bias=e5[:], scale=1.0 / 64), reads=["ps6", C_], writes=["rwyrs"])
                P.dve(lambda e, n=n: e.reciprocal(out=yrs[:, 0:n], in_=yrs[:, 0:n]), reads=["rwyrs"], writes=["rwyrs"])
                P.dve(lambda e, n=n: e.tensor_tensor(out=ycen[:, 0:n], in0=ycen[:, 0:n], in1=yrs[:, 0:n], op=ALU.mult), reads=["rwycen", "rwyrs"], writes=["rwycen"])
                P.dve(lambda e, n=n, c=c: e.tensor_scalar(out=ycen[:, 0:n], in0=ycen[:, 0:n], scalar1=lng[:, c:c + 1], scalar2=lnb[:, c:c + 1], op0=ALU.mult, op1=ALU.add), reads=["rwycen", C_], writes=["rwycen"])
                P.dve(lambda e, n=n, c=c: e.scalar_tensor_tensor(out=ypr[:, 0:n], in0=Bk[c][:, RR, 0:n], scalar=rkc[:, c:c + 1], in1=Bk[c][:, KP, 0:n], op0=ALU.mult, op1=ALU.mult), reads=[bn, C_], writes=["rwypr"])
                P.pe(lambda e, n=n: e.matmul(pM[:, 0:n], bdo[:], ypr[:, 0:n], start=True, stop=True), reads=["rwypr", C_], writes=["ps5"])
                P.dve(lambda e, n=n, c=c: e.tensor_tensor(out=ypr[:, 0:n], in0=pM[:, 0:n], in1=Bk[c][:, VV, 0:n], op=ALU.mult), reads=["ps5", bn], writes=["rwypr"])
                P.dve(lambda e, n=n: e.tensor_tensor(out=ycen[:, 0:n], in0=ycen[:, 0:n], in1=ypr[:, 0:n], op=ALU.add), reads=["rwycen", "rwypr"], writes=["rwycen"])
                ti0 = min(t0 // 512, 4)
                P.dve(lambda e, n=n, c=c, t0=t0: e.tensor_tensor(out=ymix[:, 2 + c, t0:t0 + n], in0=ycen[:, 0:n], in1=Bk[c][:, GG, 0:n], op=ALU.mult), reads=["rwycen", bn],
                      writes=["ymix%d_%d" % (2 + c, ti0)])

            def store_state(c, dst_fn, idx):
                so = Sout[idx % 2]; son = "rwSout%d" % (idx % 2)
                for hp in range(2):
                    P.pe(lambda e, c=c, hp=hp, q=par[c]: e.matmul(pX[0:64, 256 + 64 * hp:256 + 64 * hp + 64], STp[c][q][:], Ehp[:, hp, :], start=True, stop=True), reads=["rwST%d_%d" % (c, par[c]), C_], writes=["ps7"])
                P.act(lambda e, so=so: e.activation(out=so[:].rearrange("p a b -> p (a b)"), in_=pX[0:64, 256:384], func=AF.Copy), reads=["ps7"], writes=[son])
                for hp in range(2):
                    P.dma(lambda e, so=so, hp=hp, c=c: e.dma_start(out=dst_fn(2 * c + hp), in_=so[:, hp, :]), reads=[son])

            for b0 in range(0, TP, NBK):
                for c in range(2):
                    load_block(c, b0, NBK)
                for j in range(NBK):
                    if j % 8 == 0:
                        for c in range(2):
                            vb_group(c, j, 8)
                    step2(j, j)
                flush_y()
                for c in range(2):
                    post(c, b0, NBK)
            for c in range(2):
                store_state(c, lambda h: dr["rwkv_p"][l, h], c)
            for c in range(2):
                load_block(c, TP, TSAMP)
            for s in range(NSEQ):
                for c in range(2):
                    z = Zl[c]; zn = "rwZl%d" % c
                    for hp in range(2):
                        P.dma(lambda e, z=z, hp=hp, s=s, c=c: e.dma_start(out=z[:, hp, 64 * hp:64 * hp + 64], in_=dr["st_rwkv"][l, s, 2 * c + hp]), writes=[zn])
                    for hp in range(2):
                        P.pe(lambda e, z=z, hp=hp: e.matmul(pX[:, 384:448], z[:, hp, :], ident[0:64, 0:64], start=(hp == 0), stop=(hp == 1)), reads=[zn, "ident"], writes=["ps7"])
                    P.act(lambda e, c=c, q=par[c]: e.activation(out=STp[c][q][:], in_=pX[:, 384:448], func=AF.Copy), reads=["ps7"], writes=["rwST%d_%d" % (c, par[c])])
                for t in range(TS):
                    j = TS * s + t
                    if j % 8 == 0:
                        for c in range(2):
                            vb_group(c, j, 8)
                    step2(j, j)
                flush_y()
                for c in range(2):
                    store_state(c, (lambda s: (lambda h: dr["rwkv_s"][l, s, h]))(s), s * 2 + c)
            for c in range(2):
                post(c, TP, TSAMP)
            P.flush()

    def outproj(self, l):
        P = self.P
        xT, ymix = self.xT, self.ymix
        w_v = self.dram["w_out"][l].rearrange("(k p) c -> p k c", p=128)
        with contextlib.ExitStack() as st:
            stg = [P.sb("opstg%d" % i, [128, 8, 128], F32, stack=st) for i in range(2)]
            wbs = [P.sb("opwb%d" % i, [128, 8, 128], BF16, stack=st) for i in range(2)]
            for m in range(8):
                i = m % 2
                b = stg[i]; w = wbs[i]
                P.dma(lambda e, b=b, m=m: e.dma_start(out=b[:], in_=w_v[:, :, m * 128:(m + 1) * 128]), writes=["opstg%d" % i])
                P.pool(lambda e, b=b, w=w: e.tensor_copy(out=w[:], in_=b[:]), reads=["opstg%d" % i], writes=["opwb%d" % i])
                for ti, (s, n) in enumerate(TILES):
                    c = m * 5 + ti
                    ps = self.psum[c % 2]; pn = "ps%d" % (c % 2)
                    for k in range(8):
                        P.pe(lambda e, ps=ps, w=w, k=k, s=s, n=n: e.matmul(ps[:, 0:n], w[:, k, :], ymix[:, k, s:s + n], start=(k == 0), stop=(k == 7)),
                             reads=["opwb%d" % i, "ymix%d_%d" % (k, ti)], writes=[pn])
                    P.dve(lambda e, ps=ps, m=m, s=s, n=n: e.tensor_tensor(out=xT[:, m, s:s + n], in0=ps[:, 0:n], in1=xT[:, m, s:s + n], op=ALU.add),
                          reads=[pn, "xT%d_%d" % (m, ti)], writes=["xT%d_%d" % (m, ti)])
            P.flush()

    def final(self):
        P = self.P
        xT, onesb, gains, epsc, ident = self.xT, self.onesb, self.gains, self.epsc, self.ident
        with contextlib.ExitStack() as st:
            sq = P.sb("fsq", [128, 8, 512], BF16, stack=st)
            rstd = P.sb("frstd", [128, 512], F32, stack=st)
            xo = P.sb("fxo", [128, 8, 512], F32, stack=st)
            ytok = [P.sb("fytok%d" % i, [128, D], F32, stack=st) for i in range(2)]
            nblk = 0
            for ti, (s, n) in enumerate(TILES):
                ps = self.psum[6]; pn = "ps6"
                for k in range(8):
                    P.act(lambda e, k=k, s=s, n=n: e.activation(out=sq[:, k, 0:n], in_=xT[:, k, s:s + n], func=AF.Square),
                          reads=["xT%d_%d" % (k, ti)], writes=["fsq_%d" % k])
                for k in range(8):
                    P.pe(lambda e, k=k, n=n: e.matmul(ps[:, 0:n], onesb[:], sq[:, k, 0:n], start=(k == 0), stop=(k == 7)),
                         reads=["fsq_%d" % k, "onesb"], writes=[pn])
                P.act(lambda e, n=n: e.activation(out=rstd[:, 0:n], in_=ps[:, 0:n], func=AF.Sqrt, bias=epsc[:], scale=1.0 / D), reads=[pn, "epsc"], writes=["frstd"])
                P.dve(lambda e, n=n: e.reciprocal(out=rstd[:, 0:n], in_=rstd[:, 0:n]), reads=["frstd"], writes=["frstd"])
                for k in range(8):
                    P.dve(lambda e, k=k, s=s, n=n: e.scalar_tensor_tensor(out=xo[:, k, 0:n], in0=xT[:, k, s:s + n], scalar=gains[:, 6, k:k + 1], in1=rstd[:, 0:n],
                                                                        op0=ALU.mult, op1=ALU.mult),
                          reads=["xT%d_%d" % (k, ti), "frstd", "gains"], writes=["fxo_%d" % k])
                for b0 in range(0, n, 128):
                    nb = min(128, n - b0)
                    yt = ytok[nblk % 2]; ytn = "fytok%d" % (nblk % 2)
                    nblk += 1
                    for half in range(2):
                        pt = self.psum[half * 2 + (nblk % 2)]; ptn = "ps%d" % (half * 2 + (nblk % 2))
                        for kk in range(4):
                            k = half * 4 + kk
                            P.pe(lambda e, pt=pt, k=k, kk=kk, b0=b0, nb=nb: e.transpose(pt[0:nb, kk * 128:(kk + 1) * 128], xo[:, k, b0:b0 + nb], ident[:]),
                                 reads=["fxo_%d" % k, "ident"], writes=[ptn])
                        if half == 0:
                            P.act(lambda e, pt=pt, yt=yt, nb=nb: e.activation(out=yt[0:nb, 0:512], in_=pt[0:nb, :], func=AF.Copy), reads=[ptn], writes=[ytn])
                        else:
                            P.dve(lambda e, pt=pt, yt=yt, nb=nb: e.tensor_copy(out=yt[0:nb, 512:1024], in_=pt[0:nb, :]), reads=[ptn], writes=[ytn])
                    if s < TP:
                        dst = self.dram["y_p"][s + b0:s + b0 + nb, :]
                    else:
                        dst = self.dram["y_s"][b0:b0 + nb, :]
                    P.dma(lambda e, dst=dst, yt=yt, nb=nb: e.dma_start(out=dst, in_=yt[0:nb, :]), reads=[ytn])
            P.flush()

    def layer(self, l):
        P = self.P
        self.ffn(l, 1, self.dram["ffn1_in"][l], self.dram["ffn1_out"][l])
        with contextlib.ExitStack() as st:
            self.rmsnorm(l * 3 + 1, st)
            P.flush()
        with contextlib.ExitStack() as st:
            self.ymix = P.sb("ymix", [128, 8, T], BF16, stack=st)
            ymix = self.ymix
            done = set()
            if self.stage >= 3:
                self.mix_pool(l); done |= {6, 7}
            if self.stage >= 4:
                self.mix_s5(l); done |= {4, 5}
            if self.stage >= 5:
                self.mix_ssd(l); done |= {0, 1}
            if self.stage >= 6:
                self.mix_rwkv(l); done |= {2, 3}
            else:
                self.mix_rwkv_stub(l)
            for k in range(8):
                if k not in done:
                    P.pool(lambda e, k=k: e.memset(ymix[:, k, :], 0.0), writes=["ymix%d_%d" % (k, ti) for ti in range(5)])
            self.outproj(l)
        self.ffn(l, 2, self.dram["ffn2_in"][l], self.dram["ffn2_out"][l])

    def build(self):
        self.declare()
        self.setup()
        self.load_x()
        for l in range(DEPTH):
            self.layer(l)
        self.final()
        self.P.finish()


def build_nc(stage):
    nc = bass.Bass("TRN2", target_bir_lowering=False)
    with contextlib.ExitStack() as stack:
        stack.enter_context(nc.allow_non_contiguous_dma(reason="small strided parameter/state transfers"))
        b = Builder(nc, stack, stage)
        b.build()
    return nc


IN_NAMES = ["norm_ffn1", "ffn1_in", "ffn1_out", "norm_mix", "w_in", "ssd_conv_w", "ssd_conv_b", "ssd_dt_bias",
            "ssd_a_log", "ssd_d", "ssd_norm", "rwkv_mu", "rwkv_w0", "rwkv_w2", "rwkv_a0", "rwkv_a2", "rwkv_g2",
            "rwkv_k_k", "rwkv_k_a", "rwkv_r_k", "rwkv_ln_g", "rwkv_ln_b", "s5_lam_re", "s5_lam_im", "s5_log_step",
            "s5_b_re", "s5_b_im", "s5_c_re", "s5_c_im", "s5_d", "s5_glu_w", "s5_glu_b", "pool_w", "pool_scale",
            "w_out", "norm_ffn2", "ffn2_in", "ffn2_out", "norm_final"]
STATE_MAP = [("state_ssd_conv", "st_conv"), ("state_ssd", "st_ssd"), ("state_rwkv_shift", "st_shift"),
             ("state_rwkv", "st_rwkv"), ("state_s5_re", "st_s5re"), ("state_s5_im", "st_s5im"), ("state_pool", "st_pool")]
OUT_ORDER = [("y_p", "y_s"), ("conv_p", "conv_s"), ("ssd_p", "ssd_s"), ("shift_p", "shift_s"), ("rwkv_p", "rwkv_s"),
             ("s5re_p", "s5re_s"), ("s5im_p", "s5im_s"), ("pool_p", "pool_s")]


def host_consts():
    c = {}
    wins = [2, 4, 8, 16]
    cinv = np.zeros((128, 2, 15), np.float32)
    for ch in range(2):
        for p in range(128):
            w = wins[2 * ch + p // 64]
            for t in range(15):
                cinv[p, ch, t] = 1.0 / min(t + 1, w)
    c["c_pool_cinv"] = cinv
    gm = np.zeros((128, 8), np.float32)
    for p in range(128):
        gm[p, p // 16] = 1.0
    c["c_gm"] = gm
    sw = np.zeros((128, 128), np.float32)
    for k in range(128):
        sw[k, (k + 64) % 128] = 1.0
    c["c_swap"] = sw
    cm = np.ones((4, T), np.float32)
    cm[:, 0:TP:128] = 0.0
    cm[:, TP:T:TS] = 0.0
    c["c_cmask"] = cm
    ii = np.arange(128)
    c["c_mask128"] = (ii[None, :] >= ii[:, None]).astype(np.float32)
    c["c_U128"] = (ii[:, None] > ii[None, :]).astype(np.float32)
    c["c_L128"] = (ii[:, None] <= ii[None, :]).astype(np.float32)
    el = np.zeros((128, 128), np.float32); el[127, :] = 1.0
    c["c_elast"] = el
    i6 = np.arange(64)
    same = (i6[:, None] // TS) == (i6[None, :] // TS)
    c["c_maskS"] = (same & (i6[None, :] >= i6[:, None])).astype(np.float32)
    c["c_US"] = (same & (i6[:, None] > i6[None, :])).astype(np.float32)
    c["c_LS"] = (same & (i6[:, None] <= i6[None, :])).astype(np.float32)
    c["c_seqm"] = ((i6[:, None] // TS) == np.arange(NSEQ)[None, :]).astype(np.float32)
    sel = np.zeros((4, 2, 128), np.float32)
    for cc in range(2):
        for m in range(128):
            sel[2 * cc + m // 64, cc, m] = 1.0
    c["c_sel"] = sel
    c["c_bdones"] = ((ii[:, None] // 64) == (ii[None, :] // 64)).astype(np.float32)
    I2 = np.zeros((128, 64), np.float32)
    for p in range(128):
        I2[p, p % 64] = 1.0
    c["c_I2"] = I2
    Es = np.zeros((64, 2, 128), np.float32)
    Eh = np.zeros((128, 2, 64), np.float32)
    for i in range(64):
        for hp in range(2):
            Es[i, hp, 64 * hp + i] = 1.0
            Eh[64 * hp + i, hp, i] = 1.0
    c["c_Esel"] = Es
    hp_ = np.zeros((128, 2), np.float32); hp_[:64, 0] = 1.0; hp_[64:, 1] = 1.0
    c["c_hpsel"] = hp_
    c["c_Ehp"] = Eh
    return c


def kernel(stage=6, **inputs):
    f = lambda a: np.ascontiguousarray(np.asarray(a, dtype=np.float32))
    shared = {nm: f(inputs[nm]) for nm in IN_NAMES}
    shared.update(host_consts())
    in_maps = []
    for c in range(NCORES):
        m = dict(shared)
        m["xp"] = f(inputs["x_prompt"][c])
        m["xs"] = f(inputs["x_sample"][c * NSEQ:(c + 1) * NSEQ]).reshape(TSAMP, D)
        for src, dst in STATE_MAP:
            m[dst] = f(inputs[src][:, c * NSEQ:(c + 1) * NSEQ])
        in_maps.append(m)
    nc = build_nc(stage)
    import os
    if os.environ.get("KTRACE"):
        res = run_bass_kernel_spmd(nc, in_maps, core_ids=list(range(NCORES)), trace=True)
        print("EXEC_TIME_NS", res.exec_time_ns)
    else:
        res = run_bass_kernel_spmd(nc, in_maps, core_ids=list(range(NCORES)))
    R = res.results
    global DBG
    DBG = None
    outs = []
    yp = np.stack([R[c]["y_p"] for c in range(NCORES)], 0)
    ys = np.concatenate([R[c]["y_s"].reshape(NSEQ, TS, D) for c in range(NCORES)], 0)
    outs += [yp, ys]
    for pn, sn in OUT_ORDER[1:]:
        p = np.stack([R[c][pn] for c in range(NCORES)], 1)
        s = np.concatenate([R[c][sn] for c in range(NCORES)], 1)
        outs += [p, s]
    return tuple(np.ascontiguousarray(o, dtype=np.float32) for o in outs)
```

```python
import contextlib
import numpy as np
import concourse.bass as bass
import concourse.mybir as mybir
from concourse.bass_utils import run_bass_kernel_spmd

F32 = mybir.dt.float32
BF16 = mybir.dt.bfloat16
AF = mybir.ActivationFunctionType
ALU = mybir.AluOpType
AX = mybir.AxisListType

NCORES = 8
D = 1024
TP = 2048
NSEQ = 16
TS = 4
TSAMP = NSEQ * TS
T = TP + TSAMP
DFF = 2816
NFF = DFF // 128
INP = 2564
DEPTH = 2
TILES = [(0, 512), (512, 512), (1024, 512), (1536, 512), (2048, 64)]


class Op:
    __slots__ = ("idx", "eng", "fn", "deps", "is_dma", "group", "marked", "count", "pos")

    def __init__(self, idx, eng, fn, is_dma, group):
        self.idx = idx
        self.eng = eng
        self.fn = fn
        self.deps = set()
        self.is_dma = is_dma
        self.group = group
        self.marked = False
        self.count = 0
        self.pos = 0


class Prog:
    def __init__(self, nc, stack):
        self.nc = nc
        self.stack = stack
        self.ops = []
        self.last_writer = {}
        self.readers = {}
        self.eng_obj = {"pe": nc.tensor, "act": nc.scalar, "dve": nc.vector,
                        "pool": nc.gpsimd, "sp": nc.sync}
        self.cnt = {}
        self.sems = {}
        self.waited = {}
        self.pos = {}
        self.emitted = 0
        self.barrier_req = None
        self.nwait = 0

    def sb(self, name, shape, dtype=F32, stack=None):
        self.uid = getattr(self, "uid", 0) + 1
        return (stack or self.stack).enter_context(self.nc.sbuf_tensor("%s_%d" % (name, self.uid), list(shape), dtype))

    def ps(self, name, shape, dtype=F32):
        return self.stack.enter_context(self.nc.psum_tensor(name, list(shape), dtype))

    def add(self, eng, fn, reads=(), writes=(), group=None):
        is_dma = group is not None
        op = Op(len(self.ops), eng, fn, is_dma, group)
        for r in reads:
            w = self.last_writer.get(r)
            if w is not None:
                op.deps.add(w)
            self.readers.setdefault(r, []).append(op.idx)
        for wr in writes:
            w = self.last_writer.get(wr)
            if w is not None:
                op.deps.add(w)
            for rd in self.readers.get(wr, ()):
                if rd != op.idx:
                    op.deps.add(rd)
            self.last_writer[wr] = op.idx
            self.readers[wr] = []
        self.ops.append(op)
        return op

    def pe(self, fn, reads=(), writes=()):
        return self.add("pe", fn, reads, writes)

    def act(self, fn, reads=(), writes=()):
        return self.add("act", fn, reads, writes)

    def dve(self, fn, reads=(), writes=()):
        return self.add("dve", fn, reads, writes)

    def pool(self, fn, reads=(), writes=()):
        return self.add("pool", fn, reads, writes)

    def dma(self, fn, reads=(), writes=(), group=None, q="sp"):
        if group is None:
            group = writes[0] if writes else "st:" + reads[0]
        return self.add(q, fn, reads, writes, group=group)

    def _sem(self, key):
        s = self.sems.get(key)
        if s is None:
            s = self.stack.enter_context(self.nc.semaphore("s_%s_%s" % key))
            self.sems[key] = s
        return s

    def flush(self, barrier=True):
        ops = self.ops
        batch = ops[self.emitted:]
        first = self.emitted
        for op in batch:
            p = self.pos.get(op.eng, 0)
            op.pos = p
            self.pos[op.eng] = p + 1
        need = []
        for op in batch:
            lst = []
            for d in op.deps:
                if d < first:
                    continue
                p = ops[d]
                if p.is_dma:
                    lst.append(d)
                elif p.eng != op.eng or op.is_dma:
                    lst.append(d)
                    p.marked = True
                else:
                    if op.eng == "pe":
                        continue
                    if op.pos - p.pos <= 1:
                        lst.append(d)
                        p.marked = True
            need.append(lst)
        if barrier:
            last = {}
            for op in batch:
                if not op.is_dma:
                    last[op.eng] = op
            for op in last.values():
                op.marked = True
        for op in batch:
            if op.is_dma:
                key = ("g", op.group)
                self.cnt[key] = self.cnt.get(key, 0) + 16
                op.count = self.cnt[key]
            elif op.marked:
                key = ("e", op.eng)
                self.cnt[key] = self.cnt.get(key, 0) + 1
                op.count = self.cnt[key]
        seen_eng = set()
        for op, lst in zip(batch, need):
            eng = self.eng_obj[op.eng]
            reqs = {}
            if self.barrier_req is not None and op.eng not in seen_eng:
                reqs.update(self.barrier_req)
            seen_eng.add(op.eng)
            for d in lst:
                p = ops[d]
                key = ("g", p.group) if p.is_dma else ("e", p.eng)
                if p.count > reqs.get(key, 0):
                    reqs[key] = p.count
            for key, val in reqs.items():
                if key == ("e", op.eng) and not op.is_dma and self.barrier_req is not None \
                        and val <= self.barrier_req.get(key, 0):
                    continue
                wk = (op.eng, key)
                if self.waited.get(wk, 0) >= val:
                    continue
                self.waited[wk] = val
                eng.wait_ge(self._sem(key), val)
                self.nwait += 1
                self.ninstr = getattr(self, "ninstr", {})
                self.ninstr[op.eng] = self.ninstr.get(op.eng, 0) + 1
            ins = op.fn(eng)
            self.ninstr = getattr(self, "ninstr", {})
            self.ninstr[op.eng] = self.ninstr.get(op.eng, 0) + 1
            if op.is_dma:
                ins.then_inc(self._sem(("g", op.group)), 16)
            elif op.marked:
                ins.then_inc(self._sem(("e", op.eng)), 1)
            op.fn = None
        self.emitted = len(ops)
        if barrier:
            self.barrier_req = dict(self.cnt)

    def finish(self):
        self.flush(barrier=True)
        for key, val in self.cnt.items():
            if key[0] == "g":
                self.nc.sync.wait_ge(self._sem(key), val)


class Builder:
    def __init__(self, nc, stack, stage):
        self.nc = nc
        self.P = Prog(nc, stack)
        self.stage = stage
        self.stack = stack
        self.dram = {}
        self.wq = 0

    def din(self, name, shape):
        ap = self.nc.dram_tensor(name, list(shape), F32, kind="ExternalInput").ap()
        self.dram[name] = ap
        return ap

    def dout(self, name, shape):
        ap = self.nc.dram_tensor(name, list(shape), F32, kind="ExternalOutput").ap()
        self.dram[name] = ap
        return ap

    def declare(self):
        d = self.din
        d("xp", [TP, D]); d("xs", [TSAMP, D])
        d("st_conv", [DEPTH, NSEQ, 3, 768]); d("st_ssd", [DEPTH, NSEQ, 4, 64, 128])
        d("st_shift", [DEPTH, NSEQ, 1024]); d("st_rwkv", [DEPTH, NSEQ, 4, 64, 64])
        d("st_s5re", [DEPTH, NSEQ, 16, 64]); d("st_s5im", [DEPTH, NSEQ, 16, 64])
        d("st_pool", [DEPTH, NSEQ, 15, 256])
        d("norm_ffn1", [DEPTH, D]); d("ffn1_in", [DEPTH, D, 2 * DFF]); d("ffn1_out", [DEPTH, DFF, D])
        d("norm_mix", [DEPTH, D]); d("w_in", [DEPTH, D, INP])
        d("ssd_conv_w", [DEPTH, 4, 768]); d("ssd_conv_b", [DEPTH, 768]); d("ssd_dt_bias", [DEPTH, 4])
        d("ssd_a_log", [DEPTH, 4]); d("ssd_d", [DEPTH, 4]); d("ssd_norm", [DEPTH, 256])
        d("rwkv_mu", [DEPTH, 1024]); d("rwkv_w0", [DEPTH, 256]); d("rwkv_w2", [DEPTH, 64, 256])
        d("rwkv_a0", [DEPTH, 256]); d("rwkv_a2", [DEPTH, 64, 256]); d("rwkv_g2", [DEPTH, 128, 256])
        d("rwkv_k_k", [DEPTH, 256]); d("rwkv_k_a", [DEPTH, 256]); d("rwkv_r_k", [DEPTH, 4, 64])
        d("rwkv_ln_g", [DEPTH, 256]); d("rwkv_ln_b", [DEPTH, 256])
        d("s5_lam_re", [DEPTH, 16, 64]); d("s5_lam_im", [DEPTH, 16, 64]); d("s5_log_step", [DEPTH, 16])
        d("s5_b_re", [DEPTH, 16, 64, 16]); d("s5_b_im", [DEPTH, 16, 64, 16])
        d("s5_c_re", [DEPTH, 16, 16, 64]); d("s5_c_im", [DEPTH, 16, 16, 64]); d("s5_d", [DEPTH, 256])
        d("s5_glu_w", [DEPTH, 256, 512]); d("s5_glu_b", [DEPTH, 512])
        d("pool_w", [DEPTH, 4, 64, 64]); d("pool_scale", [DEPTH, 256])
        d("w_out", [DEPTH, D, D]); d("norm_ffn2", [DEPTH, D]); d("ffn2_in", [DEPTH, D, 2 * DFF])
        d("ffn2_out", [DEPTH, DFF, D]); d("norm_final", [D])
        d("c_pool_cinv", [128, 2, 15]); d("c_gm", [128, 8]); d("c_swap", [128, 128])
        d("c_cmask", [4, T]); d("c_mask128", [128, 128]); d("c_U128", [128, 128]); d("c_L128", [128, 128]); d("c_elast", [128, 128])
        d("c_hpsel", [128, 2]); d("c_bdones", [128, 128]); d("c_I2", [128, 64]); d("c_Esel", [64, 2, 128]); d("c_Ehp", [128, 2, 64])
        d("c_maskS", [64, 64]); d("c_US", [64, 64]); d("c_LS", [64, 64]); d("c_seqm", [64, NSEQ]); d("c_sel", [4, 2, 128])
        o = self.dout
        o("y_p", [TP, D]); o("y_s", [TSAMP, D])
        o("conv_p", [DEPTH, 3, 768]); o("conv_s", [DEPTH, NSEQ, 3, 768])
        o("ssd_p", [DEPTH, 4, 64, 128]); o("ssd_s", [DEPTH, NSEQ, 4, 64, 128])
        o("shift_p", [DEPTH, 1024]); o("shift_s", [DEPTH, NSEQ, 1024])
        o("rwkv_p", [DEPTH, 4, 64, 64]); o("rwkv_s", [DEPTH, NSEQ, 4, 64, 64])
        o("s5re_p", [DEPTH, 16, 64]); o("s5re_s", [DEPTH, NSEQ, 16, 64])
        o("s5im_p", [DEPTH, 16, 64]); o("s5im_s", [DEPTH, NSEQ, 16, 64])
        o("pool_p", [DEPTH, 15, 256]); o("pool_s", [DEPTH, NSEQ, 15, 256])
        o("rw_scr", [2, 7, 128, T])

    def dump_x(self):
        xT = self.xT
        self.P.dma(lambda e: e.dma_start(out=self.dram["dbg"], in_=xT[:]),
                   reads=["xT%d_%d" % (k, ti) for k in range(8) for ti in range(5)], group="dbg")
        self.P.flush()

    def setup(self):
        P = self.P
        self.xT = P.sb("xT", [128, 8, T], F32)
        self.xn = P.sb("xn", [128, 8, T], BF16)
        self.ident = P.sb("ident", [128, 128], F32)
        self.identb = P.sb("identb", [128, 128], BF16)
        self.onesb = P.sb("onesb", [128, 128], BF16)
        self.gains = P.sb("gains", [128, 7, 8], F32)
        self.epsc = P.sb("epsc", [128, 1], F32)
        self.psum = [P.ps("ps%d" % i, [128, 512], F32) for i in range(8)]
        ident, identb, onesb = self.ident, self.identb, self.onesb
        P.dve(lambda e: e.memset(ident[:], 0.0), writes=["ident"])
        P.pool(lambda e: e.affine_select(out=ident[:], in_=ident[:], pattern=[[-1, 128]],
                                         compare_op=ALU.not_equal, fill=1.0, base=0,
                                         channel_multiplier=1), reads=["ident"], writes=["ident"])
        P.dve(lambda e: e.tensor_copy(out=identb[:], in_=ident[:]), reads=["ident"], writes=["identb"])
        P.dve(lambda e: e.memset(onesb[:], 1.0), writes=["onesb"])
        epsc = self.epsc
        P.dve(lambda e: e.memset(epsc[:], 1e-6), writes=["epsc"])
        gains = self.gains
        names = ["norm_ffn1", "norm_mix", "norm_ffn2"]
        for l in range(DEPTH):
            for i, nm in enumerate(names):
                src = self.dram[nm][l].rearrange("(k p) -> p k", p=128)
                P.dma(lambda e, s=src, w=l * 3 + i: e.dma_start(out=gains[:, w, :], in_=s),
                      writes=["gains"], group="const")
        src = self.dram["norm_final"].rearrange("(k p) -> p k", p=128)
        P.dma(lambda e, s=src: e.dma_start(out=gains[:, 6, :], in_=s), writes=["gains"], group="const")

    def load_x(self):
        P = self.P
        xT, ident = self.xT, self.ident
        with contextlib.ExitStack() as st:
            xtok = [P.sb("xtok%d" % i, [128, D], F32, stack=st) for i in range(3)]
            ntile = TP // 128 + 1
            for ti in range(ntile):
                buf = xtok[ti % 3]
                bn = "xtok%d" % (ti % 3)
                if ti < TP // 128:
                    src = self.dram["xp"][ti * 128:(ti + 1) * 128, :]
                    n = 128
                else:
                    src = self.dram["xs"][:, :]
                    n = TSAMP
                P.dma(lambda e, b=buf, s=src, n=n: e.dma_start(out=b[0:n, :], in_=s), writes=[bn])
                for half in range(2):
                    ps = self.psum[(ti * 2 + half) % 4]
                    pn = "ps%d" % ((ti * 2 + half) % 4)
                    for kk in range(4):
                        k = half * 4 + kk
                        P.pe(lambda e, ps=ps, b=buf, k=k, kk=kk, n=n: e.transpose(
                            ps[:, kk * 128:kk * 128 + n], b[0:n, k * 128:(k + 1) * 128], ident[0:n, 0:n]),
                            reads=[bn, "ident"], writes=[pn])
                    dst = xT[:, half * 4:half * 4 + 4, ti * 128:ti * 128 + n]
                    srcp = ps[:].rearrange("p (a b) -> p a b", a=4)[:, :, 0:n]
                    wr = ["xT%d_%d" % (k, ti // 4) for k in range(half * 4, half * 4 + 4)]
                    if half == 0:
                        P.dve(lambda e, d=dst, s=srcp: e.tensor_copy(out=d, in_=s), reads=[pn], writes=wr)
                    else:
                        P.act(lambda e, d=dst, s=srcp: e.activation(out=d, in_=s, func=AF.Copy), reads=[pn], writes=wr)
            P.flush()

    def rmsnorm(self, which, st):
        P = self.P
        xT, xn, onesb, gains, epsc = self.xT, self.xn, self.onesb, self.gains, self.epsc
        sq = [P.sb("nsq%d" % i, [128, 8, 512], BF16, stack=st) for i in range(2)]
        rstd = [P.sb("nrstd%d" % i, [128, 512], F32, stack=st) for i in range(2)]
        for ti, (s, n) in enumerate(TILES):
            q = sq[ti % 2]; qn = "nsq%d" % (ti % 2)
            r = rstd[ti % 2]; rn = "nrstd%d" % (ti % 2)
            ps = self.psum[6 + ti % 2]; pn = "ps%d" % (6 + ti % 2)
            for k in range(8):
                P.act(lambda e, q=q, k=k, s=s, n=n: e.activation(out=q[:, k, 0:n], in_=xT[:, k, s:s + n], func=AF.Square),
                      reads=["xT%d_%d" % (k, ti)], writes=[qn + "_%d" % k])
            for k in range(8):
                P.pe(lambda e, ps=ps, q=q, k=k, n=n: e.matmul(ps[:, 0:n], onesb[:], q[:, k, 0:n], start=(k == 0), stop=(k == 7)),
                     reads=[qn + "_%d" % k, "onesb"], writes=[pn])
            P.act(lambda e, r=r, ps=ps, n=n: e.activation(out=r[:, 0:n], in_=ps[:, 0:n], func=AF.Sqrt, bias=epsc[:], scale=1.0 / D),
                  reads=[pn, "epsc"], writes=[rn])
            P.dve(lambda e, r=r, n=n: e.reciprocal(out=r[:, 0:n], in_=r[:, 0:n]), reads=[rn], writes=[rn])
            for k in range(8):
                P.dve(lambda e, r=r, k=k, s=s, n=n: e.scalar_tensor_tensor(
                    out=xn[:, k, s:s + n], in0=xT[:, k, s:s + n], scalar=gains[:, which, k:k + 1], in1=r[:, 0:n],
                    op0=ALU.mult, op1=ALU.mult),
                    reads=["xT%d_%d" % (k, ti), rn, "gains"], writes=["xn%d_%d" % (k, ti)])

    def ffn(self, l, which, w_in, w_out):
        P = self.P
        xT, xn = self.xT, self.xn
        with contextlib.ExitStack() as st:
            self.rmsnorm(l * 3 + (0 if which == 1 else 2), st)
            P.flush()
        with contextlib.ExitStack() as st:
            NH = NFF // 2
            hid = P.sb("hid", [128, NH, T], BF16, stack=st)
            NS = 3
            stg = [P.sb("wstg%d" % i, [128, 8, 256], F32, stack=st) for i in range(NS)]
            wb = [P.sb("wbf%d" % i, [128, 8, 256], BF16, stack=st) for i in range(NS)]
            NSB = 2
            stgo = [P.sb("wostg%d" % i, [128, NH, 128], F32, stack=st) for i in range(NSB)]
            wob = [P.sb("wobf%d" % i, [128, NH, 128], BF16, stack=st) for i in range(NSB)]
            gt = [P.sb("gtmp%d" % i, [128, 512], F32, stack=st) for i in range(2)]
            w_in_v = w_in.rearrange("(k p) c -> p k c", p=128)
            w_out_v = w_out.rearrange("(j p) c -> p j c", p=128)
            jobs = []
            for half in range(2):
                for jl in range(NH):
                    jobs.append(("a", half, jl))
                for m in range(8):
                    jobs.append(("b", half, m))
            na = [0]
            nb = [0]
            slots = {}

            def issue_load(ji):
                kind, half, idx = jobs[ji]
                if kind == "a":
                    i = na[0] % NS; na[0] += 1
                    slots[ji] = i
                    j = half * NH + idx
                    s1 = w_in_v[:, :, j * 128:(j + 1) * 128]
                    s2 = w_in_v[:, :, DFF + j * 128:DFF + (j + 1) * 128]
                    b = stg[i]
                    P.dma(lambda e: e.dma_start(out=b[:, :, 0:128], in_=s1), writes=["wstg%d" % i])
                    P.dma(lambda e: e.dma_start(out=b[:, :, 128:256], in_=s2), writes=["wstg%d" % i])
                else:
                    i = nb[0] % NSB; nb[0] += 1
                    slots[ji] = i
                    s1 = w_out_v[:, half * NH:(half + 1) * NH, idx * 128:(idx + 1) * 128]
                    b = stgo[i]
                    P.dma(lambda e: e.dma_start(out=b[:], in_=s1), writes=["wostg%d" % i])

            def issue_cast(ji):
                kind, half, idx = jobs[ji]
                i = slots[ji]
                if kind == "a":
                    P.pool(lambda e: e.tensor_copy(out=wb[i][:], in_=stg[i][:]), reads=["wstg%d" % i], writes=["wbf%d" % i])
                else:
                    P.pool(lambda e: e.tensor_copy(out=wob[i][:], in_=stgo[i][:]), reads=["wostg%d" % i], writes=["wobf%d" % i])

            cnt = [0]

            def compute(ji):
                kind, half, idx = jobs[ji]
                i = slots[ji]
                if kind == "a":
                    w = wb[i]; wn = "wbf%d" % i
                    for ti, (s, n) in enumerate(TILES):
                        c = cnt[0]; cnt[0] += 1
                        pg = self.psum[(c % 2) * 2]; pgn = "ps%d" % ((c % 2) * 2)
                        pu = self.psum[(c % 2) * 2 + 1]; pun = "ps%d" % ((c % 2) * 2 + 1)
                        g = gt[c % 2]; gn = "gtmp%d" % (c % 2)
                        for k in range(8):
                            P.pe(lambda e, k=k, pg=pg, s=s, n=n: e.matmul(pg[:, 0:n], w[:, k, 0:128], xn[:, k, s:s + n], start=(k == 0), stop=(k == 7)),
                                 reads=[wn, "xn%d_%d" % (k, ti)], writes=[pgn])
                        for k in range(8):
                            P.pe(lambda e, k=k, pu=pu, s=s, n=n: e.matmul(pu[:, 0:n], w[:, k, 128:256], xn[:, k, s:s + n], start=(k == 0), stop=(k == 7)),
                                 reads=[wn, "xn%d_%d" % (k, ti)], writes=[pun])
                        P.act(lambda e, g=g, pg=pg, n=n: e.activation(out=g[:, 0:n], in_=pg[:, 0:n], func=AF.Silu), reads=[pgn], writes=[gn])
                        P.dve(lambda e, g=g, pu=pu, s=s, n=n: e.tensor_tensor(out=hid[:, idx, s:s + n], in0=g[:, 0:n], in1=pu[:, 0:n], op=ALU.mult),
                              reads=[gn, pun], writes=["hid%d_%d" % (idx, ti)])
                else:
                    w = wob[i]; wn = "wobf%d" % i
                    m = idx
                    for ti, (s, n) in enumerate(TILES):
                        c = cnt[0]; cnt[0] += 1
                        po = self.psum[4 + c % 2]; pon = "ps%d" % (4 + c % 2)
                        for jl in range(NH):
                            P.pe(lambda e, jl=jl, po=po, s=s, n=n: e.matmul(po[:, 0:n], w[:, jl, :], hid[:, jl, s:s + n], start=(jl == 0), stop=(jl == NH - 1)),
                                 reads=[wn, "hid%d_%d" % (jl, ti)], writes=[pon])
                        P.dve(lambda e, po=po, s=s, n=n: e.scalar_tensor_tensor(
                            out=xT[:, m, s:s + n], in0=po[:, 0:n], scalar=0.5, in1=xT[:, m, s:s + n], op0=ALU.mult, op1=ALU.add),
                            reads=[pon, "xT%d_%d" % (m, ti)], writes=["xT%d_%d" % (m, ti)])

            nj = len(jobs)
            issue_load(0); issue_load(1)
            issue_cast(0)
            for ji in range(nj):
                if ji + 2 < nj:
                    issue_load(ji + 2)
                if ji + 1 < nj:
                    issue_cast(ji + 1)
                compute(ji)
            P.flush()

    def inproj(self, l, chunks, st, dtype=BF16, pad=0):
        P = self.P
        xn = self.xn
        w_v = self.dram["w_in"][l].rearrange("(k p) c -> p k c", p=128)
        out = {}
        for ci, (name, c0, ncol) in enumerate(chunks):
            out[name] = P.sb("u_" + name, [128, pad + T], dtype, stack=st)
        with contextlib.ExitStack() as st2:
            self._inproj_body(l, chunks, out, st2, pad)
            P.flush()
        return out

    def _inproj_body(self, l, chunks, out, st, pad):
        P = self.P
        xn = self.xn
        w_v = self.dram["w_in"][l].rearrange("(k p) c -> p k c", p=128)
        stg = [P.sb("ipstg%d" % i, [128, 8, 128], F32, stack=st) for i in range(3)]
        wbs = [P.sb("ipwb%d" % i, [128, 8, 128], BF16, stack=st) for i in range(3)]
        for ci, (name, c0, ncol) in enumerate(chunks):
            if pad:
                P.dve(lambda e, t=out[name]: e.memset(t[:, 0:pad], 0.0), writes=["u_%s_pad" % name])
        for ci, (name, c0, ncol) in enumerate(chunks):
            i = ci % 3
            b = stg[i]; w = wbs[i]
            P.dma(lambda e, b=b, c0=c0, ncol=ncol: e.dma_start(out=b[:, :, 0:ncol], in_=w_v[:, :, c0:c0 + ncol]),
                  writes=["ipstg%d" % i])
            P.pool(lambda e, b=b, w=w, ncol=ncol: e.tensor_copy(out=w[:, :, 0:ncol], in_=b[:, :, 0:ncol]),
                   reads=["ipstg%d" % i], writes=["ipwb%d" % i])
            u = out[name]
            for ti, (s, n) in enumerate(TILES):
                c = ci * len(TILES) + ti
                ps = self.psum[c % 4]; pn = "ps%d" % (c % 4)
                for k in range(8):
                    P.pe(lambda e, k=k, ps=ps, w=w, s=s, n=n, ncol=ncol: e.matmul(
                        ps[0:ncol, 0:n], w[:, k, 0:ncol], xn[:, k, s:s + n], start=(k == 0), stop=(k == 7)),
                        reads=["ipwb%d" % i, "xn%d_%d" % (k, ti)], writes=[pn])
                if c % 2 == 0:
                    P.act(lambda e, u=u, ps=ps, s=s, n=n, ncol=ncol: e.activation(out=u[0:ncol, pad + s:pad + s + n], in_=ps[0:ncol, 0:n], func=AF.Copy),
                          reads=[pn], writes=["u_%s_%d" % (name, ti)])
                else:
                    P.dve(lambda e, u=u, ps=ps, s=s, n=n, ncol=ncol: e.tensor_copy(out=u[0:ncol, pad + s:pad + s + n], in_=ps[0:ncol, 0:n]),
                          reads=[pn], writes=["u_%s_%d" % (name, ti)])
        return out

    def ures(self, name, tis=None):
        if tis is None:
            tis = range(len(TILES))
        return ["u_%s_%d" % (name, ti) for ti in tis]

    def store_cols(self, u, nrows, col0, ncols, dst, reads, group="sto"):
        P = self.P
        P.dma(lambda e: e.dma_start(out=dst.rearrange("t f -> f t"), in_=u[0:nrows, col0:col0 + ncols],
                                    allow_slow_non_contiguous=True),
              reads=reads, group=group)

    def mix_rwkv_stub(self, l):
        P = self.P
        base = 1028
        with contextlib.ExitStack() as st:
            chunks = [("r0", base, 128), ("r1", base + 128, 128), ("k0", base + 256, 128), ("k1", base + 384, 128),
                      ("v0", base + 512, 128), ("v1", base + 640, 128), ("wa", base + 768, 128), ("gd", base + 896, 128)]
            u = self.inproj(l, chunks, st, dtype=BF16)
            shf = P.sb("shf", [128, 8, 1 + NSEQ], F32, stack=st)
            for ci, (name, c0, ncol) in enumerate(chunks):
                f0 = c0 - base
                P.dve(lambda e, ci=ci, t=u[name]: e.tensor_copy(out=shf[:, ci, 0:1], in_=t[:, TP - 1:TP]), reads=self.ures(name, [3]), writes=["shf%d" % ci])
                P.dve(lambda e, ci=ci, t=u[name]: e.tensor_copy(out=shf[:, ci, 1:1 + NSEQ], in_=t[:, TP:T].rearrange("p (s t) -> p s t", t=TS)[:, :, TS - 1]),
                      reads=self.ures(name, [4]), writes=["shf%d" % ci])
                dst = self.dram["shift_p"][l:l + 1, f0:f0 + ncol]
                P.dma(lambda e, dst=dst, ci=ci: e.dma_start(out=dst.rearrange("t f -> f t"), in_=shf[:, ci, 0:1]), reads=["shf%d" % ci], group="sto")
                dst = self.dram["shift_s"][l, :, f0:f0 + ncol]
                P.dma(lambda e, dst=dst, ci=ci: e.dma_start(out=dst.rearrange("s f -> f s"), in_=shf[:, ci, 1:1 + NSEQ]), reads=["shf%d" % ci], group="sto")
            P.flush()

    def mix_pool(self, l):
        P = self.P
        ymix = self.ymix
        PADP = 16
        with contextlib.ExitStack() as st:
            chunks = [("p0", 2308, 128), ("p1", 2436, 128)]
            u = self.inproj(l, chunks, st, dtype=F32, pad=PADP)
            N = PADP + TP
            lev = [P.sb("pl%d" % i, [128, N], F32, stack=st) for i in range(2)]
            slev = [P.sb("psl%d" % i, [128, NSEQ, 20], F32, stack=st) for i in range(2)]
            Es = P.sb("pEs", [128, NSEQ, 20], F32, stack=st)
            pooled = P.sb("ppooled", [128, T], BF16, stack=st)
            tmp15 = P.sb("ptmp15", [128, 15], F32, stack=st)
            Wf = P.sb("pWf", [128, 128], F32, stack=st)
            Wb = P.sb("pWb", [128, 128], BF16, stack=st)
            psc = P.sb("ppsc", [128, 2], F32, stack=st)
            cinv = P.sb("pcinv", [128, 2, 15], F32, stack=st)
            P.dma(lambda e: e.dma_start(out=psc[:], in_=self.dram["pool_scale"][l].rearrange("(c p) -> p c", p=128)), writes=["ppsc"])
            P.dma(lambda e: e.dma_start(out=cinv[:], in_=self.dram["c_pool_cinv"]), writes=["pcinv"])
            wins = [2, 4, 8, 16]
            for c, (name, c0, ncol) in enumerate(chunks):
                E = u[name]
                allr = self.ures(name) + ["u_%s_pad" % name]
                nlev = 2 if c == 0 else 4
                for i in range(nlev):
                    sh = 1 << i
                    a = E if i == 0 else lev[(i - 1) % 2]
                    o = lev[i % 2]
                    P.dve(lambda e, a=a, o=o, sh=sh: e.tensor_tensor(out=o[:, 2 * sh:N], in0=a[:, 2 * sh:N], in1=a[:, sh:N - sh], op=ALU.add),
                          reads=(allr if i == 0 else ["pl%d" % ((i - 1) % 2)]), writes=["pl%d" % (i % 2)])
                P.dve(lambda e: e.memset(Es[:, :, 0:1], 0.0), writes=["pEs"])
                for sq_ in range(NSEQ):
                    P.dma(lambda e, c=c, sq_=sq_: e.dma_start(out=Es[:, sq_, 1:16], in_=self.dram["st_pool"][l, sq_][:, c * 128:(c + 1) * 128].rearrange("r f -> f r")),
                          writes=["pEs"])
                P.dve(lambda e, E=E: e.tensor_copy(out=Es[:, :, 16:20], in_=E[:, PADP + TP:PADP + T].rearrange("p (s t) -> p s t", t=TS)),
                      reads=allr, writes=["pEs"])
                for i in range(nlev):
                    sh = 1 << i
                    a = Es if i == 0 else slev[(i - 1) % 2]
                    o = slev[i % 2]
                    P.dve(lambda e, a=a, o=o, sh=sh: e.tensor_tensor(out=o[:, :, 2 * sh:20], in0=a[:, :, 2 * sh:20], in1=a[:, :, sh:20 - sh], op=ALU.add),
                          reads=(["pEs"] if i == 0 else ["psl%d" % ((i - 1) % 2)]), writes=["psl%d" % (i % 2)])
                for hf in range(2):
                    rows = slice(hf * 64, hf * 64 + 64)
                    gi = 2 * c + hf
                    li = gi % 2
                    sw = lev[li]; ssw = slev[li]
                    winv = 1.0 / wins[gi]
                    P.dve(lambda e, sw=sw, rows=rows, winv=winv, E=E: e.scalar_tensor_tensor(
                        out=pooled[rows, 0:TP], in0=sw[rows, PADP:N], scalar=winv, in1=E[rows, PADP:N], op0=ALU.mult, op1=ALU.subtract),
                        reads=["pl%d" % li] + allr, writes=["ppooled"])
                    P.dve(lambda e, sw=sw, rows=rows, c=c: e.tensor_tensor(out=tmp15[rows, :], in0=sw[rows, PADP:PADP + 15], in1=cinv[rows, c, :], op=ALU.mult),
                          reads=["pl%d" % li, "pcinv"], writes=["ptmp15"])
                    P.dve(lambda e, rows=rows, E=E: e.tensor_tensor(out=pooled[rows, 0:15], in0=tmp15[rows, :], in1=E[rows, PADP:PADP + 15], op=ALU.subtract),
                          reads=["ptmp15"] + allr, writes=["ppooled"])
                    P.dve(lambda e, ssw=ssw, rows=rows, winv=winv: e.scalar_tensor_tensor(
                        out=pooled[rows, TP:T].rearrange("p (s t) -> p s t", t=TS), in0=ssw[rows, :, 16:20], scalar=winv, in1=Es[rows, :, 16:20],
                        op0=ALU.mult, op1=ALU.subtract), reads=["psl%d" % li, "pEs"], writes=["ppooled"])
                dst = self.dram["pool_p"][l][:, c * 128:(c + 1) * 128]
                self.store_cols(E, 128, PADP + TP - 15, 15, dst, allr)
                for sq_ in range(NSEQ):
                    P.dma(lambda e, c=c, sq_=sq_: e.dma_start(out=self.dram["pool_s"][l, sq_][:, c * 128:(c + 1) * 128].rearrange("r f -> f r"), in_=Es[:, sq_, 5:20]),
                          reads=["pEs"])
                P.dve(lambda e: e.memset(Wf[:], 0.0), writes=["pWf"])
                P.dma(lambda e, c=c: e.dma_start(out=Wf[0:64, 0:64], in_=self.dram["pool_w"][l, 2 * c]), writes=["pWf"])
                P.dma(lambda e, c=c: e.dma_start(out=Wf[64:128, 64:128], in_=self.dram["pool_w"][l, 2 * c + 1]), writes=["pWf"])
                P.dve(lambda e: e.tensor_copy(out=Wb[:], in_=Wf[:]), reads=["pWf"], writes=["pWb"])
                for ti, (s, n) in enumerate(TILES):
                    ps = self.psum[ti % 2]; pn = "ps%d" % (ti % 2)
                    P.pe(lambda e, ps=ps, s=s, n=n: e.matmul(ps[:, 0:n], Wb[:], pooled[:, s:s + n], start=True, stop=True),
                         reads=["pWb", "ppooled"], writes=[pn])
                    P.act(lambda e, ps=ps, s=s, n=n, c=c: e.activation(out=ymix[:, 6 + c, s:s + n], in_=ps[:, 0:n], func=AF.Copy, scale=psc[:, c:c + 1]),
                          reads=[pn, "ppsc"], writes=["ymix%d_%d" % (6 + c, ti)])
            P.flush()

    def mix_s5(self, l):
        P = self.P
        ymix, ident = self.ymix, self.ident
        HW = TP + NSEQ * 5
        PI = 3.14159265358979
        with contextlib.ExitStack() as st:
            u = self.inproj(l, [("s0", 2052, 128), ("s1", 2180, 128)], st, dtype=BF16)
            t_ = lambda nm, shp, dt=F32: P.sb("s5" + nm, shp, dt, stack=st)
            lre, lim, dl, ee, th, rr, kf, ff, s1, s2, c1, sn, cs = [t_(n, [128, 16]) for n in
                                                                     "lre lim dl ee th rr kf ff s1 s2 c1 sn cs".split()]
            ki = t_("ki", [128, 16], mybir.dt.int32)
            ar, ai, xx, den, cr, ci, tq, ais = [t_(n, [128, 16]) for n in "ar ai xx den cr ci tq ais".split()]
            X1 = t_("X1", [128, 16, 16]); X2 = t_("X2", [128, 16, 16]); bb = t_("bb", [128, 16, 16]); bb2 = t_("bb2", [128, 16, 16])
            BT = t_("BT", [128, 2, 128], BF16)
            CT = t_("CT", [128, 16, 16]); CTm = t_("CTm", [128, 2, 128], BF16)
            gm = t_("gm", [128, 8]); swp = t_("swp", [128, 128])
            dsk = t_("dsk", [128, 2]); glb = t_("glb", [128, 4])
            S = "s5c"
            dr = self.dram
            for hf in range(2):
                rs = slice(hf * 64, hf * 64 + 64)
                P.dma(lambda e, rs=rs: e.dma_start(out=lre[rs, :], in_=dr["s5_lam_re"][l].rearrange("g n -> n g")), writes=[S])
                P.dma(lambda e, rs=rs: e.dma_start(out=lim[rs, :], in_=dr["s5_lam_im"][l].rearrange("g n -> n g")), writes=[S])
            P.dma(lambda e: e.dma_start(out=dl[:], in_=dr["s5_log_step"][l].partition_broadcast(128)), writes=[S])
            P.dma(lambda e: e.dma_start(out=X1[0:64], in_=dr["s5_b_re"][l].rearrange("g n c -> n g c")), writes=[S])
            P.dma(lambda e: e.dma_start(out=X1[64:128], in_=dr["s5_b_im"][l].rearrange("g n c -> n g c")), writes=[S])
            P.dma(lambda e: e.dma_start(out=X2[0:64], in_=dr["s5_b_im"][l].rearrange("g n c -> n g c")), writes=[S])
            P.dma(lambda e: e.dma_start(out=X2[64:128], in_=dr["s5_b_re"][l].rearrange("g n c -> n g c")), writes=[S])
            P.dma(lambda e: e.dma_start(out=CT[0:64], in_=dr["s5_c_re"][l].rearrange("g c n -> n g c")), writes=[S])
            P.dma(lambda e: e.dma_start(out=CT[64:128], in_=dr["s5_c_im"][l].rearrange("g c n -> n g c")), writes=[S])
            P.dma(lambda e: e.dma_start(out=gm[:], in_=dr["c_gm"]), writes=[S])
            P.dma(lambda e: e.dma_start(out=swp[:], in_=dr["c_swap"]), writes=[S])
            P.dma(lambda e: e.dma_start(out=dsk[:], in_=dr["s5_d"][l].rearrange("(c p) -> p c", p=128)), writes=[S])
            P.dma(lambda e: e.dma_start(out=glb[:], in_=dr["s5_glu_b"][l].rearrange("(c p) -> p c", p=128)), writes=[S])
            V = lambda fn: P.dve(fn, reads=[S], writes=[S])
            A = lambda fn: P.act(fn, reads=[S], writes=[S])
            A(lambda e: e.activation(out=dl[:], in_=dl[:], func=AF.Exp))
            V(lambda e: e.tensor_tensor(out=th[:], in0=lre[:], in1=dl[:], op=ALU.mult))
            A(lambda e: e.activation(out=ee[:], in_=th[:], func=AF.Exp))
            V(lambda e: e.tensor_tensor(out=th[:], in0=lim[:], in1=dl[:], op=ALU.mult))
            V(lambda e: e.tensor_scalar(out=rr[:], in0=th[:], scalar1=1.0 / (2 * PI), scalar2=None, op0=ALU.mult))
            V(lambda e: e.tensor_copy(out=ki[:], in_=rr[:]))
            V(lambda e: e.tensor_copy(out=kf[:], in_=ki[:]))
            V(lambda e: e.tensor_tensor(out=ff[:], in0=rr[:], in1=kf[:], op=ALU.subtract))
            V(lambda e: e.tensor_scalar(out=s1[:], in0=ff[:], scalar1=2 * PI / 8, scalar2=None, op0=ALU.mult))
            V(lambda e: e.tensor_tensor(out=s2[:], in0=s1[:], in1=s1[:], op=ALU.mult))
            V(lambda e: e.tensor_scalar(out=sn[:], in0=s2[:], scalar1=1.0 / 362880, scalar2=None, op0=ALU.mult))
            for cf in (-1.0 / 5040, 1.0 / 120, -1.0 / 6):
                V(lambda e, cf=cf: e.scalar_tensor_tensor(out=sn[:], in0=sn[:], scalar=cf, in1=s2[:], op0=ALU.add, op1=ALU.mult))
            V(lambda e: e.scalar_tensor_tensor(out=sn[:], in0=sn[:], scalar=1.0, in1=s1[:], op0=ALU.add, op1=ALU.mult))
            V(lambda e: e.tensor_scalar(out=cs[:], in0=s2[:], scalar1=-1.0 / 3628800, scalar2=None, op0=ALU.mult))
            for cf in (1.0 / 40320, -1.0 / 720, 1.0 / 24, -0.5):
                V(lambda e, cf=cf: e.scalar_tensor_tensor(out=cs[:], in0=cs[:], scalar=cf, in1=s2[:], op0=ALU.add, op1=ALU.mult))
            V(lambda e: e.tensor_scalar(out=cs[:], in0=cs[:], scalar1=1.0, scalar2=None, op0=ALU.add))
            for _ in range(3):
                V(lambda e: e.tensor_tensor(out=c1[:], in0=cs[:], in1=cs[:], op=ALU.mult))
                V(lambda e: e.tensor_tensor(out=s1[:], in0=sn[:], in1=sn[:], op=ALU.mult))
                V(lambda e: e.scalar_tensor_tensor(out=sn[:], in0=sn[:], scalar=2.0, in1=cs[:], op0=ALU.mult, op1=ALU.mult))
                V(lambda e: e.tensor_tensor(out=cs[:], in0=c1[:], in1=s1[:], op=ALU.subtract))
            V(lambda e: e.tensor_tensor(out=ar[:], in0=ee[:], in1=cs[:], op=ALU.mult))
            V(lambda e: e.tensor_tensor(out=ai[:], in0=ee[:], in1=sn[:], op=ALU.mult))
            V(lambda e: e.tensor_scalar(out=xx[:], in0=ar[:], scalar1=-1.0, scalar2=None, op0=ALU.add))
            V(lambda e: e.tensor_tensor(out=den[:], in0=lre[:], in1=lre[:], op=ALU.mult))
            V(lambda e: e.tensor_tensor(out=tq[:], in0=lim[:], in1=lim[:], op=ALU.mult))
            V(lambda e: e.tensor_tensor(out=den[:], in0=den[:], in1=tq[:], op=ALU.add))
            V(lambda e: e.reciprocal(out=den[:], in_=den[:]))
            V(lambda e: e.tensor_tensor(out=cr[:], in0=xx[:], in1=lre[:], op=ALU.mult))
            V(lambda e: e.tensor_tensor(out=tq[:], in0=ai[:], in1=lim[:], op=ALU.mult))
            V(lambda e: e.tensor_tensor(out=cr[:], in0=cr[:], in1=tq[:], op=ALU.add))
            V(lambda e: e.tensor_tensor(out=cr[:], in0=cr[:], in1=den[:], op=ALU.mult))
            V(lambda e: e.tensor_tensor(out=ci[:], in0=ai[:], in1=lre[:], op=ALU.mult))
            V(lambda e: e.tensor_tensor(out=tq[:], in0=xx[:], in1=lim[:], op=ALU.mult))
            V(lambda e: e.tensor_tensor(out=ci[:], in0=ci[:], in1=tq[:], op=ALU.subtract))
            V(lambda e: e.tensor_tensor(out=ci[:], in0=ci[:], in1=den[:], op=ALU.mult))
            V(lambda e: e.tensor_scalar(out=ci[0:64, :], in0=ci[0:64, :], scalar1=-1.0, scalar2=None, op0=ALU.mult))
            V(lambda e: e.tensor_copy(out=ais[:], in_=ai[:]))
            V(lambda e: e.tensor_scalar(out=ais[64:128, :], in0=ais[64:128, :], scalar1=-1.0, scalar2=None, op0=ALU.mult))
            V(lambda e: e.tensor_scalar(out=CT[64:128], in0=CT[64:128], scalar1=-1.0, scalar2=None, op0=ALU.mult))
            V(lambda e: e.tensor_tensor(out=bb[:], in0=X1[:], in1=cr[:].unsqueeze(2).to_broadcast([128, 16, 16]), op=ALU.mult))
            V(lambda e: e.tensor_tensor(out=bb2[:], in0=X2[:], in1=ci[:].unsqueeze(2).to_broadcast([128, 16, 16]), op=ALU.mult))
            V(lambda e: e.tensor_tensor(out=bb[:], in0=bb[:], in1=bb2[:], op=ALU.add))
            for ch in range(2):
                ps = self.psum[ch]; pn = "ps%d" % ch
                P.pe(lambda e, ps=ps, ch=ch: e.transpose(ps[:, 0:128], bb[:, 8 * ch:8 * ch + 8, :].rearrange("p g c -> p (g c)"), ident[:]),
                     reads=[S, "ident"], writes=[pn])
                P.dve(lambda e, ps=ps, ch=ch: e.tensor_copy(out=BT[:, ch, :], in_=ps[:, 0:128]), reads=[pn], writes=[S])
            NB = 2
            Hb = [t_("Hb%d" % i, [128, HW], BF16) for i in range(NB)]
            ark = t_("ark", [128, 16]); aik = t_("aik", [128, 16]); ta = t_("ta", [128, 16]); tb = t_("tb", [128, 16])
            ysum = t_("ysum", [128, 2, T])
            Hlast = t_("Hlast", [128, 16]); HlastS = t_("HlastS", [128, 16, NSEQ])
            stA = contextlib.ExitStack()
            tA = lambda nm, shp, dt=F32: P.sb("s5" + nm, shp, dt, stack=stA)
            Hs = [tA("H%d" % i, [128, HW]) for i in range(NB)]
            Bl = [tA("Bl%d" % i, [128, 128], BF16) for i in range(NB)]
            Ml = [tA("Ml%d" % i, [128, 128], BF16) for i in range(2 * NB)]
            Mlo = [tA("Mlo%d" % i, [128, 128], BF16) for i in range(2 * NB)]
            Mf = [tA("Mf%d" % i, [128, 128]) for i in range(2)]
            mtmp = [tA("mt%d" % i, [128, 128]) for i in range(2)]
            shifts = [1 << k for k in range(11)]
            mcount = [0]
            for g0 in range(0, 16, NB):
                grp = list(range(g0, g0 + NB))
                ch = g0 // 8
                uc = u["s%d" % ch]
                ur = self.ures("s%d" % ch)
                for bi, g in enumerate(grp):
                    H = Hs[bi]; Hn = "s5H%d" % bi; HN = ["s5H%d_%d" % (bi, b_) for b_ in range(5)]
                    P.dve(lambda e, bi=bi, g=g, ch=ch: e.tensor_scalar(out=Bl[bi][:], in0=BT[:, ch, :], scalar1=gm[:, g % 8:g % 8 + 1], scalar2=None, op0=ALU.mult),
                          reads=[S], writes=["s5Bl%d" % bi])
                    for ti, (s, n) in enumerate(TILES):
                        ps = self.psum[2 + (ti % 2)]; pn = "ps%d" % (2 + ti % 2)
                        P.pe(lambda e, ps=ps, bi=bi, s=s, n=n, uc=uc: e.matmul(ps[:, 0:n], Bl[bi][:], uc[:, s:s + n], start=True, stop=True),
                             reads=["s5Bl%d" % bi] + ur, writes=[pn])
                        if s < TP:
                            P.act(lambda e, ps=ps, H=H, s=s, n=n: e.activation(out=H[:, s:s + n], in_=ps[:, 0:n], func=AF.Copy), reads=[pn], writes=HN)
                        else:
                            P.act(lambda e, ps=ps, H=H: e.activation(out=H[:, TP:HW].rearrange("p (s t) -> p s t", t=5)[:, :, 1:5],
                                                                     in_=ps[:, 0:TSAMP].rearrange("p (s t) -> p s t", t=TS), func=AF.Copy), reads=[pn], writes=HN)
                    P.dma(lambda e, H=H, g=g: e.dma_start(out=H[0:64, TP:HW].rearrange("p (s t) -> p s t", t=5)[:, :, 0], in_=dr["st_s5re"][l][:, g, :].rearrange("s n -> n s")),
                          writes=HN)
                    P.dma(lambda e, H=H, g=g: e.dma_start(out=H[64:128, TP:HW].rearrange("p (s t) -> p s t", t=5)[:, :, 0], in_=dr["st_s5im"][l][:, g, :].rearrange("s n -> n s")),
                          writes=HN)
                P.dve(lambda e: e.tensor_copy(out=ark[:], in_=ar[:]), reads=[S], writes=["s5pw"])
                P.dve(lambda e: e.tensor_copy(out=aik[:], in_=ais[:]), reads=[S], writes=["s5pw"])
                for k, sh in enumerate(shifts):
                    for bi, g in enumerate(grp):
                        H = Hs[bi]; Hn = "s5H%d" % bi; HN = ["s5H%d_%d" % (bi, b_) for b_ in range(5)]
                        hb = Hb[bi]; hbn = "s5Hb%d" % bi; HBN = ["s5Hb%d_%d" % (bi, b_) for b_ in range(5)]
                        mi = mcount[0] % (2 * NB); mcount[0] += 1
                        M = Ml[mi]; Mn = "s5M%d" % mi
                        mt = mtmp[mi % 2]; mtn = "s5mt%d" % (mi % 2)
                        P.dve(lambda e, mt=mt, g=g: e.tensor_scalar(out=mt[:], in0=swp[:], scalar1=aik[:, g:g + 1], scalar2=None, op0=ALU.mult),
                              reads=[S, "s5pw"], writes=[mtn])
                        mf = Mf[mi % 2]; mfn = "s5Mf%d" % (mi % 2)
                        M2 = Mlo[mi]
                        P.dve(lambda e, mt=mt, mf=mf, g=g: e.scalar_tensor_tensor(out=mf[:], in0=ident[:], scalar=ark[:, g:g + 1], in1=mt[:], op0=ALU.mult, op1=ALU.add),
                              reads=[mtn, "s5pw", "ident"], writes=[mfn])
                        P.act(lambda e, mf=mf, M=M: e.activation(out=M[:], in_=mf[:], func=AF.Copy), reads=[mfn], writes=[Mn])
                        P.dve(lambda e, mf=mf, M=M, M2=M2: e.tensor_tensor(out=M2[:], in0=mf[:], in1=M[:], op=ALU.subtract), reads=[mfn, Mn], writes=[Mn + "lo"])
                        for b_ in range(5):
                            lo_, hi_ = (512 * b_, 512 * b_ + 512) if b_ < 4 else (TP, HW)
                            if b_ < 4 and sh >= TP - 512 * b_:
                                continue
                            if b_ == 4 and sh >= 5:
                                continue
                            P.act(lambda e, hb=hb, H=H, lo_=lo_, hi_=hi_: e.activation(out=hb[:, lo_:hi_], in_=H[:, lo_:hi_], func=AF.Copy), reads=[HN[b_]], writes=[HBN[b_]])
                        if sh < TP:
                            L = TP - sh
                            pcs = [(c0, min(512, L - c0)) for c0 in range(0, L, 512)]
                            for pi, (c0, n) in enumerate(pcs):
                                ps = self.psum[4 + pi]; pn = "ps%d" % (4 + pi)
                                P.pe(lambda e, ps=ps, M=M, hb=hb, c0=c0, n=n: e.matmul(ps[:, 0:n], M[:], hb[:, c0:c0 + n], start=True, stop=False),
                                     reads=[Mn, HBN[c0 // 512]], writes=[pn])
                                P.pe(lambda e, ps=ps, M2=M2, hb=hb, c0=c0, n=n: e.matmul(ps[:, 0:n], M2[:], hb[:, c0:c0 + n], start=False, stop=True),
                                     reads=[Mn + "lo", HBN[c0 // 512]], writes=[pn])
                            for pi, (c0, n) in enumerate(pcs):
                                ps = self.psum[4 + pi]; pn = "ps%d" % (4 + pi)
                                P.dve(lambda e, ps=ps, H=H, c0=c0, n=n, sh=sh: e.tensor_tensor(out=H[:, sh + c0:sh + c0 + n], in0=H[:, sh + c0:sh + c0 + n], in1=ps[:, 0:n], op=ALU.add),
                                      reads=[pn] + HN[(sh + c0) // 512:(sh + c0 + n - 1) // 512 + 1], writes=HN[(sh + c0) // 512:(sh + c0 + n - 1) // 512 + 1])
                        if sh < 5:
                            ps = self.psum[2]; pn = "ps2"
                            w5 = 5 - sh
                            P.pe(lambda e, ps=ps, M=M, hb=hb, w5=w5: e.matmul(ps[:, 0:NSEQ * w5], M[:], hb[:, TP:HW].rearrange("p (s t) -> p s t", t=5)[:, :, 0:w5], start=True, stop=False),
                                 reads=[Mn, HBN[4]], writes=[pn])
                            P.pe(lambda e, ps=ps, M2=M2, hb=hb, w5=w5: e.matmul(ps[:, 0:NSEQ * w5], M2[:], hb[:, TP:HW].rearrange("p (s t) -> p s t", t=5)[:, :, 0:w5], start=False, stop=True),
                                 reads=[Mn + "lo", HBN[4]], writes=[pn])
                            P.dve(lambda e, ps=ps, H=H, w5=w5, sh=sh: e.tensor_tensor(
                                out=H[:, TP:HW].rearrange("p (s t) -> p s t", t=5)[:, :, sh:5], in0=H[:, TP:HW].rearrange("p (s t) -> p s t", t=5)[:, :, sh:5],
                                in1=ps[:, 0:NSEQ * w5].rearrange("p (s t) -> p s t", t=w5), op=ALU.add), reads=[pn, HN[4]], writes=[HN[4]])
                    W_ = lambda fn: P.dve(fn, reads=["s5pw"], writes=["s5pw"])
                    W_(lambda e: e.tensor_tensor(out=ta[:], in0=ark[:], in1=ark[:], op=ALU.mult))
                    W_(lambda e: e.tensor_tensor(out=tb[:], in0=aik[:], in1=aik[:], op=ALU.mult))
                    W_(lambda e: e.scalar_tensor_tensor(out=aik[:], in0=ark[:], scalar=2.0, in1=aik[:], op0=ALU.mult, op1=ALU.mult))
                    W_(lambda e: e.tensor_tensor(out=ark[:], in0=ta[:], in1=tb[:], op=ALU.subtract))
                for bi, g in enumerate(grp):
                    H = Hs[bi]; Hn = "s5H%d" % bi; HN = ["s5H%d_%d" % (bi, b_) for b_ in range(5)]
                    hb = Hb[bi]; hbn = "s5Hb%d" % bi; HBN = ["s5Hb%d_%d" % (bi, b_) for b_ in range(5)]
                    P.act(lambda e, hb=hb, H=H: e.activation(out=hb[:], in_=H[:], func=AF.Copy), reads=HN, writes=HBN)
                    P.dve(lambda e, H=H, g=g: e.tensor_copy(out=Hlast[:, g:g + 1], in_=H[:, TP - 1:TP]), reads=HN, writes=["s5last"])
                    P.dve(lambda e, H=H, g=g: e.tensor_copy(out=HlastS[:, g, :], in_=H[:, TP:HW].rearrange("p (s t) -> p s t", t=5)[:, :, 4]), reads=HN, writes=["s5last"])
                P.dve(lambda e: e.memset(CTm[:], 0.0), writes=["s5CTm"])
                for bi, g in enumerate(grp):
                    P.dve(lambda e, g=g, bi=bi: e.tensor_copy(out=CTm[:, bi, 16 * (g % 8):16 * (g % 8) + 16], in_=CT[:, g, :]), reads=[S], writes=["s5CTm"])
                for ti, (s, n) in enumerate(TILES):
                    ps = self.psum[ti % 2]; pn = "ps%d" % (ti % 2)
                    for bi, g in enumerate(grp):
                        hb = Hb[bi]; hbn = "s5Hb%d" % bi; HBN = ["s5Hb%d_%d" % (bi, b_) for b_ in range(5)]
                        if s < TP:
                            rhs = hb[:, s:s + n]
                        else:
                            rhs = hb[:, TP:HW].rearrange("p (s t) -> p s t", t=5)[:, :, 1:5]
                        P.pe(lambda e, ps=ps, g=g, rhs=rhs, n=n, bi=bi: e.matmul(ps[:, 0:n], CTm[:, bi, :], rhs, start=(bi == 0), stop=(bi == NB - 1)),
                             reads=["s5CTm"] + HBN, writes=[pn])
                    yn = "s5ys%d_%d" % (ch, ti)
                    if g0 % 8 == 0:
                        P.dve(lambda e, ps=ps, ch=ch, s=s, n=n, uc=uc: e.scalar_tensor_tensor(out=ysum[:, ch, s:s + n], in0=uc[:, s:s + n], scalar=dsk[:, ch:ch + 1], in1=ps[:, 0:n],
                                                                                       op0=ALU.mult, op1=ALU.add), reads=[pn, S] + ur, writes=[yn])
                    else:
                        P.dve(lambda e, ps=ps, ch=ch, s=s, n=n: e.tensor_tensor(out=ysum[:, ch, s:s + n], in0=ysum[:, ch, s:s + n], in1=ps[:, 0:n], op=ALU.add),
                              reads=[pn, yn], writes=[yn])
            P.dma(lambda e: e.dma_start(out=dr["s5re_p"][l].rearrange("g n -> n g"), in_=Hlast[0:64, :]), reads=["s5last"], group="sto")
            P.dma(lambda e: e.dma_start(out=dr["s5im_p"][l].rearrange("g n -> n g"), in_=Hlast[64:128, :]), reads=["s5last"], group="sto")
            for g in range(16):
                P.dma(lambda e, g=g: e.dma_start(out=dr["s5re_s"][l][:, g, :].rearrange("s n -> n s"), in_=HlastS[0:64, g, :]), reads=["s5last"], group="sto")
                P.dma(lambda e, g=g: e.dma_start(out=dr["s5im_s"][l][:, g, :].rearrange("s n -> n s"), in_=HlastS[64:128, g, :]), reads=["s5last"], group="sto")
            P.flush()
            stA.close()
            gwf = t_("gwf", [128, 2, 512]); gwb = t_("gwb", [128, 2, 512], BF16)
            P.dma(lambda e: e.dma_start(out=gwf[:], in_=dr["s5_glu_w"][l].rearrange("(k p) c -> p k c", p=128)), writes=["s5gw"])
            P.dve(lambda e: e.tensor_copy(out=gwb[:], in_=gwf[:]), reads=["s5gw"], writes=["s5gw"])
            g1 = [t_("g1_%d" % i, [128, 512]) for i in range(2)]
            g2 = [t_("g2_%d" % i, [128, 512]) for i in range(2)]
            cc = 0
            for ch in range(2):
                for ti, (s, n) in enumerate(TILES):
                    a = g1[cc % 2]; an = "s5g1_%d" % (cc % 2)
                    b = g2[cc % 2]; bn = "s5g2_%d" % (cc % 2)
                    cc += 1
                    yn = "s5ys%d_%d" % (ch, ti)
                    ys = ysum[:, ch, s:s + n]
                    P.act(lambda e, a=a, ys=ys, n=n: e.activation(out=a[:, 0:n], in_=ys, func=AF.Square), reads=[yn], writes=[an])
                    P.dve(lambda e, a=a, n=n: e.tensor_scalar(out=a[:, 0:n], in0=a[:, 0:n], scalar1=0.044715, scalar2=1.0, op0=ALU.mult, op1=ALU.add), reads=[an], writes=[an])
                    P.dve(lambda e, a=a, b=b, ys=ys, n=n: e.tensor_tensor(out=b[:, 0:n], in0=a[:, 0:n], in1=ys, op=ALU.mult), reads=[an, yn], writes=[bn])
                    P.act(lambda e, b=b, n=n: e.activation(out=b[:, 0:n], in_=b[:, 0:n], func=AF.Sigmoid, scale=1.5957691216), reads=[bn], writes=[bn])
                    P.dve(lambda e, b=b, ys=ys, ch=ch, s=s, n=n: e.tensor_tensor(out=Hb[ch][:, s:s + n], in0=b[:, 0:n], in1=ys, op=ALU.mult), reads=[bn, yn], writes=["s5yg%d_%d" % (ch, ti)])
            for m in range(2):
                for ti, (s, n) in enumerate(TILES):
                    pa = self.psum[(ti % 2) * 2]; pan = "ps%d" % ((ti % 2) * 2)
                    pb = self.psum[(ti % 2) * 2 + 1]; pbn = "ps%d" % ((ti % 2) * 2 + 1)
                    sg = g1[ti % 2]; sgn = "s5g1_%d" % (ti % 2)
                    for k in range(2):
                        P.pe(lambda e, pa=pa, k=k, m=m, s=s, n=n: e.matmul(pa[:, 0:n], gwb[:, k, m * 128:(m + 1) * 128], Hb[k][:, s:s + n], start=(k == 0), stop=(k == 1)),
                             reads=["s5gw", "s5yg%d_%d" % (k, ti)], writes=[pan])
                    for k in range(2):
                        P.pe(lambda e, pb=pb, k=k, m=m, s=s, n=n: e.matmul(pb[:, 0:n], gwb[:, k, 256 + m * 128:256 + (m + 1) * 128], Hb[k][:, s:s + n], start=(k == 0), stop=(k == 1)),
                             reads=["s5gw", "s5yg%d_%d" % (k, ti)], writes=[pbn])
                    P.act(lambda e, pb=pb, sg=sg, m=m, n=n: e.activation(out=sg[:, 0:n], in_=pb[:, 0:n], func=AF.Sigmoid, bias=glb[:, 2 + m:3 + m], scale=1.0), reads=[pbn, S], writes=[sgn])
                    P.dve(lambda e, pa=pa, sg=sg, m=m, s=s, n=n: e.scalar_tensor_tensor(out=ymix[:, 4 + m, s:s + n], in0=pa[:, 0:n], scalar=glb[:, m:m + 1], in1=sg[:, 0:n],
                                                                                  op0=ALU.add, op1=ALU.mult), reads=[pan, sgn, S], writes=["ymix%d_%d" % (4 + m, ti)])
            P.flush()

    def mix_ssd(self, l):
        P = self.P
        dr = self.dram
        ymix, ident, identb, onesb = self.ymix, self.ident, self.identb, self.onesb
        NCH = TP // 128
        with contextlib.ExitStack() as st:
            t_ = lambda nm, shp, dt=F32: P.sb("sd" + nm, shp, dt, stack=st)
            C_ = "sdc"
            tokq = t_("tokq", [128, 4, NCH + 1, 4])
            cdS = t_("cdS", [128, 2, NSEQ])
            eaS = t_("eaS", [128, 2, TSAMP])
            cw = t_("cw", [128, 6, 4]); cb = t_("cb", [128, 6]); dcol = t_("dcol", [128, 4]); ng = t_("ng", [128, 2])
            dtb = t_("dtb", [4, 1]); aneg = t_("aneg", [4, 1]); one4 = t_("one4", [4, 1])
            mk = t_("mk", [128, 128]); Ust = t_("Ust", [128, 128]); Lin = t_("Lin", [128, 128]); elast = t_("elast", [128, 128])
            mkS = t_("mkS", [64, 64]); UstS = t_("UstS", [64, 64]); LinS = t_("LinS", [64, 64]); seqm = t_("seqm", [64, NSEQ])
            sel = t_("sel", [4, 2, 128])
            for j in range(4):
                P.dma(lambda e, j=j: e.dma_start(out=cw[:, :, j], in_=dr["ssd_conv_w"][l, j].rearrange("(c p) -> p c", p=128)), writes=[C_])
            P.dma(lambda e: e.dma_start(out=cb[:], in_=dr["ssd_conv_b"][l].rearrange("(c p) -> p c", p=128)), writes=[C_])
            P.dma(lambda e: e.dma_start(out=dcol[:], in_=dr["ssd_d"][l].partition_broadcast(128)), writes=[C_])
            P.dma(lambda e: e.dma_start(out=ng[:], in_=dr["ssd_norm"][l].rearrange("(c p) -> p c", p=128)), writes=[C_])
            P.dma(lambda e: e.dma_start(out=dtb[:], in_=dr["ssd_dt_bias"][l].rearrange("(h o) -> h o", o=1)), writes=[C_])
            P.dma(lambda e: e.dma_start(out=aneg[:], in_=dr["ssd_a_log"][l].rearrange("(h o) -> h o", o=1)), writes=[C_])
            P.dma(lambda e: e.dma_start(out=mk[:], in_=dr["c_mask128"]), writes=[C_])
            P.dma(lambda e: e.dma_start(out=Ust[:], in_=dr["c_U128"]), writes=[C_])
            P.dma(lambda e: e.dma_start(out=Lin[:], in_=dr["c_L128"]), writes=[C_])
            P.dma(lambda e: e.dma_start(out=elast[:], in_=dr["c_elast"]), writes=[C_])
            P.dma(lambda e: e.dma_start(out=mkS[:], in_=dr["c_maskS"]), writes=[C_])
            P.dma(lambda e: e.dma_start(out=UstS[:], in_=dr["c_US"]), writes=[C_])
            P.dma(lambda e: e.dma_start(out=LinS[:], in_=dr["c_LS"]), writes=[C_])
            P.dma(lambda e: e.dma_start(out=seqm[:], in_=dr["c_seqm"]), writes=[C_])
            P.dma(lambda e: e.dma_start(out=sel[:], in_=dr["c_sel"]), writes=[C_])
            P.act(lambda e: e.activation(out=aneg[:], in_=aneg[:], func=AF.Exp), reads=[C_], writes=[C_])
            P.dve(lambda e: e.tensor_scalar(out=aneg[:], in0=aneg[:], scalar1=-1.0, scalar2=None, op0=ALU.mult), reads=[C_], writes=[C_])
            P.dve(lambda e: e.memset(one4[:], 1.0), writes=[C_])
            with contextlib.ExitStack() as s1:
                ud = self.inproj(l, [("dt", 1024, 4)], s1, dtype=F32)["dt"]
                dA = P.sb("sddA", [4, T], F32, stack=s1)
                dB = P.sb("sddB", [4, T], F32, stack=s1)
                cm = P.sb("sdcm", [4, T], F32, stack=s1)
                aend = P.sb("sdaend", [4, NCH + NSEQ], F32, stack=s1)
                P.dma(lambda e: e.dma_start(out=cm[:], in_=dr["c_cmask"]), writes=["sdcm"])
                Rd = self.ures("dt")

                def to_tok(src, q, rn):
                    ps = self.psum[q % 2]; pn = "ps%d" % (q % 2)
                    for c in range(NCH + 1):
                        L = 128 if c < NCH else TSAMP
                        P.pe(lambda e, ps=ps, c=c, L=L, src=src: e.transpose(ps[0:L, 4 * c:4 * c + 4], src[0:4, 128 * c:128 * c + L], ident[0:4, 0:4]),
                             reads=[rn, "ident"], writes=[pn])
                    P.dve(lambda e, ps=ps, q=q: e.tensor_copy(out=tokq[:, q, :, :], in_=ps[:, 0:4 * (NCH + 1)].rearrange("p (c h) -> p c h", h=4)),
                          reads=[pn], writes=["sdtokq"])
                P.act(lambda e: e.activation(out=dA[:], in_=ud[0:4, :], func=AF.Exp, bias=dtb[:], scale=1.0), reads=Rd + [C_], writes=["sddA"])
                P.act(lambda e: e.activation(out=dA[:], in_=dA[:], func=AF.Ln, bias=one4[:], scale=1.0), reads=["sddA", C_], writes=["sddA"])
                to_tok(dA, 0, "sddA")
                P.dve(lambda e: e.tensor_scalar(out=dB[:], in0=dA[:], scalar1=aneg[:, 0:1], scalar2=None, op0=ALU.mult), reads=["sddA", C_], writes=["sddB"])
                to_tok(dB, 1, "sddB")
                P.dve(lambda e: e.tensor_tensor_scan(out=dA[:], data0=cm[:], data1=dB[:], initial=0.0, op0=ALU.mult, op1=ALU.add),
                      reads=["sddB", "sdcm"], writes=["sddA"])
                P.dve(lambda e: e.tensor_copy(out=aend[:, 0:NCH], in_=dA[:, 0:TP].rearrange("p (c t) -> p c t", t=128)[:, :, 127]), reads=["sddA"], writes=["sdaend"])
                P.dve(lambda e: e.tensor_copy(out=aend[:, NCH:NCH + NSEQ], in_=dA[:, TP:T].rearrange("p (c t) -> p c t", t=TS)[:, :, TS - 1]), reads=["sddA"], writes=["sdaend"])
                P.act(lambda e: e.activation(out=dB[:], in_=dA[:], func=AF.Exp), reads=["sddA"], writes=["sddB"])
                to_tok(dB, 3, "sddB")
                for c in range(2):
                    ps = self.psum[2 + c]; pn = "ps%d" % (2 + c)
                    P.pe(lambda e, ps=ps, c=c: e.matmul(ps[:, 0:TSAMP], sel[:, c, :], dB[:, TP:T], start=True, stop=True), reads=["sddB", C_], writes=[pn])
                    P.act(lambda e, ps=ps, c=c: e.activation(out=eaS[:, c, :], in_=ps[:, 0:TSAMP], func=AF.Copy), reads=[pn], writes=["sdeaS"])
                P.dve(lambda e: e.tensor_tensor(out=dA[:, 0:TP].rearrange("p (c t) -> p c t", t=128), in0=aend[:, 0:NCH].unsqueeze(2).to_broadcast([4, NCH, 128]),
                                                in1=dA[:, 0:TP].rearrange("p (c t) -> p c t", t=128), op=ALU.subtract), reads=["sddA", "sdaend"], writes=["sddA"])
                P.dve(lambda e: e.tensor_tensor(out=dA[:, TP:T].rearrange("p (c t) -> p c t", t=TS), in0=aend[:, NCH:NCH + NSEQ].unsqueeze(2).to_broadcast([4, NSEQ, TS]),
                                                in1=dA[:, TP:T].rearrange("p (c t) -> p c t", t=TS), op=ALU.subtract), reads=["sddA", "sdaend"], writes=["sddA"])
                P.act(lambda e: e.activation(out=dA[:], in_=dA[:], func=AF.Exp), reads=["sddA"], writes=["sddA"])
                to_tok(dA, 2, "sddA")
                P.act(lambda e: e.activation(out=aend[:], in_=aend[:], func=AF.Exp), reads=["sdaend"], writes=["sdaend"])
                for c in range(2):
                    ps = self.psum[2 + c]; pn = "ps%d" % (2 + c)
                    P.pe(lambda e, ps=ps, c=c: e.matmul(ps[:, 0:NSEQ], sel[:, c, :], aend[:, NCH:NCH + NSEQ], start=True, stop=True), reads=["sdaend", C_], writes=[pn])
                    P.act(lambda e, ps=ps, c=c: e.activation(out=cdS[:, c, :], in_=ps[:, 0:NSEQ], func=AF.Copy), reads=[pn], writes=["sdcdS"])
                P.flush()
            import os
            SSDPH = int(os.environ.get("SSDPH", "9"))
            SUB = int(os.environ.get("SUB", "99"))
            if SSDPH < 2:
                return
            zz = self.inproj(l, [("z0", 0, 128), ("z1", 128, 128)], st, dtype=BF16)
            xc = [t_("xc%d" % c, [128, T], BF16) for c in range(6)]
            cst = t_("cst", [128, 6, 1 + NSEQ, 3])
            for c in range(6):
                with contextlib.ExitStack() as s2:
                    nm = "xb%d" % c
                    u = self.inproj(l, [(nm, 256 + 128 * c, 128)], s2, dtype=BF16, pad=3)[nm]
                    acc = P.sb("sdacc", [128, TP], F32, stack=s2)
                    Es = P.sb("sdEs", [128, NSEQ, 7], F32, stack=s2)
                    accs = P.sb("sdaccs", [128, NSEQ, TS], F32, stack=s2)
                    R = self.ures(nm) + ["u_%s_pad" % nm]
                    P.dve(lambda e, c=c, u=u: e.tensor_copy(out=cst[:, c, 0, :], in_=u[:, 3 + TP - 3:3 + TP]), reads=R, writes=["sdcst"])
                    P.dve(lambda e, c=c, u=u: e.tensor_copy(out=cst[:, c, 1:1 + NSEQ, :], in_=u[:, 3 + TP:3 + T].rearrange("p (s t) -> p s t", t=TS)[:, :, 1:4]), reads=R, writes=["sdcst"])
                    P.dma(lambda e, c=c: e.dma_start(out=dr["conv_p"][l][:, c * 128:(c + 1) * 128].rearrange("r f -> f r"), in_=cst[:, c, 0, :]), reads=["sdcst"], group="sto")
                    for r in range(3):
                        P.dma(lambda e, c=c, r=r: e.dma_start(out=dr["conv_s"][l][:, r, c * 128:(c + 1) * 128].rearrange("s f -> f s"), in_=cst[:, c, 1:1 + NSEQ, r]), reads=["sdcst"], group="sto")
                        P.dma(lambda e, c=c, r=r: e.dma_start(out=Es[:, :, r], in_=dr["st_conv"][l][:, r, c * 128:(c + 1) * 128].rearrange("s f -> f s")), writes=["sdEs"])
                    P.dve(lambda e, u=u: e.tensor_copy(out=Es[:, :, 3:7], in_=u[:, 3 + TP:3 + T].rearrange("p (s t) -> p s t", t=TS)), reads=R, writes=["sdEs"])
                    P.dve(lambda e, c=c, u=u: e.tensor_scalar(out=acc[:], in0=u[:, 3:3 + TP], scalar1=cw[:, c, 3:4], scalar2=cb[:, c:c + 1], op0=ALU.mult, op1=ALU.add),
                          reads=R + [C_], writes=["sdacc"])
                    for j in range(3):
                        P.dve(lambda e, c=c, u=u, j=j: e.scalar_tensor_tensor(out=acc[:], in0=u[:, j:j + TP], scalar=cw[:, c, j:j + 1], in1=acc[:], op0=ALU.mult, op1=ALU.add),
                              reads=R + [C_, "sdacc"], writes=["sdacc"])
                    P.act(lambda e, c=c: e.activation(out=xc[c][:, 0:TP], in_=acc[:], func=AF.Silu), reads=["sdacc"], writes=["sdxc%d" % c])
                    P.dve(lambda e, c=c: e.tensor_scalar(out=accs[:], in0=Es[:, :, 3:7], scalar1=cw[:, c, 3:4], scalar2=cb[:, c:c + 1], op0=ALU.mult, op1=ALU.add),
                          reads=["sdEs", C_], writes=["sdaccs"])
                    for j in range(3):
                        P.dve(lambda e, c=c, j=j: e.scalar_tensor_tensor(out=accs[:], in0=Es[:, :, j:j + TS], scalar=cw[:, c, j:j + 1], in1=accs[:], op0=ALU.mult, op1=ALU.add),
                              reads=["sdEs", C_, "sdaccs"], writes=["sdaccs"])
                    P.act(lambda e, c=c: e.activation(out=xc[c][:, TP:T].rearrange("p (s t) -> p s t", t=TS), in_=accs[:], func=AF.Silu), reads=["sdaccs"], writes=["sdxc%d" % c])
                    P.flush()
            if SSDPH < 3:
                return
            yss = t_("yss", [128, 2, T], BF16)
            hT = t_("hT", [128, 2, 128]); hTb = t_("hTb", [128, 2, 128], BF16)
            cdB = t_("cdB", [128, 4])
            xtok = t_("xtok", [128, 256]); xdt = t_("xdt", [128, 256], BF16); xdd = t_("xdd", [128, 256], BF16); Btok = t_("Btok", [128, 256], BF16)
            CBm = [t_("CBm%d" % g, [128, 128]) for g in range(2)]
            Ul = [t_("Ul%d" % i, [128, 128]) for i in range(2)]
            Ex = [t_("Ex%d" % i, [128, 128]) for i in range(2)]
            sc = [t_("sc%d" % h, [128, 128], BF16) for h in range(4)]
            ydg = t_("ydg", [128, 256]); ytok = t_("ytok", [128, 256])
            h0n = [t_("h0n%d" % i, [128, 2, 128]) for i in range(2)]
            h0T = [t_("h0T%d" % i, [128, 2, 128], BF16) for i in range(2)]
            hnew = [t_("hnew%d" % i, [128, 2, 128]) for i in range(2)]
            xdm = [t_("xdm%d" % i, [64, 256], BF16) for i in range(2)]
            yoS = t_("yoS", [128, 2, TSAMP])
            P.dve(lambda e: e.memset(hT[:], 0.0), writes=["sdhT"])
            P.dve(lambda e: e.memset(hTb[:], 0.0), writes=["sdhTb"])
            XS = ["sdxc0", "sdxc1"]; BS = ["sdxc2", "sdxc3"]; CS = ["sdxc4", "sdxc5"]
            for c in range(NCH + 1):
                if SSDPH == 3 and c >= 2:
                    break
                if SSDPH == 4 and c >= NCH:
                    break
                samp = (c == NCH)
                L = TSAMP if samp else 128
                t0 = 128 * c
                mK, uK, lK = (mkS, UstS, LinS) if samp else (mk, Ust, Lin)
                pxb = self.psum[0]
                for q in range(4):
                    P.pe(lambda e, q=q, L=L, t0=t0: e.matmul(pxb[0:L, 128 * q:128 * q + 128], xc[q][:, t0:t0 + L], identb[:], start=True, stop=True),
                         reads=["sdxc%d" % q, "identb"], writes=["ps0"])
                P.act(lambda e, L=L: e.activation(out=xtok[0:L, :], in_=pxb[0:L, 0:256], func=AF.Copy), reads=["ps0"], writes=["sdxtok"])
                P.act(lambda e, L=L: e.activation(out=Btok[0:L, :], in_=pxb[0:L, 256:512], func=AF.Copy), reads=["ps0"], writes=["sdBtok"])
                if SUB <= 1:
                    continue
                for h in range(4):
                    P.dve(lambda e, h=h, L=L, c=c: e.tensor_scalar(out=xdt[0:L, 64 * h:64 * h + 64], in0=xtok[0:L, 64 * h:64 * h + 64], scalar1=(1.0 if os.environ.get("IMM") else tokq[0:L, 0, c, h:h + 1]), scalar2=None, op0=ALU.mult),
                          reads=["sdxtok", "sdtokq"], writes=["sdxdt"])
                    P.dve(lambda e, h=h, L=L, c=c: e.tensor_scalar(out=xdd[0:L, 64 * h:64 * h + 64], in0=xtok[0:L, 64 * h:64 * h + 64], scalar1=(1.0 if os.environ.get("IMM") else tokq[0:L, 0, c, h:h + 1]), scalar2=(1.0 if os.environ.get("IMM") else tokq[0:L, 2, c, h:h + 1]), op0=ALU.mult, op1=ALU.mult),
                          reads=["sdxtok", "sdtokq"], writes=["sdxdd"])
                if SUB <= 2:
                    continue
                for g in range(2):
                    pc = self.psum[1 + g]; pcn = "ps%d" % (1 + g)
                    P.pe(lambda e, g=g, pc=pc, L=L, t0=t0: e.matmul(pc[0:L, 0:L], xc[2 + g][:, t0:t0 + L], xc[4 + g][:, t0:t0 + L], start=True, stop=True),
                         reads=[BS[g], CS[g]], writes=[pcn])
                    P.dve(lambda e, g=g, pc=pc, L=L, mK=mK: e.tensor_tensor(out=CBm[g][0:L, 0:L], in0=pc[0:L, 0:L], in1=mK[0:L, 0:L], op=ALU.mult),
                          reads=[pcn, C_], writes=["sdCBm%d" % g])
                if SUB <= 3:
                    continue
                for h in range(4):
                    ul = Ul[h % 2]; uln = "sdUl%d" % (h % 2)
                    ex = Ex[h % 2]; exn = "sdEx%d" % (h % 2)
                    pg = self.psum[3 + (h % 2)]; pgn = "ps%d" % (3 + h % 2)
                    P.dve(lambda e, h=h, ul=ul, L=L, c=c, uK=uK: e.tensor_scalar(out=ul[0:L, 0:L], in0=uK[0:L, 0:L], scalar1=tokq[0:L, 1, c, h:h + 1], scalar2=None, op0=ALU.mult),
                          reads=[C_, "sdtokq"], writes=[uln])
                    P.pe(lambda e, pg=pg, ul=ul, L=L, lK=lK: e.matmul(pg[0:L, 0:L], ul[0:L, 0:L], lK[0:L, 0:L], start=True, stop=True), reads=[uln, C_], writes=[pgn])
                    P.act(lambda e, pg=pg, ex=ex, L=L: e.activation(out=ex[0:L, 0:L], in_=pg[0:L, 0:L], func=AF.Exp), reads=[pgn], writes=[exn])
                    P.dve(lambda e, h=h, ex=ex, L=L: e.tensor_tensor(out=sc[h][0:L, 0:L], in0=ex[0:L, 0:L], in1=CBm[h // 2][0:L, 0:L], op=ALU.mult),
                          reads=[exn, "sdCBm%d" % (h // 2)], writes=["sdsc%d" % h])
                if SUB <= 4:
                    continue
                pyd = self.psum[5]
                for h in range(4):
                    P.pe(lambda e, h=h, L=L: e.matmul(pyd[0:L, 64 * h:64 * h + 64], sc[h][0:L, 0:L], xdt[0:L, 64 * h:64 * h + 64], start=True, stop=True),
                         reads=["sdsc%d" % h, "sdxdt"], writes=["ps5"])
                if SUB <= 5:
                    continue
                if not samp:
                    P.act(lambda e, L=L: e.activation(out=ydg[0:L, :], in_=pyd[0:L, 0:256], func=AF.Copy), reads=["ps5"], writes=["sdydg"])
                    pyo = self.psum[6]
                    for g in range(2):
                        P.pe(lambda e, g=g, t0=t0: e.matmul(pyo[:, 128 * g:128 * g + 128], xc[4 + g][:, t0:t0 + 128], hTb[:, g, :], start=True, stop=True),
                             reads=[CS[g], "sdhTb"], writes=["ps6"])
                    for h in range(4):
                        P.dve(lambda e, h=h, c=c: e.scalar_tensor_tensor(out=ytok[:, 64 * h:64 * h + 64], in0=pyo[:, 64 * h:64 * h + 64], scalar=tokq[:, 3, c, h:h + 1], in1=ydg[:, 64 * h:64 * h + 64],
                                                                      op0=ALU.mult, op1=ALU.add), reads=["ps6", "sdydg", "sdtokq"], writes=["sdytok"])
                else:
                    P.act(lambda e, L=L: e.activation(out=ytok[0:L, :], in_=pyd[0:L, 0:256], func=AF.Copy), reads=["ps5"], writes=["sdytok"])
                if SUB <= 6:
                    continue
                for h in range(4):
                    P.dve(lambda e, h=h, L=L: e.scalar_tensor_tensor(out=ytok[0:L, 64 * h:64 * h + 64], in0=xtok[0:L, 64 * h:64 * h + 64], scalar=dcol[0:L, h:h + 1], in1=ytok[0:L, 64 * h:64 * h + 64],
                                                                  op0=ALU.mult, op1=ALU.add), reads=["sdxtok", "sdytok", C_], writes=["sdytok"])
                if SUB <= 7:
                    continue
                pyt = self.psum[7]
                for g in range(2):
                    P.pe(lambda e, g=g, L=L: e.transpose(pyt[:, 128 * g:128 * g + L], ytok[0:L, 128 * g:128 * g + 128], ident[0:L, 0:L]), reads=["sdytok", "ident"], writes=["ps7"])
                if not samp:
                    P.act(lambda e, t0=t0: e.activation(out=yss[:, :, t0:t0 + 128], in_=pyt[:, 0:256].rearrange("p (g t) -> p g t", t=128), func=AF.Copy), reads=["ps7"], writes=["sdyss"])
                    if SUB <= 8:
                        continue
                    pcd = self.psum[6]
                    P.pe(lambda e, c=c: e.matmul(pcd[:, 256:260], elast[:], tokq[:, 3, c, :], start=True, stop=True), reads=[C_, "sdtokq"], writes=["ps6"])
                    P.act(lambda e: e.activation(out=cdB[:], in_=pcd[:, 256:260], func=AF.Copy), reads=["ps6"], writes=["sdcdB"])
                    pst = self.psum[0]
                    for g in range(2):
                        P.pe(lambda e, g=g: e.matmul(pst[:, 128 * g:128 * g + 128], Btok[:, 128 * g:128 * g + 128], xdd[:, 128 * g:128 * g + 128], start=True, stop=True),
                             reads=["sdBtok", "sdxdd"], writes=["ps0"])
                    for h in range(4):
                        g, hh = h // 2, h % 2
                        P.dve(lambda e, h=h, g=g, hh=hh: e.scalar_tensor_tensor(out=hT[:, g, 64 * hh:64 * hh + 64], in0=hT[:, g, 64 * hh:64 * hh + 64], scalar=cdB[:, h:h + 1],
                                                                             in1=pst[:, 64 * h:64 * h + 64], op0=ALU.mult, op1=ALU.add), reads=["sdhT", "sdcdB", "ps0"], writes=["sdhT"])
                    P.act(lambda e: e.activation(out=hTb[:], in_=hT[:], func=AF.Copy), reads=["sdhT"], writes=["sdhTb"])
                else:
                    pyo = self.psum[6]
                    for s in range(NSEQ):
                        hn = h0n[s % 2]; hnn = "sdh0n%d" % (s % 2)
                        ht = h0T[s % 2]; htn = "sdh0T%d" % (s % 2)
                        hw = hnew[s % 2]; hwn = "sdhnew%d" % (s % 2)
                        xm = xdm[s % 2]; xmn = "sdxdm%d" % (s % 2)
                        pt = self.psum[1 + (s % 2)]; ptn = "ps%d" % (1 + s % 2)
                        pn_ = self.psum[3 + (s % 2)]; pnn = "ps%d" % (3 + s % 2)
                        P.dma(lambda e, s=s, hn=hn: e.dma_start(out=hn[:], in_=dr["st_ssd"][l, s].rearrange("(g a) p n -> (a p) g n", g=2)), writes=[hnn])
                        for g in range(2):
                            P.pe(lambda e, g=g, hn=hn, pt=pt: e.transpose(pt[:, 128 * g:128 * g + 128], hn[:, g, :], ident[:]), reads=[hnn, "ident"], writes=[ptn])
                        P.act(lambda e, ht=ht, pt=pt: e.activation(out=ht[:], in_=pt[:, 0:256].rearrange("p (g t) -> p g t", t=128), func=AF.Copy), reads=[ptn], writes=[htn])
                        for g in range(2):
                            P.pe(lambda e, g=g, s=s, ht=ht: e.matmul(pyo[:, 64 * g + TS * s:64 * g + TS * s + TS], ht[:, g, :], xc[4 + g][:, TP + TS * s:TP + TS * s + TS], start=True, stop=True),
                                 reads=[htn, CS[g]], writes=["ps6"])
                        P.dve(lambda e, s=s, xm=xm: e.tensor_scalar(out=xm[:], in0=xdd[0:64, :], scalar1=seqm[:, s:s + 1], scalar2=None, op0=ALU.mult), reads=["sdxdd", C_], writes=[xmn])
                        for g in range(2):
                            P.pe(lambda e, g=g, xm=xm, pn_=pn_: e.matmul(pn_[:, 128 * g:128 * g + 128], xm[:, 128 * g:128 * g + 128], Btok[0:64, 128 * g:128 * g + 128], start=True, stop=True),
                                 reads=[xmn, "sdBtok"], writes=[pnn])
                        for g in range(2):
                            P.dve(lambda e, g=g, s=s, hn=hn, hw=hw, pn_=pn_: e.scalar_tensor_tensor(out=hw[:, g, :], in0=hn[:, g, :], scalar=cdS[:, g, s:s + 1], in1=pn_[:, 128 * g:128 * g + 128],
                                                                                              op0=ALU.mult, op1=ALU.add), reads=[hnn, "sdcdS", pnn], writes=[hwn])
                        P.dma(lambda e, s=s, hw=hw: e.dma_start(out=dr["ssd_s"][l, s].rearrange("(g a) p n -> (a p) g n", g=2), in_=hw[:]), reads=[hwn])
                    for g in range(2):
                        P.dve(lambda e, g=g: e.tensor_tensor(out=yoS[:, g, :], in0=pyo[:, 64 * g:64 * g + 64], in1=eaS[:, g, :], op=ALU.mult), reads=["ps6", "sdeaS"], writes=["sdyoS"])
                        P.dve(lambda e, g=g: e.tensor_tensor(out=yss[:, g, TP:T], in0=pyt[:, 128 * g:128 * g + TSAMP], in1=yoS[:, g, :], op=ALU.add), reads=["ps7", "sdyoS"], writes=["sdyss"])
            pf = self.psum[1]
            for g in range(2):
                P.pe(lambda e, g=g: e.transpose(pf[:, 128 * g:128 * g + 128], hT[:, g, :], ident[:]), reads=["sdhT", "ident"], writes=["ps1"])
            P.act(lambda e: e.activation(out=hnew[0][:], in_=pf[:, 0:256].rearrange("p (g t) -> p g t", t=128), func=AF.Copy), reads=["ps1"], writes=["sdhnew0"])
            P.dma(lambda e: e.dma_start(out=dr["ssd_p"][l].rearrange("(g a) p n -> (a p) g n", g=2), in_=hnew[0][:]), reads=["sdhnew0"])
            P.flush()
            gq = [t_("gq%d" % i, [128, 2, 512]) for i in range(1)] * 2
            gsq = [t_("gsq%d" % i, [128, 2, 512], BF16) for i in range(1)] * 2
            grs = [t_("grs%d" % i, [128, 512]) for i in range(1)] * 2
            epsc = self.epsc
            for ti, (s, n) in enumerate(TILES):
                q = gq[0]; qn = "sdgq0"
                sq = gsq[0]; sqn = "sdgsq0"
                rs = grs[0]; rsn = "sdgrs0"
                ps = self.psum[ti % 2]; pn = "ps%d" % (ti % 2)
                for k in range(2):
                    P.act(lambda e, k=k, q=q, s=s, n=n: e.activation(out=q[:, k, 0:n], in_=zz["z%d" % k][:, s:s + n], func=AF.Silu), reads=self.ures("z%d" % k, [ti]), writes=[qn])
                    P.dve(lambda e, k=k, q=q, s=s, n=n: e.tensor_tensor(out=q[:, k, 0:n], in0=q[:, k, 0:n], in1=yss[:, k, s:s + n], op=ALU.mult), reads=[qn, "sdyss"], writes=[qn])
                    P.act(lambda e, k=k, q=q, sq=sq, n=n: e.activation(out=sq[:, k, 0:n], in_=q[:, k, 0:n], func=AF.Square), reads=[qn], writes=[sqn])
                for k in range(2):
                    P.pe(lambda e, k=k, ps=ps, sq=sq, n=n: e.matmul(ps[:, 0:n], onesb[:], sq[:, k, 0:n], start=(k == 0), stop=(k == 1)), reads=[sqn, "onesb"], writes=[pn])
                P.act(lambda e, rs=rs, ps=ps, n=n: e.activation(out=rs[:, 0:n], in_=ps[:, 0:n], func=AF.Sqrt, bias=epsc[:], scale=1.0 / 256), reads=[pn, "epsc"], writes=[rsn])
                P.dve(lambda e, rs=rs, n=n: e.reciprocal(out=rs[:, 0:n], in_=rs[:, 0:n]), reads=[rsn], writes=[rsn])
                for k in range(2):
                    P.dve(lambda e, k=k, q=q, rs=rs, s=s, n=n: e.scalar_tensor_tensor(out=ymix[:, k, s:s + n], in0=q[:, k, 0:n], scalar=ng[:, k:k + 1], in1=rs[:, 0:n], op0=ALU.mult, op1=ALU.mult),
                          reads=[qn, rsn, C_], writes=["ymix%d_%d" % (k, ti)])
            P.flush()

    def mix_rwkv(self, l):
        P = self.P
        dr = self.dram
        ymix, ident = self.ymix, self.ident
        base = 1028
        scr = dr["rw_scr"]
        NKK, WW, KP, BB, RR, VV, GG = range(7)
        with contextlib.ExitStack() as st:
            t_ = lambda nm, shp, dt=F32: P.sb("rw" + nm, shp, dt, stack=st)
            C_ = "rwc"
            chunks = [("r0", base, 128), ("r1", base + 128, 128), ("k0", base + 256, 128), ("k1", base + 384, 128),
                      ("v0", base + 512, 128), ("v1", base + 640, 128), ("wa", base + 768, 128), ("gd", base + 896, 128)]
            U = self.inproj(l, chunks, st, dtype=BF16, pad=1)
            shf = t_("shf", [128, 8, 1 + NSEQ])
            for ci, (name, c0, ncol) in enumerate(chunks):
                f0 = c0 - base
                R = self.ures(name)
                P.dve(lambda e, ci=ci, t=U[name]: e.tensor_copy(out=shf[:, ci, 0:1], in_=t[:, TP:TP + 1]), reads=R, writes=["shf%d" % ci])
                P.dve(lambda e, ci=ci, t=U[name]: e.tensor_copy(out=shf[:, ci, 1:1 + NSEQ], in_=t[:, 1 + TP:1 + T].rearrange("p (s t) -> p s t", t=TS)[:, :, TS - 1]),
                      reads=R, writes=["shf%d" % ci])
                P.dma(lambda e, f0=f0, ci=ci: e.dma_start(out=dr["shift_p"][l:l + 1, f0:f0 + 128].rearrange("t f -> f t"), in_=shf[:, ci, 0:1]), reads=["shf%d" % ci], group="sto")
                P.dma(lambda e, f0=f0, ci=ci: e.dma_start(out=dr["shift_s"][l, :, f0:f0 + 128].rearrange("s f -> f s"), in_=shf[:, ci, 1:1 + NSEQ]), reads=["shf%d" % ci], group="sto")
            mu = t_("mu", [128, 8]); w0n = t_("w0n", [128, 2]); a0 = t_("a0", [128, 2]); kkc = t_("kkc", [128, 2]); kac = t_("kac", [128, 2]); ka1 = t_("ka1", [128, 2])
            rkc = t_("rkc", [128, 2]); lng = t_("lng", [128, 2]); lnb = t_("lnb", [128, 2]); mh = t_("mh", [128, 1]); e5 = t_("e5", [128, 1]); one1 = t_("one1", [128, 1])
            WAf = t_("WAf", [128, 256]); WAb = t_("WAb", [128, 256], BF16); G2f = t_("G2f", [128, 256]); G2b = t_("G2b", [128, 256], BF16)
            bdo = t_("bdo", [128, 128]); I2 = t_("I2", [128, 64]); Esel = t_("Esel", [64, 2, 128]); Ehp = t_("Ehp", [128, 2, 64]); onec = t_("onec", [128, 1])
            ld = lambda dst, src: P.dma(lambda e: e.dma_start(out=dst, in_=src), writes=[C_])
            ld(mu[:], dr["rwkv_mu"][l].rearrange("(c p) -> p c", p=128))
            for tile_, nm in [(w0n, "rwkv_w0"), (a0, "rwkv_a0"), (kkc, "rwkv_k_k"), (kac, "rwkv_k_a"), (lng, "rwkv_ln_g"), (lnb, "rwkv_ln_b")]:
                ld(tile_[:], dr[nm][l].rearrange("(c p) -> p c", p=128))
            ld(rkc[:], dr["rwkv_r_k"][l].rearrange("(c a) j -> (a j) c", c=2))
            ld(WAf[0:64, :], dr["rwkv_w2"][l]); ld(WAf[64:128, :], dr["rwkv_a2"][l]); ld(G2f[:], dr["rwkv_g2"][l])
            ld(bdo[:], dr["c_bdones"]); ld(I2[:], dr["c_I2"]); ld(Esel[:], dr["c_Esel"]); ld(Ehp[:], dr["c_Ehp"])
            V = lambda fn: P.dve(fn, reads=[C_], writes=[C_])
            V(lambda e: e.tensor_copy(out=WAb[:], in_=WAf[:]))
            V(lambda e: e.tensor_copy(out=G2b[:], in_=G2f[:]))
            V(lambda e: e.tensor_scalar(out=w0n[:], in0=w0n[:], scalar1=-1.0, scalar2=None, op0=ALU.mult))
            V(lambda e: e.tensor_scalar(out=ka1[:], in0=kac[:], scalar1=-1.0, scalar2=1.0, op0=ALU.mult, op1=ALU.add))
            V(lambda e: e.memset(mh[:], -0.5)); V(lambda e: e.memset(e5[:], 64e-5)); V(lambda e: e.memset(one1[:], 1.0)); V(lambda e: e.memset(onec[:], 1.0))
            with contextlib.ExitStack() as s2:
                w_ = lambda nm, dt=F32: P.sb("rw2" + nm, [128, 256], dt, stack=s2)
                dd = w_("dd"); xwa = w_("xwa"); xgd = w_("xgd"); wab = w_("wab", BF16); sgb = w_("sgb", BF16)
                xk = w_("xk"); kk = w_("kk"); sq = w_("sq"); nr = w_("nr"); ta = w_("ta"); tt = w_("tt")
                outs = [[w_("o%d_%d" % (i, j)) for j in range(7)] for i in range(2)]
                prevS = P.sb("rw2prevS", [128, 8, TSAMP], F32, stack=s2)
                for ci, (name, c0, ncol) in enumerate(chunks):
                    f0 = c0 - base
                    P.dma(lambda e, ci=ci, f0=f0: e.dma_start(out=prevS[:, ci, :].rearrange("p (s t) -> p s t", t=TS)[:, :, 0], in_=dr["st_shift"][l][:, f0:f0 + 128].rearrange("s f -> f s")),
                          writes=["rwprevS%d" % ci])
                    P.dve(lambda e, ci=ci, t=U[name]: e.tensor_copy(out=prevS[:, ci, :].rearrange("p (s t) -> p s t", t=TS)[:, :, 1:TS], in_=t[:, 1 + TP:1 + T].rearrange("p (s t) -> p s t", t=TS)[:, :, 0:TS - 1]),
                          reads=self.ures(name), writes=["rwprevS%d" % ci])
                cidx = {nm: i for i, (nm, _, _) in enumerate(chunks)}

                def xs_of(name, ti, s, n, dst, rn):
                    ci = cidx[name]
                    u = U[name]
                    R = self.ures(name) + ["u_%s_pad" % name]
                    if s < TP:
                        P.dve(lambda e, u=u, s=s, n=n: e.tensor_tensor(out=dd[:, 0:n], in0=u[:, s:s + n], in1=u[:, 1 + s:1 + s + n], op=ALU.subtract), reads=R, writes=["rwdd"])
                    else:
                        P.dve(lambda e, u=u, s=s, n=n, ci=ci: e.tensor_tensor(out=dd[:, 0:n], in0=prevS[:, ci, :], in1=u[:, 1 + s:1 + s + n], op=ALU.subtract),
                              reads=R + ["rwprevS%d" % ci], writes=["rwdd"])
                    P.dve(lambda e, u=u, s=s, n=n, ci=ci, dst=dst: e.scalar_tensor_tensor(out=dst[:, 0:n], in0=dd[:, 0:n], scalar=mu[:, ci:ci + 1], in1=u[:, 1 + s:1 + s + n], op0=ALU.mult, op1=ALU.add),
                          reads=R + ["rwdd", C_], writes=[rn])

                for ti, (s, n) in enumerate([(i * 256, 256) for i in range(8)] + [(TP, TSAMP)]):
                    xs_of("wa", ti, s, n, xwa, "rwxwa")
                    xs_of("gd", ti, s, n, xgd, "rwxgd")
                    P.act(lambda e, n=n: e.activation(out=wab[0:64, 0:n], in_=xwa[0:64, 0:n], func=AF.Tanh), reads=["rwxwa"], writes=["rwwab"])
                    P.act(lambda e, n=n: e.activation(out=wab[64:128, 0:n], in_=xwa[64:128, 0:n], func=AF.Copy), reads=["rwxwa"], writes=["rwwab"])
                    P.act(lambda e, n=n: e.activation(out=sgb[:, 0:n], in_=xgd[:, 0:n], func=AF.Sigmoid), reads=["rwxgd"], writes=["rwsgb"])
                    for c in range(2):
                        O = outs[c]
                        on = ["rwo%d_%d" % (c, j) for j in range(7)]
                        pw, pa, pg, pq = self.psum[0], self.psum[1], self.psum[2], self.psum[3]
                        cs = slice(128 * c, 128 * c + 128)
                        P.pe(lambda e, n=n, cs=cs: e.matmul(pw[:, 0:n], WAb[0:64, cs], wab[0:64, 0:n], start=True, stop=True), reads=["rwwab", C_], writes=["ps0"])
                        P.pe(lambda e, n=n, cs=cs: e.matmul(pa[:, 0:n], WAb[64:128, cs], wab[64:128, 0:n], start=True, stop=True), reads=["rwwab", C_], writes=["ps1"])
                        P.pe(lambda e, n=n, cs=cs: e.matmul(pg[:, 0:n], G2b[:, cs], sgb[:, 0:n], start=True, stop=True), reads=["rwsgb", C_], writes=["ps2"])
                        P.act(lambda e, n=n, c=c: e.activation(out=tt[:, 0:n], in_=pw[:, 0:n], func=AF.Exp, bias=w0n[:, c:c + 1], scale=-1.0), reads=["ps0", C_], writes=["rwtt"])
                        P.act(lambda e, n=n: e.activation(out=tt[:, 0:n], in_=tt[:, 0:n], func=AF.Ln, bias=one1[:], scale=1.0), reads=["rwtt", C_], writes=["rwtt"])
                        P.act(lambda e, n=n: e.activation(out=tt[:, 0:n], in_=tt[:, 0:n], func=AF.Exp, bias=mh[:], scale=-1.0), reads=["rwtt", C_], writes=["rwtt"])
                        P.act(lambda e, n=n, O=O: e.activation(out=O[WW][:, 0:n], in_=tt[:, 0:n], func=AF.Exp, scale=-1.0), reads=["rwtt"], writes=[on[WW]])
                        P.act(lambda e, n=n, c=c: e.activation(out=ta[:, 0:n], in_=pa[:, 0:n], func=AF.Sigmoid, bias=a0[:, c:c + 1], scale=1.0), reads=["ps1", C_], writes=["rwta"])
                        P.act(lambda e, n=n, O=O: e.activation(out=O[GG][:, 0:n], in_=pg[:, 0:n], func=AF.Copy), reads=["ps2"], writes=[on[GG]])
                        xs_of("k%d" % c, ti, s, n, xk, "rwxk")
                        xs_of("r%d" % c, ti, s, n, O[RR], on[RR])
                        xs_of("v%d" % c, ti, s, n, O[VV], on[VV])
                        P.dve(lambda e, n=n, c=c: e.tensor_scalar(out=kk[:, 0:n], in0=xk[:, 0:n], scalar1=kkc[:, c:c + 1], scalar2=None, op0=ALU.mult), reads=["rwxk", C_], writes=["rwkk"])
                        P.act(lambda e, n=n: e.activation(out=sq[:, 0:n], in_=kk[:, 0:n], func=AF.Square), reads=["rwkk"], writes=["rwsq"])
                        P.pe(lambda e, n=n: e.matmul(pq[:, 0:n], bdo[:], sq[:, 0:n], start=True, stop=True), reads=["rwsq", C_], writes=["ps3"])
                        P.act(lambda e, n=n: e.activation(out=nr[:, 0:n], in_=pq[:, 0:n], func=AF.Sqrt), reads=["ps3"], writes=["rwnr"])
                        P.dve(lambda e, n=n: e.tensor_scalar(out=nr[:, 0:n], in0=nr[:, 0:n], scalar1=1e-12, scalar2=None, op0=ALU.max), reads=["rwnr"], writes=["rwnr"])
                        P.dve(lambda e, n=n: e.reciprocal(out=nr[:, 0:n], in_=nr[:, 0:n]), reads=["rwnr"], writes=["rwnr"])
                        P.dve(lambda e, n=n, O=O: e.scalar_tensor_tensor(out=O[NKK][:, 0:n], in0=kk[:, 0:n], scalar=-1.0, in1=nr[:, 0:n], op0=ALU.mult, op1=ALU.mult), reads=["rwkk", "rwnr"], writes=[on[NKK]])
                        P.dve(lambda e, n=n, O=O: e.scalar_tensor_tensor(out=O[BB][:, 0:n], in0=O[NKK][:, 0:n], scalar=-1.0, in1=ta[:, 0:n], op0=ALU.mult, op1=ALU.mult), reads=[on[NKK], "rwta"], writes=[on[BB]])
                        P.dve(lambda e, n=n, c=c: e.tensor_scalar(out=ta[:, 0:n], in0=ta[:, 0:n], scalar1=kac[:, c:c + 1], scalar2=ka1[:, c:c + 1], op0=ALU.mult, op1=ALU.add), reads=["rwta", C_, on[BB]], writes=["rwta"])
                        P.dve(lambda e, n=n, O=O: e.tensor_tensor(out=O[KP][:, 0:n], in0=xk[:, 0:n], in1=ta[:, 0:n], op=ALU.mult), reads=["rwxk", "rwta"], writes=[on[KP]])
                        for j in range(7):
                            P.dma(lambda e, c=c, j=j, s=s, n=n, O=O: e.dma_start(out=scr[c, j, :, s:s + n], in_=O[j][:, 0:n]), reads=[on[j]], writes=["rwscr"], group="rwscr")
                P.flush()
            NBK = 256
            Bk = [t_("Bk%d" % c, [128, 7, NBK]) for c in range(2)]
            STp = [[t_("ST%d_%d" % (c, q), [128, 64]) for q in range(2)] for c in range(2)]
            par = [0, 0]
            STk = [t_("STk%d" % c, [128, 64], BF16) for c in range(2)]
            STr = [t_("STr%d" % c, [128, 64], BF16) for c in range(2)]
            bdob = t_("bdob", [128, 128], BF16); hps = t_("hps", [128, 2]); hpsb = t_("hpsb", [128, 2], BF16)
            P.dma(lambda e: e.dma_start(out=hps[:], in_=dr["c_hpsel"]), writes=["rwhps"])
            P.dve(lambda e: e.tensor_copy(out=hpsb[:], in_=hps[:]), reads=["rwhps"], writes=["rwhps"])
            P.dve(lambda e: e.tensor_copy(out=bdob[:], in_=bdo[:]), reads=[C_], writes=["rwhps"])
            T1 = [t_("T1%d" % c, [128, 64]) for c in range(2)]
            T2 = [t_("T2%d" % c, [128, 64]) for c in range(2)]
            Vex = [t_("Vex%d" % c, [128, 8, 64], BF16) for c in range(2)]
            Zl = [t_("Zl%d" % i, [64, 2, 128]) for i in range(2)]
            Sout = [t_("Sout%d" % i, [64, 2, 64]) for i in range(2)]
            Ysb = t_("Ysb", [64, 2 * NBK]); ysc = t_("ysc", [128, NBK]); ycen = t_("ycen", [128, NBK]); ysq = t_("ysq", [128, NBK]); yrs = t_("yrs", [128, NBK]); ypr = t_("ypr", [128, NBK])
            pSA = self.psum[0]
            pVb = [self.psum[1], self.psum[2]]
            pY = [self.psum[3], self.psum[4]]
            pM = self.psum[5]; pV2 = self.psum[6]; pX = self.psum[7]
            for c in range(2):
                P.dve(lambda e, c=c: e.memset(STp[c][0][:], 0.0), writes=["rwST%d_0" % c])
            for i in range(2):
                P.dve(lambda e, i=i: e.memset(Zl[i][:], 0.0), writes=["rwZl%d" % i])
            sa_slot = [0]

            def load_block(c, t0, n):
                P.dma(lambda e, c=c, t0=t0, n=n: e.dma_start(out=Bk[c][:, :, 0:n], in_=scr[c, :, :, t0:t0 + n].rearrange("a p t -> p a t")),
                      reads=["rwscr"], writes=["rwBk%d" % c])

            def vb_group(c, j0, ng):
                P.pool(lambda e, c=c, j0=j0, ng=ng: e.tensor_tensor(out=Vex[c][:, 0:ng, :], in0=I2[:].unsqueeze(1).to_broadcast([128, ng, 64]),
                                                                   in1=Bk[c][:, VV, j0:j0 + ng].unsqueeze(2).to_broadcast([128, ng, 64]), op=ALU.mult),
                       reads=["rwBk%d" % c, C_], writes=["rwVex%d" % c])
                P.pe(lambda e, c=c, ng=ng: e.matmul(pVb[c][:, 0:64 * ng], bdob[:], Vex[c][:, 0:ng, :].rearrange("p a b -> p (a b)"), start=True, stop=True),
                     reads=["rwVex%d" % c, "rwhps"], writes=["ps%d" % (1 + c)])

            pend = []

            def flush_y():
                for (c, j, ycol, q) in pend:
                    P.act(lambda e, c=c, j=j, q=q: e.activation(out=STr[c][:], in_=STp[c][q][:], func=AF.Copy, scale=Bk[c][:, RR, j:j + 1]),
                          reads=["rwST%d_%d" % (c, q), "rwBk%d" % c], writes=["rwSTr%d" % c])
                for (c, j, ycol, q) in pend:
                    P.pe(lambda e, c=c, ycol=ycol: e.matmul(pY[c][0:64, 2 * ycol:2 * ycol + 2], STr[c][:], hpsb[:], start=True, stop=True),
                         reads=["rwSTr%d" % c, "rwhps"], writes=["ps%d" % (3 + c)])
                del pend[:]

            def step2(j, ycol):
                vcol = (j % 8) * 64
                slots = []
                for c in range(2):
                    slot = sa_slot[0] % 8; sa_slot[0] += 1
                    slots.append(slot)
                qo = [par[0], par[1]]
                qn = [1 - par[0], 1 - par[1]]
                for c in range(2):
                    P.dve(lambda e, c=c, j=j, q=qo[c]: e.tensor_scalar(out=STk[c][:], in0=STp[c][q][:], scalar1=Bk[c][:, NKK, j:j + 1], scalar2=None, op0=ALU.mult),
                          reads=["rwST%d_%d" % (c, qo[c]), "rwBk%d" % c], writes=["rwSTk%d" % c])
                for c in range(2):
                    P.act(lambda e, c=c, j=j, q=qo[c]: e.activation(out=T1[c][:], in_=STp[c][q][:], func=AF.Copy, scale=Bk[c][:, WW, j:j + 1]),
                          reads=["rwST%d_%d" % (c, qo[c]), "rwBk%d" % c], writes=["rwT1%d" % c])
                for c in range(2):
                    P.pe(lambda e, c=c, slot=slots[c]: e.matmul(pSA[:, 64 * slot:64 * slot + 64], bdob[:], STk[c][:], start=True, stop=True),
                         reads=["rwSTk%d" % c, "rwhps"], writes=["rwsa%d" % slots[c]])
                flush_y()
                for c in range(2):
                    P.dve(lambda e, c=c, j=j, vcol=vcol: e.scalar_tensor_tensor(out=T2[c][:], in0=pVb[c][:, vcol:vcol + 64], scalar=Bk[c][:, KP, j:j + 1], in1=T1[c][:], op0=ALU.mult, op1=ALU.add),
                          reads=["ps%d" % (1 + c), "rwBk%d" % c, "rwT1%d" % c], writes=["rwT2%d" % c])
                for c in range(2):
                    P.dve(lambda e, c=c, j=j, slot=slots[c], q=qn[c]: e.scalar_tensor_tensor(out=STp[c][q][:], in0=pSA[:, 64 * slot:64 * slot + 64], scalar=Bk[c][:, BB, j:j + 1], in1=T2[c][:],
                                                                                       op0=ALU.mult, op1=ALU.add),
                          reads=["rwsa%d" % slots[c], "rwBk%d" % c, "rwT2%d" % c], writes=["rwST%d_%d" % (c, qn[c])])
                for c in range(2):
                    par[c] = qn[c]
                    pend.append((c, j, ycol, qn[c]))

            def post(c, t0, n):
                bn = "rwBk%d" % c
                P.act(lambda e, c=c: e.activation(out=Ysb[:], in_=pY[c][0:64, :], func=AF.Copy), reads=["ps%d" % (3 + c)], writes=["rwYsb"])
                for hp in range(2):
                    P.pe(lambda e, hp=hp, n=n: e.matmul(pX[:, 0:n], Esel[:, hp, :], Ysb[:, 0:2 * n].rearrange("p (t h) -> p t h", h=2)[:, :, hp], start=(hp == 0), stop=(hp == 1)), reads=["rwYsb", C_], writes=["ps7"])
                P.act(lambda e, n=n: e.activation(out=ysc[:, 0:n], in_=pX[:, 0:n], func=AF.Copy), reads=["ps7"], writes=["rwysc"])
                P.pe(lambda e, n=n: e.matmul(pM[:, 0:n], bdo[:], ysc[:, 0:n], start=True, stop=True), reads=["rwysc", C_], writes=["ps5"])
                P.dve(lambda e, n=n: e.scalar_tensor_tensor(out=ycen[:, 0:n], in0=pM[:, 0:n], scalar=-1.0 / 64, in1=ysc[:, 0:n], op0=ALU.mult, op1=ALU.add), reads=["ps5", "rwysc"], writes=["rwycen"])
                P.act(lambda e, n=n: e.activation(out=ysq[:, 0:n], in_=ycen[:, 0:n], func=AF.Square), reads=["rwycen"], writes=["rwysq"])
                P.pe(lambda e, n=n: e.matmul(pV2[:, 0:n], bdo[:], ysq[:, 0:n], start=True, stop=True), reads=["rwysq", C_], writes=["ps6"])
                P.act(lambda e, n=n: e.activation(out=yrs[:, 0:n], in_=pV2[:, 0:n], func=AF.Sqrt, bias=e5[:], scale=1.0 / 64), reads=["ps6", C_], writes=["rwyrs"])
                P.dve(lambda e, n=n: e.reciprocal(out=yrs[:, 0:n], in_=yrs[:, 0:n]), reads=["rwyrs"], writes=["rwyrs"])
                P.dve(lambda e, n=n: e.tensor_tensor(out=ycen[:, 0:n], in0=ycen[:, 0:n], in1=yrs[:, 0:n], op=ALU.mult), reads=["rwycen", "rwyrs"], writes=["rwycen"])
                P.dve(lambda e, n=n, c=c: e.tensor_scalar(out=ycen[:, 0:n], in0=ycen[:, 0:n], scalar1=lng[:, c:c + 1], scalar2=lnb[:, c:c + 1], op0=ALU.mult, op1=ALU.add), reads=["rwycen", C_], writes=["rwycen"])
                P.dve(lambda e, n=n, c=c: e.scalar_tensor_tensor(out=ypr[:, 0:n], in0=Bk[c][:, RR, 0:n], scalar=rkc[:, c:c + 1], in1=Bk[c][:, KP, 0:n], op0=ALU.mult, op1=ALU.mult), reads=[bn, C_], writes=["rwypr"])
                P.pe(lambda e, n=n: e.matmul(pM[:, 0:n], bdo[:], ypr[:, 0:n], start=True, stop=True), reads=["rwypr", C_], writes=["ps5"])
                P.dve(lambda e, n=n, c=c: e.tensor_tensor(out=ypr[:, 0:n], in0=pM[:, 0:n], in1=Bk[c][:, VV, 0:n], op=ALU.mult), reads=["ps5", bn], writes=["rwypr"])
                P.dve(lambda e, n=n: e.tensor_tensor(out=ycen[:, 0:n], in0=ycen[:, 0:n], in1=ypr[:, 0:n], op=ALU.add), reads=["rwycen", "rwypr"], writes=["rwycen"])
                ti0 = min(t0 // 512, 4)
                P.dve(lambda e, n=n, c=c, t0=t0: e.tensor_tensor(out=ymix[:, 2 + c, t0:t0 + n], in0=ycen[:, 0:n], in1=Bk[c][:, GG, 0:n], op=ALU.mult), reads=["rwycen", bn],
                      writes=["ymix%d_%d" % (2 + c, ti0)])

            def store_state(c, dst_fn, idx):
                so = Sout[idx % 2]; son = "rwSout%d" % (idx % 2)
                for hp in range(2):
                    P.pe(lambda e, c=c, hp=hp, q=par[c]: e.matmul(pX[0:64, 256 + 64 * hp:256 + 64 * hp + 64], STp[c][q][:], Ehp[:, hp, :], start=True, stop=True), reads=["rwST%d_%d" % (c, par[c]), C_], writes=["ps7"])
                P.act(lambda e, so=so: e.activation(out=so[:].rearrange("p a b -> p (a b)"), in_=pX[0:64, 256:384], func=AF.Copy), reads=["ps7"], writes=[son])
                for hp in range(2):
                    P.dma(lambda e, so=so, hp=hp, c=c: e.dma_start(out=dst_fn(2 * c + hp), in_=so[:, hp, :]), reads=[son])

            for b0 in range(0, TP, NBK):
                for c in range(2):
                    load_block(c, b0, NBK)
                for j in range(NBK):
                    if j % 8 == 0:
                        for c in range(2):
                            vb_group(c, j, 8)
                    step2(j, j)
                flush_y()
                for c in range(2):
                    post(c, b0, NBK)
            for c in range(2):
                store_state(c, lambda h: dr["rwkv_p"][l, h], c)
            for c in range(2):
                load_block(c, TP, TSAMP)
            for s in range(NSEQ):
                for c in range(2):
                    z = Zl[c]; zn = "rwZl%d" % c
                    for hp in range(2):
                        P.dma(lambda e, z=z, hp=hp, s=s, c=c: e.dma_start(out=z[:, hp, 64 * hp:64 * hp + 64], in_=dr["st_rwkv"][l, s, 2 * c + hp]), writes=[zn])
                    for hp in range(2):
                        P.pe(lambda e, z=z, hp=hp: e.matmul(pX[:, 384:448], z[:, hp, :], ident[0:64, 0:64], start=(hp == 0), stop=(hp == 1)), reads=[zn, "ident"], writes=["ps7"])
                    P.act(lambda e, c=c, q=par[c]: e.activation(out=STp[c][q][:], in_=pX[:, 384:448], func=AF.Copy), reads=["ps7"], writes=["rwST%d_%d" % (c, par[c])])
                for t in range(TS):
                    j = TS * s + t
                    if j % 8 == 0:
                        for c in range(2):
                            vb_group(c, j, 8)
                    step2(j, j)
                flush_y()
                for c in range(2):
                    store_state(c, (lambda s: (lambda h: dr["rwkv_s"][l, s, h]))(s), s * 2 + c)
            for c in range(2):
                post(c, TP, TSAMP)
            P.flush()

    def outproj(self, l):
        P = self.P
        xT, ymix = self.xT, self.ymix
        w_v = self.dram["w_out"][l].rearrange("(k p) c -> p k c", p=128)
        with contextlib.ExitStack() as st:
            stg = [P.sb("opstg%d" % i, [128, 8, 128], F32, stack=st) for i in range(2)]
            wbs = [P.sb("opwb%d" % i, [128, 8, 128], BF16, stack=st) for i in range(2)]
            for m in range(8):
                i = m % 2
                b = stg[i]; w = wbs[i]
                P.dma(lambda e, b=b, m=m: e.dma_start(out=b[:], in_=w_v[:, :, m * 128:(m + 1) * 128]), writes=["opstg%d" % i])
                P.pool(lambda e, b=b, w=w: e.tensor_copy(out=w[:], in_=b[:]), reads=["opstg%d" % i], writes=["opwb%d" % i])
                for ti, (s, n) in enumerate(TILES):
                    c = m * 5 + ti
                    ps = self.psum[c % 2]; pn = "ps%d" % (c % 2)
                    for k in range(8):
                        P.pe(lambda e, ps=ps, w=w, k=k, s=s, n=n: e.matmul(ps[:, 0:n], w[:, k, :], ymix[:, k, s:s + n], start=(k == 0), stop=(k == 7)),
                             reads=["opwb%d" % i, "ymix%d_%d" % (k, ti)], writes=[pn])
                    P.dve(lambda e, ps=ps, m=m, s=s, n=n: e.tensor_tensor(out=xT[:, m, s:s + n], in0=ps[:, 0:n], in1=xT[:, m, s:s + n], op=ALU.add),
                          reads=[pn, "xT%d_%d" % (m, ti)], writes=["xT%d_%d" % (m, ti)])
            P.flush()

    def final(self):
        P = self.P
        xT, onesb, gains, epsc, ident = self.xT, self.onesb, self.gains, self.epsc, self.ident
        with contextlib.ExitStack() as st:
            sq = P.sb("fsq", [128, 8, 512], BF16, stack=st)
            rstd = P.sb("frstd", [128, 512], F32, stack=st)
            xo = P.sb("fxo", [128, 8, 512], F32, stack=st)
            ytok = [P.sb("fytok%d" % i, [128, D], F32, stack=st) for i in range(2)]
            nblk = 0
            for ti, (s, n) in enumerate(TILES):
                ps = self.psum[6]; pn = "ps6"
                for k in range(8):
                    P.act(lambda e, k=k, s=s, n=n: e.activation(out=sq[:, k, 0:n], in_=xT[:, k, s:s + n], func=AF.Square),
                          reads=["xT%d_%d" % (k, ti)], writes=["fsq_%d" % k])
                for k in range(8):
                    P.pe(lambda e, k=k, n=n: e.matmul(ps[:, 0:n], onesb[:], sq[:, k, 0:n], start=(k == 0), stop=(k == 7)),
                         reads=["fsq_%d" % k, "onesb"], writes=[pn])
                P.act(lambda e, n=n: e.activation(out=rstd[:, 0:n], in_=ps[:, 0:n], func=AF.Sqrt, bias=epsc[:], scale=1.0 / D), reads=[pn, "epsc"], writes=["frstd"])
                P.dve(lambda e, n=n: e.reciprocal(out=rstd[:, 0:n], in_=rstd[:, 0:n]), reads=["frstd"], writes=["frstd"])
                for k in range(8):
                    P.dve(lambda e, k=k, s=s, n=n: e.scalar_tensor_tensor(out=xo[:, k, 0:n], in0=xT[:, k, s:s + n], scalar=gains[:, 6, k:k + 1], in1=rstd[:, 0:n],
                                                                        op0=ALU.mult, op1=ALU.mult),
                          reads=["xT%d_%d" % (k, ti), "frstd", "gains"], writes=["fxo_%d" % k])
                for b0 in range(0, n, 128):
                    nb = min(128, n - b0)
                    yt = ytok[nblk % 2]; ytn = "fytok%d" % (nblk % 2)
                    nblk += 1
                    for half in range(2):
                        pt = self.psum[half * 2 + (nblk % 2)]; ptn = "ps%d" % (half * 2 + (nblk % 2))
                        for kk in range(4):
                            k = half * 4 + kk
                            P.pe(lambda e, pt=pt, k=k, kk=kk, b0=b0, nb=nb: e.transpose(pt[0:nb, kk * 128:(kk + 1) * 128], xo[:, k, b0:b0 + nb], ident[:]),
                                 reads=["fxo_%d" % k, "ident"], writes=[ptn])
                        if half == 0:
                            P.act(lambda e, pt=pt, yt=yt, nb=nb: e.activation(out=yt[0:nb, 0:512], in_=pt[0:nb, :], func=AF.Copy), reads=[ptn], writes=[ytn])
                        else:
                            P.dve(lambda e, pt=pt, yt=yt, nb=nb: e.tensor_copy(out=yt[0:nb, 512:1024], in_=pt[0:nb, :]), reads=[ptn], writes=[ytn])
                    if s < TP:
                        dst = self.dram["y_p"][s + b0:s + b0 + nb, :]
                    else:
                        dst = self.dram["y_s"][b0:b0 + nb, :]
                    P.dma(lambda e, dst=dst, yt=yt, nb=nb: e.dma_start(out=dst, in_=yt[0:nb, :]), reads=[ytn])
            P.flush()

    def layer(self, l):
        P = self.P
        self.ffn(l, 1, self.dram["ffn1_in"][l], self.dram["ffn1_out"][l])
        with contextlib.ExitStack() as st:
            self.rmsnorm(l * 3 + 1, st)
            P.flush()
        with contextlib.ExitStack() as st:
            self.ymix = P.sb("ymix", [128, 8, T], BF16, stack=st)
            ymix = self.ymix
            done = set()
            if self.stage >= 3:
                self.mix_pool(l); done |= {6, 7}
            if self.stage >= 4:
                self.mix_s5(l); done |= {4, 5}
            if self.stage >= 5:
                self.mix_ssd(l); done |= {0, 1}
            if self.stage >= 6:
                self.mix_rwkv(l); done |= {2, 3}
            else:
                self.mix_rwkv_stub(l)
            for k in range(8):
                if k not in done:
                    P.pool(lambda e, k=k: e.memset(ymix[:, k, :], 0.0), writes=["ymix%d_%d" % (k, ti) for ti in range(5)])
            self.outproj(l)
        self.ffn(l, 2, self.dram["ffn2_in"][l], self.dram["ffn2_out"][l])

    def build(self):
        self.declare()
        self.setup()
        self.load_x()
        for l in range(DEPTH):
            self.layer(l)
        self.final()
        self.P.finish()


def build_nc(stage):
    nc = bass.Bass("TRN2", target_bir_lowering=False)
    with contextlib.ExitStack() as stack:
        stack.enter_context(nc.allow_non_contiguous_dma(reason="small strided parameter/state transfers"))
        b = Builder(nc, stack, stage)
        b.build()
    return nc


IN_NAMES = ["norm_ffn1", "ffn1_in", "ffn1_out", "norm_mix", "w_in", "ssd_conv_w", "ssd_conv_b", "ssd_dt_bias",
            "ssd_a_log", "ssd_d", "ssd_norm", "rwkv_mu", "rwkv_w0", "rwkv_w2", "rwkv_a0", "rwkv_a2", "rwkv_g2",
            "rwkv_k_k", "rwkv_k_a", "rwkv_r_k", "rwkv_ln_g", "rwkv_ln_b", "s5_lam_re", "s5_lam_im", "s5_log_step",
            "s5_b_re", "s5_b_im", "s5_c_re", "s5_c_im", "s5_d", "s5_glu_w", "s5_glu_b", "pool_w", "pool_scale",
            "w_out", "norm_ffn2", "ffn2_in", "ffn2_out", "norm_final"]
STATE_MAP = [("state_ssd_conv", "st_conv"), ("state_ssd", "st_ssd"), ("state_rwkv_shift", "st_shift"),
             ("state_rwkv", "st_rwkv"), ("state_s5_re", "st_s5re"), ("state_s5_im", "st_s5im"), ("state_pool", "st_pool")]
OUT_ORDER = [("y_p", "y_s"), ("conv_p", "conv_s"), ("ssd_p", "ssd_s"), ("shift_p", "shift_s"), ("rwkv_p", "rwkv_s"),
             ("s5re_p", "s5re_s"), ("s5im_p", "s5im_s"), ("pool_p", "pool_s")]


def host_consts():
    c = {}
    wins = [2, 4, 8, 16]
    cinv = np.zeros((128, 2, 15), np.float32)
    for ch in range(2):
        for p in range(128):
            w = wins[2 * ch + p // 64]
            for t in range(15):
                cinv[p, ch, t] = 1.0 / min(t + 1, w)
    c["c_pool_cinv"] = cinv
    gm = np.zeros((128, 8), np.float32)
    for p in range(128):
        gm[p, p // 16] = 1.0
    c["c_gm"] = gm
    sw = np.zeros((128, 128), np.float32)
    for k in range(128):
        sw[k, (k + 64) % 128] = 1.0
    c["c_swap"] = sw
    cm = np.ones((4, T), np.float32)
    cm[:, 0:TP:128] = 0.0
    cm[:, TP:T:TS] = 0.0
    c["c_cmask"] = cm
    ii = np.arange(128)
    c["c_mask128"] = (ii[None, :] >= ii[:, None]).astype(np.float32)
    c["c_U128"] = (ii[:, None] > ii[None, :]).astype(np.float32)
    c["c_L128"] = (ii[:, None] <= ii[None, :]).astype(np.float32)
    el = np.zeros((128, 128), np.float32); el[127, :] = 1.0
    c["c_elast"] = el
    i6 = np.arange(64)
    same = (i6[:, None] // TS) == (i6[None, :] // TS)
    c["c_maskS"] = (same & (i6[None, :] >= i6[:, None])).astype(np.float32)
    c["c_US"] = (same & (i6[:, None] > i6[None, :])).astype(np.float32)
    c["c_LS"] = (same & (i6[:, None] <= i6[None, :])).astype(np.float32)
    c["c_seqm"] = ((i6[:, None] // TS) == np.arange(NSEQ)[None, :]).astype(np.float32)
    sel = np.zeros((4, 2, 128), np.float32)
    for cc in range(2):
        for m in range(128):
            sel[2 * cc + m // 64, cc, m] = 1.0
    c["c_sel"] = sel
    c["c_bdones"] = ((ii[:, None] // 64) == (ii[None, :] // 64)).astype(np.float32)
    I2 = np.zeros((128, 64), np.float32)
    for p in range(128):
        I2[p, p % 64] = 1.0
    c["c_I2"] = I2
    Es = np.zeros((64, 2, 128), np.float32)
    Eh = np.zeros((128, 2, 64), np.float32)
    for i in range(64):
        for hp in range(2):
            Es[i, hp, 64 * hp + i] = 1.0
            Eh[64 * hp + i, hp, i] = 1.0
    c["c_Esel"] = Es
    hp_ = np.zeros((128, 2), np.float32); hp_[:64, 0] = 1.0; hp_[64:, 1] = 1.0
    c["c_hpsel"] = hp_
    c["c_Ehp"] = Eh
    return c


def kernel(stage=6, **inputs):
    f = lambda a: np.ascontiguousarray(np.asarray(a, dtype=np.float32))
    shared = {nm: f(inputs[nm]) for nm in IN_NAMES}
    shared.update(host_consts())
    in_maps = []
    for c in range(NCORES):
        m = dict(shared)
        m["xp"] = f(inputs["x_prompt"][c])
        m["xs"] = f(inputs["x_sample"][c * NSEQ:(c + 1) * NSEQ]).reshape(TSAMP, D)
        for src, dst in STATE_MAP:
            m[dst] = f(inputs[src][:, c * NSEQ:(c + 1) * NSEQ])
        in_maps.append(m)
    nc = build_nc(stage)
    import os
    if os.environ.get("KTRACE"):
        res = run_bass_kernel_spmd(nc, in_maps, core_ids=list(range(NCORES)), trace=True)
        print("EXEC_TIME_NS", res.exec_time_ns)
    else:
        res = run_bass_kernel_spmd(nc, in_maps, core_ids=list(range(NCORES)))
    R = res.results
    global DBG
    DBG = None
    outs = []
    yp = np.stack([R[c]["y_p"] for c in range(NCORES)], 0)
    ys = np.concatenate([R[c]["y_s"].reshape(NSEQ, TS, D) for c in range(NCORES)], 0)
    outs += [yp, ys]
    for pn, sn in OUT_ORDER[1:]:
        p = np.stack([R[c][pn] for c in range(NCORES)], 1)
        s = np.concatenate([R[c][sn] for c in range(NCORES)], 1)
        outs += [p, s]
    return tuple(np.ascontiguousarray(o, dtype=np.float32) for o in outs)
```
